# Optimizing a Trainium2 kernel written in Bass

```python
import math
import jax, jax.numpy as jnp
from jax import lax
import numpy as np

D_MODEL = 1024
BATCH = 4
SEQ = 8192
DEPTH = 4
DEC_BATCH = 8
DEC_SEQ = 16
PAST_LEN = 2048

CHUNK = 64
D_MIX = D_MODEL
BRANCH = D_MIX // 4
NORM_EPS = 1e-6
RW_HEADS = 4
RW_HD = BRANCH // RW_HEADS
RW_LORA_W = 64
RW_LORA_A = 64
RW_SHIFT = 3 * BRANCH + RW_LORA_W + RW_LORA_A
RW_GN_EPS = 64e-5
MLA_HEADS = 4
NOPE = 64
ROPE = 32
VHD = BRANCH // MLA_HEADS
QK_HD = NOPE + ROPE
Q_RANK = 192
KV_RANK = 128
ROPE_THETA = 10000.0
MLA_QBLOCK = 128
SGU_CHUNK = 128
SGU_HEADS = 4
SGU_HD = BRANCH // SGU_HEADS
SGU_EPS = 1e-5
POOL_GROUPS = 4
POOL_WINDOWS = (2, 4, 8, 16)
POOL_GD = BRANCH // POOL_GROUPS
POOL_HIST = max(POOL_WINDOWS) - 1
OFF_A = 0
OFF_B = OFF_A + RW_SHIFT
OFF_C = OFF_B + Q_RANK + KV_RANK + ROPE
OFF_D = OFF_C + 2 * BRANCH
OFF_G = OFF_D + BRANCH
D_IN = OFF_G + 4 * BRANCH

kernel_name = 'hymba_style_streaming_rwkv7_mla_sgu_pool'


def rms_norm(x, g, eps=NORM_EPS):
    xf = x.astype(jnp.float32)
    y = xf * lax.rsqrt(jnp.mean(xf * xf, axis=-1, keepdims=True) + eps)
    return (y * g.astype(jnp.float32)).astype(x.dtype)


def layer_norm(x, g, b, eps):
    xf = x.astype(jnp.float32)
    xc = xf - jnp.mean(xf, axis=-1, keepdims=True)
    var = jnp.mean(xc * xc, axis=-1, keepdims=True)
    return (xc * lax.rsqrt(var + eps) * g.astype(jnp.float32) + b.astype(jnp.float32)).astype(x.dtype)


def rope(x, pos):
    half = ROPE // 2
    inv = ROPE_THETA ** (-jnp.arange(half, dtype=jnp.float32) / half)
    ang = pos.astype(jnp.float32)[:, None] * inv[None, :]
    shape = (pos.shape[0],) + (1,) * (x.ndim - 3) + (half,)
    cos = jnp.cos(ang).reshape(shape)
    sin = jnp.sin(ang).reshape(shape)
    xf = x.astype(jnp.float32)
    x1, x2 = xf[..., :half], xf[..., half:]
    return jnp.concatenate([x1 * cos - x2 * sin, x1 * sin + x2 * cos], axis=-1).astype(x.dtype)


def rwkv7_mix(za, shift_prev, s0, p):
    f32 = jnp.float32
    b, t, _ = za.shape
    prev = jnp.concatenate([shift_prev[:, None, :].astype(za.dtype), za[:, :-1]], axis=1)
    zs = (za + p['rw_mu'] * (prev - za)).astype(f32)
    r = zs[..., :BRANCH]
    k = zs[..., BRANCH:2 * BRANCH]
    v = zs[..., 2 * BRANCH:3 * BRANCH]
    wd = zs[..., 3 * BRANCH:3 * BRANCH + RW_LORA_W]
    ad = zs[..., 3 * BRANCH + RW_LORA_W:]
    w_log = -jax.nn.softplus(-(p['rw_w0'].astype(f32) + jnp.tanh(wd) @ p['rw_w2'].astype(f32))) - 0.5
    decay = jnp.exp(-jnp.exp(w_log))
    a = jax.nn.sigmoid(p['rw_a0'].astype(f32) + ad @ p['rw_a2'].astype(f32))
    heads = lambda u: u.reshape(b, t, RW_HEADS, RW_HD)
    kk = heads(k * p['rw_kk'].astype(f32))
    kk = kk * lax.rsqrt(jnp.maximum(jnp.sum(kk * kk, axis=-1, keepdims=True), 1e-24))
    k = k * (1.0 + (a - 1.0) * p['rw_ka'].astype(f32))
    r, k, v, decay, a = heads(r), heads(k), heads(v), heads(decay), heads(a)

    def step(s, inp):
        r_t, w_t, k_t, v_t, kk_t, a_t = inp
        sa = jnp.einsum('bhij,bhj->bhi', s, -kk_t)
        s = (s * w_t[:, :, None, :] + sa[:, :, :, None] * (kk_t * a_t)[:, :, None, :]
             + v_t[:, :, :, None] * k_t[:, :, None, :])
        return s, jnp.einsum('bhij,bhj->bhi', s, r_t)

    xs = tuple(jnp.moveaxis(u, 1, 0) for u in (r, decay, k, v, kk, a))
    s_t, ys = lax.scan(step, s0.astype(f32), xs)
    y = layer_norm(jnp.moveaxis(ys, 0, 1), p['rw_gn_g'], p['rw_gn_b'], RW_GN_EPS)
    bonus = jnp.sum(r * k * p['rw_rk'].astype(f32), axis=-1, keepdims=True) * v
    out = (y + bonus).reshape(b, t, BRANCH).astype(za.dtype)
    return out, za[:, -1], s_t.astype(za.dtype)


def mla_project(zb, pos, p):
    b, t, _ = zb.shape
    qa = rms_norm(zb[..., :Q_RANK], p['mla_qa_g'])
    q = (qa @ p['mla_w_uq']).reshape(b, t, MLA_HEADS, QK_HD)
    q_nope = rms_norm(q[..., :NOPE], p['mla_q_norm_g'][:NOPE])
    q_rope = rope(rms_norm(q[..., NOPE:], p['mla_q_norm_g'][NOPE:]), pos)
    ckv = rms_norm(zb[..., Q_RANK:Q_RANK + KV_RANK], p['mla_kva_g'])
    krope = rope(rms_norm(zb[..., Q_RANK + KV_RANK:], p['mla_k_norm_g'][NOPE:]), pos)
    return q_nope, q_rope, ckv, krope


def mla_keys(ckv, p):
    b, n, _ = ckv.shape
    k_nope = rms_norm((ckv @ p['mla_w_uk']).reshape(b, n, MLA_HEADS, NOPE), p['mla_k_norm_g'][:NOPE])
    v = (ckv @ p['mla_w_uv']).reshape(b, n, MLA_HEADS, VHD)
    return k_nope, v


def mla_attend(q_nope, q_rope, qpos, k_nope, krope, v, kpos):
    s = (jnp.einsum('bqhd,bkhd->bhqk', q_nope, k_nope).astype(jnp.float32)
         + jnp.einsum('bqhd,bkd->bhqk', q_rope, krope).astype(jnp.float32)) * (1.0 / math.sqrt(QK_HD))
    mask = (qpos // CHUNK)[:, None] >= (kpos // CHUNK)[None, :]
    s = jnp.where(mask[None, None], s, -1e30)
    pr = jax.nn.softmax(s, axis=-1).astype(v.dtype)
    return jnp.einsum('bhqk,bkhd->bqhd', pr, v)


def mla_prompt(zb, p):
    b, t, _ = zb.shape
    pos = jnp.arange(t)
    q_nope, q_rope, ckv, krope = mla_project(zb, pos, p)
    k_nope, v = mla_keys(ckv, p)
    nb = t // MLA_QBLOCK
    blk = lambda u: jnp.moveaxis(u.reshape((b, nb, MLA_QBLOCK) + u.shape[2:]), 1, 0)
    o = lax.map(lambda a: mla_attend(a[0], a[1], a[2], k_nope, krope, v, pos),
                (blk(q_nope), blk(q_rope), pos.reshape(nb, MLA_QBLOCK)))
    o = jnp.moveaxis(o, 0, 1).reshape(b, t, BRANCH)
    return o, ckv, krope


def mla_sample(zb, cache_ckv, cache_krope, p):
    b, t, _ = zb.shape
    past = cache_ckv.shape[1]
    pos = past + jnp.arange(t)
    q_nope, q_rope, ckv, krope = mla_project(zb, pos, p)
    ckv_all = jnp.concatenate([cache_ckv.astype(ckv.dtype), ckv], axis=1)
    kr_all = jnp.concatenate([cache_krope.astype(krope.dtype), krope], axis=1)
    k_nope, v = mla_keys(ckv_all, p)
    o = mla_attend(q_nope, q_rope, pos, k_nope, kr_all, v, jnp.arange(past + t))
    return o.reshape(b, t, BRANCH), ckv, krope


def sgu_mix(zc, p):
    b, t, _ = zc.shape
    u = zc[..., :BRANCH]
    vn = layer_norm(zc[..., BRANCH:], p['sgu_ln_g'], p['sgu_ln_b'], SGU_EPS)
    nc = -(-t // SGU_CHUNK)
    tp = nc * SGU_CHUNK
    vp = jnp.pad(vn, ((0, 0), (0, tp - t), (0, 0))).reshape(b, nc, SGU_CHUNK, SGU_HEADS, SGU_HD)
    tri = jnp.tril(jnp.ones((SGU_CHUNK, SGU_CHUNK), dtype=bool))
    ws = jnp.where(tri[None], p['sgu_w'], 0.0).astype(vp.dtype)
    s = jnp.einsum('hij,bnjhc->bnihc', ws, vp) + jnp.transpose(p['sgu_b'])[None, None, :, :, None]
    s = s.reshape(b, tp, BRANCH)[:, :t]
    return u * s, vn


def pool_mix(zd, hist, p):
    b, t, _ = zd.shape
    xin = zd if hist is None else jnp.concatenate([hist.astype(zd.dtype), zd], axis=1)
    n_len = xin.shape[1]
    n_hist = n_len - t
    xf = xin.astype(jnp.float32).reshape(b, n_len, POOL_GROUPS, POOL_GD)
    c = jnp.concatenate([jnp.zeros((b, 1, POOL_GROUPS, POOL_GD), jnp.float32), jnp.cumsum(xf, axis=1)], axis=1)
    j = n_hist + jnp.arange(t)
    win = jnp.array(POOL_WINDOWS, dtype=jnp.int32)
    lo = jnp.maximum(j[:, None] + 1 - win[None, :], 0)
    ssum = c[:, j + 1] - c[:, lo, jnp.arange(POOL_GROUPS)[None, :]]
    cnt = jnp.minimum(j[:, None] + 1, win[None, :]).astype(jnp.float32)
    d = ssum / cnt[None, :, :, None] - xf[:, n_hist:]
    out = jnp.einsum('btgc,gcd->btgd', d, p['pool_w'].astype(jnp.float32))
    out = out.reshape(b, t, BRANCH) * p['pool_scale'].astype(jnp.float32)
    return out.astype(zd.dtype), xin[:, n_len - POOL_HIST:]


def trunk_layer(x, p, hist):
    b, t, _ = x.shape
    z = rms_norm(x, p['norm_g']) @ p['w_in']
    za, zb = z[..., OFF_A:OFF_B], z[..., OFF_B:OFF_C]
    zc, zd, gates = z[..., OFF_C:OFF_D], z[..., OFF_D:OFF_G], z[..., OFF_G:]
    if hist is None:
        shift0 = jnp.zeros((b, RW_SHIFT), x.dtype)
        s0 = jnp.zeros((b, RW_HEADS, RW_HD, RW_HD), jnp.float32)
        ya, shift_n, wkv_n = rwkv7_mix(za, shift0, s0, p)
        yb, ckv, kr = mla_prompt(zb, p)
        yd, pool_n = pool_mix(zd, None, p)
    else:
        c_ckv, c_kr, s_wkv, s_shift, s_pool = hist
        ya, shift_n, wkv_n = rwkv7_mix(za, s_shift, s_wkv, p)
        yb, ckv, kr = mla_sample(zb, c_ckv, c_kr, p)
        yd, pool_n = pool_mix(zd, s_pool, p)
    yc, vn = sgu_mix(zc, p)
    y = jnp.concatenate([ya, yb, yc.astype(x.dtype), yd], axis=-1) * jax.nn.silu(gates)
    out = x + y @ p['w_out']
    return out, (ckv, kr, wkv_n, shift_n, pool_n, vn)


def setup_inputs(seed: int = 0) -> dict:
    key = jax.random.key(seed)
    ks = iter(jax.random.split(key, 40))
    nrm = lambda shape, s: s * jax.random.normal(next(ks), shape, jnp.float32)
    one = lambda shape, s: 1.0 + s * jax.random.normal(next(ks), shape, jnp.float32)
    return {
        'x_prompt': nrm((BATCH, SEQ, D_MODEL), 1.0),
        'x_sample': nrm((DEC_BATCH, DEC_SEQ, D_MODEL), 1.0),
        'cache_ckv': nrm((DEPTH, DEC_BATCH, PAST_LEN, KV_RANK), 1.0),
        'cache_krope': nrm((DEPTH, DEC_BATCH, PAST_LEN, ROPE), 1.0),
        'state_wkv': nrm((DEPTH, DEC_BATCH, RW_HEADS, RW_HD, RW_HD), 1.0),
        'state_shift': nrm((DEPTH, DEC_BATCH, RW_SHIFT), 1.0),
        'state_pool': nrm((DEPTH, DEC_BATCH, POOL_HIST, BRANCH), 1.0),
        'norm_g': one((DEPTH, D_MODEL), 0.1),
        'w_in': nrm((DEPTH, D_MODEL, D_IN), D_MODEL ** -0.5),
        'w_out': nrm((DEPTH, D_MIX, D_MODEL), 0.5 * D_MIX ** -0.5),
        'rw_mu': jax.random.uniform(next(ks), (DEPTH, RW_SHIFT), jnp.float32, 0.1, 0.9),
        'rw_w0': nrm((DEPTH, BRANCH), 0.5),
        'rw_w2': nrm((DEPTH, RW_LORA_W, BRANCH), 0.5 * RW_LORA_W ** -0.5),
        'rw_a0': nrm((DEPTH, BRANCH), 0.5),
        'rw_a2': nrm((DEPTH, RW_LORA_A, BRANCH), 0.5 * RW_LORA_A ** -0.5),
        'rw_kk': one((DEPTH, BRANCH), 0.1),
        'rw_ka': one((DEPTH, BRANCH), 0.1),
        'rw_rk': nrm((DEPTH, RW_HEADS, RW_HD), 0.1),
        'rw_gn_g': one((DEPTH, RW_HEADS, RW_HD), 0.1),
        'rw_gn_b': nrm((DEPTH, RW_HEADS, RW_HD), 0.02),
        'mla_qa_g': one((DEPTH, Q_RANK), 0.1),
        'mla_w_uq': nrm((DEPTH, Q_RANK, MLA_HEADS * QK_HD), Q_RANK ** -0.5),
        'mla_kva_g': one((DEPTH, KV_RANK), 0.1),
        'mla_w_uk': nrm((DEPTH, KV_RANK, MLA_HEADS * NOPE), KV_RANK ** -0.5),
        'mla_w_uv': nrm((DEPTH, KV_RANK, MLA_HEADS * VHD), KV_RANK ** -0.5),
        'mla_q_norm_g': one((DEPTH, QK_HD), 0.1),
        'mla_k_norm_g': one((DEPTH, QK_HD), 0.1),
        'sgu_w': nrm((DEPTH, SGU_HEADS, SGU_CHUNK, SGU_CHUNK), 0.5 * SGU_CHUNK ** -0.5),
        'sgu_b': one((DEPTH, SGU_HEADS, SGU_CHUNK), 0.1),
        'sgu_ln_g': one((DEPTH, BRANCH), 0.1),
        'sgu_ln_b': nrm((DEPTH, BRANCH), 0.02),
        'pool_w': nrm((DEPTH, POOL_GROUPS, POOL_GD, POOL_GD), POOL_GD ** -0.5),
        'pool_scale': one((DEPTH, BRANCH), 0.1),
    }


def reference(x_prompt, x_sample, cache_ckv, cache_krope, state_wkv, state_shift, state_pool,
              norm_g, w_in, w_out, rw_mu, rw_w0, rw_w2, rw_a0, rw_a2, rw_kk, rw_ka, rw_rk,
              rw_gn_g, rw_gn_b, mla_qa_g, mla_w_uq, mla_kva_g, mla_w_uk, mla_w_uv,
              mla_q_norm_g, mla_k_norm_g, sgu_w, sgu_b, sgu_ln_g, sgu_ln_b, pool_w, pool_scale):
    yp, ys = x_prompt, x_sample
    new_p = [[] for _ in range(5)]
    new_s = [[] for _ in range(6)]
    for l in range(DEPTH):
        p = {
            'norm_g': norm_g[l], 'w_in': w_in[l], 'w_out': w_out[l],
            'rw_mu': rw_mu[l], 'rw_w0': rw_w0[l], 'rw_w2': rw_w2[l], 'rw_a0': rw_a0[l],
            'rw_a2': rw_a2[l], 'rw_kk': rw_kk[l], 'rw_ka': rw_ka[l], 'rw_rk': rw_rk[l],
            'rw_gn_g': rw_gn_g[l], 'rw_gn_b': rw_gn_b[l],
            'mla_qa_g': mla_qa_g[l], 'mla_w_uq': mla_w_uq[l], 'mla_kva_g': mla_kva_g[l],
            'mla_w_uk': mla_w_uk[l], 'mla_w_uv': mla_w_uv[l],
            'mla_q_norm_g': mla_q_norm_g[l], 'mla_k_norm_g': mla_k_norm_g[l],
            'sgu_w': sgu_w[l], 'sgu_b': sgu_b[l], 'sgu_ln_g': sgu_ln_g[l], 'sgu_ln_b': sgu_ln_b[l],
            'pool_w': pool_w[l], 'pool_scale': pool_scale[l],
        }
        yp, st_p = trunk_layer(yp, p, None)
        ys, st_s = trunk_layer(ys, p, (cache_ckv[l], cache_krope[l], state_wkv[l], state_shift[l], state_pool[l]))
        for lst, arr in zip(new_p, st_p[:5]):
            lst.append(arr)
        for lst, arr in zip(new_s, st_s):
            lst.append(arr)
    ckv_p, kr_p, wkv_p, shift_p, pool_p = [jnp.stack(a, axis=0) for a in new_p]
    ckv_s, kr_s, wkv_s, shift_s, pool_s, sgu_v_s = [jnp.stack(a, axis=0) for a in new_s]
    return (yp, ys, ckv_p, kr_p, wkv_p, shift_p, pool_p, ckv_s, kr_s, wkv_s, shift_s, pool_s, sgu_v_s)
```

```python
import numpy as np
from contextlib import ExitStack
import concourse.bass as bass
import concourse.mybir as mybir
from concourse.bass_utils import run_bass_kernel_spmd

F32 = mybir.dt.float32
BF16 = mybir.dt.bfloat16
AF = mybir.ActivationFunctionType
ALU = mybir.AluOpType
AX = mybir.AxisListType

D = 1024
DIN = 3040
OFF_A, OFF_B, OFF_C, OFF_D, OFF_G = 0, 896, 1248, 1760, 2016
PAST = 2048
TS = 16
WINS = (2, 4, 8, 16)
CDEC = -0.6065306597126334
ILV = 2
ILV2 = 2


class Buf:
    def __init__(self, name):
        self.name = name
        self.w = None
        self.r = []


class TB:
    def __init__(self, t, name):
        self.t = t
        self.b = Buf(name)

    def __getitem__(self, k):
        return self.t[k]


class Eng:
    def __init__(self, S, name, eng):
        self.name = name
        self.eng = eng
        self.sem = S.newsem("e_" + name)
        self.cnt = 0
        self.waited = {}

    def wait_tok(self, tok):
        if tok is None:
            return
        sem, val = tok
        if sem is self.sem and self.name == "pe":
            return
        key = id(sem)
        if self.waited.get(key, 0) >= val:
            return
        self.eng.wait_ge(sem, val)
        self.waited[key] = val


class Sched:
    def __init__(self, nc, ctx):
        self.nc = nc
        self.ctx = ctx
        self.pe = Eng(self, "pe", nc.tensor)
        self.act = Eng(self, "act", nc.scalar)
        self.dve = Eng(self, "dve", nc.vector)
        self.pool = Eng(self, "pool", nc.gpsimd)
        self.sp = Eng(self, "sp", nc.sync)
        self.engs = [self.pe, self.act, self.dve, self.pool, self.sp]
        self.dma_sems = {}
        self.ninst = 0

    def newsem(self, name):
        return self.ctx.enter_context(self.nc.semaphore(name))

    def _bufs(self, L):
        return [x.b if isinstance(x, TB) else x for x in L]

    def deps(self, E, R, W):
        for b in R:
            E.wait_tok(b.w)
        for b in W:
            E.wait_tok(b.w)
            for t in (b.r.values() if isinstance(b.r, dict) else b.r):
                E.wait_tok(t)

    def _mark(self, tok, R, W):
        for b in W:
            b.w = tok
            b.r = {}
        for b in R:
            if isinstance(b.r, list):
                b.r = {}
            k = id(tok[0])
            if k not in b.r or b.r[k][1] < tok[1]:
                b.r[k] = tok

    enabled = True

    def ck(self, name):
        import os
        if os.environ.get("KSTOP") == name:
            self.enabled = False

    def op(self, E, fn, *args, R=(), W=(), **kw):
        if not self.enabled:
            return None
        R = self._bufs(R)
        W = self._bufs(W)
        self.deps(E, R, W)
        ins = getattr(E.eng, fn)(*args, **kw)
        E.cnt += 1
        ins.then_inc(E.sem, 1)
        self.ninst += 1
        self._mark((E.sem, E.cnt), R, W)
        return ins

    def dma(self, out, in_, R=(), W=(), key=None, Q=None, **kw):
        if not self.enabled:
            return None
        Q = Q or self.sp
        R = self._bufs(R)
        W = self._bufs(W)
        kb = key.b if isinstance(key, TB) else key
        self.deps(Q, R, W)
        if kb.name not in self.dma_sems:
            self.dma_sems[kb.name] = [self.newsem("d_" + kb.name), 0]
        ent = self.dma_sems[kb.name]
        ins = Q.eng.dma_start(out=out, in_=in_, **kw)
        ent[1] += 16
        ins.then_inc(ent[0], 16)
        self.ninst += 1
        self._mark((ent[0], ent[1]), R, W)
        return ins

    def all_tokens(self):
        toks = [(e.sem, e.cnt) for e in self.engs if e.cnt > 0]
        toks += [(s, v) for (s, v) in self.dma_sems.values()]
        return toks

    def barrier(self):
        toks = self.all_tokens()
        for e in self.engs:
            for t in toks:
                e.wait_tok(t)

    def final_wait(self):
        for (s, v) in self.dma_sems.values():
            self.sp.wait_tok((s, v))


def make_consts():
    c = {}
    i = np.arange(128)
    s = i[:, None]
    t = i[None, :]
    ident = (s == t).astype(np.float32)
    tri_ui = (s <= t).astype(np.float32)
    tri_ut = (s < t).astype(np.float32)
    tri_lt = (s > t).astype(np.float32)
    bands = []
    bandp = []
    for w in WINS:
        bands.append(((s <= t) & (s > t - w)).astype(np.float32))
        bandp.append(((s - 128) > (t - w)).astype(np.float32))
    ones = np.ones((128, 128), np.float32)
    onehot_last = np.zeros((128, 2), np.float32)
    onehot_last[127, 0] = 1.0
    onehot_last[15, 1] = 1.0
    cf = np.concatenate([ident, tri_ui, ones] + bands + bandp + [onehot_last], axis=1)
    sh = (t == s + 1).astype(np.float32)
    elast = np.zeros((128, 128), np.float32)
    elast[127, 0] = 1.0
    cb = np.concatenate([ident, tri_lt, tri_ut, tri_lt, tri_ui, sh, elast], axis=1)
    invc = np.zeros((128, 24), np.float32)
    invc[:, 8:11] = np.array([1 / 192.0, 1 / 128.0, 1 / 32.0], np.float32)
    invc[:, 11:15] = 1 / 64.0
    invc[:, 15:19] = 1 / 32.0
    for g, w in enumerate(WINS):
        invc[:, g] = 1.0 / np.minimum(i + 1, w)
        invc[:, 4 + g] = 1.0 / w
    return cf.astype(np.float32), cb.astype(np.float32), invc


def rope_table(pos):
    half = 16
    inv = (10000.0 ** (-np.arange(half, dtype=np.float32) / half)).astype(np.float32)
    ang = pos.astype(np.float32)[:, None] * inv[None, :]
    cos = np.cos(ang).astype(np.float32)
    sin = np.sin(ang).astype(np.float32)
    return np.concatenate([np.tile(cos, (1, 4)), np.tile(sin, (1, 4))], axis=1).astype(np.float32)


CF_ID, CF_TRI, CF_ONES, CF_BAND, CF_BANDP, CF_OH = 0, 128, 256, 384, 896, 1408
CB_ID, CB_M3, CB_UI, CB_SH, CB_EL = 0, 128, 512, 640, 768

WNAMES = ["norm_g", "w_in", "w_out", "rw_mu", "rw_w0", "rw_w2", "rw_a0", "rw_a2", "rw_kk", "rw_ka", "rw_rk",
          "rw_gn_g", "rw_gn_b", "mla_qa_g", "mla_w_uq", "mla_kva_g", "mla_w_uk", "mla_w_uv",
          "mla_q_norm_g", "mla_k_norm_g", "sgu_w", "sgu_b", "sgu_ln_g", "sgu_ln_b", "pool_w", "pool_scale"]
WSHAPES = {
    "norm_g": [D], "w_in": [D, DIN], "w_out": [D, D], "rw_mu": [896], "rw_w0": [256], "rw_w2": [64, 256],
    "rw_a0": [256], "rw_a2": [64, 256], "rw_kk": [256], "rw_ka": [256], "rw_rk": [256], "rw_gn_g": [256],
    "rw_gn_b": [256], "mla_qa_g": [192], "mla_w_uq": [192, 384], "mla_kva_g": [128], "mla_w_uk": [128, 256],
    "mla_w_uv": [128, 256], "mla_q_norm_g": [96], "mla_k_norm_g": [96], "sgu_w": [4, 128, 128], "sgu_b": [4, 128],
    "sgu_ln_g": [256], "sgu_ln_b": [256], "pool_w": [4, 64, 64], "pool_scale": [256],
}


def build(T, NL, do_sample=True):
    nc = bass.Bass("TRN2", target_bir_lowering=False)
    ctx = ExitStack()
    NT = T // 128

    def din(name, shape, dt=F32):
        return nc.dram_tensor(name, list(shape), dt, kind="ExternalInput").ap()

    def dout(name, shape, dt=F32):
        return nc.dram_tensor(name, list(shape), dt, kind="ExternalOutput").ap()

    def dscr(name, shape, dt):
        return nc.dram_tensor(name, list(shape), dt, kind="Internal").ap()

    x_p = din("x_p", [T, D])
    x_s = din("x_s", [TS, D])
    c_ckv = din("c_ckv", [NL, PAST, 128])
    c_kr = din("c_kr", [NL, PAST, 32])
    st_wkv = din("st_wkv", [NL, 4, 64, 64])
    st_shift = din("st_shift", [NL, 896])
    st_pool = din("st_pool", [NL, 15, 256])
    Wd = {n: din(n, [NL] + WSHAPES[n]) for n in WNAMES}
    cf_d = din("cf", [128, 1410])
    cb_d = din("cb", [128, 896])
    invc_d = din("invc", [128, 24])
    rope_p = din("rope_p", [T, 128])
    rope_s = din("rope_s", [TS, 128])

    o_yp = dout("o_yp", [T, D])
    o_ys = dout("o_ys", [TS, D])
    o_ckvp = dout("o_ckvp", [NL, T, 128])
    o_krp = dout("o_krp", [NL, T, 32])
    o_wkvp = dout("o_wkvp", [NL, 4, 64, 64])
    o_shp = dout("o_shp", [NL, 896])
    o_plp = dout("o_plp", [NL, 15, 256])
    o_ckvs = dout("o_ckvs", [NL, TS, 128])
    o_krs = dout("o_krs", [NL, TS, 32])
    o_wkvs = dout("o_wkvs", [NL, 4, 64, 64])
    o_shs = dout("o_shs", [NL, 896])
    o_pls = dout("o_pls", [NL, 15, 256])
    o_sgv = dout("o_sgv", [NL, TS, 256])

    def scr(tag, TT, TK):
        return dict(
            yg=dscr("yg_" + tag, [TT, 768], BF16), gb=dscr("gb_" + tag, [TT, 256], BF16),
            qt=dscr("qt_" + tag, [4, 96, TT], BF16), kt=dscr("kt_" + tag, [4, 96, TK], BF16),
            v=dscr("v_" + tag, [TK, 260], BF16))
    scr_p = scr("p", T, T)
    scr_s = scr("s", TS, PAST + 128)

    S = Sched(nc, ctx)
    V = lambda fn, *a, **k: S.op(S.dve, fn, *a, **k)
    A = lambda fn, *a, **k: S.op(S.act, fn, *a, **k)
    G = lambda fn, *a, **k: S.op(S.pool, fn, *a, **k)
    M = lambda fn, *a, **k: S.op(S.pe, fn, *a, **k)

    uid = [0]

    def uname(name):
        uid[0] += 1
        return "s%d_%s" % (uid[0], name)

    def sb(name, shape, dt, c=None):
        t = (c or ctx).enter_context(nc.sbuf_tensor(uname(name), list(shape), dt))
        return TB(t, name)

    def ps(name, shape, dt):
        t = ctx.enter_context(nc.psum_tensor(name, list(shape), dt))
        return TB(t, name)

    PF = [ps("pf%d" % i, [128, 512], F32) for i in range(6)]
    PB = [ps("pb%d" % i, [128, 1024], BF16) for i in range(2)]

    cf = sb("cf", [128, 1410], F32)
    cb = sb("cb", [128, 896], BF16)
    cb_stage = sb("cb_stage", [128, 896], F32)
    invc = sb("invc", [128, 24], F32)
    S.dma(cf[:], cf_d[:, :], W=[cf], key=cf)
    S.dma(cb_stage[:], cb_d[:, :], W=[cb_stage], key=cb_stage)
    S.dma(invc[:], invc_d[:, :], W=[invc], key=invc)
    V("tensor_copy", cb[:], cb_stage[:], R=[cb_stage], W=[cb])
    cbr = sb("cbr", [128, 4, 4, 128], BF16)
    for kind, off in enumerate((CB_M3, CB_M3 + 128, CB_UI, CB_ID)):
        for h in range(4):
            V("tensor_copy", cbr[:, kind, h, :], cb[:, off:off + 128], R=[cb], W=[cbr])
    identf = lambda n: cf[0:n, CF_ID:CF_ID + n]
    identb = lambda n: cb[0:n, CB_ID:CB_ID + n]

    wout_b = sb("wout_b", [128, 8, D], BF16)
    wuq_b = sb("wuq_b", [128, 2, 384], BF16)
    wukv_b = sb("wukv_b", [128, 512], BF16)
    w2a2_b = sb("w2a2_b", [128, 512], BF16)
    wsT_b = sb("wsT_b", [128, 4, 128], BF16)
    poolw_b = sb("poolw_b", [128, 2, 128], BF16)
    sgub = sb("sgub", [128, 4], F32)
    ng = sb("ng", [128, 8], F32)
    qag = sb("qag", [128, 2], F32)
    BC_SPEC = [("rw_mu", 896), ("omm", 896), ("rw_w0", 256), ("rw_a0", 256), ("rw_kk", 256), ("rw_ka", 256),
               ("rw_rk", 256), ("rw_gn_g", 256), ("rw_gn_b", 256), ("mla_kva_g", 128), ("mla_q_norm_g", 96),
               ("mla_k_norm_g", 96), ("sgu_ln_g", 256), ("sgu_ln_b", 256)]
    bc_off = {}
    o = 0
    for n, w in BC_SPEC:
        bc_off[n] = (o, w)
        o += w
    bc = sb("bc", [128, o], F32)
    BCv = lambda n, P, a=0, b=None: bc[0:P, bc_off[n][0] + a: bc_off[n][0] + (bc_off[n][1] if b is None else b)]
    stage = sb("stage", [128, 1024], F32)
    stage2 = sb("stage2", [128, 1024], F32)

    WORK = []

    def wsb(name, shape, dt):
        tb = TB(None, name)
        WORK.append((tb, name, list(shape), dt))
        return tb

    def alloc_work(c):
        for tb, name, shape, dt in WORK:
            tb.t = c.enter_context(nc.sbuf_tensor(uname(name), shape, dt))
            tb.b = Buf(name)
        for i in range(4):
            STb[i].w = None
            STb[i].r = {}
        V("memset", vaug[:], 1.0, W=[vaug])
        V("memset", x_f[:], 0.0, W=[x_f])
        V("memset", p_f[:], 0.0, W=[p_f])
        V("memset", qaT[:], 0.0, W=[qaT])

    xt = [wsb("xt%d" % i, [128, D], F32) for i in range(2)]
    ropet = [wsb("ropet%d" % i, [128, 128], F32) for i in range(2)]
    junk = wsb("junk", [128, D], BF16)
    xn_b = wsb("xn_b", [128, D], BF16)
    xnT = wsb("xnT", [128, 8, 128], BF16)
    st1 = wsb("st1", [128, 16], F32)
    st2 = wsb("st2", [128, 16], F32)
    st3 = wsb("st3", [128, 16], F32)
    st2o = wsb("st2o", [128, 16], F32)
    st3o = wsb("st3o", [128, 16], F32)
    w1o = wsb("w1o", [128, 256], F32)
    w2o = wsb("w2o", [128, 64], F32)
    sg = wsb("sg", [128, D], BF16)
    tmpm = [wsb("tmpm%d" % i, [128, 896], BF16) for i in range(2)]
    zs = wsb("zs", [128, 896], F32)
    lin = wsb("lin", [128, 128], BF16)
    linT = wsb("linT", [128, 128], BF16)
    w1 = wsb("w1", [128, 256], F32)
    w2 = wsb("w2", [128, 256], F32)
    w3 = wsb("w3", [128, 256], F32)
    sw = wsb("sw", [128, 256], F32)
    sa = wsb("sa", [128, 256], F32)
    kk = wsb("kk", [128, 256], F32)
    kp = wsb("kp", [128, 256], F32)
    bb = wsb("bb", [128, 256], F32)
    bcf = wsb("bcf", [128, 4], F32)
    cs_sb = wsb("cs_sb", [128, 256], F32)
    e1 = wsb("e1", [128, 256], F32)
    e2 = wsb("e2", [128, 256], F32)
    e3 = wsb("e3", [128, 256], F32)
    e4 = wsb("e4", [128, 256], F32)
    hat = wsb("hat", [128, 4, 256], BF16)
    bp_b = wsb("bp_b", [128, 256], BF16)
    kp4 = wsb("kp4", [128, 256], F32)
    hT = wsb("hT", [128, 4, 2, 128], BF16)
    wc = wsb("wc", [128, 2], F32)
    scL = wsb("scL", [128, 4, 128], BF16)
    scN = wsb("scN", [128, 4, 128], BF16)
    scK = wsb("scK", [128, 4, 128], BF16)
    mbr_b = wsb("mbr_b", [128, 4, 128], BF16)
    mkr_f = wsb("mkr_f", [128, 4, 128], F32)
    Ab = [wsb("Ab%d" % i, [128, 2, 4, 128], BF16) for i in range(2)]
    Ttb = [wsb("Ttb%d" % i, [128, 4, 128], BF16) for i in range(2)]
    G_b = wsb("G_b", [128, 4, 128], BF16)
    H_b = wsb("H_b", [128, 4, 64], BF16)
    x_f = wsb("x_f", [128, 4, 128], F32)
    z_f = wsb("z_f", [128, 4, 128], F32)
    q_f = wsb("q_f", [128, 2, 128], F32)
    p_f = wsb("p_f", [128, 2, 128], F32)
    ST = wsb("ST", [128, 2, 64], F32)
    STb = [Buf("ST%d" % h) for h in range(4)]
    yrw = wsb("yrw", [128, 256], F32)
    ygp = wsb("ygp", [128, 768], BF16)
    zb_f = wsb("zb_f", [128, 384], F32)
    qa_b = wsb("qa_b", [128, 192], BF16)
    qaT = wsb("qaT", [128, 2, 128], BF16)
    qn = wsb("qn", [128, 4, 96], F32)
    qfb = wsb("qfb", [128, 4, 96], BF16)
    qT = wsb("qT", [96, 4, 128], BF16)
    ckv_f = wsb("ckv_f", [128, 128], F32)
    ckv_b = wsb("ckv_b", [128, 128], BF16)
    ckvT = wsb("ckvT", [128, 128], BF16)
    kr_f = wsb("kr_f", [128, 32], F32)
    kr_r = wsb("kr_r", [128, 32], F32)
    kfull = wsb("kfull", [128, 4, 96], BF16)
    kT = wsb("kT", [96, 4, 128], BF16)
    vaug = wsb("vaug", [128, 4, 65], BF16)
    ug = wsb("ug", [128, 256], F32)
    vn = wsb("vn", [128, 256], F32)
    vn_b = wsb("vn_b", [128, 256], BF16)
    bnst = wsb("bnst", [128, 8], F32)
    zd_f = [wsb("zd_f%d" % i, [128, 256], F32) for i in range(2)]
    d_b = wsb("d_b", [128, 256], BF16)
    dT = wsb("dT", [128, 2, 128], BF16)
    za_f = wsb("za_f", [128, 896], F32)
    wkv_o = wsb("wkv_o", [64, 4, 64], F32)


    def load_small(l):
        P = 128
        for n, w in BC_SPEC:
            if n == "omm":
                continue
            S.dma(BCv(n, P), Wd[n][l].partition_broadcast(128), W=[bc], key=bc)
        V("tensor_scalar", BCv("omm", P), BCv("rw_mu", P), -1.0, 1.0, ALU.mult, ALU.add, R=[bc], W=[bc])
        S.dma(ng[:], Wd["norm_g"][l].rearrange("(k p) -> p k", p=128), W=[ng], key=ng, allow_slow_non_contiguous=True)
        S.dma(qag[:, 0:1], Wd["mla_qa_g"][l][0:128].rearrange("(p o) -> p o", o=1), W=[qag], key=qag)
        S.dma(qag[0:64, 1:2], Wd["mla_qa_g"][l][128:192].rearrange("(p o) -> p o", o=1), W=[qag], key=qag)
        S.dma(sgub[:], Wd["sgu_b"][l].rearrange("h i -> i h"), W=[sgub], key=sgub, allow_slow_non_contiguous=True)
        for k in range(8):
            S.dma(stage[:, 0:D], Wd["w_out"][l][k * 128:(k + 1) * 128, :], W=[stage], key=stage)
            V("tensor_copy", wout_b[:, k, :], stage[:, 0:D], R=[stage], W=[wout_b])
        S.dma(stage[:, 0:384], Wd["mla_w_uq"][l][0:128, :], W=[stage], key=stage)
        V("tensor_scalar", wuq_b[:, 0, :], stage[:, 0:384], qag[:, 0:1], None, ALU.mult, R=[stage, qag], W=[wuq_b])
        S.dma(stage[0:64, 0:384], Wd["mla_w_uq"][l][128:192, :], W=[stage], key=stage)
        V("memset", wuq_b[:, 1, :], 0.0, W=[wuq_b])
        V("tensor_scalar", wuq_b[0:64, 1, :], stage[0:64, 0:384], qag[0:64, 1:2], None, ALU.mult, R=[stage, qag], W=[wuq_b])
        S.dma(stage[:, 0:256], Wd["mla_w_uk"][l], W=[stage], key=stage)
        S.dma(stage[:, 256:512], Wd["mla_w_uv"][l], W=[stage], key=stage)
        V("tensor_copy", wukv_b[:], stage[:, 0:512], R=[stage], W=[wukv_b])
        V("memset", stage[:, 0:512], 0.0, W=[stage])
        S.dma(stage[0:64, 0:256], Wd["rw_w2"][l], W=[stage], key=stage)
        S.dma(stage[64:128, 256:512], Wd["rw_a2"][l], W=[stage], key=stage)
        V("tensor_copy", w2a2_b[:], stage[:, 0:512], R=[stage], W=[w2a2_b])
        S.dma(stage[:, 0:512].rearrange("p (h j) -> p h j", h=4), Wd["sgu_w"][l].rearrange("h i j -> i h j"), W=[stage], key=stage)
        for h in range(4):
            V("tensor_tensor", stage2[:, h * 128:(h + 1) * 128], stage[:, h * 128:(h + 1) * 128], cb[:, CB_M3 + 128:CB_M3 + 256],
              ALU.mult, R=[stage, cb], W=[stage2])
            V("tensor_sub", stage2[:, h * 128:(h + 1) * 128], stage[:, h * 128:(h + 1) * 128], stage2[:, h * 128:(h + 1) * 128],
              R=[stage, stage2], W=[stage2])
            M("transpose", PF[5][:, h * 128:(h + 1) * 128], stage2[:, h * 128:(h + 1) * 128], identf(128), R=[stage2, cf], W=[PF[5]])
        V("tensor_copy", wsT_b[:].rearrange("p h i -> p (h i)"), PF[5][:, 0:512], R=[PF[5]], W=[wsT_b])
        V("memset", stage[:, 0:256], 0.0, W=[stage])
        for g in range(4):
            c = g // 2
            r0 = 64 * (g % 2)
            S.dma(stage[r0:r0 + 64, c * 128 + r0: c * 128 + r0 + 64], Wd["pool_w"][l][g], W=[stage], key=stage)
        S.dma(stage2[:, 0:256], Wd["pool_scale"][l].partition_broadcast(128), W=[stage2], key=stage2)
        V("tensor_tensor", poolw_b[:].rearrange("p c d -> p (c d)"), stage[:, 0:256], stage2[:, 0:256], ALU.mult,
          R=[stage, stage2], W=[poolw_b])

    def load_win(l, win_b):
        for k in range(8):
            for c0 in range(0, DIN, 1024):
                n = min(1024, DIN - c0)
                st = stage if ((k * 3 + c0 // 1024) % 2 == 0) else stage2
                S.dma(st[:, 0:n], Wd["w_in"][l][k * 128:(k + 1) * 128, c0:c0 + n], W=[st], key=st)
                V("tensor_scalar", win_b[:, k, c0:c0 + n], st[:, 0:n], ng[:, k:k + 1], None, ALU.mult, R=[st, ng], W=[win_b])

    def rstd_from_ss(ss_ap, out_ap, P, n, eps, Rb, Wb, tmp_ap):
        V("tensor_scalar", tmp_ap, ss_ap, 1.0 / n, eps, ALU.mult, ALU.add, R=Rb, W=Wb)
        A("activation", out=tmp_ap, in_=tmp_ap, func=AF.Sqrt, R=Wb, W=Wb)
        V("reciprocal", out_ap, tmp_ap, R=Wb, W=Wb)

    def transposes_b(src, P, widths, pb, dst_ap_fn, Rb, Wdst, evac="act"):
        for i, (ap, w) in enumerate(zip(src, widths)):
            M("transpose", pb[0:w, i * P:(i + 1) * P], ap, identb(P), R=Rb + [cb], W=[pb])

    def rope_apply(dst_a, dst_b, x1, x2, cosv, sinv, t1, t2, Rb, Wb, Tb):
        V("tensor_tensor", t1, x1, cosv, ALU.mult, R=Rb, W=Tb)
        V("tensor_tensor", t2, x2, sinv, ALU.mult, R=Rb, W=Tb)
        V("tensor_tensor", dst_a, t1, t2, ALU.subtract, R=Tb, W=Wb)
        V("tensor_tensor", t1, x1, sinv, ALU.mult, R=Rb + Wb, W=Tb)
        V("tensor_tensor", t2, x2, cosv, ALU.mult, R=Rb, W=Tb)
        V("tensor_tensor", dst_b, t1, t2, ALU.add, R=Tb, W=Wb)

    def kv_from_ckv(P, ckvf_ap, krf_ap, Rb, sc, tok0):
        V("tensor_copy", ckv_b[0:P, :], ckvf_ap, R=Rb, W=[ckv_b])
        M("transpose", PB[1][:, 0:P], ckv_b[0:P, :], identb(P), R=[ckv_b, cb], W=[PB[1]])
        A("copy", ckvT[:, 0:P], PB[1][:, 0:P], R=[PB[1]], W=[ckvT])
        M("matmul", PF[1][0:P, :], ckvT[:, 0:P], wukv_b[:], start=True, stop=True, R=[ckvT, wukv_b], W=[PF[1]])
        A("activation", out=w1o[0:P, :], in_=PF[1][0:P, 0:256], func=AF.Square, R=[PF[1]], W=[w1o])
        V("tensor_reduce", st2o[0:P, 0:4], w1o[0:P, :].rearrange("p (h d) -> p h d", h=4), AX.X, ALU.add, R=[w1o], W=[st2o])
        rstd_from_ss(st2o[0:P, 0:4], st2o[0:P, 4:8], P, 64.0, 1e-6, [st2o], [st2o], st2o[0:P, 8:12])
        for h in range(4):
            V("scalar_tensor_tensor", kfull[0:P, h, 0:64], PF[1][0:P, 64 * h:64 * h + 64], st2o[0:P, 4 + h:5 + h],
              BCv("mla_k_norm_g", P, 0, 64), ALU.mult, ALU.mult, R=[PF[1], st2o, bc], W=[kfull])
            V("tensor_copy", kfull[0:P, h, 64:96], krf_ap, R=Rb, W=[kfull])
        V("tensor_copy", vaug[0:P, :, 0:64], PF[1][0:P, 256:512].rearrange("p (h d) -> p h d", h=4), R=[PF[1]], W=[vaug])
        for h in range(4):
            M("transpose", PB[1][0:96, h * P:(h + 1) * P], kfull[0:P, h, :], identb(P), R=[kfull, cb], W=[PB[1]])
        A("copy", kT[:, :, 0:P], PB[1][0:96, 0:4 * P].rearrange("p (h t) -> p h t", h=4), R=[PB[1]], W=[kT])
        S.dma(sc["kt"][:, :, tok0:tok0 + P].rearrange("h d t -> d h t"), kT[:, :, 0:P], R=[kT], key=kT)
        S.dma(sc["v"][tok0:tok0 + P, :], vaug[0:P, :, :].rearrange("p h d -> p (h d)"), R=[vaug], key=vaug)

    def phase1_tile(l, grp, ti, P, xsrc, win_b, sc, first, last, tok0, nlev, st):
        xb_ = xt[ti % 2]
        rp_ = ropet[ti % 2]
        cosv = rp_[0:P, 0:64].rearrange("p (h d) -> p h d", h=4)
        sinv = rp_[0:P, 64:128].rearrange("p (h d) -> p h d", h=4)
        A("activation", out=junk[0:P, :], in_=xb_[0:P, :], func=AF.Square, accum_out=st1[0:P, 0:1], R=[xb_], W=[junk, st1])
        rstd_from_ss(st1[0:P, 0:1], st1[0:P, 1:2], P, float(D), 1e-6, [st1], [st1], st1[0:P, 2:3])
        V("tensor_scalar", xn_b[0:P, :], xb_[0:P, :], st1[0:P, 1:2], None, ALU.mult, R=[xb_, st1], W=[xn_b])
        for k in range(8):
            M("transpose", PB[0][:, k * P:(k + 1) * P], xn_b[0:P, k * 128:(k + 1) * 128], identb(P), R=[xn_b, cb], W=[PB[0]])
        A("copy", xnT[:, :, 0:P], PB[0][:, 0:8 * P].rearrange("p (k t) -> p k t", k=8), R=[PB[0]], W=[xnT])

        def proj(bank, c0, n):
            for k in range(8):
                M("matmul", bank[0:P, 0:n], xnT[:, k, 0:P], win_b[:, k, c0:c0 + n], start=(k == 0), stop=(k == 7),
                  R=[xnT, win_b], W=[bank])

        proj(PF[0], OFF_G, 512)
        proj(PF[1], OFF_G + 512, 512)
        A("activation", out=sg[0:P, 0:512], in_=PF[0][0:P, :], func=AF.Silu, R=[PF[0]], W=[sg])
        A("activation", out=sg[0:P, 512:1024], in_=PF[1][0:P, :], func=AF.Silu, R=[PF[1]], W=[sg])
        S.dma(sc["gb"][tok0:tok0 + P, :], sg[0:P, 256:512], R=[sg], key=sg)

        proj(PF[0], OFF_A, 512)
        proj(PF[1], OFF_A + 512, 384)
        tc_, tp_ = tmpm[ti % 2], tmpm[(ti + 1) % 2]
        V("tensor_tensor", tc_[0:P, 0:512], PF[0][0:P, 0:512], BCv("rw_mu", P, 0, 512), ALU.mult, R=[PF[0], bc], W=[tc_])
        V("tensor_tensor", tc_[0:P, 512:896], PF[1][0:P, 0:384], BCv("rw_mu", P, 512, 896), ALU.mult, R=[PF[1], bc], W=[tc_])
        if last:
            V("tensor_copy", za_f[0:P, 0:512], PF[0][0:P, 0:512], R=[PF[0]], W=[za_f])
            V("tensor_copy", za_f[0:P, 512:896], PF[1][0:P, 0:384], R=[PF[1]], W=[za_f])
            S.dma(st["o_shift"][l:l + 1, :], za_f[P - 1:P, :], R=[za_f], key=za_f)
        V("tensor_tensor", zs[0:P, 0:512], PF[0][0:P, 0:512], BCv("omm", P, 0, 512), ALU.mult, R=[PF[0], bc], W=[zs])
        V("tensor_tensor", zs[0:P, 512:896], PF[1][0:P, 0:384], BCv("omm", P, 512, 896), ALU.mult, R=[PF[1], bc], W=[zs])
        def gen_R():
            for (bank, c0, n) in ((PF[2], 0, 512), (PF[3], 512, 384)):
                M("matmul", bank[0:P, 0:n], cb[0:P, CB_SH:CB_SH + P], tc_[0:P, c0:c0 + n], start=True, stop=False, R=[cb, tc_], W=[bank])
                M("matmul", bank[0:P, 0:n], cb[:, CB_EL:CB_EL + P], tp_[:, c0:c0 + n], start=False, stop=True, R=[cb, tp_], W=[bank])
            V("tensor_add", zs[0:P, 0:512], zs[0:P, 0:512], PF[2][0:P, 0:512], R=[zs, PF[2]], W=[zs])
            V("tensor_add", zs[0:P, 512:896], zs[0:P, 512:896], PF[3][0:P, 0:384], R=[zs, PF[3]], W=[zs])
            r_ = zs[0:P, 0:256]
            k_ = zs[0:P, 256:512]
            v_ = zs[0:P, 512:768]
            A("activation", out=lin[0:P, 0:64], in_=zs[0:P, 768:832], func=AF.Tanh, R=[zs], W=[lin])
            A("copy", lin[0:P, 64:128], zs[0:P, 832:896], R=[zs], W=[lin])
            M("transpose", PB[0][:, 0:P], lin[0:P, :], identb(P), R=[lin, cb], W=[PB[0]])
            A("copy", linT[:, 0:P], PB[0][:, 0:P], R=[PB[0]], W=[linT])
            M("matmul", PF[2][0:P, :], linT[:, 0:P], w2a2_b[:], start=True, stop=True, R=[linT, w2a2_b], W=[PF[2]])
            V("tensor_add", w1[0:P, :], PF[2][0:P, 0:256], BCv("rw_w0", P), R=[PF[2], bc], W=[w1])
            A("activation", out=sw[0:P, :], in_=w1[0:P, :], func=AF.Sigmoid, R=[w1], W=[sw])
            V("tensor_add", w2[0:P, :], PF[2][0:P, 256:512], BCv("rw_a0", P), R=[PF[2], bc], W=[w2])
            A("activation", out=sa[0:P, :], in_=w2[0:P, :], func=AF.Sigmoid, R=[w2], W=[sa])
            V("tensor_tensor", kk[0:P, :], k_, BCv("rw_kk", P), ALU.mult, R=[zs, bc], W=[kk])
            V("tensor_tensor", w3[0:P, :], kk[0:P, :], kk[0:P, :], ALU.mult, R=[kk], W=[w3])
            V("tensor_reduce", st2[0:P, 0:4], w3[0:P, :].rearrange("p (h d) -> p h d", h=4), AX.X, ALU.add, R=[w3], W=[st2])
            V("tensor_scalar", st2[0:P, 4:8], st2[0:P, 0:4], 1e-24, None, ALU.max, R=[st2], W=[st2])
            A("activation", out=st2[0:P, 4:8], in_=st2[0:P, 4:8], func=AF.Sqrt, R=[st2], W=[st2])
            V("reciprocal", st2[0:P, 8:12], st2[0:P, 4:8], R=[st2], W=[st2])
            for h in range(4):
                V("tensor_scalar", kk[0:P, 64 * h:64 * h + 64], kk[0:P, 64 * h:64 * h + 64], st2[0:P, 8 + h:9 + h], None, ALU.mult,
                  R=[kk, st2], W=[kk])
            V("scalar_tensor_tensor", w1[0:P, :], sa[0:P, :], -1.0, BCv("rw_ka", P), ALU.add, ALU.mult, R=[sa, bc], W=[w1])
            V("scalar_tensor_tensor", kp[0:P, :], w1[0:P, :], 1.0, k_, ALU.add, ALU.mult, R=[w1, zs], W=[kp])
            V("tensor_tensor", bb[0:P, :], kk[0:P, :], sa[0:P, :], ALU.mult, R=[kk, sa], W=[bb])
            G("tensor_tensor", w2[0:P, :], r_, kp[0:P, :], ALU.mult, R=[zs, kp], W=[w2])
            G("tensor_tensor", w2[0:P, :], w2[0:P, :], BCv("rw_rk", P), ALU.mult, R=[w2, bc], W=[w2])
            V("tensor_reduce", bcf[0:P, 0:4], w2[0:P, :].rearrange("p (h d) -> p h d", h=4), AX.X, ALU.add, R=[w2], W=[bcf])
            M("matmul", PF[2][0:P, 0:256], cf[0:P, CF_TRI:CF_TRI + P], sw[0:P, :], start=True, stop=True, R=[cf, sw], W=[PF[2]])
            M("matmul", PF[2][0:P, 256:512], cf[0:P, CF_ONES:CF_ONES + P], sw[0:P, :], start=True, stop=True, R=[cf, sw], W=[PF[2]])
            V("tensor_copy", cs_sb[0:P, :], PF[2][0:P, 0:256], R=[PF[2]], W=[cs_sb])
            A("activation", out=e1[0:P, :], in_=PF[2][0:P, 0:256], func=AF.Exp, scale=CDEC, R=[PF[2]], W=[e1])
            A("activation", out=e2[0:P, :], in_=PF[2][0:P, 0:256], func=AF.Exp, scale=-CDEC, R=[PF[2]], W=[e2])
            V("tensor_sub", w1[0:P, :], cs_sb[0:P, :], sw[0:P, :], R=[cs_sb, sw], W=[w1])
            A("activation", out=e3[0:P, :], in_=w1[0:P, :], func=AF.Exp, scale=CDEC, R=[w1], W=[e3])
            V("tensor_sub", w3[0:P, :], PF[2][0:P, 256:512], cs_sb[0:P, :], R=[PF[2], cs_sb], W=[w3])
            A("activation", out=e4[0:P, :], in_=w3[0:P, :], func=AF.Exp, scale=CDEC, R=[w3], W=[e4])
            V("scalar_tensor_tensor", hat[0:P, 0, :], kk[0:P, :], -1.0, e3[0:P, :], ALU.mult, ALU.mult, R=[kk, e3], W=[hat])
            V("tensor_tensor", hat[0:P, 1, :], bb[0:P, :], e2[0:P, :], ALU.mult, R=[bb, e2], W=[hat])
            V("tensor_tensor", hat[0:P, 2, :], kp[0:P, :], e2[0:P, :], ALU.mult, R=[kp, e2], W=[hat])
            V("tensor_tensor", hat[0:P, 3, :], r_, e1[0:P, :], ALU.mult, R=[zs, e1], W=[hat])
            G("tensor_tensor", bp_b[0:P, :], bb[0:P, :], e4[0:P, :], ALU.mult, R=[bb, e4], W=[bp_b])
            G("tensor_tensor", kp4[0:P, :], kp[0:P, :], e4[0:P, :], ALU.mult, R=[kp, e4], W=[kp4])
            ohc = CF_OH + (0 if P == 128 else 1)
            for p in range(2):
                M("matmul", PF[3][:, p:p + 1], e1[0:P, p * 128:(p + 1) * 128], cf[0:P, ohc:ohc + 1], start=True, stop=True,
                  R=[e1, cf], W=[PF[3]])
            V("tensor_copy", wc[:, 0:2], PF[3][:, 0:2], R=[PF[3]], W=[wc])
            for vi in range(4):
                for p in range(2):
                    M("transpose", PB[0][:, (vi * 2 + p) * P:(vi * 2 + p + 1) * P], hat[0:P, vi, p * 128:(p + 1) * 128], identb(P),
                      R=[hat, cb], W=[PB[0]])
            A("copy", hT[:, :, :, 0:P], PB[0][:, 0:8 * P].rearrange("p (v q t) -> p v q t", v=4, q=2), R=[PB[0]], W=[hT])
            def fm(h, vi):
                return hT[64 * (h % 2):64 * (h % 2) + 64, vi, h // 2, 0:P]

            def pv(bank, w=128, n=None):
                n = P if n is None else n
                return bank[0:P, 0:4 * w].rearrange("p (h t) -> p h t", h=4)[:, :, 0:n]

            def msk(kind):
                return cbr[0:P, kind, :, 0:P]

            for h in range(4):
                M("matmul", PF[2][0:P, h * 128:h * 128 + P], fm(h, 0), fm(h, 1), start=True, stop=True, R=[hT], W=[PF[2]])
                M("matmul", PF[3][0:P, h * 128:h * 128 + P], fm(h, 1), fm(h, 0), start=True, stop=True, R=[hT], W=[PF[3]])
                M("matmul", PF[4][0:P, h * 128:h * 128 + P], fm(h, 0), fm(h, 2), start=True, stop=True, R=[hT], W=[PF[4]])
                M("matmul", PF[5][0:P, h * 128:h * 128 + P], fm(h, 1), fm(h, 3), start=True, stop=True, R=[hT], W=[PF[5]])
            V("tensor_tensor", scL[0:P, :, 0:P], pv(PF[2]), msk(0), ALU.mult, R=[PF[2], cbr], W=[scL])
            V("tensor_tensor", scN[0:P, :, 0:P], pv(PF[3]), msk(1), ALU.mult, R=[PF[3], cbr], W=[scN])
            for h in range(4):
                bk = PF[2 + (h % 2)]
                M("matmul", bk[0:P, h * 128:h * 128 + P], fm(h, 2), fm(h, 3), start=True, stop=True, R=[hT], W=[bk])
            V("tensor_add", Ttb[0][0:P, :, 0:P], scL[0:P, :, 0:P], msk(3), R=[scL, cbr], W=[Ttb[0]])
            V("tensor_tensor", scK[0:P, :, 0:P], pv(PF[4]), msk(0), ALU.mult, R=[PF[4], cbr], W=[scK])
            V("tensor_tensor", mbr_b[0:P, :, 0:P], pv(PF[5]), msk(2), ALU.mult, R=[PF[5], cbr], W=[mbr_b])
            for h in range(4):
                bk = PF[2 + (h % 2)]
                V("tensor_tensor", mkr_f[0:P, h, 0:P], bk[0:P, h * 128:h * 128 + P], cb[0:P, CB_UI:CB_UI + P], ALU.mult, R=[bk, cb], W=[mkr_f])
            def acc_(tb, idx):
                return (lambda h: tb[0:P, h, 0:P]) if idx is None else (lambda h: tb[0:P, idx, h, 0:P])

            def squares(Af, ATf, Rb, dst, need_A):
                if need_A:
                    for h in range(4):
                        M("matmul", PF[2][0:P, h * 128:h * 128 + P], ATf(h), Af(h), start=True, stop=True, R=Rb, W=[PF[2]])
                for h in range(4):
                    M("matmul", PF[3][0:P, h * 128:h * 128 + P], Af(h), ATf(h), start=True, stop=True, R=Rb, W=[PF[3]])

            def squares_evac(dst, need_A):
                if need_A:
                    V("tensor_copy", dst[0:P, 0, :, 0:P], pv(PF[2]), R=[PF[2]], W=[dst])
                A("copy", dst[0:P, 1, :, 0:P], pv(PF[3]), R=[PF[3]], W=[dst])

            tcur = 0
            squares(acc_(scL, None), acc_(scN, None), [scL, scN], Ab[1], nlev > 2)
            squares_evac(Ab[1], nlev > 2)
            for k in range(1, nlev):
                cur = Ab[k % 2]
                Af, ATf = acc_(cur, 0), acc_(cur, 1)
                lastk = (k == nlev - 1)
                pbank = PF[4 + (k % 2)]
                for h in range(4):
                    M("matmul", pbank[0:P, h * 128:h * 128 + P], ATf(h), Ttb[tcur][0:P, h, 0:P], start=True, stop=True,
                      R=[cur, Ttb[tcur]], W=[pbank])
                if not lastk:
                    nxt = Ab[(k + 1) % 2]
                    squares(Af, ATf, [cur], nxt, k + 1 < nlev - 1)
                yield "mm"
                V("tensor_add", Ttb[1 - tcur][0:P, :, 0:P], pv(pbank), Ttb[tcur][0:P, :, 0:P], R=[pbank, Ttb[tcur]], W=[Ttb[1 - tcur]])
                if not lastk:
                    squares_evac(nxt, k + 1 < nlev - 1)
                tcur = 1 - tcur
                yield "ev"
            Tt = Ttb[tcur]
            for h in range(4):
                M("matmul", PF[2][0:P, h * 128:h * 128 + P], Tt[0:P, h, 0:P], mbr_b[0:P, h, 0:P], start=True, stop=True, R=[Tt, mbr_b], W=[PF[2]])
            for h in range(4):
                M("matmul", PF[3][0:P, h * 64:h * 64 + 64], Tt[0:P, h, 0:P], bp_b[0:P, 64 * h:64 * h + 64], start=True, stop=True, R=[Tt, bp_b], W=[PF[3]])
            A("copy", G_b[0:P, :, 0:P], pv(PF[2]), R=[PF[2]], W=[G_b])
            V("tensor_copy", H_b[0:P, :, :], PF[3][0:P, 0:256].rearrange("p (h d) -> p h d", h=4), R=[PF[3]], W=[H_b])
            for h in range(4):
                M("matmul", PF[2][0:P, h * 128:h * 128 + P], scK[0:P, h, 0:P], G_b[0:P, h, 0:P], start=True, stop=True, R=[scK, G_b], W=[PF[2]])
            for h in range(4):
                M("matmul", PF[3][:, h * 128:h * 128 + P], hat[0:P, 0, (h // 2) * 128:(h // 2) * 128 + 128], G_b[0:P, h, 0:P], start=True, stop=True,
                  R=[hat, G_b], W=[PF[3]])
            for h in range(4):
                M("matmul", PF[4][0:P, h * 64:h * 64 + 64], scK[0:P, h, 0:P], H_b[0:P, h, :], start=True, stop=True, R=[scK, H_b], W=[PF[4]])
            for h in range(4):
                M("matmul", PF[4][:, 256 + h * 64:256 + h * 64 + 64], hat[0:P, 0, (h // 2) * 128:(h // 2) * 128 + 128], H_b[0:P, h, :],
                  start=True, stop=True, R=[hat, H_b], W=[PF[4]])
            V("tensor_add", z_f[0:P, :, 0:P], pv(PF[2]), mkr_f[0:P, :, 0:P], R=[PF[2], mkr_f], W=[z_f])
            for o_ in (0, 64):
                h0 = o_ // 64
                qv = PF[3][o_:o_ + 64, 0:512].rearrange("p (q r t) -> p q r t", q=2, r=2)[:, :, h0, 0:P]
                V("tensor_add", q_f[o_:o_ + 64, :, 0:P], qv, hT[o_:o_ + 64, 3, :, 0:P], R=[PF[3], hT], W=[q_f])
            for h in range(4):
                p = h // 2
                o_ = 64 * (h % 2)
                V("tensor_add", x_f[0:P, h, o_:o_ + 64], PF[4][0:P, h * 64:h * 64 + 64], kp4[0:P, 64 * h:64 * h + 64], R=[PF[4], kp4], W=[x_f])
                V("scalar_tensor_tensor", p_f[o_:o_ + 64, p, o_:o_ + 64], cf[o_:o_ + 64, CF_ID + o_:CF_ID + o_ + 64], wc[o_:o_ + 64, p:p + 1],
                  PF[4][o_:o_ + 64, 256 + h * 64:256 + h * 64 + 64], ALU.mult, ALU.add, R=[cf, wc, PF[4]], W=[p_f])
            for h in range(4):
                p = h // 2
                o_ = 64 * (h % 2)
                vh = zs[0:P, 512 + 64 * h:512 + 64 * h + 64]
                M("matmul", PF[5][0:P, 64 * h:64 * h + 64], q_f[o_:o_ + 64, p, 0:P], ST[o_:o_ + 64, p, :], start=True, stop=False, R=[q_f, ST], W=[PF[5]])
                M("matmul", PF[5][0:P, 64 * h:64 * h + 64], z_f[0:P, h, 0:P], vh, start=False, stop=True, R=[z_f, zs], W=[PF[5]])
            for p in range(2):
                M("matmul", PF[5][:, 256 + 64 * p:256 + 64 * p + 64], p_f[:, p, :], ST[:, p, :], start=True, stop=False, R=[p_f, ST], W=[PF[5]])
                for h in (2 * p, 2 * p + 1):
                    M("matmul", PF[5][:, 256 + 64 * p:256 + 64 * p + 64], x_f[0:P, h, :], zs[0:P, 512 + 64 * h:512 + 64 * h + 64],
                      start=False, stop=(h == 2 * p + 1), R=[x_f, zs], W=[PF[5]])
            V("tensor_copy", ST[:, :, :], PF[5][:, 256:384].rearrange("k (p v) -> k p v", p=2), R=[PF[5]], W=[ST])
            Y = PF[5]
            V("tensor_reduce", st3[0:P, 0:4], Y[0:P, 0:256].rearrange("p (h d) -> p h d", h=4), AX.X, ALU.add, R=[Y], W=[st3])
            V("tensor_scalar", st3[0:P, 0:4], st3[0:P, 0:4], 1.0 / 64, None, ALU.mult, R=[st3], W=[st3])
            for h in range(4):
                V("tensor_scalar", yrw[0:P, 64 * h:64 * h + 64], Y[0:P, 64 * h:64 * h + 64], st3[0:P, h:h + 1], None, ALU.subtract,
                  R=[Y, st3], W=[yrw])
            V("tensor_tensor", w1[0:P, :], yrw[0:P, :], yrw[0:P, :], ALU.mult, R=[yrw], W=[w1])
            V("tensor_reduce", st3[0:P, 4:8], w1[0:P, :].rearrange("p (h d) -> p h d", h=4), AX.X, ALU.add, R=[w1], W=[st3])
            rstd_from_ss(st3[0:P, 4:8], st3[0:P, 8:12], P, 64.0, 64e-5, [st3], [st3], st3[0:P, 12:16])
            for h in range(4):
                V("tensor_scalar", yrw[0:P, 64 * h:64 * h + 64], yrw[0:P, 64 * h:64 * h + 64], st3[0:P, 8 + h:9 + h], None, ALU.mult,
                  R=[yrw, st3], W=[yrw])
            V("tensor_tensor", yrw[0:P, :], yrw[0:P, :], BCv("rw_gn_g", P), ALU.mult, R=[yrw, bc], W=[yrw])
            V("tensor_add", yrw[0:P, :], yrw[0:P, :], BCv("rw_gn_b", P), R=[yrw, bc], W=[yrw])
            for h in range(4):
                V("scalar_tensor_tensor", yrw[0:P, 64 * h:64 * h + 64], zs[0:P, 512 + 64 * h:512 + 64 * h + 64], bcf[0:P, h:h + 1],
                  yrw[0:P, 64 * h:64 * h + 64], ALU.mult, ALU.add, R=[zs, bcf, yrw], W=[yrw])
            V("tensor_tensor", ygp[0:P, 0:256], yrw[0:P, :], sg[0:P, 0:256], ALU.mult, R=[yrw, sg], W=[ygp])
            if last:
                for p in range(2):
                    M("transpose", PF[2][0:64, p * 128:(p + 1) * 128], ST[:, p, :], identf(128), R=[ST, cf], W=[PF[2]])
                V("tensor_copy", wkv_o[:].rearrange("i h j -> i (h j)"), PF[2][0:64, 0:256], R=[PF[2]], W=[wkv_o])
                S.dma(st["o_wkv"][l].rearrange("h i j -> i h j"), wkv_o[:], R=[wkv_o], key=wkv_o)
            yield

        def gen_O():
            proj(PF[0], OFF_B, 352)
            ZB = PF[0]
            A("activation", out=junk[0:P, 0:192], in_=ZB[0:P, 0:192], func=AF.Square, accum_out=st1[0:P, 4:5], R=[ZB], W=[junk, st1])
            yield
            A("activation", out=junk[0:P, 0:128], in_=ZB[0:P, 192:320], func=AF.Square, accum_out=st1[0:P, 5:6], R=[ZB], W=[junk, st1])
            A("activation", out=junk[0:P, 0:32], in_=ZB[0:P, 320:352], func=AF.Square, accum_out=st1[0:P, 6:7], R=[ZB], W=[junk, st1])
            V("tensor_tensor", st1[0:P, 12:15], st1[0:P, 4:7], invc[0:P, 8:11], ALU.mult, R=[st1, invc], W=[st1])
            V("tensor_scalar", st1[0:P, 12:15], st1[0:P, 12:15], 1e-6, None, ALU.add, R=[st1], W=[st1])
            A("activation", out=st1[0:P, 12:15], in_=st1[0:P, 12:15], func=AF.Sqrt, R=[st1], W=[st1])
            V("reciprocal", st1[0:P, 8:11], st1[0:P, 12:15], R=[st1], W=[st1])
            yield
            V("tensor_scalar", qa_b[0:P, :], ZB[0:P, 0:192], st1[0:P, 8:9], None, ALU.mult, R=[ZB, st1], W=[qa_b])
            yield
            M("transpose", PB[1][:, 0:P], qa_b[0:P, 0:128], identb(P), R=[qa_b, cb], W=[PB[1]])
            M("transpose", PB[1][0:64, P:2 * P], qa_b[0:P, 128:192], identb(P), R=[qa_b, cb], W=[PB[1]])
            A("copy", qaT[:, 0, 0:P], PB[1][:, 0:P], R=[PB[1]], W=[qaT])
            yield
            A("copy", qaT[0:64, 1, 0:P], PB[1][0:64, P:2 * P], R=[PB[1]], W=[qaT])
            M("matmul", PF[1][0:P, 0:384], qaT[:, 0, 0:P], wuq_b[:, 0, :], start=True, stop=False, R=[qaT, wuq_b], W=[PF[1]])
            M("matmul", PF[1][0:P, 0:384], qaT[:, 1, 0:P], wuq_b[:, 1, :], start=False, stop=True, R=[qaT, wuq_b], W=[PF[1]])
            yield
            A("activation", out=zb_f[0:P, :], in_=PF[1][0:P, 0:384], func=AF.Square, R=[PF[1]], W=[zb_f])
            sq3 = zb_f[0:P, :].rearrange("p (h d) -> p h d", h=4)
            V("tensor_reduce", st2o[0:P, 0:4], sq3[:, :, 0:64], AX.X, ALU.add, R=[zb_f], W=[st2o])
            yield
            V("tensor_reduce", st2o[0:P, 4:8], sq3[:, :, 64:96], AX.X, ALU.add, R=[zb_f], W=[st2o])
            V("tensor_tensor", st3o[0:P, 0:8], st2o[0:P, 0:8], invc[0:P, 11:19], ALU.mult, R=[st2o, invc], W=[st3o])
            V("tensor_scalar", st3o[0:P, 0:8], st3o[0:P, 0:8], 1e-6, None, ALU.add, R=[st3o], W=[st3o])
            A("activation", out=st3o[0:P, 0:8], in_=st3o[0:P, 0:8], func=AF.Sqrt, R=[st3o], W=[st3o])
            V("reciprocal", st2o[0:P, 8:16], st3o[0:P, 0:8], R=[st3o], W=[st2o])
            yield
            for h in range(4):
                V("scalar_tensor_tensor", qn[0:P, h, 0:64], PF[1][0:P, 96 * h:96 * h + 64], st2o[0:P, 8 + h:9 + h],
                  BCv("mla_q_norm_g", P, 0, 64), ALU.mult, ALU.mult, R=[PF[1], st2o, bc], W=[qn])
                V("scalar_tensor_tensor", qn[0:P, h, 64:96], PF[1][0:P, 96 * h + 64:96 * h + 96], st2o[0:P, 12 + h:13 + h],
                  BCv("mla_q_norm_g", P, 64, 96), ALU.mult, ALU.mult, R=[PF[1], st2o, bc], W=[qn])
            V("tensor_copy", qfb[0:P, :, 0:64], qn[0:P, :, 0:64], R=[qn], W=[qfb])
            t1 = w1o[0:P, 0:64].rearrange("p (h d) -> p h d", h=4)
            yield
            t2 = w2o[0:P, 0:64].rearrange("p (h d) -> p h d", h=4)
            rope_apply(qfb[0:P, :, 64:80], qfb[0:P, :, 80:96], qn[0:P, :, 64:80], qn[0:P, :, 80:96], cosv, sinv, t1, t2,
                       [qn, rp_], [qfb], [w1o, w2o])
            for h in range(4):
                M("transpose", PB[1][0:96, h * P:(h + 1) * P], qfb[0:P, h, :], identb(P), R=[qfb, cb], W=[PB[1]])
            yield
            A("copy", qT[:, :, 0:P], PB[1][0:96, 0:4 * P].rearrange("p (h t) -> p h t", h=4), R=[PB[1]], W=[qT])
            S.dma(sc["qt"][:, :, tok0:tok0 + P].rearrange("h d t -> d h t"), qT[:, :, 0:P], R=[qT], key=qT)
            V("scalar_tensor_tensor", ckv_f[0:P, :], ZB[0:P, 192:320], st1[0:P, 9:10], BCv("mla_kva_g", P), ALU.mult, ALU.mult,
              R=[ZB, st1, bc], W=[ckv_f])
            yield
            S.dma(st["o_ckv"][l, tok0:tok0 + P, :] if grp == "p" else st["o_ckv"][l, 0:P, :], ckv_f[0:P, :], R=[ckv_f], key=ckv_f)
            V("scalar_tensor_tensor", kr_f[0:P, :], ZB[0:P, 320:352], st1[0:P, 10:11], BCv("mla_k_norm_g", P, 64, 96), ALU.mult, ALU.mult,
              R=[ZB, st1, bc], W=[kr_f])
            rope_apply(kr_r[0:P, 0:16], kr_r[0:P, 16:32], kr_f[0:P, 0:16], kr_f[0:P, 16:32], rp_[0:P, 0:16], rp_[0:P, 64:80],
                       w1o[0:P, 0:16], w2o[0:P, 0:16], [kr_f, rp_], [kr_r], [w1o, w2o])
            yield
            S.dma(st["o_kr"][l, tok0:tok0 + P, :] if grp == "p" else st["o_kr"][l, 0:P, :], kr_r[0:P, :], R=[kr_r], key=kr_r)
            kv_from_ckv(P, ckv_f[0:P, :], kr_r[0:P, :], [ckv_f, kr_r], sc, st["ktok0"] + tok0)

            proj(PF[0], OFF_C, 512)
            yield
            ZC = PF[0]
            V("tensor_tensor", ug[0:P, :], ZC[0:P, 0:256], sg[0:P, 512:768], ALU.mult, R=[ZC, sg], W=[ug])
            V("bn_stats", bnst[0:P, 0:6], ZC[0:P, 256:512], R=[ZC], W=[bnst])
            yield
            V("bn_aggr", bnst[0:P, 6:8], bnst[0:P, 0:6], R=[bnst], W=[bnst])
            rstd_from_ss(bnst[0:P, 7:8], st1[0:P, 3:4], P, 1.0, 1e-5, [bnst], [st1], st1[0:P, 15:16])
            V("tensor_scalar", vn[0:P, :], ZC[0:P, 256:512], bnst[0:P, 6:7], st1[0:P, 3:4], ALU.subtract, ALU.mult, R=[ZC, bnst, st1], W=[vn])
            yield
            V("tensor_tensor", vn[0:P, :], vn[0:P, :], BCv("sgu_ln_g", P), ALU.mult, R=[vn, bc], W=[vn])
            V("tensor_add", vn[0:P, :], vn[0:P, :], BCv("sgu_ln_b", P), R=[vn, bc], W=[vn])
            if grp == "s":
                S.dma(o_sgv[l, 0:P, :], vn[0:P, :], R=[vn], key=vn)
            yield
            V("tensor_copy", vn_b[0:P, :], vn[0:P, :], R=[vn], W=[vn_b])
            for h in range(4):
                M("matmul", PF[1][0:P, 64 * h:64 * h + 64], wsT_b[0:P, h, 0:P], vn_b[0:P, 64 * h:64 * h + 64], start=True, stop=True,
                  R=[wsT_b, vn_b], W=[PF[1]])
            for h in range(4):
                V("scalar_tensor_tensor", ygp[0:P, 256 + 64 * h:256 + 64 * h + 64], PF[1][0:P, 64 * h:64 * h + 64], sgub[0:P, h:h + 1],
                  ug[0:P, 64 * h:64 * h + 64], ALU.add, ALU.mult, R=[PF[1], sgub, ug], W=[ygp])

            yield
            proj(PF[0], OFF_D, 256)
            zc_, zp_ = zd_f[ti % 2], zd_f[(ti + 1) % 2]
            V("tensor_copy", zc_[0:P, :], PF[0][0:P, 0:256], R=[PF[0]], W=[zc_])
            yield
            if last:
                if grp == "p":
                    S.dma(st["o_pool"][l], zc_[P - 15:P, :], R=[zc_], key=zc_)
                else:
                    S.dma(st["o_pool"][l], zc_[1:16, :], R=[zc_], key=zc_)
            for g in range(4):
                M("matmul", PF[1][0:P, 64 * g:64 * g + 64], cf[0:P, CF_BAND + 128 * g:CF_BAND + 128 * g + P], zc_[0:P, 64 * g:64 * g + 64],
                  start=True, stop=False, R=[cf, zc_], W=[PF[1]])
                M("matmul", PF[1][0:P, 64 * g:64 * g + 64], cf[:, CF_BANDP + 128 * g:CF_BANDP + 128 * g + P], zp_[:, 64 * g:64 * g + 64],
                  start=False, stop=True, R=[cf, zp_], W=[PF[1]])
            for g in range(4):
                ic = invc[0:P, g:g + 1] if (first and grp == "p") else invc[0:P, 4 + g:5 + g]
                V("scalar_tensor_tensor", d_b[0:P, 64 * g:64 * g + 64], PF[1][0:P, 64 * g:64 * g + 64], ic, zc_[0:P, 64 * g:64 * g + 64],
                  ALU.mult, ALU.subtract, R=[PF[1], invc, zc_], W=[d_b])
            yield
            for c in range(2):
                M("transpose", PB[1][:, c * P:(c + 1) * P], d_b[0:P, c * 128:(c + 1) * 128], identb(P), R=[d_b, cb], W=[PB[1]])
            A("copy", dT[:, :, 0:P], PB[1][:, 0:2 * P].rearrange("p (c t) -> p c t", c=2), R=[PB[1]], W=[dT])
            for c in range(2):
                M("matmul", PF[0][0:P, c * 128:(c + 1) * 128], dT[:, c, 0:P], poolw_b[:, c, :], start=True, stop=True, R=[dT, poolw_b], W=[PF[0]])
            yield
            V("tensor_tensor", ygp[0:P, 512:768], PF[0][0:P, 0:256], sg[0:P, 768:1024], ALU.mult, R=[PF[0], sg], W=[ygp])
            yield

        gr, go = gen_R(), gen_O()
        if P == 128 and ILV > 0:
            for tag in gr:
                if tag == "mm":
                    for _k in range(ILV):
                        next(go, None)
                elif tag == "ev":
                    for _k in range(ILV2):
                        next(go, None)
        for _ in gr:
            pass
        for _ in go:
            pass
        S.dma(sc["yg"][tok0:tok0 + P, :], ygp[0:P, :], R=[ygp], key=ygp)

    def phase2(l, grp, Tq, QB, nkt_total, klast, xsrc, ydst, sc, KT, VV, bufs):
        (QTb, PT, yfull4, yT, gbt4, xres, xout, rr) = bufs
        nqb = Tq // QB
        nsub = max(1, QB // 128)
        Pq = min(QB, 128)
        for qb in range(nqb):
            q0 = qb * QB
            S.dma(QTb[:, :, 0:QB], sc["qt"][:, :, q0:q0 + QB].rearrange("h d t -> d h t"), W=[QTb], key=QTb)
            for i in range(nsub):
                t0 = q0 + i * 128
                S.dma(yfull4[i][0:Pq, 0:256], sc["yg"][t0:t0 + Pq, 0:256], W=[yfull4[i]], key=yfull4[i])
                S.dma(yfull4[i][0:Pq, 512:1024], sc["yg"][t0:t0 + Pq, 256:768], W=[yfull4[i]], key=yfull4[i])
                S.dma(gbt4[i][0:Pq, :], sc["gb"][t0:t0 + Pq, :], W=[gbt4[i]], key=gbt4[i])
            if grp == "p":
                nkt = 4 * qb + 4
            else:
                nkt = nkt_total
            for h in range(4):
                def kinfo(kt):
                    kp_ = 128 if (grp == "p" or kt < nkt_total - 1) else klast
                    jd = kt - 4 * qb if grp == "p" else -1
                    c0 = 128 * jd if jd > 0 else 0
                    return kp_, jd, c0

                def emit_scores(kt):
                    kp_, jd, c0 = kinfo(kt)
                    n = QB - c0
                    sbk = (PF[4], PF[5], PB[1])[kt % 3]
                    sview = sbk[0:kp_, 0:n] if (kt % 3) < 2 else PB[1][0:kp_, 0:1024].bitcast(F32)[:, 0:n]
                    M("matmul", sview, KT[:, h, kt * 128:kt * 128 + kp_], QTb[:, h, c0:QB], start=True, stop=True,
                      R=[KT, QTb], W=[sbk])
                    pt_ = PT[kt % 3]
                    A("activation", out=pt_[0:kp_, c0:QB], in_=sview, func=AF.Exp, scale=float(1.0 / np.sqrt(96.0)),
                      R=[sbk], W=[pt_])
                    if jd >= 0:
                        G("memset", pt_[64:128, c0:c0 + 64], 0.0, W=[pt_])

                def emit_pv(kt):
                    kp_, jd, c0 = kinfo(kt)
                    pt_ = PT[kt % 3]
                    for i in range(c0 // 128, nsub):
                        first_k = (kt == 0)
                        last_k = (kt == (4 * qb + i if grp == "p" else nkt - 1))
                        M("matmul", PF[i][0:Pq, 0:65], pt_[0:kp_, i * 128:i * 128 + Pq], VV[0:kp_, kt, 65 * h:65 * h + 65],
                          start=first_k, stop=last_k, R=[pt_, VV], W=[PF[i]])

                emit_scores(0)
                if nkt > 1:
                    emit_scores(1)
                for kt in range(nkt):
                    if kt + 2 < nkt:
                        emit_scores(kt + 2)
                    emit_pv(kt)
                for i in range(nsub):
                    V("reciprocal", rr[0:Pq, h:h + 1], PF[i][0:Pq, 64:65], R=[PF[i]], W=[rr])
                    V("scalar_tensor_tensor", yfull4[i][0:Pq, 256 + 64 * h:256 + 64 * h + 64], PF[i][0:Pq, 0:64], rr[0:Pq, h:h + 1],
                      gbt4[i][0:Pq, 64 * h:64 * h + 64], ALU.mult, ALU.mult, R=[PF[i], rr, gbt4[i]], W=[yfull4[i]])
            for i in range(nsub):
                t0 = q0 + i * 128
                yfull = yfull4[i]
                S.dma(xres[0:Pq, :], xsrc[t0:t0 + Pq, :], W=[xres], key=xres)
                for k in range(8):
                    M("transpose", PB[0][:, k * Pq:(k + 1) * Pq], yfull[0:Pq, k * 128:(k + 1) * 128], identb(Pq), R=[yfull, cb], W=[PB[0]])
                A("copy", yT[:, :, 0:Pq], PB[0][:, 0:8 * Pq].rearrange("p (k t) -> p k t", k=8), R=[PB[0]], W=[yT])
                for cbk in range(2):
                    bank = PF[4 + cbk]
                    for k in range(8):
                        M("matmul", bank[0:Pq, :], yT[:, k, 0:Pq], wout_b[:, k, cbk * 512:(cbk + 1) * 512], start=(k == 0), stop=(k == 7),
                          R=[yT, wout_b], W=[bank])
                    V("tensor_add", xout[0:Pq, cbk * 512:(cbk + 1) * 512], bank[0:Pq, :], xres[0:Pq, cbk * 512:(cbk + 1) * 512],
                      R=[bank, xres], W=[xout])
                S.dma(ydst[t0:t0 + Pq, :], xout[0:Pq, :], R=[xout], key=xout)

    stp = dict(o_shift=o_shp, o_wkv=o_wkvp, o_pool=o_plp, o_ckv=o_ckvp, o_kr=o_krp, ktok0=0)
    sts = dict(o_shift=o_shs, o_wkv=o_wkvs, o_pool=o_pls, o_ckv=o_ckvs, o_kr=o_krs, ktok0=PAST)
    nlev_p = 7
    nlev_s = 4
    for l in range(NL):
        load_small(l)
        xsrc_p = x_p if l == 0 else o_yp
        xsrc_s = x_s if l == 0 else o_ys
        with ExitStack() as c1:
            alloc_work(c1)
            win_b = sb("win_b", [128, 8, DIN], BF16, c1)
            load_win(l, win_b)
            V("memset", tmpm[1][:], 0.0, W=[tmpm[1]])
            V("memset", zd_f[1][:], 0.0, W=[zd_f[1]])
            V("memset", ST[:], 0.0, W=[ST])
            S.dma(xt[0][:], xsrc_p[0:128, :], W=[xt[0]], key=xt[0])
            S.dma(ropet[0][:], rope_p[0:128, :], W=[ropet[0]], key=ropet[0])
            for ti in range(NT):
                if ti + 1 < NT:
                    S.dma(xt[(ti + 1) % 2][:], xsrc_p[(ti + 1) * 128:(ti + 2) * 128, :], W=[xt[(ti + 1) % 2]], key=xt[(ti + 1) % 2])
                    S.dma(ropet[(ti + 1) % 2][:], rope_p[(ti + 1) * 128:(ti + 2) * 128, :], W=[ropet[(ti + 1) % 2]], key=ropet[(ti + 1) % 2])
                phase1_tile(l, "p", ti, 128, xsrc_p, win_b, scr_p, ti == 0, ti == NT - 1, ti * 128, nlev_p, stp)
            if do_sample:
                P = TS
                V("memset", stage[:, 0:896], 0.0, W=[stage])
                S.dma(stage[127:128, 0:896], st_shift[l:l + 1, :], W=[stage], key=stage)
                V("tensor_tensor", tmpm[1][:, :], stage[:, 0:896], BCv("rw_mu", 128), ALU.mult, R=[stage, bc], W=[tmpm[1]])
                V("memset", zd_f[1][:], 0.0, W=[zd_f[1]])
                S.dma(zd_f[1][113:128, :], st_pool[l], W=[zd_f[1]], key=zd_f[1])
                S.dma(stage2[0:64, 0:256].rearrange("i (h j) -> i h j", h=4), st_wkv[l].rearrange("h i j -> i h j"), W=[stage2], key=stage2)
                for p in range(2):
                    M("transpose", PF[2][:, p * 64:(p + 1) * 64], stage2[0:64, p * 128:(p + 1) * 128], identf(64), R=[stage2, cf], W=[PF[2]])
                V("tensor_copy", ST[:].rearrange("k p v -> k (p v)"), PF[2][:, 0:128], R=[PF[2]], W=[ST])
                for kt in range(PAST // 128):
                    S.dma(stage[:, 0:128], c_ckv[l, kt * 128:(kt + 1) * 128, :], W=[stage], key=stage)
                    S.dma(stage[:, 128:160], c_kr[l, kt * 128:(kt + 1) * 128, :], W=[stage], key=stage)
                    kv_from_ckv(128, stage[:, 0:128], stage[:, 128:160], [stage], scr_s, kt * 128)
                S.dma(xt[0][0:P, :], xsrc_s[0:P, :], W=[xt[0]], key=xt[0])
                S.dma(ropet[0][0:P, :], rope_s[0:P, :], W=[ropet[0]], key=ropet[0])
                phase1_tile(l, "s", 0, P, xsrc_s, win_b, scr_s, True, True, 0, nlev_s, sts)
            S.barrier()
        with ExitStack() as c2:
            KT = sb("KT", [96, 4, max(T, PAST + 128)], BF16, c2)
            VV = sb("VV", [128, max(NT, PAST // 128 + 1), 260], BF16, c2)
            QTb = sb("QTb", [96, 4, 512], BF16, c2)
            PT = [sb("PT%d" % i, [128, 512], BF16, c2) for i in range(3)]
            yfull4 = [sb("yfull%d" % i, [128, D], BF16, c2) for i in range(4)]
            yT = sb("yT", [128, 8, 128], BF16, c2)
            gbt4 = [sb("gbt%d" % i, [128, 256], BF16, c2) for i in range(4)]
            xres = sb("xres", [128, D], F32, c2)
            xout = sb("xout", [128, D], F32, c2)
            rr = sb("rr", [128, 4], F32, c2)
            bufs = (QTb, PT, yfull4, yT, gbt4, xres, xout, rr)
            for h in range(4):
                S.dma(KT[:, h, 0:T], scr_p["kt"][h], W=[KT], key=KT)
            vview = scr_p["v"].rearrange("(n p) c -> p n c", p=128)
            for n0 in range(0, NT, 8):
                n1 = min(NT, n0 + 8)
                S.dma(VV[:, n0:n1, :], vview[:, n0:n1, :], W=[VV], key=VV)
            phase2(l, "p", T, 512, NT, 128, xsrc_p, o_yp, scr_p, KT, VV, bufs)
            if do_sample:
                NKS = PAST // 128 + 1
                for h in range(4):
                    S.dma(KT[:, h, 0:PAST + TS], scr_s["kt"][h][:, 0:PAST + TS], W=[KT], key=KT)
                vview = scr_s["v"].rearrange("(n p) c -> p n c", p=128)
                for n0 in range(0, NKS - 1, 8):
                    n1 = min(NKS - 1, n0 + 8)
                    S.dma(VV[:, n0:n1, :], vview[:, n0:n1, :], W=[VV], key=VV)
                S.dma(VV[0:TS, NKS - 1, :], scr_s["v"][PAST:PAST + TS, :], W=[VV], key=VV)
                phase2(l, "s", TS, TS, NKS, TS, xsrc_s, o_ys, scr_s, KT, VV, bufs)
            S.barrier()
    S.final_wait()
    ctx.close()
    return nc, S.ninst


_CACHE = {}


def kernel(**inputs):
    x_prompt = np.asarray(inputs["x_prompt"], np.float32)
    x_sample = np.asarray(inputs["x_sample"], np.float32)
    B, T, _ = x_prompt.shape
    NL = inputs["norm_g"].shape[0]
    key = (T, NL)
    if key not in _CACHE:
        _CACHE[key] = build(T, NL)[0]
    nc = _CACHE[key]
    cf, cb, invc = make_consts()
    rope_p = rope_table(np.arange(T))
    rope_s = rope_table(PAST + np.arange(TS))
    in_maps = []
    for c in range(8):
        m = {
            "x_p": np.ascontiguousarray(x_prompt[c // 2]),
            "x_s": np.ascontiguousarray(x_sample[c]),
            "c_ckv": np.ascontiguousarray(np.asarray(inputs["cache_ckv"], np.float32)[:, c]),
            "c_kr": np.ascontiguousarray(np.asarray(inputs["cache_krope"], np.float32)[:, c]),
            "st_wkv": np.ascontiguousarray(np.asarray(inputs["state_wkv"], np.float32)[:, c]),
            "st_shift": np.ascontiguousarray(np.asarray(inputs["state_shift"], np.float32)[:, c]),
            "st_pool": np.ascontiguousarray(np.asarray(inputs["state_pool"], np.float32)[:, c]),
            "cf": cf, "cb": cb, "invc": invc, "rope_p": rope_p, "rope_s": rope_s,
        }
        for n in WNAMES:
            m[n] = np.ascontiguousarray(np.asarray(inputs[n], np.float32).reshape([NL] + WSHAPES[n]))
        in_maps.append(m)
    res = run_bass_kernel_spmd(nc, in_maps, core_ids=list(range(8)))
    R = res.results
    pc = [R[2 * b] for b in range(B)]
    sc = [R[c] for c in range(8)]
    stk = lambda L, n, ax: np.stack([np.asarray(r[n], np.float32) for r in L], axis=ax)
    out = (
        stk(pc, "o_yp", 0), stk(sc, "o_ys", 0),
        stk(pc, "o_ckvp", 1), stk(pc, "o_krp", 1), stk(pc, "o_wkvp", 1), stk(pc, "o_shp", 1), stk(pc, "o_plp", 1),
        stk(sc, "o_ckvs", 1), stk(sc, "o_krs", 1), stk(sc, "o_wkvs", 1), stk(sc, "o_shs", 1), stk(sc, "o_pls", 1),
        stk(sc, "o_sgv", 1),
    )
    return out
```

```python
import numpy as np
from contextlib import ExitStack
import concourse.bass as bass
import concourse.mybir as mybir
from concourse.bass_utils import run_bass_kernel_spmd

F32 = mybir.dt.float32
BF16 = mybir.dt.bfloat16
AF = mybir.ActivationFunctionType
ALU = mybir.AluOpType
AX = mybir.AxisListType

D = 1024
DIN = 3040
OFF_A, OFF_B, OFF_C, OFF_D, OFF_G = 0, 896, 1248, 1760, 2016
PAST = 2048
TS = 16
WINS = (2, 4, 8, 16)
CDEC = -0.6065306597126334
ILV = 2
ILV2 = 2


class Buf:
    def __init__(self, name):
        self.name = name
        self.w = None
        self.r = []


class TB:
    def __init__(self, t, name):
        self.t = t
        self.b = Buf(name)

    def __getitem__(self, k):
        return self.t[k]


class Eng:
    def __init__(self, S, name, eng):
        self.name = name
        self.eng = eng
        self.sem = S.newsem("e_" + name)
        self.cnt = 0
        self.waited = {}

    def wait_tok(self, tok):
        if tok is None:
            return
        sem, val = tok
        if sem is self.sem and self.name == "pe":
            return
        key = id(sem)
        if self.waited.get(key, 0) >= val:
            return
        self.eng.wait_ge(sem, val)
        self.waited[key] = val


class Sched:
    def __init__(self, nc, ctx):
        self.nc = nc
        self.ctx = ctx
        self.pe = Eng(self, "pe", nc.tensor)
        self.act = Eng(self, "act", nc.scalar)
        self.dve = Eng(self, "dve", nc.vector)
        self.pool = Eng(self, "pool", nc.gpsimd)
        self.sp = Eng(self, "sp", nc.sync)
        self.engs = [self.pe, self.act, self.dve, self.pool, self.sp]
        self.dma_sems = {}
        self.ninst = 0

    def newsem(self, name):
        return self.ctx.enter_context(self.nc.semaphore(name))

    def _bufs(self, L):
        return [x.b if isinstance(x, TB) else x for x in L]

    def deps(self, E, R, W):
        for b in R:
            E.wait_tok(b.w)
        for b in W:
            E.wait_tok(b.w)
            for t in (b.r.values() if isinstance(b.r, dict) else b.r):
                E.wait_tok(t)

    def _mark(self, tok, R, W):
        for b in W:
            b.w = tok
            b.r = {}
        for b in R:
            if isinstance(b.r, list):
                b.r = {}
            k = id(tok[0])
            if k not in b.r or b.r[k][1] < tok[1]:
                b.r[k] = tok

    enabled = True

    def ck(self, name):
        import os
        if os.environ.get("KSTOP") == name:
            self.enabled = False

    def op(self, E, fn, *args, R=(), W=(), **kw):
        if not self.enabled:
            return None
        R = self._bufs(R)
        W = self._bufs(W)
        self.deps(E, R, W)
        ins = getattr(E.eng, fn)(*args, **kw)
        E.cnt += 1
        ins.then_inc(E.sem, 1)
        self.ninst += 1
        self._mark((E.sem, E.cnt), R, W)
        return ins

    def dma(self, out, in_, R=(), W=(), key=None, Q=None, **kw):
        if not self.enabled:
            return None
        Q = Q or self.sp
        R = self._bufs(R)
        W = self._bufs(W)
        kb = key.b if isinstance(key, TB) else key
        self.deps(Q, R, W)
        if kb.name not in self.dma_sems:
            self.dma_sems[kb.name] = [self.newsem("d_" + kb.name), 0]
        ent = self.dma_sems[kb.name]
        ins = Q.eng.dma_start(out=out, in_=in_, **kw)
        ent[1] += 16
        ins.then_inc(ent[0], 16)
        self.ninst += 1
        self._mark((ent[0], ent[1]), R, W)
        return ins

    def all_tokens(self):
        toks = [(e.sem, e.cnt) for e in self.engs if e.cnt > 0]
        toks += [(s, v) for (s, v) in self.dma_sems.values()]
        return toks

    def barrier(self):
        toks = self.all_tokens()
        for e in self.engs:
            for t in toks:
                e.wait_tok(t)

    def final_wait(self):
        for (s, v) in self.dma_sems.values():
            self.sp.wait_tok((s, v))


def make_consts():
    c = {}
    i = np.arange(128)
    s = i[:, None]
    t = i[None, :]
    ident = (s == t).astype(np.float32)
    tri_ui = (s <= t).astype(np.float32)
    tri_ut = (s < t).astype(np.float32)
    tri_lt = (s > t).astype(np.float32)
    bands = []
    bandp = []
    for w in WINS:
        bands.append(((s <= t) & (s > t - w)).astype(np.float32))
        bandp.append(((s - 128) > (t - w)).astype(np.float32))
    ones = np.ones((128, 128), np.float32)
    onehot_last = np.zeros((128, 2), np.float32)
    onehot_last[127, 0] = 1.0
    onehot_last[15, 1] = 1.0
    cf = np.concatenate([ident, tri_ui, ones] + bands + bandp + [onehot_last], axis=1)
    sh = (t == s + 1).astype(np.float32)
    elast = np.zeros((128, 128), np.float32)
    elast[127, 0] = 1.0
    cb = np.concatenate([ident, tri_lt, tri_ut, tri_lt, tri_ui, sh, elast], axis=1)
    invc = np.zeros((128, 24), np.float32)
    invc[:, 8:11] = np.array([1 / 192.0, 1 / 128.0, 1 / 32.0], np.float32)
    invc[:, 11:15] = 1 / 64.0
    invc[:, 15:19] = 1 / 32.0
    for g, w in enumerate(WINS):
        invc[:, g] = 1.0 / np.minimum(i + 1, w)
        invc[:, 4 + g] = 1.0 / w
    return cf.astype(np.float32), cb.astype(np.float32), invc


def rope_table(pos):
    half = 16
    inv = (10000.0 ** (-np.arange(half, dtype=np.float32) / half)).astype(np.float32)
    ang = pos.astype(np.float32)[:, None] * inv[None, :]
    cos = np.cos(ang).astype(np.float32)
    sin = np.sin(ang).astype(np.float32)
    return np.concatenate([np.tile(cos, (1, 4)), np.tile(sin, (1, 4))], axis=1).astype(np.float32)


CF_ID, CF_TRI, CF_ONES, CF_BAND, CF_BANDP, CF_OH = 0, 128, 256, 384, 896, 1408
CB_ID, CB_M3, CB_UI, CB_SH, CB_EL = 0, 128, 512, 640, 768

WNAMES = ["norm_g", "w_in", "w_out", "rw_mu", "rw_w0", "rw_w2", "rw_a0", "rw_a2", "rw_kk", "rw_ka", "rw_rk",
          "rw_gn_g", "rw_gn_b", "mla_qa_g", "mla_w_uq", "mla_kva_g", "mla_w_uk", "mla_w_uv",
          "mla_q_norm_g", "mla_k_norm_g", "sgu_w", "sgu_b", "sgu_ln_g", "sgu_ln_b", "pool_w", "pool_scale"]
WSHAPES = {
    "norm_g": [D], "w_in": [D, DIN], "w_out": [D, D], "rw_mu": [896], "rw_w0": [256], "rw_w2": [64, 256],
    "rw_a0": [256], "rw_a2": [64, 256], "rw_kk": [256], "rw_ka": [256], "rw_rk": [256], "rw_gn_g": [256],
    "rw_gn_b": [256], "mla_qa_g": [192], "mla_w_uq": [192, 384], "mla_kva_g": [128], "mla_w_uk": [128, 256],
    "mla_w_uv": [128, 256], "mla_q_norm_g": [96], "mla_k_norm_g": [96], "sgu_w": [4, 128, 128], "sgu_b": [4, 128],
    "sgu_ln_g": [256], "sgu_ln_b": [256], "pool_w": [4, 64, 64], "pool_scale": [256],
}


def build(T, NL, do_sample=True):
    nc = bass.Bass("TRN2", target_bir_lowering=False)
    ctx = ExitStack()
    NT = T // 128

    def din(name, shape, dt=F32):
        return nc.dram_tensor(name, list(shape), dt, kind="ExternalInput").ap()

    def dout(name, shape, dt=F32):
        return nc.dram_tensor(name, list(shape), dt, kind="ExternalOutput").ap()

    def dscr(name, shape, dt):
        return nc.dram_tensor(name, list(shape), dt, kind="Internal").ap()

    x_p = din("x_p", [T, D])
    x_s = din("x_s", [TS, D])
    c_ckv = din("c_ckv", [NL, PAST, 128])
    c_kr = din("c_kr", [NL, PAST, 32])
    st_wkv = din("st_wkv", [NL, 4, 64, 64])
    st_shift = din("st_shift", [NL, 896])
    st_pool = din("st_pool", [NL, 15, 256])
    Wd = {n: din(n, [NL] + WSHAPES[n]) for n in WNAMES}
    cf_d = din("cf", [128, 1410])
    cb_d = din("cb", [128, 896])
    invc_d = din("invc", [128, 24])
    rope_p = din("rope_p", [T, 128])
    rope_s = din("rope_s", [TS, 128])

    o_yp = dout("o_yp", [T, D])
    o_ys = dout("o_ys", [TS, D])
    o_ckvp = dout("o_ckvp", [NL, T, 128])
    o_krp = dout("o_krp", [NL, T, 32])
    o_wkvp = dout("o_wkvp", [NL, 4, 64, 64])
    o_shp = dout("o_shp", [NL, 896])
    o_plp = dout("o_plp", [NL, 15, 256])
    o_ckvs = dout("o_ckvs", [NL, TS, 128])
    o_krs = dout("o_krs", [NL, TS, 32])
    o_wkvs = dout("o_wkvs", [NL, 4, 64, 64])
    o_shs = dout("o_shs", [NL, 896])
    o_pls = dout("o_pls", [NL, 15, 256])
    o_sgv = dout("o_sgv", [NL, TS, 256])

    def scr(tag, TT, TK):
        return dict(
            yg=dscr("yg_" + tag, [TT, 768], BF16), gb=dscr("gb_" + tag, [TT, 256], BF16),
            qt=dscr("qt_" + tag, [4, 96, TT], BF16), kt=dscr("kt_" + tag, [4, 96, TK], BF16),
            v=dscr("v_" + tag, [TK, 260], BF16))
    scr_p = scr("p", T, T)
    scr_s = scr("s", TS, PAST + 128)

    S = Sched(nc, ctx)
    V = lambda fn, *a, **k: S.op(S.dve, fn, *a, **k)
    A = lambda fn, *a, **k: S.op(S.act, fn, *a, **k)
    G = lambda fn, *a, **k: S.op(S.pool, fn, *a, **k)
    M = lambda fn, *a, **k: S.op(S.pe, fn, *a, **k)

    uid = [0]

    def uname(name):
        uid[0] += 1
        return "s%d_%s" % (uid[0], name)

    def sb(name, shape, dt, c=None):
        t = (c or ctx).enter_context(nc.sbuf_tensor(uname(name), list(shape), dt))
        return TB(t, name)

    def ps(name, shape, dt):
        t = ctx.enter_context(nc.psum_tensor(name, list(shape), dt))
        return TB(t, name)

    PF = [ps("pf%d" % i, [128, 512], F32) for i in range(6)]
    PB = [ps("pb%d" % i, [128, 1024], BF16) for i in range(2)]

    cf = sb("cf", [128, 1410], F32)
    cb = sb("cb", [128, 896], BF16)
    cb_stage = sb("cb_stage", [128, 896], F32)
    invc = sb("invc", [128, 24], F32)
    S.dma(cf[:], cf_d[:, :], W=[cf], key=cf)
    S.dma(cb_stage[:], cb_d[:, :], W=[cb_stage], key=cb_stage)
    S.dma(invc[:], invc_d[:, :], W=[invc], key=invc)
    V("tensor_copy", cb[:], cb_stage[:], R=[cb_stage], W=[cb])
    cbr = sb("cbr", [128, 4, 4, 128], BF16)
    for kind, off in enumerate((CB_M3, CB_M3 + 128, CB_UI, CB_ID)):
        for h in range(4):
            V("tensor_copy", cbr[:, kind, h, :], cb[:, off:off + 128], R=[cb], W=[cbr])
    identf = lambda n: cf[0:n, CF_ID:CF_ID + n]
    identb = lambda n: cb[0:n, CB_ID:CB_ID + n]

    wout_b = sb("wout_b", [128, 8, D], BF16)
    wuq_b = sb("wuq_b", [128, 2, 384], BF16)
    wukv_b = sb("wukv_b", [128, 512], BF16)
    w2a2_b = sb("w2a2_b", [128, 512], BF16)
    wsT_b = sb("wsT_b", [128, 4, 128], BF16)
    poolw_b = sb("poolw_b", [128, 2, 128], BF16)
    sgub = sb("sgub", [128, 4], F32)
    ng = sb("ng", [128, 8], F32)
    qag = sb("qag", [128, 2], F32)
    BC_SPEC = [("rw_mu", 896), ("omm", 896), ("rw_w0", 256), ("rw_a0", 256), ("rw_kk", 256), ("rw_ka", 256),
               ("rw_rk", 256), ("rw_gn_g", 256), ("rw_gn_b", 256), ("mla_kva_g", 128), ("mla_q_norm_g", 96),
               ("mla_k_norm_g", 96), ("sgu_ln_g", 256), ("sgu_ln_b", 256)]
    bc_off = {}
    o = 0
    for n, w in BC_SPEC:
        bc_off[n] = (o, w)
        o += w
    bc = sb("bc", [128, o], F32)
    BCv = lambda n, P, a=0, b=None: bc[0:P, bc_off[n][0] + a: bc_off[n][0] + (bc_off[n][1] if b is None else b)]
    stage = sb("stage", [128, 1024], F32)
    stage2 = sb("stage2", [128, 1024], F32)

    WORK = []

    def wsb(name, shape, dt):
        tb = TB(None, name)
        WORK.append((tb, name, list(shape), dt))
        return tb

    def alloc_work(c):
        for tb, name, shape, dt in WORK:
            tb.t = c.enter_context(nc.sbuf_tensor(uname(name), shape, dt))
            tb.b = Buf(name)
        for i in range(4):
            STb[i].w = None
            STb[i].r = {}
        V("memset", vaug[:], 1.0, W=[vaug])
        V("memset", x_f[:], 0.0, W=[x_f])
        V("memset", p_f[:], 0.0, W=[p_f])
        V("memset", qaT[:], 0.0, W=[qaT])

    xt = [wsb("xt%d" % i, [128, D], F32) for i in range(2)]
    ropet = [wsb("ropet%d" % i, [128, 128], F32) for i in range(2)]
    junk = wsb("junk", [128, D], BF16)
    xn_b = wsb("xn_b", [128, D], BF16)
    xnT = wsb("xnT", [128, 8, 128], BF16)
    st1 = wsb("st1", [128, 16], F32)
    st2 = wsb("st2", [128, 16], F32)
    st3 = wsb("st3", [128, 16], F32)
    cstage = [wsb("cstage%d" % i, [128, 160], F32) for i in range(2)]
    st2o = wsb("st2o", [128, 16], F32)
    st3o = wsb("st3o", [128, 16], F32)
    w1o = wsb("w1o", [128, 256], F32)
    w2o = wsb("w2o", [128, 64], F32)
    sg = wsb("sg", [128, D], BF16)
    tmpm = [wsb("tmpm%d" % i, [128, 896], BF16) for i in range(2)]
    zs = wsb("zs", [128, 896], F32)
    lin = wsb("lin", [128, 128], BF16)
    linT = wsb("linT", [128, 128], BF16)
    w1 = wsb("w1", [128, 256], F32)
    w2 = wsb("w2", [128, 256], F32)
    w3 = wsb("w3", [128, 256], F32)
    sw = wsb("sw", [128, 256], F32)
    sa = wsb("sa", [128, 256], F32)
    kk = wsb("kk", [128, 256], F32)
    kp = wsb("kp", [128, 256], F32)
    bb = wsb("bb", [128, 256], F32)
    bcf = wsb("bcf", [128, 4], F32)
    cs_sb = wsb("cs_sb", [128, 256], F32)
    e1 = wsb("e1", [128, 256], F32)
    e2 = wsb("e2", [128, 256], F32)
    e3 = wsb("e3", [128, 256], F32)
    e4 = wsb("e4", [128, 256], F32)
    hat = wsb("hat", [128, 4, 256], BF16)
    bp_b = wsb("bp_b", [128, 256], BF16)
    kp4 = wsb("kp4", [128, 256], F32)
    hT = wsb("hT", [128, 4, 2, 128], BF16)
    wc = wsb("wc", [128, 2], F32)
    scL = wsb("scL", [128, 4, 128], BF16)
    scN = wsb("scN", [128, 4, 128], BF16)
    scK = wsb("scK", [128, 4, 128], BF16)
    mbr_b = wsb("mbr_b", [128, 4, 128], BF16)
    mkr_f = wsb("mkr_f", [128, 4, 128], F32)
    Ab = [wsb("Ab%d" % i, [128, 2, 4, 128], BF16) for i in range(2)]
    Ttb = [wsb("Ttb%d" % i, [128, 4, 128], BF16) for i in range(2)]
    G_b = wsb("G_b", [128, 4, 128], BF16)
    H_b = wsb("H_b", [128, 4, 64], BF16)
    x_f = wsb("x_f", [128, 4, 128], F32)
    z_f = wsb("z_f", [128, 4, 128], F32)
    q_f = wsb("q_f", [128, 2, 128], F32)
    p_f = wsb("p_f", [128, 2, 128], F32)
    ST = wsb("ST", [128, 2, 64], F32)
    STb = [Buf("ST%d" % h) for h in range(4)]
    yrw = wsb("yrw", [128, 256], F32)
    ygp = wsb("ygp", [128, 768], BF16)
    zb_f = wsb("zb_f", [128, 384], F32)
    qa_b = wsb("qa_b", [128, 192], BF16)
    qaT = wsb("qaT", [128, 2, 128], BF16)
    qn = wsb("qn", [128, 4, 96], F32)
    qfb = wsb("qfb", [128, 4, 96], BF16)
    qT = wsb("qT", [96, 4, 128], BF16)
    ckv_f = wsb("ckv_f", [128, 128], F32)
    ckv_b = wsb("ckv_b", [128, 128], BF16)
    ckvT = wsb("ckvT", [128, 128], BF16)
    kr_f = wsb("kr_f", [128, 32], F32)
    kr_r = wsb("kr_r", [128, 32], F32)
    kfull = wsb("kfull", [128, 4, 96], BF16)
    kT = wsb("kT", [96, 4, 128], BF16)
    vaug = wsb("vaug", [128, 4, 65], BF16)
    ug = wsb("ug", [128, 256], F32)
    vn = wsb("vn", [128, 256], F32)
    vn_b = wsb("vn_b", [128, 256], BF16)
    bnst = wsb("bnst", [128, 8], F32)
    zd_f = [wsb("zd_f%d" % i, [128, 256], F32) for i in range(2)]
    d_b = wsb("d_b", [128, 256], BF16)
    dT = wsb("dT", [128, 2, 128], BF16)
    za_f = wsb("za_f", [128, 896], F32)
    wkv_o = wsb("wkv_o", [64, 4, 64], F32)


    def load_small(l):
        P = 128
        for n, w in BC_SPEC:
            if n == "omm":
                continue
            S.dma(BCv(n, P), Wd[n][l].partition_broadcast(128), W=[bc], key=bc)
        V("tensor_scalar", BCv("omm", P), BCv("rw_mu", P), -1.0, 1.0, ALU.mult, ALU.add, R=[bc], W=[bc])
        S.dma(ng[:], Wd["norm_g"][l].rearrange("(k p) -> p k", p=128), W=[ng], key=ng, allow_slow_non_contiguous=True)
        S.dma(qag[:, 0:1], Wd["mla_qa_g"][l][0:128].rearrange("(p o) -> p o", o=1), W=[qag], key=qag)
        S.dma(qag[0:64, 1:2], Wd["mla_qa_g"][l][128:192].rearrange("(p o) -> p o", o=1), W=[qag], key=qag)
        S.dma(sgub[:], Wd["sgu_b"][l].rearrange("h i -> i h"), W=[sgub], key=sgub, allow_slow_non_contiguous=True)
        for k in range(8):
            S.dma(stage[:, 0:D], Wd["w_out"][l][k * 128:(k + 1) * 128, :], W=[stage], key=stage)
            V("tensor_copy", wout_b[:, k, :], stage[:, 0:D], R=[stage], W=[wout_b])
        S.dma(stage[:, 0:384], Wd["mla_w_uq"][l][0:128, :], W=[stage], key=stage)
        V("tensor_scalar", wuq_b[:, 0, :], stage[:, 0:384], qag[:, 0:1], None, ALU.mult, R=[stage, qag], W=[wuq_b])
        S.dma(stage[0:64, 0:384], Wd["mla_w_uq"][l][128:192, :], W=[stage], key=stage)
        V("memset", wuq_b[:, 1, :], 0.0, W=[wuq_b])
        V("tensor_scalar", wuq_b[0:64, 1, :], stage[0:64, 0:384], qag[0:64, 1:2], None, ALU.mult, R=[stage, qag], W=[wuq_b])
        S.dma(stage[:, 0:256], Wd["mla_w_uk"][l], W=[stage], key=stage)
        S.dma(stage[:, 256:512], Wd["mla_w_uv"][l], W=[stage], key=stage)
        V("tensor_copy", wukv_b[:], stage[:, 0:512], R=[stage], W=[wukv_b])
        V("memset", stage[:, 0:512], 0.0, W=[stage])
        S.dma(stage[0:64, 0:256], Wd["rw_w2"][l], W=[stage], key=stage)
        S.dma(stage[64:128, 256:512], Wd["rw_a2"][l], W=[stage], key=stage)
        V("tensor_copy", w2a2_b[:], stage[:, 0:512], R=[stage], W=[w2a2_b])
        S.dma(stage[:, 0:512].rearrange("p (h j) -> p h j", h=4), Wd["sgu_w"][l].rearrange("h i j -> i h j"), W=[stage], key=stage)
        for h in range(4):
            V("tensor_tensor", stage2[:, h * 128:(h + 1) * 128], stage[:, h * 128:(h + 1) * 128], cb[:, CB_M3 + 128:CB_M3 + 256],
              ALU.mult, R=[stage, cb], W=[stage2])
            V("tensor_sub", stage2[:, h * 128:(h + 1) * 128], stage[:, h * 128:(h + 1) * 128], stage2[:, h * 128:(h + 1) * 128],
              R=[stage, stage2], W=[stage2])
            M("transpose", PF[5][:, h * 128:(h + 1) * 128], stage2[:, h * 128:(h + 1) * 128], identf(128), R=[stage2, cf], W=[PF[5]])
        V("tensor_copy", wsT_b[:].rearrange("p h i -> p (h i)"), PF[5][:, 0:512], R=[PF[5]], W=[wsT_b])
        V("memset", stage[:, 0:256], 0.0, W=[stage])
        for g in range(4):
            c = g // 2
            r0 = 64 * (g % 2)
            S.dma(stage[r0:r0 + 64, c * 128 + r0: c * 128 + r0 + 64], Wd["pool_w"][l][g], W=[stage], key=stage)
        S.dma(stage2[:, 0:256], Wd["pool_scale"][l].partition_broadcast(128), W=[stage2], key=stage2)
        V("tensor_tensor", poolw_b[:].rearrange("p c d -> p (c d)"), stage[:, 0:256], stage2[:, 0:256], ALU.mult,
          R=[stage, stage2], W=[poolw_b])

    def load_win(l, win_b):
        for k in range(8):
            yield
            for c0 in range(0, DIN, 1024):
                n = min(1024, DIN - c0)
                st = stage if ((k * 3 + c0 // 1024) % 2 == 0) else stage2
                S.dma(st[:, 0:n], Wd["w_in"][l][k * 128:(k + 1) * 128, c0:c0 + n], W=[st], key=st)
                V("tensor_scalar", win_b[:, k, c0:c0 + n], st[:, 0:n], ng[:, k:k + 1], None, ALU.mult, R=[st, ng], W=[win_b])

    def rstd_from_ss(ss_ap, out_ap, P, n, eps, Rb, Wb, tmp_ap):
        V("tensor_scalar", tmp_ap, ss_ap, 1.0 / n, eps, ALU.mult, ALU.add, R=Rb, W=Wb)
        A("activation", out=tmp_ap, in_=tmp_ap, func=AF.Sqrt, R=Wb, W=Wb)
        V("reciprocal", out_ap, tmp_ap, R=Wb, W=Wb)

    def transposes_b(src, P, widths, pb, dst_ap_fn, Rb, Wdst, evac="act"):
        for i, (ap, w) in enumerate(zip(src, widths)):
            M("transpose", pb[0:w, i * P:(i + 1) * P], ap, identb(P), R=Rb + [cb], W=[pb])

    def rope_apply(dst_a, dst_b, x1, x2, cosv, sinv, t1, t2, Rb, Wb, Tb):
        V("tensor_tensor", t1, x1, cosv, ALU.mult, R=Rb, W=Tb)
        V("tensor_tensor", t2, x2, sinv, ALU.mult, R=Rb, W=Tb)
        V("tensor_tensor", dst_a, t1, t2, ALU.subtract, R=Tb, W=Wb)
        V("tensor_tensor", t1, x1, sinv, ALU.mult, R=Rb + Wb, W=Tb)
        V("tensor_tensor", t2, x2, cosv, ALU.mult, R=Rb, W=Tb)
        V("tensor_tensor", dst_b, t1, t2, ALU.add, R=Tb, W=Wb)

    KEY_KT_SW = Buf("kT_sw")
    KEY_V_SW = Buf("vaug_sw")

    def kv_from_ckv(P, ckvf_ap, krf_ap, Rb, sc, tok0, Q=None):
        V("tensor_copy", ckv_b[0:P, :], ckvf_ap, R=Rb, W=[ckv_b])
        M("transpose", PB[1][:, 0:P], ckv_b[0:P, :], identb(P), R=[ckv_b, cb], W=[PB[1]])
        A("copy", ckvT[:, 0:P], PB[1][:, 0:P], R=[PB[1]], W=[ckvT])
        M("matmul", PF[1][0:P, :], ckvT[:, 0:P], wukv_b[:], start=True, stop=True, R=[ckvT, wukv_b], W=[PF[1]])
        A("activation", out=w1o[0:P, :], in_=PF[1][0:P, 0:256], func=AF.Square, R=[PF[1]], W=[w1o])
        V("tensor_reduce", st2o[0:P, 0:4], w1o[0:P, :].rearrange("p (h d) -> p h d", h=4), AX.X, ALU.add, R=[w1o], W=[st2o])
        rstd_from_ss(st2o[0:P, 0:4], st2o[0:P, 4:8], P, 64.0, 1e-6, [st2o], [st2o], st2o[0:P, 8:12])
        for h in range(4):
            V("scalar_tensor_tensor", kfull[0:P, h, 0:64], PF[1][0:P, 64 * h:64 * h + 64], st2o[0:P, 4 + h:5 + h],
              BCv("mla_k_norm_g", P, 0, 64), ALU.mult, ALU.mult, R=[PF[1], st2o, bc], W=[kfull])
            V("tensor_copy", kfull[0:P, h, 64:96], krf_ap, R=Rb, W=[kfull])
        V("tensor_copy", vaug[0:P, :, 0:64], PF[1][0:P, 256:512].rearrange("p (h d) -> p h d", h=4), R=[PF[1]], W=[vaug])
        for h in range(4):
            M("transpose", PB[1][0:96, h * P:(h + 1) * P], kfull[0:P, h, :], identb(P), R=[kfull, cb], W=[PB[1]])
        A("copy", kT[:, :, 0:P], PB[1][0:96, 0:4 * P].rearrange("p (h t) -> p h t", h=4), R=[PB[1]], W=[kT])
        S.dma(sc["kt"][:, :, tok0:tok0 + P].rearrange("h d t -> d h t"), kT[:, :, 0:P], R=[kT], key=(kT if Q is None else KEY_KT_SW), Q=Q)
        S.dma(sc["v"][tok0:tok0 + P, :], vaug[0:P, :, :].rearrange("p h d -> p (h d)"), R=[vaug], key=(vaug if Q is None else KEY_V_SW), Q=Q)

    def phase1_tile(l, grp, ti, P, xsrc, win_b, sc, first, last, tok0, nlev, st):
        xb_ = xt[ti % 2]
        rp_ = ropet[ti % 2]
        cosv = rp_[0:P, 0:64].rearrange("p (h d) -> p h d", h=4)
        sinv = rp_[0:P, 64:128].rearrange("p (h d) -> p h d", h=4)
        A("activation", out=junk[0:P, :], in_=xb_[0:P, :], func=AF.Square, accum_out=st1[0:P, 0:1], R=[xb_], W=[junk, st1])
        rstd_from_ss(st1[0:P, 0:1], st1[0:P, 1:2], P, float(D), 1e-6, [st1], [st1], st1[0:P, 2:3])
        V("tensor_scalar", xn_b[0:P, :], xb_[0:P, :], st1[0:P, 1:2], None, ALU.mult, R=[xb_, st1], W=[xn_b])
        for k in range(8):
            M("transpose", PB[0][:, k * P:(k + 1) * P], xn_b[0:P, k * 128:(k + 1) * 128], identb(P), R=[xn_b, cb], W=[PB[0]])
        A("copy", xnT[:, :, 0:P], PB[0][:, 0:8 * P].rearrange("p (k t) -> p k t", k=8), R=[PB[0]], W=[xnT])

        def proj(bank, c0, n):
            for k in range(8):
                M("matmul", bank[0:P, 0:n], xnT[:, k, 0:P], win_b[:, k, c0:c0 + n], start=(k == 0), stop=(k == 7),
                  R=[xnT, win_b], W=[bank])

        proj(PF[0], OFF_G, 512)
        proj(PF[1], OFF_G + 512, 512)
        A("activation", out=sg[0:P, 0:512], in_=PF[0][0:P, :], func=AF.Silu, R=[PF[0]], W=[sg])
        A("activation", out=sg[0:P, 512:1024], in_=PF[1][0:P, :], func=AF.Silu, R=[PF[1]], W=[sg])
        S.dma(sc["gb"][tok0:tok0 + P, :], sg[0:P, 256:512], R=[sg], key=sg)

        proj(PF[0], OFF_A, 512)
        proj(PF[1], OFF_A + 512, 384)
        tc_, tp_ = tmpm[ti % 2], tmpm[(ti + 1) % 2]
        V("tensor_tensor", tc_[0:P, 0:512], PF[0][0:P, 0:512], BCv("rw_mu", P, 0, 512), ALU.mult, R=[PF[0], bc], W=[tc_])
        V("tensor_tensor", tc_[0:P, 512:896], PF[1][0:P, 0:384], BCv("rw_mu", P, 512, 896), ALU.mult, R=[PF[1], bc], W=[tc_])
        if last:
            V("tensor_copy", za_f[0:P, 0:512], PF[0][0:P, 0:512], R=[PF[0]], W=[za_f])
            V("tensor_copy", za_f[0:P, 512:896], PF[1][0:P, 0:384], R=[PF[1]], W=[za_f])
            S.dma(st["o_shift"][l:l + 1, :], za_f[P - 1:P, :], R=[za_f], key=za_f)
        V("tensor_tensor", zs[0:P, 0:512], PF[0][0:P, 0:512], BCv("omm", P, 0, 512), ALU.mult, R=[PF[0], bc], W=[zs])
        V("tensor_tensor", zs[0:P, 512:896], PF[1][0:P, 0:384], BCv("omm", P, 512, 896), ALU.mult, R=[PF[1], bc], W=[zs])
        def gen_R():
            for (bank, c0, n) in ((PF[2], 0, 512), (PF[3], 512, 384)):
                M("matmul", bank[0:P, 0:n], cb[0:P, CB_SH:CB_SH + P], tc_[0:P, c0:c0 + n], start=True, stop=False, R=[cb, tc_], W=[bank])
                M("matmul", bank[0:P, 0:n], cb[:, CB_EL:CB_EL + P], tp_[:, c0:c0 + n], start=False, stop=True, R=[cb, tp_], W=[bank])
            V("tensor_add", zs[0:P, 0:512], zs[0:P, 0:512], PF[2][0:P, 0:512], R=[zs, PF[2]], W=[zs])
            V("tensor_add", zs[0:P, 512:896], zs[0:P, 512:896], PF[3][0:P, 0:384], R=[zs, PF[3]], W=[zs])
            r_ = zs[0:P, 0:256]
            k_ = zs[0:P, 256:512]
            v_ = zs[0:P, 512:768]
            A("activation", out=lin[0:P, 0:64], in_=zs[0:P, 768:832], func=AF.Tanh, R=[zs], W=[lin])
            A("copy", lin[0:P, 64:128], zs[0:P, 832:896], R=[zs], W=[lin])
            M("transpose", PB[0][:, 0:P], lin[0:P, :], identb(P), R=[lin, cb], W=[PB[0]])
            A("copy", linT[:, 0:P], PB[0][:, 0:P], R=[PB[0]], W=[linT])
            M("matmul", PF[2][0:P, :], linT[:, 0:P], w2a2_b[:], start=True, stop=True, R=[linT, w2a2_b], W=[PF[2]])
            V("tensor_add", w1[0:P, :], PF[2][0:P, 0:256], BCv("rw_w0", P), R=[PF[2], bc], W=[w1])
            A("activation", out=sw[0:P, :], in_=w1[0:P, :], func=AF.Sigmoid, R=[w1], W=[sw])
            V("tensor_add", w2[0:P, :], PF[2][0:P, 256:512], BCv("rw_a0", P), R=[PF[2], bc], W=[w2])
            A("activation", out=sa[0:P, :], in_=w2[0:P, :], func=AF.Sigmoid, R=[w2], W=[sa])
            V("tensor_tensor", kk[0:P, :], k_, BCv("rw_kk", P), ALU.mult, R=[zs, bc], W=[kk])
            V("tensor_tensor", w3[0:P, :], kk[0:P, :], kk[0:P, :], ALU.mult, R=[kk], W=[w3])
            V("tensor_reduce", st2[0:P, 0:4], w3[0:P, :].rearrange("p (h d) -> p h d", h=4), AX.X, ALU.add, R=[w3], W=[st2])
            V("tensor_scalar", st2[0:P, 4:8], st2[0:P, 0:4], 1e-24, None, ALU.max, R=[st2], W=[st2])
            A("activation", out=st2[0:P, 4:8], in_=st2[0:P, 4:8], func=AF.Sqrt, R=[st2], W=[st2])
            V("reciprocal", st2[0:P, 8:12], st2[0:P, 4:8], R=[st2], W=[st2])
            for h in range(4):
                V("tensor_scalar", kk[0:P, 64 * h:64 * h + 64], kk[0:P, 64 * h:64 * h + 64], st2[0:P, 8 + h:9 + h], None, ALU.mult,
                  R=[kk, st2], W=[kk])
            V("scalar_tensor_tensor", w1[0:P, :], sa[0:P, :], -1.0, BCv("rw_ka", P), ALU.add, ALU.mult, R=[sa, bc], W=[w1])
            V("scalar_tensor_tensor", kp[0:P, :], w1[0:P, :], 1.0, k_, ALU.add, ALU.mult, R=[w1, zs], W=[kp])
            V("tensor_tensor", bb[0:P, :], kk[0:P, :], sa[0:P, :], ALU.mult, R=[kk, sa], W=[bb])
            G("tensor_tensor", w2[0:P, :], r_, kp[0:P, :], ALU.mult, R=[zs, kp], W=[w2])
            G("tensor_tensor", w2[0:P, :], w2[0:P, :], BCv("rw_rk", P), ALU.mult, R=[w2, bc], W=[w2])
            V("tensor_reduce", bcf[0:P, 0:4], w2[0:P, :].rearrange("p (h d) -> p h d", h=4), AX.X, ALU.add, R=[w2], W=[bcf])
            M("matmul", PF[2][0:P, 0:256], cf[0:P, CF_TRI:CF_TRI + P], sw[0:P, :], start=True, stop=True, R=[cf, sw], W=[PF[2]])
            M("matmul", PF[2][0:P, 256:512], cf[0:P, CF_ONES:CF_ONES + P], sw[0:P, :], start=True, stop=True, R=[cf, sw], W=[PF[2]])
            V("tensor_copy", cs_sb[0:P, :], PF[2][0:P, 0:256], R=[PF[2]], W=[cs_sb])
            A("activation", out=e1[0:P, :], in_=PF[2][0:P, 0:256], func=AF.Exp, scale=CDEC, R=[PF[2]], W=[e1])
            A("activation", out=e2[0:P, :], in_=PF[2][0:P, 0:256], func=AF.Exp, scale=-CDEC, R=[PF[2]], W=[e2])
            V("tensor_sub", w1[0:P, :], cs_sb[0:P, :], sw[0:P, :], R=[cs_sb, sw], W=[w1])
            A("activation", out=e3[0:P, :], in_=w1[0:P, :], func=AF.Exp, scale=CDEC, R=[w1], W=[e3])
            V("tensor_sub", w3[0:P, :], PF[2][0:P, 256:512], cs_sb[0:P, :], R=[PF[2], cs_sb], W=[w3])
            A("activation", out=e4[0:P, :], in_=w3[0:P, :], func=AF.Exp, scale=CDEC, R=[w3], W=[e4])
            V("scalar_tensor_tensor", hat[0:P, 0, :], kk[0:P, :], -1.0, e3[0:P, :], ALU.mult, ALU.mult, R=[kk, e3], W=[hat])
            V("tensor_tensor", hat[0:P, 1, :], bb[0:P, :], e2[0:P, :], ALU.mult, R=[bb, e2], W=[hat])
            V("tensor_tensor", hat[0:P, 2, :], kp[0:P, :], e2[0:P, :], ALU.mult, R=[kp, e2], W=[hat])
            V("tensor_tensor", hat[0:P, 3, :], r_, e1[0:P, :], ALU.mult, R=[zs, e1], W=[hat])
            G("tensor_tensor", bp_b[0:P, :], bb[0:P, :], e4[0:P, :], ALU.mult, R=[bb, e4], W=[bp_b])
            G("tensor_tensor", kp4[0:P, :], kp[0:P, :], e4[0:P, :], ALU.mult, R=[kp, e4], W=[kp4])
            ohc = CF_OH + (0 if P == 128 else 1)
            for p in range(2):
                M("matmul", PF[3][:, p:p + 1], e1[0:P, p * 128:(p + 1) * 128], cf[0:P, ohc:ohc + 1], start=True, stop=True,
                  R=[e1, cf], W=[PF[3]])
            V("tensor_copy", wc[:, 0:2], PF[3][:, 0:2], R=[PF[3]], W=[wc])
            for vi in range(4):
                for p in range(2):
                    M("transpose", PB[0][:, (vi * 2 + p) * P:(vi * 2 + p + 1) * P], hat[0:P, vi, p * 128:(p + 1) * 128], identb(P),
                      R=[hat, cb], W=[PB[0]])
            A("copy", hT[:, :, :, 0:P], PB[0][:, 0:8 * P].rearrange("p (v q t) -> p v q t", v=4, q=2), R=[PB[0]], W=[hT])
            def fm(h, vi):
                return hT[64 * (h % 2):64 * (h % 2) + 64, vi, h // 2, 0:P]

            def pv(bank, w=128, n=None):
                n = P if n is None else n
                return bank[0:P, 0:4 * w].rearrange("p (h t) -> p h t", h=4)[:, :, 0:n]

            def msk(kind):
                return cbr[0:P, kind, :, 0:P]

            for h in range(4):
                M("matmul", PF[2][0:P, h * 128:h * 128 + P], fm(h, 0), fm(h, 1), start=True, stop=True, R=[hT], W=[PF[2]])
                M("matmul", PF[3][0:P, h * 128:h * 128 + P], fm(h, 1), fm(h, 0), start=True, stop=True, R=[hT], W=[PF[3]])
                M("matmul", PF[4][0:P, h * 128:h * 128 + P], fm(h, 0), fm(h, 2), start=True, stop=True, R=[hT], W=[PF[4]])
                M("matmul", PF[5][0:P, h * 128:h * 128 + P], fm(h, 1), fm(h, 3), start=True, stop=True, R=[hT], W=[PF[5]])
            V("tensor_tensor", scL[0:P, :, 0:P], pv(PF[2]), msk(0), ALU.mult, R=[PF[2], cbr], W=[scL])
            V("tensor_tensor", scN[0:P, :, 0:P], pv(PF[3]), msk(1), ALU.mult, R=[PF[3], cbr], W=[scN])
            for h in range(4):
                bk = PF[2 + (h % 2)]
                M("matmul", bk[0:P, h * 128:h * 128 + P], fm(h, 2), fm(h, 3), start=True, stop=True, R=[hT], W=[bk])
            V("tensor_add", Ttb[0][0:P, :, 0:P], scL[0:P, :, 0:P], msk(3), R=[scL, cbr], W=[Ttb[0]])
            V("tensor_tensor", scK[0:P, :, 0:P], pv(PF[4]), msk(0), ALU.mult, R=[PF[4], cbr], W=[scK])
            V("tensor_tensor", mbr_b[0:P, :, 0:P], pv(PF[5]), msk(2), ALU.mult, R=[PF[5], cbr], W=[mbr_b])
            for h in range(4):
                bk = PF[2 + (h % 2)]
                V("tensor_tensor", mkr_f[0:P, h, 0:P], bk[0:P, h * 128:h * 128 + P], cb[0:P, CB_UI:CB_UI + P], ALU.mult, R=[bk, cb], W=[mkr_f])
            def acc_(tb, idx):
                return (lambda h: tb[0:P, h, 0:P]) if idx is None else (lambda h: tb[0:P, idx, h, 0:P])

            def squares(Af, ATf, Rb, dst, need_A):
                if need_A:
                    for h in range(4):
                        M("matmul", PF[2][0:P, h * 128:h * 128 + P], ATf(h), Af(h), start=True, stop=True, R=Rb, W=[PF[2]])
                for h in range(4):
                    M("matmul", PF[3][0:P, h * 128:h * 128 + P], Af(h), ATf(h), start=True, stop=True, R=Rb, W=[PF[3]])

            def squares_evac(dst, need_A):
                if need_A:
                    V("tensor_copy", dst[0:P, 0, :, 0:P], pv(PF[2]), R=[PF[2]], W=[dst])
                A("copy", dst[0:P, 1, :, 0:P], pv(PF[3]), R=[PF[3]], W=[dst])

            tcur = 0
            squares(acc_(scL, None), acc_(scN, None), [scL, scN], Ab[1], nlev > 2)
            squares_evac(Ab[1], nlev > 2)
            for k in range(1, nlev):
                cur = Ab[k % 2]
                Af, ATf = acc_(cur, 0), acc_(cur, 1)
                lastk = (k == nlev - 1)
                pbank = PF[4 + (k % 2)]
                for h in range(4):
                    M("matmul", pbank[0:P, h * 128:h * 128 + P], ATf(h), Ttb[tcur][0:P, h, 0:P], start=True, stop=True,
                      R=[cur, Ttb[tcur]], W=[pbank])
                if not lastk:
                    nxt = Ab[(k + 1) % 2]
                    squares(Af, ATf, [cur], nxt, k + 1 < nlev - 1)
                yield "mm"
                V("tensor_add", Ttb[1 - tcur][0:P, :, 0:P], pv(pbank), Ttb[tcur][0:P, :, 0:P], R=[pbank, Ttb[tcur]], W=[Ttb[1 - tcur]])
                if not lastk:
                    squares_evac(nxt, k + 1 < nlev - 1)
                tcur = 1 - tcur
                yield "ev"
            Tt = Ttb[tcur]
            for h in range(4):
                M("matmul", PF[2][0:P, h * 128:h * 128 + P], Tt[0:P, h, 0:P], mbr_b[0:P, h, 0:P], start=True, stop=True, R=[Tt, mbr_b], W=[PF[2]])
            for h in range(4):
                M("matmul", PF[3][0:P, h * 64:h * 64 + 64], Tt[0:P, h, 0:P], bp_b[0:P, 64 * h:64 * h + 64], start=True, stop=True, R=[Tt, bp_b], W=[PF[3]])
            A("copy", G_b[0:P, :, 0:P], pv(PF[2]), R=[PF[2]], W=[G_b])
            V("tensor_copy", H_b[0:P, :, :], PF[3][0:P, 0:256].rearrange("p (h d) -> p h d", h=4), R=[PF[3]], W=[H_b])
            for h in range(4):
                M("matmul", PF[2][0:P, h * 128:h * 128 + P], scK[0:P, h, 0:P], G_b[0:P, h, 0:P], start=True, stop=True, R=[scK, G_b], W=[PF[2]])
            for h in range(4):
                M("matmul", PF[3][:, h * 128:h * 128 + P], hat[0:P, 0, (h // 2) * 128:(h // 2) * 128 + 128], G_b[0:P, h, 0:P], start=True, stop=True,
                  R=[hat, G_b], W=[PF[3]])
            for h in range(4):
                M("matmul", PF[4][0:P, h * 64:h * 64 + 64], scK[0:P, h, 0:P], H_b[0:P, h, :], start=True, stop=True, R=[scK, H_b], W=[PF[4]])
            for h in range(4):
                M("matmul", PF[4][:, 256 + h * 64:256 + h * 64 + 64], hat[0:P, 0, (h // 2) * 128:(h // 2) * 128 + 128], H_b[0:P, h, :],
                  start=True, stop=True, R=[hat, H_b], W=[PF[4]])
            V("tensor_add", z_f[0:P, :, 0:P], pv(PF[2]), mkr_f[0:P, :, 0:P], R=[PF[2], mkr_f], W=[z_f])
            for o_ in (0, 64):
                h0 = o_ // 64
                qv = PF[3][o_:o_ + 64, 0:512].rearrange("p (q r t) -> p q r t", q=2, r=2)[:, :, h0, 0:P]
                V("tensor_add", q_f[o_:o_ + 64, :, 0:P], qv, hT[o_:o_ + 64, 3, :, 0:P], R=[PF[3], hT], W=[q_f])
            for h in range(4):
                p = h // 2
                o_ = 64 * (h % 2)
                V("tensor_add", x_f[0:P, h, o_:o_ + 64], PF[4][0:P, h * 64:h * 64 + 64], kp4[0:P, 64 * h:64 * h + 64], R=[PF[4], kp4], W=[x_f])
                V("scalar_tensor_tensor", p_f[o_:o_ + 64, p, o_:o_ + 64], cf[o_:o_ + 64, CF_ID + o_:CF_ID + o_ + 64], wc[o_:o_ + 64, p:p + 1],
                  PF[4][o_:o_ + 64, 256 + h * 64:256 + h * 64 + 64], ALU.mult, ALU.add, R=[cf, wc, PF[4]], W=[p_f])
            for h in range(4):
                p = h // 2
                o_ = 64 * (h % 2)
                vh = zs[0:P, 512 + 64 * h:512 + 64 * h + 64]
                M("matmul", PF[5][0:P, 64 * h:64 * h + 64], q_f[o_:o_ + 64, p, 0:P], ST[o_:o_ + 64, p, :], start=True, stop=False, R=[q_f, ST], W=[PF[5]])
                M("matmul", PF[5][0:P, 64 * h:64 * h + 64], z_f[0:P, h, 0:P], vh, start=False, stop=True, R=[z_f, zs], W=[PF[5]])
            for p in range(2):
                M("matmul", PF[5][:, 256 + 64 * p:256 + 64 * p + 64], p_f[:, p, :], ST[:, p, :], start=True, stop=False, R=[p_f, ST], W=[PF[5]])
                for h in (2 * p, 2 * p + 1):
                    M("matmul", PF[5][:, 256 + 64 * p:256 + 64 * p + 64], x_f[0:P, h, :], zs[0:P, 512 + 64 * h:512 + 64 * h + 64],
                      start=False, stop=(h == 2 * p + 1), R=[x_f, zs], W=[PF[5]])
            V("tensor_copy", ST[:, :, :], PF[5][:, 256:384].rearrange("k (p v) -> k p v", p=2), R=[PF[5]], W=[ST])
            Y = PF[5]
            V("tensor_reduce", st3[0:P, 0:4], Y[0:P, 0:256].rearrange("p (h d) -> p h d", h=4), AX.X, ALU.add, R=[Y], W=[st3])
            V("tensor_scalar", st3[0:P, 0:4], st3[0:P, 0:4], 1.0 / 64, None, ALU.mult, R=[st3], W=[st3])
            for h in range(4):
                V("tensor_scalar", yrw[0:P, 64 * h:64 * h + 64], Y[0:P, 64 * h:64 * h + 64], st3[0:P, h:h + 1], None, ALU.subtract,
                  R=[Y, st3], W=[yrw])
            V("tensor_tensor", w1[0:P, :], yrw[0:P, :], yrw[0:P, :], ALU.mult, R=[yrw], W=[w1])
            V("tensor_reduce", st3[0:P, 4:8], w1[0:P, :].rearrange("p (h d) -> p h d", h=4), AX.X, ALU.add, R=[w1], W=[st3])
            rstd_from_ss(st3[0:P, 4:8], st3[0:P, 8:12], P, 64.0, 64e-5, [st3], [st3], st3[0:P, 12:16])
            for h in range(4):
                V("tensor_scalar", yrw[0:P, 64 * h:64 * h + 64], yrw[0:P, 64 * h:64 * h + 64], st3[0:P, 8 + h:9 + h], None, ALU.mult,
                  R=[yrw, st3], W=[yrw])
            V("tensor_tensor", yrw[0:P, :], yrw[0:P, :], BCv("rw_gn_g", P), ALU.mult, R=[yrw, bc], W=[yrw])
            V("tensor_add", yrw[0:P, :], yrw[0:P, :], BCv("rw_gn_b", P), R=[yrw, bc], W=[yrw])
            for h in range(4):
                V("scalar_tensor_tensor", yrw[0:P, 64 * h:64 * h + 64], zs[0:P, 512 + 64 * h:512 + 64 * h + 64], bcf[0:P, h:h + 1],
                  yrw[0:P, 64 * h:64 * h + 64], ALU.mult, ALU.add, R=[zs, bcf, yrw], W=[yrw])
            V("tensor_tensor", ygp[0:P, 0:256], yrw[0:P, :], sg[0:P, 0:256], ALU.mult, R=[yrw, sg], W=[ygp])
            if last:
                for p in range(2):
                    M("transpose", PF[2][0:64, p * 128:(p + 1) * 128], ST[:, p, :], identf(128), R=[ST, cf], W=[PF[2]])
                V("tensor_copy", wkv_o[:].rearrange("i h j -> i (h j)"), PF[2][0:64, 0:256], R=[PF[2]], W=[wkv_o])
                S.dma(st["o_wkv"][l].rearrange("h i j -> i h j"), wkv_o[:], R=[wkv_o], key=wkv_o)
            yield

        def gen_O():
            proj(PF[0], OFF_B, 352)
            ZB = PF[0]
            A("activation", out=junk[0:P, 0:192], in_=ZB[0:P, 0:192], func=AF.Square, accum_out=st1[0:P, 4:5], R=[ZB], W=[junk, st1])
            yield
            A("activation", out=junk[0:P, 0:128], in_=ZB[0:P, 192:320], func=AF.Square, accum_out=st1[0:P, 5:6], R=[ZB], W=[junk, st1])
            A("activation", out=junk[0:P, 0:32], in_=ZB[0:P, 320:352], func=AF.Square, accum_out=st1[0:P, 6:7], R=[ZB], W=[junk, st1])
            V("tensor_tensor", st1[0:P, 12:15], st1[0:P, 4:7], invc[0:P, 8:11], ALU.mult, R=[st1, invc], W=[st1])
            V("tensor_scalar", st1[0:P, 12:15], st1[0:P, 12:15], 1e-6, None, ALU.add, R=[st1], W=[st1])
            A("activation", out=st1[0:P, 12:15], in_=st1[0:P, 12:15], func=AF.Sqrt, R=[st1], W=[st1])
            V("reciprocal", st1[0:P, 8:11], st1[0:P, 12:15], R=[st1], W=[st1])
            yield
            V("tensor_scalar", qa_b[0:P, :], ZB[0:P, 0:192], st1[0:P, 8:9], None, ALU.mult, R=[ZB, st1], W=[qa_b])
            yield
            M("transpose", PB[1][:, 0:P], qa_b[0:P, 0:128], identb(P), R=[qa_b, cb], W=[PB[1]])
            M("transpose", PB[1][0:64, P:2 * P], qa_b[0:P, 128:192], identb(P), R=[qa_b, cb], W=[PB[1]])
            A("copy", qaT[:, 0, 0:P], PB[1][:, 0:P], R=[PB[1]], W=[qaT])
            yield
            A("copy", qaT[0:64, 1, 0:P], PB[1][0:64, P:2 * P], R=[PB[1]], W=[qaT])
            M("matmul", PF[1][0:P, 0:384], qaT[:, 0, 0:P], wuq_b[:, 0, :], start=True, stop=False, R=[qaT, wuq_b], W=[PF[1]])
            M("matmul", PF[1][0:P, 0:384], qaT[:, 1, 0:P], wuq_b[:, 1, :], start=False, stop=True, R=[qaT, wuq_b], W=[PF[1]])
            yield
            A("activation", out=zb_f[0:P, :], in_=PF[1][0:P, 0:384], func=AF.Square, R=[PF[1]], W=[zb_f])
            sq3 = zb_f[0:P, :].rearrange("p (h d) -> p h d", h=4)
            V("tensor_reduce", st2o[0:P, 0:4], sq3[:, :, 0:64], AX.X, ALU.add, R=[zb_f], W=[st2o])
            yield
            V("tensor_reduce", st2o[0:P, 4:8], sq3[:, :, 64:96], AX.X, ALU.add, R=[zb_f], W=[st2o])
            V("tensor_tensor", st3o[0:P, 0:8], st2o[0:P, 0:8], invc[0:P, 11:19], ALU.mult, R=[st2o, invc], W=[st3o])
            V("tensor_scalar", st3o[0:P, 0:8], st3o[0:P, 0:8], 1e-6, None, ALU.add, R=[st3o], W=[st3o])
            A("activation", out=st3o[0:P, 0:8], in_=st3o[0:P, 0:8], func=AF.Sqrt, R=[st3o], W=[st3o])
            V("reciprocal", st2o[0:P, 8:16], st3o[0:P, 0:8], R=[st3o], W=[st2o])
            yield
            for h in range(4):
                V("scalar_tensor_tensor", qn[0:P, h, 0:64], PF[1][0:P, 96 * h:96 * h + 64], st2o[0:P, 8 + h:9 + h],
                  BCv("mla_q_norm_g", P, 0, 64), ALU.mult, ALU.mult, R=[PF[1], st2o, bc], W=[qn])
                V("scalar_tensor_tensor", qn[0:P, h, 64:96], PF[1][0:P, 96 * h + 64:96 * h + 96], st2o[0:P, 12 + h:13 + h],
                  BCv("mla_q_norm_g", P, 64, 96), ALU.mult, ALU.mult, R=[PF[1], st2o, bc], W=[qn])
            V("tensor_copy", qfb[0:P, :, 0:64], qn[0:P, :, 0:64], R=[qn], W=[qfb])
            t1 = w1o[0:P, 0:64].rearrange("p (h d) -> p h d", h=4)
            yield
            t2 = w2o[0:P, 0:64].rearrange("p (h d) -> p h d", h=4)
            rope_apply(qfb[0:P, :, 64:80], qfb[0:P, :, 80:96], qn[0:P, :, 64:80], qn[0:P, :, 80:96], cosv, sinv, t1, t2,
                       [qn, rp_], [qfb], [w1o, w2o])
            for h in range(4):
                M("transpose", PB[1][0:96, h * P:(h + 1) * P], qfb[0:P, h, :], identb(P), R=[qfb, cb], W=[PB[1]])
            yield
            A("copy", qT[:, :, 0:P], PB[1][0:96, 0:4 * P].rearrange("p (h t) -> p h t", h=4), R=[PB[1]], W=[qT])
            S.dma(sc["qt"][:, :, tok0:tok0 + P].rearrange("h d t -> d h t"), qT[:, :, 0:P], R=[qT], key=qT)
            V("scalar_tensor_tensor", ckv_f[0:P, :], ZB[0:P, 192:320], st1[0:P, 9:10], BCv("mla_kva_g", P), ALU.mult, ALU.mult,
              R=[ZB, st1, bc], W=[ckv_f])
            yield
            S.dma(st["o_ckv"][l, tok0:tok0 + P, :] if grp == "p" else st["o_ckv"][l, 0:P, :], ckv_f[0:P, :], R=[ckv_f], key=ckv_f)
            V("scalar_tensor_tensor", kr_f[0:P, :], ZB[0:P, 320:352], st1[0:P, 10:11], BCv("mla_k_norm_g", P, 64, 96), ALU.mult, ALU.mult,
              R=[ZB, st1, bc], W=[kr_f])
            rope_apply(kr_r[0:P, 0:16], kr_r[0:P, 16:32], kr_f[0:P, 0:16], kr_f[0:P, 16:32], rp_[0:P, 0:16], rp_[0:P, 64:80],
                       w1o[0:P, 0:16], w2o[0:P, 0:16], [kr_f, rp_], [kr_r], [w1o, w2o])
            yield
            S.dma(st["o_kr"][l, tok0:tok0 + P, :] if grp == "p" else st["o_kr"][l, 0:P, :], kr_r[0:P, :], R=[kr_r], key=kr_r)
            kv_from_ckv(P, ckv_f[0:P, :], kr_r[0:P, :], [ckv_f, kr_r], sc, st["ktok0"] + tok0)

            proj(PF[0], OFF_C, 512)
            yield
            ZC = PF[0]
            V("tensor_tensor", ug[0:P, :], ZC[0:P, 0:256], sg[0:P, 512:768], ALU.mult, R=[ZC, sg], W=[ug])
            V("bn_stats", bnst[0:P, 0:6], ZC[0:P, 256:512], R=[ZC], W=[bnst])
            yield
            V("bn_aggr", bnst[0:P, 6:8], bnst[0:P, 0:6], R=[bnst], W=[bnst])
            rstd_from_ss(bnst[0:P, 7:8], st1[0:P, 3:4], P, 1.0, 1e-5, [bnst], [st1], st1[0:P, 15:16])
            V("tensor_scalar", vn[0:P, :], ZC[0:P, 256:512], bnst[0:P, 6:7], st1[0:P, 3:4], ALU.subtract, ALU.mult, R=[ZC, bnst, st1], W=[vn])
            yield
            V("tensor_tensor", vn[0:P, :], vn[0:P, :], BCv("sgu_ln_g", P), ALU.mult, R=[vn, bc], W=[vn])
            V("tensor_add", vn[0:P, :], vn[0:P, :], BCv("sgu_ln_b", P), R=[vn, bc], W=[vn])
            if grp == "s":
                S.dma(o_sgv[l, 0:P, :], vn[0:P, :], R=[vn], key=vn)
            yield
            V("tensor_copy", vn_b[0:P, :], vn[0:P, :], R=[vn], W=[vn_b])
            for h in range(4):
                M("matmul", PF[1][0:P, 64 * h:64 * h + 64], wsT_b[0:P, h, 0:P], vn_b[0:P, 64 * h:64 * h + 64], start=True, stop=True,
                  R=[wsT_b, vn_b], W=[PF[1]])
            for h in range(4):
                V("scalar_tensor_tensor", ygp[0:P, 256 + 64 * h:256 + 64 * h + 64], PF[1][0:P, 64 * h:64 * h + 64], sgub[0:P, h:h + 1],
                  ug[0:P, 64 * h:64 * h + 64], ALU.add, ALU.mult, R=[PF[1], sgub, ug], W=[ygp])

            yield
            proj(PF[0], OFF_D, 256)
            zc_, zp_ = zd_f[ti % 2], zd_f[(ti + 1) % 2]
            V("tensor_copy", zc_[0:P, :], PF[0][0:P, 0:256], R=[PF[0]], W=[zc_])
            yield
            if last:
                if grp == "p":
                    S.dma(st["o_pool"][l], zc_[P - 15:P, :], R=[zc_], key=zc_)
                else:
                    S.dma(st["o_pool"][l], zc_[1:16, :], R=[zc_], key=zc_)
            for g in range(4):
                M("matmul", PF[1][0:P, 64 * g:64 * g + 64], cf[0:P, CF_BAND + 128 * g:CF_BAND + 128 * g + P], zc_[0:P, 64 * g:64 * g + 64],
                  start=True, stop=False, R=[cf, zc_], W=[PF[1]])
                M("matmul", PF[1][0:P, 64 * g:64 * g + 64], cf[:, CF_BANDP + 128 * g:CF_BANDP + 128 * g + P], zp_[:, 64 * g:64 * g + 64],
                  start=False, stop=True, R=[cf, zp_], W=[PF[1]])
            for g in range(4):
                ic = invc[0:P, g:g + 1] if (first and grp == "p") else invc[0:P, 4 + g:5 + g]
                V("scalar_tensor_tensor", d_b[0:P, 64 * g:64 * g + 64], PF[1][0:P, 64 * g:64 * g + 64], ic, zc_[0:P, 64 * g:64 * g + 64],
                  ALU.mult, ALU.subtract, R=[PF[1], invc, zc_], W=[d_b])
            yield
            for c in range(2):
                M("transpose", PB[1][:, c * P:(c + 1) * P], d_b[0:P, c * 128:(c + 1) * 128], identb(P), R=[d_b, cb], W=[PB[1]])
            A("copy", dT[:, :, 0:P], PB[1][:, 0:2 * P].rearrange("p (c t) -> p c t", c=2), R=[PB[1]], W=[dT])
            for c in range(2):
                M("matmul", PF[0][0:P, c * 128:(c + 1) * 128], dT[:, c, 0:P], poolw_b[:, c, :], start=True, stop=True, R=[dT, poolw_b], W=[PF[0]])
            yield
            V("tensor_tensor", ygp[0:P, 512:768], PF[0][0:P, 0:256], sg[0:P, 768:1024], ALU.mult, R=[PF[0], sg], W=[ygp])
            yield

        gr, go = gen_R(), gen_O()
        if P == 128 and ILV > 0:
            for tag in gr:
                if tag == "mm":
                    for _k in range(ILV):
                        next(go, None)
                elif tag == "ev":
                    for _k in range(ILV2):
                        next(go, None)
        for _ in gr:
            pass
        for _ in go:
            pass
        S.dma(sc["yg"][tok0:tok0 + P, :], ygp[0:P, :], R=[ygp], key=ygp)

    def phase2(l, grp, Tq, QB, nkt_total, klast, xsrc, ydst, sc, KT, VV, bufs):
        (QTb, PT, yfull4, yT, gbt4, xres, xout, rr) = bufs
        nqb = Tq // QB
        nsub = max(1, QB // 128)
        Pq = min(QB, 128)
        for qb in range(nqb):
            q0 = qb * QB
            S.dma(QTb[:, :, 0:QB], sc["qt"][:, :, q0:q0 + QB].rearrange("h d t -> d h t"), W=[QTb], key=QTb)
            for i in range(nsub):
                t0 = q0 + i * 128
                S.dma(yfull4[i][0:Pq, 0:256], sc["yg"][t0:t0 + Pq, 0:256], W=[yfull4[i]], key=yfull4[i])
                S.dma(yfull4[i][0:Pq, 512:1024], sc["yg"][t0:t0 + Pq, 256:768], W=[yfull4[i]], key=yfull4[i])
                S.dma(gbt4[i][0:Pq, :], sc["gb"][t0:t0 + Pq, :], W=[gbt4[i]], key=gbt4[i])
            if grp == "p":
                nkt = 4 * qb + 4
            else:
                nkt = nkt_total
            for h in range(4):
                def kinfo(kt):
                    kp_ = 128 if (grp == "p" or kt < nkt_total - 1) else klast
                    jd = kt - 4 * qb if grp == "p" else -1
                    c0 = 128 * jd if jd > 0 else 0
                    return kp_, jd, c0

                def emit_scores(kt):
                    kp_, jd, c0 = kinfo(kt)
                    n = QB - c0
                    sbk = (PF[4], PF[5], PB[1])[kt % 3]
                    sview = sbk[0:kp_, 0:n] if (kt % 3) < 2 else PB[1][0:kp_, 0:1024].bitcast(F32)[:, 0:n]
                    M("matmul", sview, KT[:, h, kt * 128:kt * 128 + kp_], QTb[:, h, c0:QB], start=True, stop=True,
                      R=[KT, QTb], W=[sbk])
                    pt_ = PT[kt % 3]
                    A("activation", out=pt_[0:kp_, c0:QB], in_=sview, func=AF.Exp, scale=float(1.0 / np.sqrt(96.0)),
                      R=[sbk], W=[pt_])
                    if jd >= 0:
                        G("memset", pt_[64:128, c0:c0 + 64], 0.0, W=[pt_])

                def emit_pv(kt):
                    kp_, jd, c0 = kinfo(kt)
                    pt_ = PT[kt % 3]
                    for i in range(c0 // 128, nsub):
                        first_k = (kt == 0)
                        last_k = (kt == (4 * qb + i if grp == "p" else nkt - 1))
                        M("matmul", PF[i][0:Pq, 0:65], pt_[0:kp_, i * 128:i * 128 + Pq], VV[0:kp_, kt, 65 * h:65 * h + 65],
                          start=first_k, stop=last_k, R=[pt_, VV], W=[PF[i]])

                emit_scores(0)
                if nkt > 1:
                    emit_scores(1)
                for kt in range(nkt):
                    if kt + 2 < nkt:
                        emit_scores(kt + 2)
                    emit_pv(kt)
                for i in range(nsub):
                    V("reciprocal", rr[0:Pq, h:h + 1], PF[i][0:Pq, 64:65], R=[PF[i]], W=[rr])
                    V("scalar_tensor_tensor", yfull4[i][0:Pq, 256 + 64 * h:256 + 64 * h + 64], PF[i][0:Pq, 0:64], rr[0:Pq, h:h + 1],
                      gbt4[i][0:Pq, 64 * h:64 * h + 64], ALU.mult, ALU.mult, R=[PF[i], rr, gbt4[i]], W=[yfull4[i]])
            for i in range(nsub):
                t0 = q0 + i * 128
                yfull = yfull4[i]
                S.dma(xres[0:Pq, :], xsrc[t0:t0 + Pq, :], W=[xres], key=xres)
                for k in range(8):
                    M("transpose", PB[0][:, k * Pq:(k + 1) * Pq], yfull[0:Pq, k * 128:(k + 1) * 128], identb(Pq), R=[yfull, cb], W=[PB[0]])
                A("copy", yT[:, :, 0:Pq], PB[0][:, 0:8 * Pq].rearrange("p (k t) -> p k t", k=8), R=[PB[0]], W=[yT])
                for cbk in range(2):
                    bank = PF[4 + cbk]
                    for k in range(8):
                        M("matmul", bank[0:Pq, :], yT[:, k, 0:Pq], wout_b[:, k, cbk * 512:(cbk + 1) * 512], start=(k == 0), stop=(k == 7),
                          R=[yT, wout_b], W=[bank])
                    V("tensor_add", xout[0:Pq, cbk * 512:(cbk + 1) * 512], bank[0:Pq, :], xres[0:Pq, cbk * 512:(cbk + 1) * 512],
                      R=[bank, xres], W=[xout])
                S.dma(ydst[t0:t0 + Pq, :], xout[0:Pq, :], R=[xout], key=xout)

    stp = dict(o_shift=o_shp, o_wkv=o_wkvp, o_pool=o_plp, o_ckv=o_ckvp, o_kr=o_krp, ktok0=0)
    sts = dict(o_shift=o_shs, o_wkv=o_wkvs, o_pool=o_pls, o_ckv=o_ckvs, o_kr=o_krs, ktok0=PAST)
    nlev_p = 7
    nlev_s = 4
    for l in range(NL):
        load_small(l)
        xsrc_p = x_p if l == 0 else o_yp
        xsrc_s = x_s if l == 0 else o_ys
        with ExitStack() as c1:
            alloc_work(c1)
            win_b = sb("win_b", [128, 8, DIN], BF16, c1)
            def cache_tiles():
                for kt in range(PAST // 128):
                    cs_ = cstage[kt % 2]
                    S.dma(cs_[:, 0:128], c_ckv[l, kt * 128:(kt + 1) * 128, :], W=[cs_], key=cs_)
                    S.dma(cs_[:, 128:160], c_kr[l, kt * 128:(kt + 1) * 128, :], W=[cs_], key=cs_)
                    kv_from_ckv(128, cs_[:, 0:128], cs_[:, 128:160], [cs_], scr_s, kt * 128, Q=S.pool)
                    yield
            gw = load_win(l, win_b)
            gc = cache_tiles() if do_sample else iter(())
            for _ in gw:
                next(gc, None)
                next(gc, None)
            for _ in gc:
                pass
            V("memset", tmpm[1][:], 0.0, W=[tmpm[1]])
            V("memset", zd_f[1][:], 0.0, W=[zd_f[1]])
            V("memset", ST[:], 0.0, W=[ST])
            S.dma(xt[0][:], xsrc_p[0:128, :], W=[xt[0]], key=xt[0])
            S.dma(ropet[0][:], rope_p[0:128, :], W=[ropet[0]], key=ropet[0])
            for ti in range(NT):
                if ti + 1 < NT:
                    S.dma(xt[(ti + 1) % 2][:], xsrc_p[(ti + 1) * 128:(ti + 2) * 128, :], W=[xt[(ti + 1) % 2]], key=xt[(ti + 1) % 2])
                    S.dma(ropet[(ti + 1) % 2][:], rope_p[(ti + 1) * 128:(ti + 2) * 128, :], W=[ropet[(ti + 1) % 2]], key=ropet[(ti + 1) % 2])
                phase1_tile(l, "p", ti, 128, xsrc_p, win_b, scr_p, ti == 0, ti == NT - 1, ti * 128, nlev_p, stp)
            if do_sample:
                P = TS
                V("memset", stage[:, 0:896], 0.0, W=[stage])
                S.dma(stage[127:128, 0:896], st_shift[l:l + 1, :], W=[stage], key=stage)
                V("tensor_tensor", tmpm[1][:, :], stage[:, 0:896], BCv("rw_mu", 128), ALU.mult, R=[stage, bc], W=[tmpm[1]])
                V("memset", zd_f[1][:], 0.0, W=[zd_f[1]])
                S.dma(zd_f[1][113:128, :], st_pool[l], W=[zd_f[1]], key=zd_f[1])
                S.dma(stage2[0:64, 0:256].rearrange("i (h j) -> i h j", h=4), st_wkv[l].rearrange("h i j -> i h j"), W=[stage2], key=stage2)
                for p in range(2):
                    M("transpose", PF[2][:, p * 64:(p + 1) * 64], stage2[0:64, p * 128:(p + 1) * 128], identf(64), R=[stage2, cf], W=[PF[2]])
                V("tensor_copy", ST[:].rearrange("k p v -> k (p v)"), PF[2][:, 0:128], R=[PF[2]], W=[ST])
                S.dma(xt[0][0:P, :], xsrc_s[0:P, :], W=[xt[0]], key=xt[0])
                S.dma(ropet[0][0:P, :], rope_s[0:P, :], W=[ropet[0]], key=ropet[0])
                phase1_tile(l, "s", 0, P, xsrc_s, win_b, scr_s, True, True, 0, nlev_s, sts)
            S.barrier()
        with ExitStack() as c2:
            KT = sb("KT", [96, 4, max(T, PAST + 128)], BF16, c2)
            VV = sb("VV", [128, max(NT, PAST // 128 + 1), 260], BF16, c2)
            QTb = sb("QTb", [96, 4, 512], BF16, c2)
            PT = [sb("PT%d" % i, [128, 512], BF16, c2) for i in range(3)]
            yfull4 = [sb("yfull%d" % i, [128, D], BF16, c2) for i in range(4)]
            yT = sb("yT", [128, 8, 128], BF16, c2)
            gbt4 = [sb("gbt%d" % i, [128, 256], BF16, c2) for i in range(4)]
            xres = sb("xres", [128, D], F32, c2)
            xout = sb("xout", [128, D], F32, c2)
            rr = sb("rr", [128, 4], F32, c2)
            bufs = (QTb, PT, yfull4, yT, gbt4, xres, xout, rr)
            for h in range(4):
                S.dma(KT[:, h, 0:T], scr_p["kt"][h], W=[KT], key=KT)
            vview = scr_p["v"].rearrange("(n p) c -> p n c", p=128)
            for n0 in range(0, NT, 8):
                n1 = min(NT, n0 + 8)
                S.dma(VV[:, n0:n1, :], vview[:, n0:n1, :], W=[VV], key=VV)
            phase2(l, "p", T, 512, NT, 128, xsrc_p, o_yp, scr_p, KT, VV, bufs)
            if do_sample:
                NKS = PAST // 128 + 1
                for h in range(4):
                    S.dma(KT[:, h, 0:PAST + TS], scr_s["kt"][h][:, 0:PAST + TS], W=[KT], key=KT)
                vview = scr_s["v"].rearrange("(n p) c -> p n c", p=128)
                for n0 in range(0, NKS - 1, 8):
                    n1 = min(NKS - 1, n0 + 8)
                    S.dma(VV[:, n0:n1, :], vview[:, n0:n1, :], W=[VV], key=VV)
                S.dma(VV[0:TS, NKS - 1, :], scr_s["v"][PAST:PAST + TS, :], W=[VV], key=VV)
                phase2(l, "s", TS, TS, NKS, TS, xsrc_s, o_ys, scr_s, KT, VV, bufs)
            S.barrier()
    S.final_wait()
    ctx.close()
    return nc, S.ninst


_CACHE = {}


def kernel(**inputs):
    x_prompt = np.asarray(inputs["x_prompt"], np.float32)
    x_sample = np.asarray(inputs["x_sample"], np.float32)
    B, T, _ = x_prompt.shape
    NL = inputs["norm_g"].shape[0]
    key = (T, NL)
    if key not in _CACHE:
        _CACHE[key] = build(T, NL)[0]
    nc = _CACHE[key]
    cf, cb, invc = make_consts()
    rope_p = rope_table(np.arange(T))
    rope_s = rope_table(PAST + np.arange(TS))
    in_maps = []
    for c in range(8):
        m = {
            "x_p": np.ascontiguousarray(x_prompt[c // 2]),
            "x_s": np.ascontiguousarray(x_sample[c]),
            "c_ckv": np.ascontiguousarray(np.asarray(inputs["cache_ckv"], np.float32)[:, c]),
            "c_kr": np.ascontiguousarray(np.asarray(inputs["cache_krope"], np.float32)[:, c]),
            "st_wkv": np.ascontiguousarray(np.asarray(inputs["state_wkv"], np.float32)[:, c]),
            "st_shift": np.ascontiguousarray(np.asarray(inputs["state_shift"], np.float32)[:, c]),
            "st_pool": np.ascontiguousarray(np.asarray(inputs["state_pool"], np.float32)[:, c]),
            "cf": cf, "cb": cb, "invc": invc, "rope_p": rope_p, "rope_s": rope_s,
        }
        for n in WNAMES:
            m[n] = np.ascontiguousarray(np.asarray(inputs[n], np.float32).reshape([NL] + WSHAPES[n]))
        in_maps.append(m)
    res = run_bass_kernel_spmd(nc, in_maps, core_ids=list(range(8)))
    R = res.results
    pc = [R[2 * b] for b in range(B)]
    sc = [R[c] for c in range(8)]
    stk = lambda L, n, ax: np.stack([np.asarray(r[n], np.float32) for r in L], axis=ax)
    out = (
        stk(pc, "o_yp", 0), stk(sc, "o_ys", 0),
        stk(pc, "o_ckvp", 1), stk(pc, "o_krp", 1), stk(pc, "o_wkvp", 1), stk(pc, "o_shp", 1), stk(pc, "o_plp", 1),
        stk(sc, "o_ckvs", 1), stk(sc, "o_krs", 1), stk(sc, "o_wkvs", 1), stk(sc, "o_shs", 1), stk(sc, "o_pls", 1),
        stk(sc, "o_sgv", 1),
    )
    return out
```

```python
import numpy as np
from contextlib import ExitStack
import concourse.bass as bass
import concourse.mybir as mybir
from concourse.bass_utils import run_bass_kernel_spmd

F32 = mybir.dt.float32
BF16 = mybir.dt.bfloat16
AF = mybir.ActivationFunctionType
ALU = mybir.AluOpType
AX = mybir.AxisListType

D = 1024
DIN = 3040
OFF_A, OFF_B, OFF_C, OFF_D, OFF_G = 0, 896, 1248, 1760, 2016
PAST = 2048
TS = 16
WINS = (2, 4, 8, 16)
CDEC = -0.6065306597126334
ILV = 2
ILV2 = 2


class Buf:
    def __init__(self, name):
        self.name = name
        self.w = None
        self.r = []


class TB:
    def __init__(self, t, name):
        self.t = t
        self.b = Buf(name)

    def __getitem__(self, k):
        return self.t[k]


class Eng:
    def __init__(self, S, name, eng):
        self.name = name
        self.eng = eng
        self.sem = S.newsem("e_" + name)
        self.cnt = 0
        self.waited = {}

    def wait_tok(self, tok):
        if tok is None:
            return
        sem, val = tok
        if sem is self.sem and self.name == "pe":
            return
        key = id(sem)
        if self.waited.get(key, 0) >= val:
            return
        self.eng.wait_ge(sem, val)
        self.waited[key] = val


class Sched:
    def __init__(self, nc, ctx):
        self.nc = nc
        self.ctx = ctx
        self.pe = Eng(self, "pe", nc.tensor)
        self.act = Eng(self, "act", nc.scalar)
        self.dve = Eng(self, "dve", nc.vector)
        self.pool = Eng(self, "pool", nc.gpsimd)
        self.sp = Eng(self, "sp", nc.sync)
        self.engs = [self.pe, self.act, self.dve, self.pool, self.sp]
        self.dma_sems = {}
        self.ninst = 0

    def newsem(self, name):
        return self.ctx.enter_context(self.nc.semaphore(name))

    def _bufs(self, L):
        return [x.b if isinstance(x, TB) else x for x in L]

    def deps(self, E, R, W):
        for b in R:
            E.wait_tok(b.w)
        for b in W:
            E.wait_tok(b.w)
            for t in (b.r.values() if isinstance(b.r, dict) else b.r):
                E.wait_tok(t)

    def _mark(self, tok, R, W):
        for b in W:
            b.w = tok
            b.r = {}
        for b in R:
            if isinstance(b.r, list):
                b.r = {}
            k = id(tok[0])
            if k not in b.r or b.r[k][1] < tok[1]:
                b.r[k] = tok

    enabled = True

    def ck(self, name):
        import os
        if os.environ.get("KSTOP") == name:
            self.enabled = False

    def op(self, E, fn, *args, R=(), W=(), **kw):
        if not self.enabled:
            return None
        R = self._bufs(R)
        W = self._bufs(W)
        self.deps(E, R, W)
        ins = getattr(E.eng, fn)(*args, **kw)
        E.cnt += 1
        ins.then_inc(E.sem, 1)
        self.ninst += 1
        self._mark((E.sem, E.cnt), R, W)
        return ins

    def dma(self, out, in_, R=(), W=(), key=None, Q=None, **kw):
        if not self.enabled:
            return None
        Q = Q or self.sp
        R = self._bufs(R)
        W = self._bufs(W)
        kb = key.b if isinstance(key, TB) else key
        self.deps(Q, R, W)
        if kb.name not in self.dma_sems:
            self.dma_sems[kb.name] = [self.newsem("d_" + kb.name), 0]
        ent = self.dma_sems[kb.name]
        ins = Q.eng.dma_start(out=out, in_=in_, **kw)
        ent[1] += 16
        ins.then_inc(ent[0], 16)
        self.ninst += 1
        self._mark((ent[0], ent[1]), R, W)
        return ins

    def all_tokens(self):
        toks = [(e.sem, e.cnt) for e in self.engs if e.cnt > 0]
        toks += [(s, v) for (s, v) in self.dma_sems.values()]
        return toks

    def barrier(self):
        toks = self.all_tokens()
        for e in self.engs:
            for t in toks:
                e.wait_tok(t)

    def final_wait(self):
        for (s, v) in self.dma_sems.values():
            self.sp.wait_tok((s, v))


def make_consts():
    c = {}
    i = np.arange(128)
    s = i[:, None]
    t = i[None, :]
    ident = (s == t).astype(np.float32)
    tri_ui = (s <= t).astype(np.float32)
    tri_ut = (s < t).astype(np.float32)
    tri_lt = (s > t).astype(np.float32)
    bands = []
    bandp = []
    for w in WINS:
        bands.append(((s <= t) & (s > t - w)).astype(np.float32))
        bandp.append(((s - 128) > (t - w)).astype(np.float32))
    ones = np.ones((128, 128), np.float32)
    onehot_last = np.zeros((128, 2), np.float32)
    onehot_last[127, 0] = 1.0
    onehot_last[15, 1] = 1.0
    cf = np.concatenate([ident, tri_ui, ones] + bands + bandp + [onehot_last], axis=1)
    sh = (t == s + 1).astype(np.float32)
    elast = np.zeros((128, 128), np.float32)
    elast[127, 0] = 1.0
    cb = np.concatenate([ident, tri_lt, tri_ut, tri_lt, tri_ui, sh, elast], axis=1)
    invc = np.zeros((128, 24), np.float32)
    invc[:, 8:11] = np.array([1 / 192.0, 1 / 128.0, 1 / 32.0], np.float32)
    invc[:, 11:15] = 1 / 64.0
    invc[:, 15:19] = 1 / 32.0
    for g, w in enumerate(WINS):
        invc[:, g] = 1.0 / np.minimum(i + 1, w)
        invc[:, 4 + g] = 1.0 / w
    return cf.astype(np.float32), cb.astype(np.float32), invc


def rope_table(pos):
    half = 16
    inv = (10000.0 ** (-np.arange(half, dtype=np.float32) / half)).astype(np.float32)
    ang = pos.astype(np.float32)[:, None] * inv[None, :]
    cos = np.cos(ang).astype(np.float32)
    sin = np.sin(ang).astype(np.float32)
    return np.concatenate([np.tile(cos, (1, 4)), np.tile(sin, (1, 4))], axis=1).astype(np.float32)


CF_ID, CF_TRI, CF_ONES, CF_BAND, CF_BANDP, CF_OH = 0, 128, 256, 384, 896, 1408
CB_ID, CB_M3, CB_UI, CB_SH, CB_EL = 0, 128, 512, 640, 768

WNAMES = ["norm_g", "w_in", "w_out", "rw_mu", "rw_w0", "rw_w2", "rw_a0", "rw_a2", "rw_kk", "rw_ka", "rw_rk",
          "rw_gn_g", "rw_gn_b", "mla_qa_g", "mla_w_uq", "mla_kva_g", "mla_w_uk", "mla_w_uv",
          "mla_q_norm_g", "mla_k_norm_g", "sgu_w", "sgu_b", "sgu_ln_g", "sgu_ln_b", "pool_w", "pool_scale"]
WSHAPES = {
    "norm_g": [D], "w_in": [D, DIN], "w_out": [D, D], "rw_mu": [896], "rw_w0": [256], "rw_w2": [64, 256],
    "rw_a0": [256], "rw_a2": [64, 256], "rw_kk": [256], "rw_ka": [256], "rw_rk": [256], "rw_gn_g": [256],
    "rw_gn_b": [256], "mla_qa_g": [192], "mla_w_uq": [192, 384], "mla_kva_g": [128], "mla_w_uk": [128, 256],
    "mla_w_uv": [128, 256], "mla_q_norm_g": [96], "mla_k_norm_g": [96], "sgu_w": [4, 128, 128], "sgu_b": [4, 128],
    "sgu_ln_g": [256], "sgu_ln_b": [256], "pool_w": [4, 64, 64], "pool_scale": [256],
}


def build(T, NL, do_sample=True):
    nc = bass.Bass("TRN2", target_bir_lowering=False)
    ctx = ExitStack()
    NT = T // 128

    def din(name, shape, dt=F32):
        return nc.dram_tensor(name, list(shape), dt, kind="ExternalInput").ap()

    def dout(name, shape, dt=F32):
        return nc.dram_tensor(name, list(shape), dt, kind="ExternalOutput").ap()

    def dscr(name, shape, dt):
        return nc.dram_tensor(name, list(shape), dt, kind="Internal").ap()

    x_p = din("x_p", [T, D])
    x_s = din("x_s", [TS, D])
    c_ckv = din("c_ckv", [NL, PAST, 128])
    c_kr = din("c_kr", [NL, PAST, 32])
    st_wkv = din("st_wkv", [NL, 4, 64, 64])
    st_shift = din("st_shift", [NL, 896])
    st_pool = din("st_pool", [NL, 15, 256])
    Wd = {n: din(n, [NL] + WSHAPES[n]) for n in WNAMES}
    cf_d = din("cf", [128, 1410])
    cb_d = din("cb", [128, 896])
    invc_d = din("invc", [128, 24])
    rope_p = din("rope_p", [T, 128])
    rope_s = din("rope_s", [TS, 128])

    o_yp = dout("o_yp", [T, D])
    o_ys = dout("o_ys", [TS, D])
    o_ckvp = dout("o_ckvp", [NL, T, 128])
    o_krp = dout("o_krp", [NL, T, 32])
    o_wkvp = dout("o_wkvp", [NL, 4, 64, 64])
    o_shp = dout("o_shp", [NL, 896])
    o_plp = dout("o_plp", [NL, 15, 256])
    o_ckvs = dout("o_ckvs", [NL, TS, 128])
    o_krs = dout("o_krs", [NL, TS, 32])
    o_wkvs = dout("o_wkvs", [NL, 4, 64, 64])
    o_shs = dout("o_shs", [NL, 896])
    o_pls = dout("o_pls", [NL, 15, 256])
    o_sgv = dout("o_sgv", [NL, TS, 256])

    def scr(tag, TT, TK):
        return dict(
            yg=dscr("yg_" + tag, [TT, 768], BF16), gb=dscr("gb_" + tag, [TT, 256], BF16),
            qt=dscr("qt_" + tag, [4, 96, TT], BF16), kt=dscr("kt_" + tag, [4, 96, TK], BF16),
            v=dscr("v_" + tag, [TK, 260], BF16))
    scr_p = scr("p", T, T)
    scr_s = scr("s", TS, PAST + 128)

    S = Sched(nc, ctx)
    V = lambda fn, *a, **k: S.op(S.dve, fn, *a, **k)
    A = lambda fn, *a, **k: S.op(S.act, fn, *a, **k)
    G = lambda fn, *a, **k: S.op(S.pool, fn, *a, **k)
    M = lambda fn, *a, **k: S.op(S.pe, fn, *a, **k)

    uid = [0]

    def uname(name):
        uid[0] += 1
        return "s%d_%s" % (uid[0], name)

    def sb(name, shape, dt, c=None):
        t = (c or ctx).enter_context(nc.sbuf_tensor(uname(name), list(shape), dt))
        return TB(t, name)

    def ps(name, shape, dt):
        t = ctx.enter_context(nc.psum_tensor(name, list(shape), dt))
        return TB(t, name)

    PF = [ps("pf%d" % i, [128, 512], F32) for i in range(6)]
    PB = [ps("pb%d" % i, [128, 1024], BF16) for i in range(2)]

    cf = sb("cf", [128, 1410], F32)
    cb = sb("cb", [128, 896], BF16)
    cb_stage = sb("cb_stage", [128, 896], F32)
    invc = sb("invc", [128, 24], F32)
    S.dma(cf[:], cf_d[:, :], W=[cf], key=cf)
    S.dma(cb_stage[:], cb_d[:, :], W=[cb_stage], key=cb_stage)
    S.dma(invc[:], invc_d[:, :], W=[invc], key=invc)
    V("tensor_copy", cb[:], cb_stage[:], R=[cb_stage], W=[cb])
    cbr = sb("cbr", [128, 4, 4, 128], BF16)
    for kind, off in enumerate((CB_M3, CB_M3 + 128, CB_UI, CB_ID)):
        for h in range(4):
            V("tensor_copy", cbr[:, kind, h, :], cb[:, off:off + 128], R=[cb], W=[cbr])
    identf = lambda n: cf[0:n, CF_ID:CF_ID + n]
    identb = lambda n: cb[0:n, CB_ID:CB_ID + n]

    wout_b = sb("wout_b", [128, 8, D], BF16)
    wuq_b = sb("wuq_b", [128, 2, 384], BF16)
    wukv_b = sb("wukv_b", [128, 512], BF16)
    w2a2_b = sb("w2a2_b", [128, 512], BF16)
    wsT_b = sb("wsT_b", [128, 4, 128], BF16)
    poolw_b = sb("poolw_b", [128, 2, 128], BF16)
    sgub = sb("sgub", [128, 4], F32)
    ng = sb("ng", [128, 8], F32)
    qag = sb("qag", [128, 2], F32)
    BC_SPEC = [("rw_mu", 896), ("omm", 896), ("rw_w0", 256), ("rw_a0", 256), ("rw_kk", 256), ("rw_ka", 256),
               ("rw_rk", 256), ("rw_gn_g", 256), ("rw_gn_b", 256), ("mla_kva_g", 128), ("mla_q_norm_g", 96),
               ("mla_k_norm_g", 96), ("sgu_ln_g", 256), ("sgu_ln_b", 256)]
    bc_off = {}
    o = 0
    for n, w in BC_SPEC:
        bc_off[n] = (o, w)
        o += w
    bc = sb("bc", [128, o], F32)
    BCv = lambda n, P, a=0, b=None: bc[0:P, bc_off[n][0] + a: bc_off[n][0] + (bc_off[n][1] if b is None else b)]
    stage = sb("stage", [128, 1024], F32)
    stage2 = sb("stage2", [128, 1024], F32)

    WORK = []

    def wsb(name, shape, dt):
        tb = TB(None, name)
        WORK.append((tb, name, list(shape), dt))
        return tb

    def alloc_work(c):
        for tb, name, shape, dt in WORK:
            tb.t = c.enter_context(nc.sbuf_tensor(uname(name), shape, dt))
            tb.b = Buf(name)
        for i in range(4):
            STb[i].w = None
            STb[i].r = {}
        V("memset", vaug[:], 1.0, W=[vaug])
        V("memset", x_f[:], 0.0, W=[x_f])
        V("memset", p_f[:], 0.0, W=[p_f])
        V("memset", qaT[:], 0.0, W=[qaT])

    xt = [wsb("xt%d" % i, [128, D], F32) for i in range(2)]
    ropet = [wsb("ropet%d" % i, [128, 128], F32) for i in range(2)]
    junk = wsb("junk", [128, D], BF16)
    xn_b = wsb("xn_b", [128, D], BF16)
    xnT = wsb("xnT", [128, 8, 128], BF16)
    st1 = wsb("st1", [128, 16], F32)
    st2 = wsb("st2", [128, 16], F32)
    st3 = wsb("st3", [128, 16], F32)
    cstage = [wsb("cstage%d" % i, [128, 160], F32) for i in range(2)]
    st2o = wsb("st2o", [128, 16], F32)
    st3o = wsb("st3o", [128, 16], F32)
    w1o = wsb("w1o", [128, 256], F32)
    w2o = wsb("w2o", [128, 64], F32)
    sg = wsb("sg", [128, D], BF16)
    tmpm = [wsb("tmpm%d" % i, [128, 896], BF16) for i in range(2)]
    zs = wsb("zs", [128, 896], F32)
    lin = wsb("lin", [128, 128], BF16)
    linT = wsb("linT", [128, 128], BF16)
    w1 = wsb("w1", [128, 256], F32)
    w2 = wsb("w2", [128, 256], F32)
    w3 = wsb("w3", [128, 256], F32)
    sw = wsb("sw", [128, 256], F32)
    sa = wsb("sa", [128, 256], F32)
    kk = wsb("kk", [128, 256], F32)
    kp = wsb("kp", [128, 256], F32)
    bb = wsb("bb", [128, 256], F32)
    bcf = wsb("bcf", [128, 4], F32)
    cs_sb = wsb("cs_sb", [128, 256], F32)
    e1 = wsb("e1", [128, 256], F32)
    e2 = wsb("e2", [128, 256], F32)
    e3 = wsb("e3", [128, 256], F32)
    e4 = wsb("e4", [128, 256], F32)
    hat = wsb("hat", [128, 4, 256], BF16)
    bp_b = wsb("bp_b", [128, 256], BF16)
    kp4 = wsb("kp4", [128, 256], F32)
    hT = wsb("hT", [128, 4, 2, 128], BF16)
    wc = wsb("wc", [128, 2], F32)
    scL = wsb("scL", [128, 4, 128], BF16)
    scN = wsb("scN", [128, 4, 128], BF16)
    scK = wsb("scK", [128, 4, 128], BF16)
    mbr_b = wsb("mbr_b", [128, 4, 128], BF16)
    mkr_f = wsb("mkr_f", [128, 4, 128], F32)
    Ab = [wsb("Ab%d" % i, [128, 2, 4, 128], BF16) for i in range(2)]
    Ttb = [wsb("Ttb%d" % i, [128, 4, 128], BF16) for i in range(2)]
    G_b = wsb("G_b", [128, 4, 128], BF16)
    H_b = wsb("H_b", [128, 4, 64], BF16)
    x_f = wsb("x_f", [128, 4, 128], F32)
    z_f = wsb("z_f", [128, 4, 128], F32)
    q_f = wsb("q_f", [128, 2, 128], F32)
    p_f = wsb("p_f", [128, 2, 128], F32)
    ST = wsb("ST", [128, 2, 64], F32)
    STb = [Buf("ST%d" % h) for h in range(4)]
    yrw = wsb("yrw", [128, 256], F32)
    ygp = wsb("ygp", [128, 768], BF16)
    zb_f = wsb("zb_f", [128, 384], F32)
    qa_b = wsb("qa_b", [128, 192], BF16)
    qaT = wsb("qaT", [128, 2, 128], BF16)
    qn = wsb("qn", [128, 4, 96], F32)
    qfb = wsb("qfb", [128, 4, 96], BF16)
    qT = wsb("qT", [96, 4, 128], BF16)
    ckv_f = wsb("ckv_f", [128, 128], F32)
    ckv_b = wsb("ckv_b", [128, 128], BF16)
    ckvT = wsb("ckvT", [128, 128], BF16)
    kr_f = wsb("kr_f", [128, 32], F32)
    kr_r = wsb("kr_r", [128, 32], F32)
    kfull = wsb("kfull", [128, 4, 96], BF16)
    kT = wsb("kT", [96, 4, 128], BF16)
    vaug = wsb("vaug", [128, 4, 65], BF16)
    ug = wsb("ug", [128, 256], F32)
    vn = wsb("vn", [128, 256], F32)
    vn_b = wsb("vn_b", [128, 256], BF16)
    bnst = wsb("bnst", [128, 8], F32)
    zd_f = [wsb("zd_f%d" % i, [128, 256], F32) for i in range(2)]
    d_b = wsb("d_b", [128, 256], BF16)
    dT = wsb("dT", [128, 2, 128], BF16)
    za_f = wsb("za_f", [128, 896], F32)
    wkv_o = wsb("wkv_o", [64, 4, 64], F32)


    def load_small(l):
        P = 128
        for n, w in BC_SPEC:
            if n == "omm":
                continue
            S.dma(BCv(n, P), Wd[n][l].partition_broadcast(128), W=[bc], key=bc)
        V("tensor_scalar", BCv("omm", P), BCv("rw_mu", P), -1.0, 1.0, ALU.mult, ALU.add, R=[bc], W=[bc])
        S.dma(ng[:], Wd["norm_g"][l].rearrange("(k p) -> p k", p=128), W=[ng], key=ng, allow_slow_non_contiguous=True)
        S.dma(qag[:, 0:1], Wd["mla_qa_g"][l][0:128].rearrange("(p o) -> p o", o=1), W=[qag], key=qag)
        S.dma(qag[0:64, 1:2], Wd["mla_qa_g"][l][128:192].rearrange("(p o) -> p o", o=1), W=[qag], key=qag)
        S.dma(sgub[:], Wd["sgu_b"][l].rearrange("h i -> i h"), W=[sgub], key=sgub, allow_slow_non_contiguous=True)
        for k in range(8):
            S.dma(stage[:, 0:D], Wd["w_out"][l][k * 128:(k + 1) * 128, :], W=[stage], key=stage)
            V("tensor_copy", wout_b[:, k, :], stage[:, 0:D], R=[stage], W=[wout_b])
        S.dma(stage[:, 0:384], Wd["mla_w_uq"][l][0:128, :], W=[stage], key=stage)
        V("tensor_scalar", wuq_b[:, 0, :], stage[:, 0:384], qag[:, 0:1], None, ALU.mult, R=[stage, qag], W=[wuq_b])
        S.dma(stage[0:64, 0:384], Wd["mla_w_uq"][l][128:192, :], W=[stage], key=stage)
        V("memset", wuq_b[:, 1, :], 0.0, W=[wuq_b])
        V("tensor_scalar", wuq_b[0:64, 1, :], stage[0:64, 0:384], qag[0:64, 1:2], None, ALU.mult, R=[stage, qag], W=[wuq_b])
        S.dma(stage[:, 0:256], Wd["mla_w_uk"][l], W=[stage], key=stage)
        S.dma(stage[:, 256:512], Wd["mla_w_uv"][l], W=[stage], key=stage)
        V("tensor_copy", wukv_b[:], stage[:, 0:512], R=[stage], W=[wukv_b])
        V("memset", stage[:, 0:512], 0.0, W=[stage])
        S.dma(stage[0:64, 0:256], Wd["rw_w2"][l], W=[stage], key=stage)
        S.dma(stage[64:128, 256:512], Wd["rw_a2"][l], W=[stage], key=stage)
        V("tensor_copy", w2a2_b[:], stage[:, 0:512], R=[stage], W=[w2a2_b])
        S.dma(stage[:, 0:512].rearrange("p (h j) -> p h j", h=4), Wd["sgu_w"][l].rearrange("h i j -> i h j"), W=[stage], key=stage)
        for h in range(4):
            V("tensor_tensor", stage2[:, h * 128:(h + 1) * 128], stage[:, h * 128:(h + 1) * 128], cb[:, CB_M3 + 128:CB_M3 + 256],
              ALU.mult, R=[stage, cb], W=[stage2])
            V("tensor_sub", stage2[:, h * 128:(h + 1) * 128], stage[:, h * 128:(h + 1) * 128], stage2[:, h * 128:(h + 1) * 128],
              R=[stage, stage2], W=[stage2])
            M("transpose", PF[5][:, h * 128:(h + 1) * 128], stage2[:, h * 128:(h + 1) * 128], identf(128), R=[stage2, cf], W=[PF[5]])
        V("tensor_copy", wsT_b[:].rearrange("p h i -> p (h i)"), PF[5][:, 0:512], R=[PF[5]], W=[wsT_b])
        V("memset", stage[:, 0:256], 0.0, W=[stage])
        for g in range(4):
            c = g // 2
            r0 = 64 * (g % 2)
            S.dma(stage[r0:r0 + 64, c * 128 + r0: c * 128 + r0 + 64], Wd["pool_w"][l][g], W=[stage], key=stage)
        S.dma(stage2[:, 0:256], Wd["pool_scale"][l].partition_broadcast(128), W=[stage2], key=stage2)
        V("tensor_tensor", poolw_b[:].rearrange("p c d -> p (c d)"), stage[:, 0:256], stage2[:, 0:256], ALU.mult,
          R=[stage, stage2], W=[poolw_b])

    def load_win(l, win_b):
        for k in range(8):
            yield
            for c0 in range(0, DIN, 1024):
                n = min(1024, DIN - c0)
                st = stage if ((k * 3 + c0 // 1024) % 2 == 0) else stage2
                S.dma(st[:, 0:n], Wd["w_in"][l][k * 128:(k + 1) * 128, c0:c0 + n], W=[st], key=st)
                V("tensor_scalar", win_b[:, k, c0:c0 + n], st[:, 0:n], ng[:, k:k + 1], None, ALU.mult, R=[st, ng], W=[win_b])

    def rstd_from_ss(ss_ap, out_ap, P, n, eps, Rb, Wb, tmp_ap):
        V("tensor_scalar", tmp_ap, ss_ap, 1.0 / n, eps, ALU.mult, ALU.add, R=Rb, W=Wb)
        A("activation", out=tmp_ap, in_=tmp_ap, func=AF.Sqrt, R=Wb, W=Wb)
        V("reciprocal", out_ap, tmp_ap, R=Wb, W=Wb)

    def transposes_b(src, P, widths, pb, dst_ap_fn, Rb, Wdst, evac="act"):
        for i, (ap, w) in enumerate(zip(src, widths)):
            M("transpose", pb[0:w, i * P:(i + 1) * P], ap, identb(P), R=Rb + [cb], W=[pb])

    def rope_apply(dst_a, dst_b, x1, x2, cosv, sinv, t1, t2, Rb, Wb, Tb):
        V("tensor_tensor", t1, x1, cosv, ALU.mult, R=Rb, W=Tb)
        V("tensor_tensor", t2, x2, sinv, ALU.mult, R=Rb, W=Tb)
        V("tensor_tensor", dst_a, t1, t2, ALU.subtract, R=Tb, W=Wb)
        V("tensor_tensor", t1, x1, sinv, ALU.mult, R=Rb + Wb, W=Tb)
        V("tensor_tensor", t2, x2, cosv, ALU.mult, R=Rb, W=Tb)
        V("tensor_tensor", dst_b, t1, t2, ALU.add, R=Tb, W=Wb)

    KEY_KT_SW = Buf("kT_sw")
    KEY_V_SW = Buf("vaug_sw")

    def kv_from_ckv(P, ckvf_ap, krf_ap, Rb, sc, tok0, Q=None):
        V("tensor_copy", ckv_b[0:P, :], ckvf_ap, R=Rb, W=[ckv_b])
        M("transpose", PB[1][:, 0:P], ckv_b[0:P, :], identb(P), R=[ckv_b, cb], W=[PB[1]])
        A("copy", ckvT[:, 0:P], PB[1][:, 0:P], R=[PB[1]], W=[ckvT])
        M("matmul", PF[1][0:P, :], ckvT[:, 0:P], wukv_b[:], start=True, stop=True, R=[ckvT, wukv_b], W=[PF[1]])
        A("activation", out=w1o[0:P, :], in_=PF[1][0:P, 0:256], func=AF.Square, R=[PF[1]], W=[w1o])
        V("tensor_reduce", st2o[0:P, 0:4], w1o[0:P, :].rearrange("p (h d) -> p h d", h=4), AX.X, ALU.add, R=[w1o], W=[st2o])
        rstd_from_ss(st2o[0:P, 0:4], st2o[0:P, 4:8], P, 64.0, 1e-6, [st2o], [st2o], st2o[0:P, 8:12])
        for h in range(4):
            V("scalar_tensor_tensor", kfull[0:P, h, 0:64], PF[1][0:P, 64 * h:64 * h + 64], st2o[0:P, 4 + h:5 + h],
              BCv("mla_k_norm_g", P, 0, 64), ALU.mult, ALU.mult, R=[PF[1], st2o, bc], W=[kfull])
            V("tensor_copy", kfull[0:P, h, 64:96], krf_ap, R=Rb, W=[kfull])
        V("tensor_copy", vaug[0:P, :, 0:64], PF[1][0:P, 256:512].rearrange("p (h d) -> p h d", h=4), R=[PF[1]], W=[vaug])
        for h in range(4):
            M("transpose", PB[1][0:96, h * P:(h + 1) * P], kfull[0:P, h, :], identb(P), R=[kfull, cb], W=[PB[1]])
        A("copy", kT[:, :, 0:P], PB[1][0:96, 0:4 * P].rearrange("p (h t) -> p h t", h=4), R=[PB[1]], W=[kT])
        S.dma(sc["kt"][:, :, tok0:tok0 + P].rearrange("h d t -> d h t"), kT[:, :, 0:P], R=[kT], key=(kT if Q is None else KEY_KT_SW), Q=Q)
        S.dma(sc["v"][tok0:tok0 + P, :], vaug[0:P, :, :].rearrange("p h d -> p (h d)"), R=[vaug], key=(vaug if Q is None else KEY_V_SW), Q=Q)

    def phase1_tile(l, grp, ti, P, xsrc, win_b, sc, first, last, tok0, nlev, st):
        xb_ = xt[ti % 2]
        rp_ = ropet[ti % 2]
        cosv = rp_[0:P, 0:64].rearrange("p (h d) -> p h d", h=4)
        sinv = rp_[0:P, 64:128].rearrange("p (h d) -> p h d", h=4)
        A("activation", out=junk[0:P, :], in_=xb_[0:P, :], func=AF.Square, accum_out=st1[0:P, 0:1], R=[xb_], W=[junk, st1])
        rstd_from_ss(st1[0:P, 0:1], st1[0:P, 1:2], P, float(D), 1e-6, [st1], [st1], st1[0:P, 2:3])
        V("tensor_scalar", xn_b[0:P, :], xb_[0:P, :], st1[0:P, 1:2], None, ALU.mult, R=[xb_, st1], W=[xn_b])
        for k in range(8):
            M("transpose", PB[0][:, k * P:(k + 1) * P], xn_b[0:P, k * 128:(k + 1) * 128], identb(P), R=[xn_b, cb], W=[PB[0]])
        A("copy", xnT[:, :, 0:P], PB[0][:, 0:8 * P].rearrange("p (k t) -> p k t", k=8), R=[PB[0]], W=[xnT])

        def proj(bank, c0, n):
            for k in range(8):
                M("matmul", bank[0:P, 0:n], xnT[:, k, 0:P], win_b[:, k, c0:c0 + n], start=(k == 0), stop=(k == 7),
                  R=[xnT, win_b], W=[bank])

        proj(PF[0], OFF_G, 512)
        proj(PF[1], OFF_G + 512, 512)
        A("activation", out=sg[0:P, 0:512], in_=PF[0][0:P, :], func=AF.Silu, R=[PF[0]], W=[sg])
        A("activation", out=sg[0:P, 512:1024], in_=PF[1][0:P, :], func=AF.Silu, R=[PF[1]], W=[sg])
        S.dma(sc["gb"][tok0:tok0 + P, :], sg[0:P, 256:512], R=[sg], key=sg)

        proj(PF[0], OFF_A, 512)
        proj(PF[1], OFF_A + 512, 384)
        tc_, tp_ = tmpm[ti % 2], tmpm[(ti + 1) % 2]
        V("tensor_tensor", tc_[0:P, 0:512], PF[0][0:P, 0:512], BCv("rw_mu", P, 0, 512), ALU.mult, R=[PF[0], bc], W=[tc_])
        V("tensor_tensor", tc_[0:P, 512:896], PF[1][0:P, 0:384], BCv("rw_mu", P, 512, 896), ALU.mult, R=[PF[1], bc], W=[tc_])
        if last:
            V("tensor_copy", za_f[0:P, 0:512], PF[0][0:P, 0:512], R=[PF[0]], W=[za_f])
            V("tensor_copy", za_f[0:P, 512:896], PF[1][0:P, 0:384], R=[PF[1]], W=[za_f])
            S.dma(st["o_shift"][l:l + 1, :], za_f[P - 1:P, :], R=[za_f], key=za_f)
        V("tensor_tensor", zs[0:P, 0:512], PF[0][0:P, 0:512], BCv("omm", P, 0, 512), ALU.mult, R=[PF[0], bc], W=[zs])
        V("tensor_tensor", zs[0:P, 512:896], PF[1][0:P, 0:384], BCv("omm", P, 512, 896), ALU.mult, R=[PF[1], bc], W=[zs])
        def gen_R():
            for (bank, c0, n) in ((PF[2], 0, 512), (PF[3], 512, 384)):
                M("matmul", bank[0:P, 0:n], cb[0:P, CB_SH:CB_SH + P], tc_[0:P, c0:c0 + n], start=True, stop=False, R=[cb, tc_], W=[bank])
                M("matmul", bank[0:P, 0:n], cb[:, CB_EL:CB_EL + P], tp_[:, c0:c0 + n], start=False, stop=True, R=[cb, tp_], W=[bank])
            V("tensor_add", zs[0:P, 0:512], zs[0:P, 0:512], PF[2][0:P, 0:512], R=[zs, PF[2]], W=[zs])
            V("tensor_add", zs[0:P, 512:896], zs[0:P, 512:896], PF[3][0:P, 0:384], R=[zs, PF[3]], W=[zs])
            r_ = zs[0:P, 0:256]
            k_ = zs[0:P, 256:512]
            v_ = zs[0:P, 512:768]
            A("activation", out=lin[0:P, 0:64], in_=zs[0:P, 768:832], func=AF.Tanh, R=[zs], W=[lin])
            A("copy", lin[0:P, 64:128], zs[0:P, 832:896], R=[zs], W=[lin])
            M("transpose", PB[0][:, 0:P], lin[0:P, :], identb(P), R=[lin, cb], W=[PB[0]])
            A("copy", linT[:, 0:P], PB[0][:, 0:P], R=[PB[0]], W=[linT])
            M("matmul", PF[2][0:P, :], linT[:, 0:P], w2a2_b[:], start=True, stop=True, R=[linT, w2a2_b], W=[PF[2]])
            V("tensor_add", w1[0:P, :], PF[2][0:P, 0:256], BCv("rw_w0", P), R=[PF[2], bc], W=[w1])
            A("activation", out=sw[0:P, :], in_=w1[0:P, :], func=AF.Sigmoid, R=[w1], W=[sw])
            V("tensor_add", w2[0:P, :], PF[2][0:P, 256:512], BCv("rw_a0", P), R=[PF[2], bc], W=[w2])
            A("activation", out=sa[0:P, :], in_=w2[0:P, :], func=AF.Sigmoid, R=[w2], W=[sa])
            V("tensor_tensor", kk[0:P, :], k_, BCv("rw_kk", P), ALU.mult, R=[zs, bc], W=[kk])
            V("tensor_tensor", w3[0:P, :], kk[0:P, :], kk[0:P, :], ALU.mult, R=[kk], W=[w3])
            V("tensor_reduce", st2[0:P, 0:4], w3[0:P, :].rearrange("p (h d) -> p h d", h=4), AX.X, ALU.add, R=[w3], W=[st2])
            V("tensor_scalar", st2[0:P, 4:8], st2[0:P, 0:4], 1e-24, None, ALU.max, R=[st2], W=[st2])
            A("activation", out=st2[0:P, 4:8], in_=st2[0:P, 4:8], func=AF.Sqrt, R=[st2], W=[st2])
            V("reciprocal", st2[0:P, 8:12], st2[0:P, 4:8], R=[st2], W=[st2])
            for h in range(4):
                V("tensor_scalar", kk[0:P, 64 * h:64 * h + 64], kk[0:P, 64 * h:64 * h + 64], st2[0:P, 8 + h:9 + h], None, ALU.mult,
                  R=[kk, st2], W=[kk])
            V("scalar_tensor_tensor", w1[0:P, :], sa[0:P, :], -1.0, BCv("rw_ka", P), ALU.add, ALU.mult, R=[sa, bc], W=[w1])
            V("scalar_tensor_tensor", kp[0:P, :], w1[0:P, :], 1.0, k_, ALU.add, ALU.mult, R=[w1, zs], W=[kp])
            V("tensor_tensor", bb[0:P, :], kk[0:P, :], sa[0:P, :], ALU.mult, R=[kk, sa], W=[bb])
            G("tensor_tensor", w2[0:P, :], r_, kp[0:P, :], ALU.mult, R=[zs, kp], W=[w2])
            G("tensor_tensor", w2[0:P, :], w2[0:P, :], BCv("rw_rk", P), ALU.mult, R=[w2, bc], W=[w2])
            V("tensor_reduce", bcf[0:P, 0:4], w2[0:P, :].rearrange("p (h d) -> p h d", h=4), AX.X, ALU.add, R=[w2], W=[bcf])
            M("matmul", PF[2][0:P, 0:256], cf[0:P, CF_TRI:CF_TRI + P], sw[0:P, :], start=True, stop=True, R=[cf, sw], W=[PF[2]])
            M("matmul", PF[2][0:P, 256:512], cf[0:P, CF_ONES:CF_ONES + P], sw[0:P, :], start=True, stop=True, R=[cf, sw], W=[PF[2]])
            V("tensor_copy", cs_sb[0:P, :], PF[2][0:P, 0:256], R=[PF[2]], W=[cs_sb])
            A("activation", out=e1[0:P, :], in_=PF[2][0:P, 0:256], func=AF.Exp, scale=CDEC, R=[PF[2]], W=[e1])
            A("activation", out=e2[0:P, :], in_=PF[2][0:P, 0:256], func=AF.Exp, scale=-CDEC, R=[PF[2]], W=[e2])
            V("tensor_sub", w1[0:P, :], cs_sb[0:P, :], sw[0:P, :], R=[cs_sb, sw], W=[w1])
            A("activation", out=e3[0:P, :], in_=w1[0:P, :], func=AF.Exp, scale=CDEC, R=[w1], W=[e3])
            V("tensor_sub", w3[0:P, :], PF[2][0:P, 256:512], cs_sb[0:P, :], R=[PF[2], cs_sb], W=[w3])
            A("activation", out=e4[0:P, :], in_=w3[0:P, :], func=AF.Exp, scale=CDEC, R=[w3], W=[e4])
            V("scalar_tensor_tensor", hat[0:P, 0, :], kk[0:P, :], -1.0, e3[0:P, :], ALU.mult, ALU.mult, R=[kk, e3], W=[hat])
            V("tensor_tensor", hat[0:P, 1, :], bb[0:P, :], e2[0:P, :], ALU.mult, R=[bb, e2], W=[hat])
            V("tensor_tensor", hat[0:P, 2, :], kp[0:P, :], e2[0:P, :], ALU.mult, R=[kp, e2], W=[hat])
            V("tensor_tensor", hat[0:P, 3, :], r_, e1[0:P, :], ALU.mult, R=[zs, e1], W=[hat])
            G("tensor_tensor", bp_b[0:P, :], bb[0:P, :], e4[0:P, :], ALU.mult, R=[bb, e4], W=[bp_b])
            G("tensor_tensor", kp4[0:P, :], kp[0:P, :], e4[0:P, :], ALU.mult, R=[kp, e4], W=[kp4])
            ohc = CF_OH + (0 if P == 128 else 1)
            for p in range(2):
                M("matmul", PF[3][:, p:p + 1], e1[0:P, p * 128:(p + 1) * 128], cf[0:P, ohc:ohc + 1], start=True, stop=True,
                  R=[e1, cf], W=[PF[3]])
            V("tensor_copy", wc[:, 0:2], PF[3][:, 0:2], R=[PF[3]], W=[wc])
            for vi in range(4):
                for p in range(2):
                    M("transpose", PB[0][:, (vi * 2 + p) * P:(vi * 2 + p + 1) * P], hat[0:P, vi, p * 128:(p + 1) * 128], identb(P),
                      R=[hat, cb], W=[PB[0]])
            A("copy", hT[:, :, :, 0:P], PB[0][:, 0:8 * P].rearrange("p (v q t) -> p v q t", v=4, q=2), R=[PB[0]], W=[hT])
            def fm(h, vi):
                return hT[64 * (h % 2):64 * (h % 2) + 64, vi, h // 2, 0:P]

            def pv(bank, w=128, n=None):
                n = P if n is None else n
                return bank[0:P, 0:4 * w].rearrange("p (h t) -> p h t", h=4)[:, :, 0:n]

            def msk(kind):
                return cbr[0:P, kind, :, 0:P]

            for h in range(4):
                M("matmul", PF[2][0:P, h * 128:h * 128 + P], fm(h, 0), fm(h, 1), start=True, stop=True, R=[hT], W=[PF[2]])
                M("matmul", PF[3][0:P, h * 128:h * 128 + P], fm(h, 1), fm(h, 0), start=True, stop=True, R=[hT], W=[PF[3]])
                M("matmul", PF[4][0:P, h * 128:h * 128 + P], fm(h, 0), fm(h, 2), start=True, stop=True, R=[hT], W=[PF[4]])
                M("matmul", PF[5][0:P, h * 128:h * 128 + P], fm(h, 1), fm(h, 3), start=True, stop=True, R=[hT], W=[PF[5]])
            V("tensor_tensor", scL[0:P, :, 0:P], pv(PF[2]), msk(0), ALU.mult, R=[PF[2], cbr], W=[scL])
            V("tensor_tensor", scN[0:P, :, 0:P], pv(PF[3]), msk(1), ALU.mult, R=[PF[3], cbr], W=[scN])
            for h in range(4):
                bk = PF[2 + (h % 2)]
                M("matmul", bk[0:P, h * 128:h * 128 + P], fm(h, 2), fm(h, 3), start=True, stop=True, R=[hT], W=[bk])
            V("tensor_add", Ttb[0][0:P, :, 0:P], scL[0:P, :, 0:P], msk(3), R=[scL, cbr], W=[Ttb[0]])
            V("tensor_tensor", scK[0:P, :, 0:P], pv(PF[4]), msk(0), ALU.mult, R=[PF[4], cbr], W=[scK])
            V("tensor_tensor", mbr_b[0:P, :, 0:P], pv(PF[5]), msk(2), ALU.mult, R=[PF[5], cbr], W=[mbr_b])
            for h in range(4):
                bk = PF[2 + (h % 2)]
                V("tensor_tensor", mkr_f[0:P, h, 0:P], bk[0:P, h * 128:h * 128 + P], cb[0:P, CB_UI:CB_UI + P], ALU.mult, R=[bk, cb], W=[mkr_f])
            def acc_(tb, idx):
                return (lambda h: tb[0:P, h, 0:P]) if idx is None else (lambda h: tb[0:P, idx, h, 0:P])

            def squares(Af, ATf, Rb, dst, need_A):
                if need_A:
                    for h in range(4):
                        M("matmul", PF[2][0:P, h * 128:h * 128 + P], ATf(h), Af(h), start=True, stop=True, R=Rb, W=[PF[2]])
                for h in range(4):
                    M("matmul", PF[3][0:P, h * 128:h * 128 + P], Af(h), ATf(h), start=True, stop=True, R=Rb, W=[PF[3]])

            def squares_evac(dst, need_A):
                if need_A:
                    V("tensor_copy", dst[0:P, 0, :, 0:P], pv(PF[2]), R=[PF[2]], W=[dst])
                A("copy", dst[0:P, 1, :, 0:P], pv(PF[3]), R=[PF[3]], W=[dst])

            tcur = 0
            squares(acc_(scL, None), acc_(scN, None), [scL, scN], Ab[1], nlev > 2)
            squares_evac(Ab[1], nlev > 2)
            for k in range(1, nlev):
                cur = Ab[k % 2]
                Af, ATf = acc_(cur, 0), acc_(cur, 1)
                lastk = (k == nlev - 1)
                pbank = PF[4 + (k % 2)]
                for h in range(4):
                    M("matmul", pbank[0:P, h * 128:h * 128 + P], ATf(h), Ttb[tcur][0:P, h, 0:P], start=True, stop=True,
                      R=[cur, Ttb[tcur]], W=[pbank])
                if not lastk:
                    nxt = Ab[(k + 1) % 2]
                    squares(Af, ATf, [cur], nxt, k + 1 < nlev - 1)
                yield "mm"
                V("tensor_add", Ttb[1 - tcur][0:P, :, 0:P], pv(pbank), Ttb[tcur][0:P, :, 0:P], R=[pbank, Ttb[tcur]], W=[Ttb[1 - tcur]])
                if not lastk:
                    squares_evac(nxt, k + 1 < nlev - 1)
                tcur = 1 - tcur
                yield "ev"
            Tt = Ttb[tcur]
            for h in range(4):
                M("matmul", PF[2][0:P, h * 128:h * 128 + P], Tt[0:P, h, 0:P], mbr_b[0:P, h, 0:P], start=True, stop=True, R=[Tt, mbr_b], W=[PF[2]])
            for h in range(4):
                M("matmul", PF[3][0:P, h * 64:h * 64 + 64], Tt[0:P, h, 0:P], bp_b[0:P, 64 * h:64 * h + 64], start=True, stop=True, R=[Tt, bp_b], W=[PF[3]])
            A("copy", G_b[0:P, :, 0:P], pv(PF[2]), R=[PF[2]], W=[G_b])
            V("tensor_copy", H_b[0:P, :, :], PF[3][0:P, 0:256].rearrange("p (h d) -> p h d", h=4), R=[PF[3]], W=[H_b])
            for h in range(4):
                M("matmul", PF[2][0:P, h * 128:h * 128 + P], scK[0:P, h, 0:P], G_b[0:P, h, 0:P], start=True, stop=True, R=[scK, G_b], W=[PF[2]])
            for h in range(4):
                M("matmul", PF[3][:, h * 128:h * 128 + P], hat[0:P, 0, (h // 2) * 128:(h // 2) * 128 + 128], G_b[0:P, h, 0:P], start=True, stop=True,
                  R=[hat, G_b], W=[PF[3]])
            for h in range(4):
                M("matmul", PF[4][0:P, h * 64:h * 64 + 64], scK[0:P, h, 0:P], H_b[0:P, h, :], start=True, stop=True, R=[scK, H_b], W=[PF[4]])
            for h in range(4):
                M("matmul", PF[4][:, 256 + h * 64:256 + h * 64 + 64], hat[0:P, 0, (h // 2) * 128:(h // 2) * 128 + 128], H_b[0:P, h, :],
                  start=True, stop=True, R=[hat, H_b], W=[PF[4]])
            V("tensor_add", z_f[0:P, :, 0:P], pv(PF[2]), mkr_f[0:P, :, 0:P], R=[PF[2], mkr_f], W=[z_f])
            for o_ in (0, 64):
                h0 = o_ // 64
                qv = PF[3][o_:o_ + 64, 0:512].rearrange("p (q r t) -> p q r t", q=2, r=2)[:, :, h0, 0:P]
                V("tensor_add", q_f[o_:o_ + 64, :, 0:P], qv, hT[o_:o_ + 64, 3, :, 0:P], R=[PF[3], hT], W=[q_f])
            for h in range(4):
                p = h // 2
                o_ = 64 * (h % 2)
                V("tensor_add", x_f[0:P, h, o_:o_ + 64], PF[4][0:P, h * 64:h * 64 + 64], kp4[0:P, 64 * h:64 * h + 64], R=[PF[4], kp4], W=[x_f])
                V("scalar_tensor_tensor", p_f[o_:o_ + 64, p, o_:o_ + 64], cf[o_:o_ + 64, CF_ID + o_:CF_ID + o_ + 64], wc[o_:o_ + 64, p:p + 1],
                  PF[4][o_:o_ + 64, 256 + h * 64:256 + h * 64 + 64], ALU.mult, ALU.add, R=[cf, wc, PF[4]], W=[p_f])
            for h in range(4):
                p = h // 2
                o_ = 64 * (h % 2)
                vh = zs[0:P, 512 + 64 * h:512 + 64 * h + 64]
                M("matmul", PF[5][0:P, 64 * h:64 * h + 64], q_f[o_:o_ + 64, p, 0:P], ST[o_:o_ + 64, p, :], start=True, stop=False, R=[q_f, ST], W=[PF[5]])
                M("matmul", PF[5][0:P, 64 * h:64 * h + 64], z_f[0:P, h, 0:P], vh, start=False, stop=True, R=[z_f, zs], W=[PF[5]])
            for p in range(2):
                M("matmul", PF[5][:, 256 + 64 * p:256 + 64 * p + 64], p_f[:, p, :], ST[:, p, :], start=True, stop=False, R=[p_f, ST], W=[PF[5]])
                for h in (2 * p, 2 * p + 1):
                    M("matmul", PF[5][:, 256 + 64 * p:256 + 64 * p + 64], x_f[0:P, h, :], zs[0:P, 512 + 64 * h:512 + 64 * h + 64],
                      start=False, stop=(h == 2 * p + 1), R=[x_f, zs], W=[PF[5]])
            V("tensor_copy", ST[:, :, :], PF[5][:, 256:384].rearrange("k (p v) -> k p v", p=2), R=[PF[5]], W=[ST])
            Y = PF[5]
            V("tensor_reduce", st3[0:P, 0:4], Y[0:P, 0:256].rearrange("p (h d) -> p h d", h=4), AX.X, ALU.add, R=[Y], W=[st3])
            V("tensor_scalar", st3[0:P, 0:4], st3[0:P, 0:4], 1.0 / 64, None, ALU.mult, R=[st3], W=[st3])
            for h in range(4):
                V("tensor_scalar", yrw[0:P, 64 * h:64 * h + 64], Y[0:P, 64 * h:64 * h + 64], st3[0:P, h:h + 1], None, ALU.subtract,
                  R=[Y, st3], W=[yrw])
            V("tensor_tensor", w1[0:P, :], yrw[0:P, :], yrw[0:P, :], ALU.mult, R=[yrw], W=[w1])
            V("tensor_reduce", st3[0:P, 4:8], w1[0:P, :].rearrange("p (h d) -> p h d", h=4), AX.X, ALU.add, R=[w1], W=[st3])
            rstd_from_ss(st3[0:P, 4:8], st3[0:P, 8:12], P, 64.0, 64e-5, [st3], [st3], st3[0:P, 12:16])
            for h in range(4):
                V("tensor_scalar", yrw[0:P, 64 * h:64 * h + 64], yrw[0:P, 64 * h:64 * h + 64], st3[0:P, 8 + h:9 + h], None, ALU.mult,
                  R=[yrw, st3], W=[yrw])
            V("tensor_tensor", yrw[0:P, :], yrw[0:P, :], BCv("rw_gn_g", P), ALU.mult, R=[yrw, bc], W=[yrw])
            V("tensor_add", yrw[0:P, :], yrw[0:P, :], BCv("rw_gn_b", P), R=[yrw, bc], W=[yrw])
            for h in range(4):
                V("scalar_tensor_tensor", yrw[0:P, 64 * h:64 * h + 64], zs[0:P, 512 + 64 * h:512 + 64 * h + 64], bcf[0:P, h:h + 1],
                  yrw[0:P, 64 * h:64 * h + 64], ALU.mult, ALU.add, R=[zs, bcf, yrw], W=[yrw])
            V("tensor_tensor", ygp[0:P, 0:256], yrw[0:P, :], sg[0:P, 0:256], ALU.mult, R=[yrw, sg], W=[ygp])
            if last:
                for p in range(2):
                    M("transpose", PF[2][0:64, p * 128:(p + 1) * 128], ST[:, p, :], identf(128), R=[ST, cf], W=[PF[2]])
                V("tensor_copy", wkv_o[:].rearrange("i h j -> i (h j)"), PF[2][0:64, 0:256], R=[PF[2]], W=[wkv_o])
                S.dma(st["o_wkv"][l].rearrange("h i j -> i h j"), wkv_o[:], R=[wkv_o], key=wkv_o)
            yield

        def gen_O():
            proj(PF[0], OFF_B, 352)
            ZB = PF[0]
            A("activation", out=junk[0:P, 0:192], in_=ZB[0:P, 0:192], func=AF.Square, accum_out=st1[0:P, 4:5], R=[ZB], W=[junk, st1])
            yield
            A("activation", out=junk[0:P, 0:128], in_=ZB[0:P, 192:320], func=AF.Square, accum_out=st1[0:P, 5:6], R=[ZB], W=[junk, st1])
            A("activation", out=junk[0:P, 0:32], in_=ZB[0:P, 320:352], func=AF.Square, accum_out=st1[0:P, 6:7], R=[ZB], W=[junk, st1])
            V("tensor_tensor", st1[0:P, 12:15], st1[0:P, 4:7], invc[0:P, 8:11], ALU.mult, R=[st1, invc], W=[st1])
            V("tensor_scalar", st1[0:P, 12:15], st1[0:P, 12:15], 1e-6, None, ALU.add, R=[st1], W=[st1])
            A("activation", out=st1[0:P, 12:15], in_=st1[0:P, 12:15], func=AF.Sqrt, R=[st1], W=[st1])
            V("reciprocal", st1[0:P, 8:11], st1[0:P, 12:15], R=[st1], W=[st1])
            yield
            V("tensor_scalar", qa_b[0:P, :], ZB[0:P, 0:192], st1[0:P, 8:9], None, ALU.mult, R=[ZB, st1], W=[qa_b])
            yield
            M("transpose", PB[1][:, 0:P], qa_b[0:P, 0:128], identb(P), R=[qa_b, cb], W=[PB[1]])
            M("transpose", PB[1][0:64, P:2 * P], qa_b[0:P, 128:192], identb(P), R=[qa_b, cb], W=[PB[1]])
            A("copy", qaT[:, 0, 0:P], PB[1][:, 0:P], R=[PB[1]], W=[qaT])
            yield
            A("copy", qaT[0:64, 1, 0:P], PB[1][0:64, P:2 * P], R=[PB[1]], W=[qaT])
            M("matmul", PF[1][0:P, 0:384], qaT[:, 0, 0:P], wuq_b[:, 0, :], start=True, stop=False, R=[qaT, wuq_b], W=[PF[1]])
            M("matmul", PF[1][0:P, 0:384], qaT[:, 1, 0:P], wuq_b[:, 1, :], start=False, stop=True, R=[qaT, wuq_b], W=[PF[1]])
            yield
            A("activation", out=zb_f[0:P, :], in_=PF[1][0:P, 0:384], func=AF.Square, R=[PF[1]], W=[zb_f])
            sq3 = zb_f[0:P, :].rearrange("p (h d) -> p h d", h=4)
            V("tensor_reduce", st2o[0:P, 0:4], sq3[:, :, 0:64], AX.X, ALU.add, R=[zb_f], W=[st2o])
            yield
            V("tensor_reduce", st2o[0:P, 4:8], sq3[:, :, 64:96], AX.X, ALU.add, R=[zb_f], W=[st2o])
            V("tensor_tensor", st3o[0:P, 0:8], st2o[0:P, 0:8], invc[0:P, 11:19], ALU.mult, R=[st2o, invc], W=[st3o])
            V("tensor_scalar", st3o[0:P, 0:8], st3o[0:P, 0:8], 1e-6, None, ALU.add, R=[st3o], W=[st3o])
            A("activation", out=st3o[0:P, 0:8], in_=st3o[0:P, 0:8], func=AF.Sqrt, R=[st3o], W=[st3o])
            V("reciprocal", st2o[0:P, 8:16], st3o[0:P, 0:8], R=[st3o], W=[st2o])
            yield
            for h in range(4):
                V("scalar_tensor_tensor", qn[0:P, h, 0:64], PF[1][0:P, 96 * h:96 * h + 64], st2o[0:P, 8 + h:9 + h],
                  BCv("mla_q_norm_g", P, 0, 64), ALU.mult, ALU.mult, R=[PF[1], st2o, bc], W=[qn])
                V("scalar_tensor_tensor", qn[0:P, h, 64:96], PF[1][0:P, 96 * h + 64:96 * h + 96], st2o[0:P, 12 + h:13 + h],
                  BCv("mla_q_norm_g", P, 64, 96), ALU.mult, ALU.mult, R=[PF[1], st2o, bc], W=[qn])
            V("tensor_copy", qfb[0:P, :, 0:64], qn[0:P, :, 0:64], R=[qn], W=[qfb])
            t1 = w1o[0:P, 0:64].rearrange("p (h d) -> p h d", h=4)
            yield
            t2 = w2o[0:P, 0:64].rearrange("p (h d) -> p h d", h=4)
            rope_apply(qfb[0:P, :, 64:80], qfb[0:P, :, 80:96], qn[0:P, :, 64:80], qn[0:P, :, 80:96], cosv, sinv, t1, t2,
                       [qn, rp_], [qfb], [w1o, w2o])
            for h in range(4):
                M("transpose", PB[1][0:96, h * P:(h + 1) * P], qfb[0:P, h, :], identb(P), R=[qfb, cb], W=[PB[1]])
            yield
            A("copy", qT[:, :, 0:P], PB[1][0:96, 0:4 * P].rearrange("p (h t) -> p h t", h=4), R=[PB[1]], W=[qT])
            S.dma(sc["qt"][:, :, tok0:tok0 + P].rearrange("h d t -> d h t"), qT[:, :, 0:P], R=[qT], key=qT)
            V("scalar_tensor_tensor", ckv_f[0:P, :], ZB[0:P, 192:320], st1[0:P, 9:10], BCv("mla_kva_g", P), ALU.mult, ALU.mult,
              R=[ZB, st1, bc], W=[ckv_f])
            yield
            S.dma(st["o_ckv"][l, tok0:tok0 + P, :] if grp == "p" else st["o_ckv"][l, 0:P, :], ckv_f[0:P, :], R=[ckv_f], key=ckv_f)
            V("scalar_tensor_tensor", kr_f[0:P, :], ZB[0:P, 320:352], st1[0:P, 10:11], BCv("mla_k_norm_g", P, 64, 96), ALU.mult, ALU.mult,
              R=[ZB, st1, bc], W=[kr_f])
            rope_apply(kr_r[0:P, 0:16], kr_r[0:P, 16:32], kr_f[0:P, 0:16], kr_f[0:P, 16:32], rp_[0:P, 0:16], rp_[0:P, 64:80],
                       w1o[0:P, 0:16], w2o[0:P, 0:16], [kr_f, rp_], [kr_r], [w1o, w2o])
            yield
            S.dma(st["o_kr"][l, tok0:tok0 + P, :] if grp == "p" else st["o_kr"][l, 0:P, :], kr_r[0:P, :], R=[kr_r], key=kr_r)
            kv_from_ckv(P, ckv_f[0:P, :], kr_r[0:P, :], [ckv_f, kr_r], sc, st["ktok0"] + tok0)

            proj(PF[0], OFF_C, 512)
            yield
            ZC = PF[0]
            V("tensor_tensor", ug[0:P, :], ZC[0:P, 0:256], sg[0:P, 512:768], ALU.mult, R=[ZC, sg], W=[ug])
            V("bn_stats", bnst[0:P, 0:6], ZC[0:P, 256:512], R=[ZC], W=[bnst])
            yield
            V("bn_aggr", bnst[0:P, 6:8], bnst[0:P, 0:6], R=[bnst], W=[bnst])
            rstd_from_ss(bnst[0:P, 7:8], st1[0:P, 3:4], P, 1.0, 1e-5, [bnst], [st1], st1[0:P, 15:16])
            V("tensor_scalar", vn[0:P, :], ZC[0:P, 256:512], bnst[0:P, 6:7], st1[0:P, 3:4], ALU.subtract, ALU.mult, R=[ZC, bnst, st1], W=[vn])
            yield
            V("tensor_tensor", vn[0:P, :], vn[0:P, :], BCv("sgu_ln_g", P), ALU.mult, R=[vn, bc], W=[vn])
            V("tensor_add", vn[0:P, :], vn[0:P, :], BCv("sgu_ln_b", P), R=[vn, bc], W=[vn])
            if grp == "s":
                S.dma(o_sgv[l, 0:P, :], vn[0:P, :], R=[vn], key=vn)
            yield
            V("tensor_copy", vn_b[0:P, :], vn[0:P, :], R=[vn], W=[vn_b])
            for h in range(4):
                M("matmul", PF[1][0:P, 64 * h:64 * h + 64], wsT_b[0:P, h, 0:P], vn_b[0:P, 64 * h:64 * h + 64], start=True, stop=True,
                  R=[wsT_b, vn_b], W=[PF[1]])
            for h in range(4):
                V("scalar_tensor_tensor", ygp[0:P, 256 + 64 * h:256 + 64 * h + 64], PF[1][0:P, 64 * h:64 * h + 64], sgub[0:P, h:h + 1],
                  ug[0:P, 64 * h:64 * h + 64], ALU.add, ALU.mult, R=[PF[1], sgub, ug], W=[ygp])

            yield
            proj(PF[0], OFF_D, 256)
            zc_, zp_ = zd_f[ti % 2], zd_f[(ti + 1) % 2]
            V("tensor_copy", zc_[0:P, :], PF[0][0:P, 0:256], R=[PF[0]], W=[zc_])
            yield
            if last:
                if grp == "p":
                    S.dma(st["o_pool"][l], zc_[P - 15:P, :], R=[zc_], key=zc_)
                else:
                    S.dma(st["o_pool"][l], zc_[1:16, :], R=[zc_], key=zc_)
            for g in range(4):
                M("matmul", PF[1][0:P, 64 * g:64 * g + 64], cf[0:P, CF_BAND + 128 * g:CF_BAND + 128 * g + P], zc_[0:P, 64 * g:64 * g + 64],
                  start=True, stop=False, R=[cf, zc_], W=[PF[1]])
                M("matmul", PF[1][0:P, 64 * g:64 * g + 64], cf[:, CF_BANDP + 128 * g:CF_BANDP + 128 * g + P], zp_[:, 64 * g:64 * g + 64],
                  start=False, stop=True, R=[cf, zp_], W=[PF[1]])
            for g in range(4):
                ic = invc[0:P, g:g + 1] if (first and grp == "p") else invc[0:P, 4 + g:5 + g]
                V("scalar_tensor_tensor", d_b[0:P, 64 * g:64 * g + 64], PF[1][0:P, 64 * g:64 * g + 64], ic, zc_[0:P, 64 * g:64 * g + 64],
                  ALU.mult, ALU.subtract, R=[PF[1], invc, zc_], W=[d_b])
            yield
            for c in range(2):
                M("transpose", PB[1][:, c * P:(c + 1) * P], d_b[0:P, c * 128:(c + 1) * 128], identb(P), R=[d_b, cb], W=[PB[1]])
            A("copy", dT[:, :, 0:P], PB[1][:, 0:2 * P].rearrange("p (c t) -> p c t", c=2), R=[PB[1]], W=[dT])
            for c in range(2):
                M("matmul", PF[0][0:P, c * 128:(c + 1) * 128], dT[:, c, 0:P], poolw_b[:, c, :], start=True, stop=True, R=[dT, poolw_b], W=[PF[0]])
            yield
            V("tensor_tensor", ygp[0:P, 512:768], PF[0][0:P, 0:256], sg[0:P, 768:1024], ALU.mult, R=[PF[0], sg], W=[ygp])
            yield

        gr, go = gen_R(), gen_O()
        if P == 128 and ILV > 0:
            for tag in gr:
                if tag == "mm":
                    for _k in range(ILV):
                        next(go, None)
                elif tag == "ev":
                    for _k in range(ILV2):
                        next(go, None)
        for _ in gr:
            pass
        for _ in go:
            pass
        S.dma(sc["yg"][tok0:tok0 + P, :], ygp[0:P, :], R=[ygp], key=ygp)

    def phase2(l, grp, Tq, QB, nkt_total, klast, xsrc, ydst, sc, KT, VV, bufs):
        (QTb, PT, yfull4, yT, gbt4, xres, xout, rr) = bufs
        nqb = Tq // QB
        nsub = max(1, QB // 128)
        Pq = min(QB, 128)
        for qb in range(nqb):
            q0 = qb * QB
            S.dma(QTb[:, :, 0:QB], sc["qt"][:, :, q0:q0 + QB].rearrange("h d t -> d h t"), W=[QTb], key=QTb)
            for i in range(nsub):
                t0 = q0 + i * 128
                S.dma(yfull4[i][0:Pq, 0:256], sc["yg"][t0:t0 + Pq, 0:256], W=[yfull4[i]], key=yfull4[i])
                S.dma(yfull4[i][0:Pq, 512:1024], sc["yg"][t0:t0 + Pq, 256:768], W=[yfull4[i]], key=yfull4[i])
                S.dma(gbt4[i][0:Pq, :], sc["gb"][t0:t0 + Pq, :], W=[gbt4[i]], key=gbt4[i])
            if grp == "p":
                nkt = 4 * qb + 4
            else:
                nkt = nkt_total
            for h in range(4):
                def kinfo(kt):
                    kp_ = 128 if (grp == "p" or kt < nkt_total - 1) else klast
                    jd = kt - 4 * qb if grp == "p" else -1
                    c0 = 128 * jd if jd > 0 else 0
                    return kp_, jd, c0

                def emit_scores(kt):
                    kp_, jd, c0 = kinfo(kt)
                    n = QB - c0
                    sbk = (PF[4], PF[5], PB[1])[kt % 3]
                    sview = sbk[0:kp_, 0:n] if (kt % 3) < 2 else PB[1][0:kp_, 0:1024].bitcast(F32)[:, 0:n]
                    M("matmul", sview, KT[:, h, kt * 128:kt * 128 + kp_], QTb[:, h, c0:QB], start=True, stop=True,
                      R=[KT, QTb], W=[sbk])
                    pt_ = PT[kt % 3]
                    A("activation", out=pt_[0:kp_, c0:QB], in_=sview, func=AF.Exp, scale=float(1.0 / np.sqrt(96.0)),
                      R=[sbk], W=[pt_])
                    if jd >= 0:
                        G("memset", pt_[64:128, c0:c0 + 64], 0.0, W=[pt_])

                def emit_pv(kt):
                    kp_, jd, c0 = kinfo(kt)
                    pt_ = PT[kt % 3]
                    for i in range(c0 // 128, nsub):
                        first_k = (kt == 0)
                        last_k = (kt == (4 * qb + i if grp == "p" else nkt - 1))
                        M("matmul", PF[i][0:Pq, 0:65], pt_[0:kp_, i * 128:i * 128 + Pq], VV[0:kp_, kt, 65 * h:65 * h + 65],
                          start=first_k, stop=last_k, R=[pt_, VV], W=[PF[i]])

                emit_scores(0)
                if nkt > 1:
                    emit_scores(1)
                for kt in range(nkt):
                    if kt + 2 < nkt:
                        emit_scores(kt + 2)
                    emit_pv(kt)
                for i in range(nsub):
                    V("reciprocal", rr[0:Pq, h:h + 1], PF[i][0:Pq, 64:65], R=[PF[i]], W=[rr])
                    V("scalar_tensor_tensor", yfull4[i][0:Pq, 256 + 64 * h:256 + 64 * h + 64], PF[i][0:Pq, 0:64], rr[0:Pq, h:h + 1],
                      gbt4[i][0:Pq, 64 * h:64 * h + 64], ALU.mult, ALU.mult, R=[PF[i], rr, gbt4[i]], W=[yfull4[i]])
            for i in range(nsub):
                t0 = q0 + i * 128
                yfull = yfull4[i]
                S.dma(xres[0:Pq, :], xsrc[t0:t0 + Pq, :], W=[xres], key=xres)
                for k in range(8):
                    M("transpose", PB[0][:, k * Pq:(k + 1) * Pq], yfull[0:Pq, k * 128:(k + 1) * 128], identb(Pq), R=[yfull, cb], W=[PB[0]])
                A("copy", yT[:, :, 0:Pq], PB[0][:, 0:8 * Pq].rearrange("p (k t) -> p k t", k=8), R=[PB[0]], W=[yT])
                for cbk in range(2):
                    bank = PF[4 + cbk]
                    for k in range(8):
                        M("matmul", bank[0:Pq, :], yT[:, k, 0:Pq], wout_b[:, k, cbk * 512:(cbk + 1) * 512], start=(k == 0), stop=(k == 7),
                          R=[yT, wout_b], W=[bank])
                    V("tensor_add", xout[0:Pq, cbk * 512:(cbk + 1) * 512], bank[0:Pq, :], xres[0:Pq, cbk * 512:(cbk + 1) * 512],
                      R=[bank, xres], W=[xout])
                S.dma(ydst[t0:t0 + Pq, :], xout[0:Pq, :], R=[xout], key=xout, Q=S.pool)

    stp = dict(o_shift=o_shp, o_wkv=o_wkvp, o_pool=o_plp, o_ckv=o_ckvp, o_kr=o_krp, ktok0=0)
    sts = dict(o_shift=o_shs, o_wkv=o_wkvs, o_pool=o_pls, o_ckv=o_ckvs, o_kr=o_krs, ktok0=PAST)
    nlev_p = 7
    nlev_s = 4
    for l in range(NL):
        load_small(l)
        xsrc_p = x_p if l == 0 else o_yp
        xsrc_s = x_s if l == 0 else o_ys
        with ExitStack() as c1:
            alloc_work(c1)
            win_b = sb("win_b", [128, 8, DIN], BF16, c1)
            def cache_tiles():
                for kt in range(PAST // 128):
                    cs_ = cstage[kt % 2]
                    S.dma(cs_[:, 0:128], c_ckv[l, kt * 128:(kt + 1) * 128, :], W=[cs_], key=cs_)
                    S.dma(cs_[:, 128:160], c_kr[l, kt * 128:(kt + 1) * 128, :], W=[cs_], key=cs_)
                    kv_from_ckv(128, cs_[:, 0:128], cs_[:, 128:160], [cs_], scr_s, kt * 128, Q=S.pool)
                    yield
            gw = load_win(l, win_b)
            gc = cache_tiles() if do_sample else iter(())
            for _ in gw:
                next(gc, None)
                next(gc, None)
            for _ in gc:
                pass
            V("memset", tmpm[1][:], 0.0, W=[tmpm[1]])
            V("memset", zd_f[1][:], 0.0, W=[zd_f[1]])
            V("memset", ST[:], 0.0, W=[ST])
            S.dma(xt[0][:], xsrc_p[0:128, :], W=[xt[0]], key=xt[0])
            S.dma(ropet[0][:], rope_p[0:128, :], W=[ropet[0]], key=ropet[0])
            for ti in range(NT):
                if ti + 1 < NT:
                    S.dma(xt[(ti + 1) % 2][:], xsrc_p[(ti + 1) * 128:(ti + 2) * 128, :], W=[xt[(ti + 1) % 2]], key=xt[(ti + 1) % 2])
                    S.dma(ropet[(ti + 1) % 2][:], rope_p[(ti + 1) * 128:(ti + 2) * 128, :], W=[ropet[(ti + 1) % 2]], key=ropet[(ti + 1) % 2])
                phase1_tile(l, "p", ti, 128, xsrc_p, win_b, scr_p, ti == 0, ti == NT - 1, ti * 128, nlev_p, stp)
            if do_sample:
                P = TS
                V("memset", stage[:, 0:896], 0.0, W=[stage])
                S.dma(stage[127:128, 0:896], st_shift[l:l + 1, :], W=[stage], key=stage)
                V("tensor_tensor", tmpm[1][:, :], stage[:, 0:896], BCv("rw_mu", 128), ALU.mult, R=[stage, bc], W=[tmpm[1]])
                V("memset", zd_f[1][:], 0.0, W=[zd_f[1]])
                S.dma(zd_f[1][113:128, :], st_pool[l], W=[zd_f[1]], key=zd_f[1])
                S.dma(stage2[0:64, 0:256].rearrange("i (h j) -> i h j", h=4), st_wkv[l].rearrange("h i j -> i h j"), W=[stage2], key=stage2)
                for p in range(2):
                    M("transpose", PF[2][:, p * 64:(p + 1) * 64], stage2[0:64, p * 128:(p + 1) * 128], identf(64), R=[stage2, cf], W=[PF[2]])
                V("tensor_copy", ST[:].rearrange("k p v -> k (p v)"), PF[2][:, 0:128], R=[PF[2]], W=[ST])
                S.dma(xt[0][0:P, :], xsrc_s[0:P, :], W=[xt[0]], key=xt[0])
                S.dma(ropet[0][0:P, :], rope_s[0:P, :], W=[ropet[0]], key=ropet[0])
                phase1_tile(l, "s", 0, P, xsrc_s, win_b, scr_s, True, True, 0, nlev_s, sts)
            S.barrier()
        with ExitStack() as c2:
            KT = sb("KT", [96, 4, max(T, PAST + 128)], BF16, c2)
            VV = sb("VV", [128, max(NT, PAST // 128 + 1), 260], BF16, c2)
            QTb = sb("QTb", [96, 4, 512], BF16, c2)
            PT = [sb("PT%d" % i, [128, 512], BF16, c2) for i in range(3)]
            yfull4 = [sb("yfull%d" % i, [128, D], BF16, c2) for i in range(4)]
            yT = sb("yT", [128, 8, 128], BF16, c2)
            gbt4 = [sb("gbt%d" % i, [128, 256], BF16, c2) for i in range(4)]
            xres = sb("xres", [128, D], F32, c2)
            xout = sb("xout", [128, D], F32, c2)
            rr = sb("rr", [128, 4], F32, c2)
            bufs = (QTb, PT, yfull4, yT, gbt4, xres, xout, rr)
            for h in range(4):
                S.dma(KT[:, h, 0:T], scr_p["kt"][h], W=[KT], key=KT)
            vview = scr_p["v"].rearrange("(n p) c -> p n c", p=128)
            for n0 in range(0, NT, 8):
                n1 = min(NT, n0 + 8)
                S.dma(VV[:, n0:n1, :], vview[:, n0:n1, :], W=[VV], key=VV)
            phase2(l, "p", T, 512, NT, 128, xsrc_p, o_yp, scr_p, KT, VV, bufs)
            if do_sample:
                NKS = PAST // 128 + 1
                for h in range(4):
                    S.dma(KT[:, h, 0:PAST + TS], scr_s["kt"][h][:, 0:PAST + TS], W=[KT], key=KT)
                vview = scr_s["v"].rearrange("(n p) c -> p n c", p=128)
                for n0 in range(0, NKS - 1, 8):
                    n1 = min(NKS - 1, n0 + 8)
                    S.dma(VV[:, n0:n1, :], vview[:, n0:n1, :], W=[VV], key=VV)
                S.dma(VV[0:TS, NKS - 1, :], scr_s["v"][PAST:PAST + TS, :], W=[VV], key=VV)
                phase2(l, "s", TS, TS, NKS, TS, xsrc_s, o_ys, scr_s, KT, VV, bufs)
            S.barrier()
    S.final_wait()
    ctx.close()
    return nc, S.ninst


_CACHE = {}


def kernel(**inputs):
    x_prompt = np.asarray(inputs["x_prompt"], np.float32)
    x_sample = np.asarray(inputs["x_sample"], np.float32)
    B, T, _ = x_prompt.shape
    NL = inputs["norm_g"].shape[0]
    key = (T, NL)
    if key not in _CACHE:
        _CACHE[key] = build(T, NL)[0]
    nc = _CACHE[key]
    cf, cb, invc = make_consts()
    rope_p = rope_table(np.arange(T))
    rope_s = rope_table(PAST + np.arange(TS))
    in_maps = []
    for c in range(8):
        m = {
            "x_p": np.ascontiguousarray(x_prompt[c // 2]),
            "x_s": np.ascontiguousarray(x_sample[c]),
            "c_ckv": np.ascontiguousarray(np.asarray(inputs["cache_ckv"], np.float32)[:, c]),
            "c_kr": np.ascontiguousarray(np.asarray(inputs["cache_krope"], np.float32)[:, c]),
            "st_wkv": np.ascontiguousarray(np.asarray(inputs["state_wkv"], np.float32)[:, c]),
            "st_shift": np.ascontiguousarray(np.asarray(inputs["state_shift"], np.float32)[:, c]),
            "st_pool": np.ascontiguousarray(np.asarray(inputs["state_pool"], np.float32)[:, c]),
            "cf": cf, "cb": cb, "invc": invc, "rope_p": rope_p, "rope_s": rope_s,
        }
        for n in WNAMES:
            m[n] = np.ascontiguousarray(np.asarray(inputs[n], np.float32).reshape([NL] + WSHAPES[n]))
        in_maps.append(m)
    res = run_bass_kernel_spmd(nc, in_maps, core_ids=list(range(8)))
    R = res.results
    pc = [R[2 * b] for b in range(B)]
    sc = [R[c] for c in range(8)]
    stk = lambda L, n, ax: np.stack([np.asarray(r[n], np.float32) for r in L], axis=ax)
    out = (
        stk(pc, "o_yp", 0), stk(sc, "o_ys", 0),
        stk(pc, "o_ckvp", 1), stk(pc, "o_krp", 1), stk(pc, "o_wkvp", 1), stk(pc, "o_shp", 1), stk(pc, "o_plp", 1),
        stk(sc, "o_ckvs", 1), stk(sc, "o_krs", 1), stk(sc, "o_wkvs", 1), stk(sc, "o_shs", 1), stk(sc, "o_pls", 1),
        stk(sc, "o_sgv", 1),
    )
    return out
```

```python
import numpy as np
from contextlib import ExitStack
import concourse.bass as bass
import concourse.mybir as mybir
from concourse.bass_utils import run_bass_kernel_spmd

F32 = mybir.dt.float32
BF16 = mybir.dt.bfloat16
AF = mybir.ActivationFunctionType
ALU = mybir.AluOpType
AX = mybir.AxisListType

D = 1024
DIN = 3040
OFF_A, OFF_B, OFF_C, OFF_D, OFF_G = 0, 896, 1248, 1760, 2016
PAST = 2048
TS = 16
WINS = (2, 4, 8, 16)
CDEC = -0.6065306597126334
ILV = 1
ILV2 = 2


class Buf:
    def __init__(self, name):
        self.name = name
        self.w = None
        self.r = []


class TB:
    def __init__(self, t, name):
        self.t = t
        self.b = Buf(name)

    def __getitem__(self, k):
        return self.t[k]


class Eng:
    def __init__(self, S, name, eng):
        self.name = name
        self.eng = eng
        self.sem = S.newsem("e_" + name)
        self.cnt = 0
        self.waited = {}

    def wait_tok(self, tok):
        if tok is None:
            return
        sem, val = tok
        if sem is self.sem and self.name == "pe":
            return
        key = id(sem)
        if self.waited.get(key, 0) >= val:
            return
        self.eng.wait_ge(sem, val)
        self.waited[key] = val


class Sched:
    def __init__(self, nc, ctx):
        self.nc = nc
        self.ctx = ctx
        self.pe = Eng(self, "pe", nc.tensor)
        self.act = Eng(self, "act", nc.scalar)
        self.dve = Eng(self, "dve", nc.vector)
        self.pool = Eng(self, "pool", nc.gpsimd)
        self.sp = Eng(self, "sp", nc.sync)
        self.engs = [self.pe, self.act, self.dve, self.pool, self.sp]
        self.dma_sems = {}
        self.ninst = 0

    def newsem(self, name):
        return self.ctx.enter_context(self.nc.semaphore(name))

    def _bufs(self, L):
        return [x.b if isinstance(x, TB) else x for x in L]

    def deps(self, E, R, W):
        for b in R:
            E.wait_tok(b.w)
        for b in W:
            E.wait_tok(b.w)
            for t in (b.r.values() if isinstance(b.r, dict) else b.r):
                E.wait_tok(t)

    def _mark(self, tok, R, W):
        for b in W:
            b.w = tok
            b.r = {}
        for b in R:
            if isinstance(b.r, list):
                b.r = {}
            k = id(tok[0])
            if k not in b.r or b.r[k][1] < tok[1]:
                b.r[k] = tok

    enabled = True

    def ck(self, name):
        import os
        if os.environ.get("KSTOP") == name:
            self.enabled = False

    def op(self, E, fn, *args, R=(), W=(), **kw):
        if not self.enabled:
            return None
        R = self._bufs(R)
        W = self._bufs(W)
        self.deps(E, R, W)
        ins = getattr(E.eng, fn)(*args, **kw)
        E.cnt += 1
        ins.then_inc(E.sem, 1)
        self.ninst += 1
        self._mark((E.sem, E.cnt), R, W)
        return ins

    def dma(self, out, in_, R=(), W=(), key=None, Q=None, **kw):
        if not self.enabled:
            return None
        Q = Q or self.sp
        R = self._bufs(R)
        W = self._bufs(W)
        kb = key.b if isinstance(key, TB) else key
        self.deps(Q, R, W)
        if kb.name not in self.dma_sems:
            self.dma_sems[kb.name] = [self.newsem("d_" + kb.name), 0]
        ent = self.dma_sems[kb.name]
        ins = Q.eng.dma_start(out=out, in_=in_, **kw)
        ent[1] += 16
        ins.then_inc(ent[0], 16)
        self.ninst += 1
        self._mark((ent[0], ent[1]), R, W)
        return ins

    def all_tokens(self):
        toks = [(e.sem, e.cnt) for e in self.engs if e.cnt > 0]
        toks += [(s, v) for (s, v) in self.dma_sems.values()]
        return toks

    def barrier(self):
        toks = self.all_tokens()
        for e in self.engs:
            for t in toks:
                e.wait_tok(t)

    def final_wait(self):
        for (s, v) in self.dma_sems.values():
            self.sp.wait_tok((s, v))


def make_consts():
    c = {}
    i = np.arange(128)
    s = i[:, None]
    t = i[None, :]
    ident = (s == t).astype(np.float32)
    tri_ui = (s <= t).astype(np.float32)
    tri_ut = (s < t).astype(np.float32)
    tri_lt = (s > t).astype(np.float32)
    bands = []
    bandp = []
    for w in WINS:
        bands.append(((s <= t) & (s > t - w)).astype(np.float32))
        bandp.append(((s - 128) > (t - w)).astype(np.float32))
    ones = np.ones((128, 128), np.float32)
    onehot_last = np.zeros((128, 2), np.float32)
    onehot_last[127, 0] = 1.0
    onehot_last[15, 1] = 1.0
    cf = np.concatenate([ident, tri_ui, ones] + bands + bandp + [onehot_last], axis=1)
    sh = (t == s + 1).astype(np.float32)
    elast = np.zeros((128, 128), np.float32)
    elast[127, 0] = 1.0
    cb = np.concatenate([ident, tri_lt, tri_ut, tri_lt, tri_ui, sh, elast], axis=1)
    invc = np.zeros((128, 24), np.float32)
    invc[:, 8:11] = np.array([1 / 192.0, 1 / 128.0, 1 / 32.0], np.float32)
    invc[:, 11:15] = 1 / 64.0
    invc[:, 15:19] = 1 / 32.0
    for g, w in enumerate(WINS):
        invc[:, g] = 1.0 / np.minimum(i + 1, w)
        invc[:, 4 + g] = 1.0 / w
    return cf.astype(np.float32), cb.astype(np.float32), invc


def rope_table(pos):
    half = 16
    inv = (10000.0 ** (-np.arange(half, dtype=np.float32) / half)).astype(np.float32)
    ang = pos.astype(np.float32)[:, None] * inv[None, :]
    cos = np.cos(ang).astype(np.float32)
    sin = np.sin(ang).astype(np.float32)
    return np.concatenate([np.tile(cos, (1, 4)), np.tile(sin, (1, 4))], axis=1).astype(np.float32)


CF_ID, CF_TRI, CF_ONES, CF_BAND, CF_BANDP, CF_OH = 0, 128, 256, 384, 896, 1408
CB_ID, CB_M3, CB_UI, CB_SH, CB_EL = 0, 128, 512, 640, 768

WNAMES = ["norm_g", "w_in", "w_out", "rw_mu", "rw_w0", "rw_w2", "rw_a0", "rw_a2", "rw_kk", "rw_ka", "rw_rk",
          "rw_gn_g", "rw_gn_b", "mla_qa_g", "mla_w_uq", "mla_kva_g", "mla_w_uk", "mla_w_uv",
          "mla_q_norm_g", "mla_k_norm_g", "sgu_w", "sgu_b", "sgu_ln_g", "sgu_ln_b", "pool_w", "pool_scale"]
WSHAPES = {
    "norm_g": [D], "w_in": [D, DIN], "w_out": [D, D], "rw_mu": [896], "rw_w0": [256], "rw_w2": [64, 256],
    "rw_a0": [256], "rw_a2": [64, 256], "rw_kk": [256], "rw_ka": [256], "rw_rk": [256], "rw_gn_g": [256],
    "rw_gn_b": [256], "mla_qa_g": [192], "mla_w_uq": [192, 384], "mla_kva_g": [128], "mla_w_uk": [128, 256],
    "mla_w_uv": [128, 256], "mla_q_norm_g": [96], "mla_k_norm_g": [96], "sgu_w": [4, 128, 128], "sgu_b": [4, 128],
    "sgu_ln_g": [256], "sgu_ln_b": [256], "pool_w": [4, 64, 64], "pool_scale": [256],
}


def build(T, NL, do_sample=True):
    nc = bass.Bass("TRN2", target_bir_lowering=False)
    ctx = ExitStack()
    NT = T // 128

    def din(name, shape, dt=F32):
        return nc.dram_tensor(name, list(shape), dt, kind="ExternalInput").ap()

    def dout(name, shape, dt=F32):
        return nc.dram_tensor(name, list(shape), dt, kind="ExternalOutput").ap()

    def dscr(name, shape, dt):
        return nc.dram_tensor(name, list(shape), dt, kind="Internal").ap()

    x_p = din("x_p", [T, D])
    x_s = din("x_s", [TS, D])
    c_ckv = din("c_ckv", [NL, PAST, 128])
    c_kr = din("c_kr", [NL, PAST, 32])
    st_wkv = din("st_wkv", [NL, 4, 64, 64])
    st_shift = din("st_shift", [NL, 896])
    st_pool = din("st_pool", [NL, 15, 256])
    Wd = {n: din(n, [NL] + WSHAPES[n]) for n in WNAMES}
    cf_d = din("cf", [128, 1410])
    cb_d = din("cb", [128, 896])
    invc_d = din("invc", [128, 24])
    rope_p = din("rope_p", [T, 128])
    rope_s = din("rope_s", [TS, 128])

    o_yp = dout("o_yp", [T, D])
    o_ys = dout("o_ys", [TS, D])
    o_ckvp = dout("o_ckvp", [NL, T, 128])
    o_krp = dout("o_krp", [NL, T, 32])
    o_wkvp = dout("o_wkvp", [NL, 4, 64, 64])
    o_shp = dout("o_shp", [NL, 896])
    o_plp = dout("o_plp", [NL, 15, 256])
    o_ckvs = dout("o_ckvs", [NL, TS, 128])
    o_krs = dout("o_krs", [NL, TS, 32])
    o_wkvs = dout("o_wkvs", [NL, 4, 64, 64])
    o_shs = dout("o_shs", [NL, 896])
    o_pls = dout("o_pls", [NL, 15, 256])
    o_sgv = dout("o_sgv", [NL, TS, 256])

    def scr(tag, TT, TK):
        return dict(
            yg=dscr("yg_" + tag, [TT, 768], BF16), gb=dscr("gb_" + tag, [TT, 256], BF16),
            qt=dscr("qt_" + tag, [4, 96, TT], BF16), kt=dscr("kt_" + tag, [4, 96, TK], BF16),
            v=dscr("v_" + tag, [TK, 260], BF16))
    scr_p = scr("p", T, T)
    scr_s = scr("s", TS, PAST + 128)

    S = Sched(nc, ctx)
    V = lambda fn, *a, **k: S.op(S.dve, fn, *a, **k)
    A = lambda fn, *a, **k: S.op(S.act, fn, *a, **k)
    G = lambda fn, *a, **k: S.op(S.pool, fn, *a, **k)
    M = lambda fn, *a, **k: S.op(S.pe, fn, *a, **k)

    uid = [0]

    def uname(name):
        uid[0] += 1
        return "s%d_%s" % (uid[0], name)

    def sb(name, shape, dt, c=None):
        t = (c or ctx).enter_context(nc.sbuf_tensor(uname(name), list(shape), dt))
        return TB(t, name)

    def ps(name, shape, dt):
        t = ctx.enter_context(nc.psum_tensor(name, list(shape), dt))
        return TB(t, name)

    PF = [ps("pf%d" % i, [128, 512], F32) for i in range(6)]
    PB = [ps("pb%d" % i, [128, 1024], BF16) for i in range(2)]

    cf = sb("cf", [128, 1410], F32)
    cb = sb("cb", [128, 896], BF16)
    cb_stage = sb("cb_stage", [128, 896], F32)
    invc = sb("invc", [128, 24], F32)
    S.dma(cf[:], cf_d[:, :], W=[cf], key=cf)
    S.dma(cb_stage[:], cb_d[:, :], W=[cb_stage], key=cb_stage)
    S.dma(invc[:], invc_d[:, :], W=[invc], key=invc)
    V("tensor_copy", cb[:], cb_stage[:], R=[cb_stage], W=[cb])
    cbr = sb("cbr", [128, 4, 4, 128], BF16)
    for kind, off in enumerate((CB_M3, CB_M3 + 128, CB_UI, CB_ID)):
        for h in range(4):
            V("tensor_copy", cbr[:, kind, h, :], cb[:, off:off + 128], R=[cb], W=[cbr])
    identf = lambda n: cf[0:n, CF_ID:CF_ID + n]
    identb = lambda n: cb[0:n, CB_ID:CB_ID + n]

    wout_b = sb("wout_b", [128, 8, D], BF16)
    wuq_b = sb("wuq_b", [128, 2, 384], BF16)
    wukv_b = sb("wukv_b", [128, 512], BF16)
    w2a2_b = sb("w2a2_b", [128, 512], BF16)
    wsT_b = sb("wsT_b", [128, 4, 128], BF16)
    poolw_b = sb("poolw_b", [128, 2, 128], BF16)
    sgub = sb("sgub", [128, 4], F32)
    ng = sb("ng", [128, 8], F32)
    qag = sb("qag", [128, 2], F32)
    BC_SPEC = [("rw_mu", 896), ("omm", 896), ("rw_w0", 256), ("rw_a0", 256), ("rw_kk", 256), ("rw_ka", 256),
               ("rw_rk", 256), ("rw_gn_g", 256), ("rw_gn_b", 256), ("mla_kva_g", 128), ("mla_q_norm_g", 96),
               ("mla_k_norm_g", 96), ("sgu_ln_g", 256), ("sgu_ln_b", 256)]
    bc_off = {}
    o = 0
    for n, w in BC_SPEC:
        bc_off[n] = (o, w)
        o += w
    bc = sb("bc", [128, o], F32)
    BCv = lambda n, P, a=0, b=None: bc[0:P, bc_off[n][0] + a: bc_off[n][0] + (bc_off[n][1] if b is None else b)]
    stage = sb("stage", [128, 1024], F32)
    stage2 = sb("stage2", [128, 1024], F32)

    WORK = []

    def wsb(name, shape, dt):
        tb = TB(None, name)
        WORK.append((tb, name, list(shape), dt))
        return tb

    def alloc_work(c):
        for tb, name, shape, dt in WORK:
            tb.t = c.enter_context(nc.sbuf_tensor(uname(name), shape, dt))
            tb.b = Buf(name)
        for i in range(4):
            STb[i].w = None
            STb[i].r = {}
        V("memset", vaug[:], 1.0, W=[vaug])
        V("memset", x_f[:], 0.0, W=[x_f])
        V("memset", p_f[:], 0.0, W=[p_f])
        V("memset", qaT[:], 0.0, W=[qaT])

    xt = [wsb("xt%d" % i, [128, D], F32) for i in range(2)]
    ropet = [wsb("ropet%d" % i, [128, 128], F32) for i in range(2)]
    junk = wsb("junk", [128, D], BF16)
    xn_b = wsb("xn_b", [128, D], BF16)
    xnT = wsb("xnT", [128, 8, 128], BF16)
    st1 = wsb("st1", [128, 16], F32)
    st2 = wsb("st2", [128, 16], F32)
    st3 = wsb("st3", [128, 16], F32)
    cstage = [wsb("cstage%d" % i, [128, 160], F32) for i in range(2)]
    st2o = wsb("st2o", [128, 16], F32)
    st3o = wsb("st3o", [128, 16], F32)
    w1o = wsb("w1o", [128, 256], F32)
    w2o = wsb("w2o", [128, 64], F32)
    sg = wsb("sg", [128, D], BF16)
    tmpm = [wsb("tmpm%d" % i, [128, 896], BF16) for i in range(2)]
    zs = wsb("zs", [128, 896], F32)
    lin = wsb("lin", [128, 128], BF16)
    linT = wsb("linT", [128, 128], BF16)
    w1 = wsb("w1", [128, 256], F32)
    w2 = wsb("w2", [128, 256], F32)
    w3 = wsb("w3", [128, 256], F32)
    sw = wsb("sw", [128, 256], F32)
    sa = wsb("sa", [128, 256], F32)
    kk = wsb("kk", [128, 256], F32)
    kp = wsb("kp", [128, 256], F32)
    bb = wsb("bb", [128, 256], F32)
    bcf = wsb("bcf", [128, 4], F32)
    cs_sb = wsb("cs_sb", [128, 256], F32)
    e1 = wsb("e1", [128, 256], F32)
    e2 = wsb("e2", [128, 256], F32)
    e3 = wsb("e3", [128, 256], F32)
    e4 = wsb("e4", [128, 256], F32)
    hat = wsb("hat", [128, 4, 256], BF16)
    bp_b = wsb("bp_b", [128, 256], BF16)
    kp4 = wsb("kp4", [128, 256], F32)
    hT = wsb("hT", [128, 4, 2, 128], BF16)
    wc = wsb("wc", [128, 2], F32)
    scL = wsb("scL", [128, 4, 128], BF16)
    scN = wsb("scN", [128, 4, 128], BF16)
    scK = wsb("scK", [128, 4, 128], BF16)
    mbr_b = wsb("mbr_b", [128, 4, 128], BF16)
    mkr_f = wsb("mkr_f", [128, 4, 128], F32)
    Ab = [wsb("Ab%d" % i, [128, 2, 4, 128], BF16) for i in range(2)]
    Ttb = [wsb("Ttb%d" % i, [128, 4, 128], BF16) for i in range(2)]
    G_b = wsb("G_b", [128, 4, 128], BF16)
    H_b = wsb("H_b", [128, 4, 64], BF16)
    x_f = wsb("x_f", [128, 4, 128], F32)
    z_f = wsb("z_f", [128, 4, 128], F32)
    q_f = wsb("q_f", [128, 2, 128], F32)
    p_f = wsb("p_f", [128, 2, 128], F32)
    ST = wsb("ST", [128, 2, 64], F32)
    STb = [Buf("ST%d" % h) for h in range(4)]
    yrw = wsb("yrw", [128, 256], F32)
    ygp = wsb("ygp", [128, 768], BF16)
    zb_f = wsb("zb_f", [128, 384], F32)
    qa_b = wsb("qa_b", [128, 192], BF16)
    qaT = wsb("qaT", [128, 2, 128], BF16)
    qn = wsb("qn", [128, 4, 96], F32)
    qfb = wsb("qfb", [128, 4, 96], BF16)
    qT = wsb("qT", [96, 4, 128], BF16)
    ckv_f = wsb("ckv_f", [128, 128], F32)
    ckv_b = wsb("ckv_b", [128, 128], BF16)
    ckvT = wsb("ckvT", [128, 128], BF16)
    kr_f = wsb("kr_f", [128, 32], F32)
    kr_r = wsb("kr_r", [128, 32], F32)
    kfull = wsb("kfull", [128, 4, 96], BF16)
    kT = wsb("kT", [96, 4, 128], BF16)
    vaug = wsb("vaug", [128, 4, 65], BF16)
    ug = wsb("ug", [128, 256], F32)
    vn = wsb("vn", [128, 256], F32)
    vn_b = wsb("vn_b", [128, 256], BF16)
    bnst = wsb("bnst", [128, 8], F32)
    zd_f = [wsb("zd_f%d" % i, [128, 256], F32) for i in range(2)]
    d_b = wsb("d_b", [128, 256], BF16)
    dT = wsb("dT", [128, 2, 128], BF16)
    za_f = wsb("za_f", [128, 896], F32)
    wkv_o = wsb("wkv_o", [64, 4, 64], F32)


    def load_small(l):
        P = 128
        for n, w in BC_SPEC:
            if n == "omm":
                continue
            S.dma(BCv(n, P), Wd[n][l].partition_broadcast(128), W=[bc], key=bc)
        V("tensor_scalar", BCv("omm", P), BCv("rw_mu", P), -1.0, 1.0, ALU.mult, ALU.add, R=[bc], W=[bc])
        S.dma(ng[:], Wd["norm_g"][l].rearrange("(k p) -> p k", p=128), W=[ng], key=ng, allow_slow_non_contiguous=True)
        S.dma(qag[:, 0:1], Wd["mla_qa_g"][l][0:128].rearrange("(p o) -> p o", o=1), W=[qag], key=qag)
        S.dma(qag[0:64, 1:2], Wd["mla_qa_g"][l][128:192].rearrange("(p o) -> p o", o=1), W=[qag], key=qag)
        S.dma(sgub[:], Wd["sgu_b"][l].rearrange("h i -> i h"), W=[sgub], key=sgub, allow_slow_non_contiguous=True)
        for k in range(8):
            S.dma(stage[:, 0:D], Wd["w_out"][l][k * 128:(k + 1) * 128, :], W=[stage], key=stage)
            V("tensor_copy", wout_b[:, k, :], stage[:, 0:D], R=[stage], W=[wout_b])
        S.dma(stage[:, 0:384], Wd["mla_w_uq"][l][0:128, :], W=[stage], key=stage)
        V("tensor_scalar", wuq_b[:, 0, :], stage[:, 0:384], qag[:, 0:1], None, ALU.mult, R=[stage, qag], W=[wuq_b])
        S.dma(stage[0:64, 0:384], Wd["mla_w_uq"][l][128:192, :], W=[stage], key=stage)
        V("memset", wuq_b[:, 1, :], 0.0, W=[wuq_b])
        V("tensor_scalar", wuq_b[0:64, 1, :], stage[0:64, 0:384], qag[0:64, 1:2], None, ALU.mult, R=[stage, qag], W=[wuq_b])
        S.dma(stage[:, 0:256], Wd["mla_w_uk"][l], W=[stage], key=stage)
        S.dma(stage[:, 256:512], Wd["mla_w_uv"][l], W=[stage], key=stage)
        V("tensor_copy", wukv_b[:], stage[:, 0:512], R=[stage], W=[wukv_b])
        V("memset", stage[:, 0:512], 0.0, W=[stage])
        S.dma(stage[0:64, 0:256], Wd["rw_w2"][l], W=[stage], key=stage)
        S.dma(stage[64:128, 256:512], Wd["rw_a2"][l], W=[stage], key=stage)
        V("tensor_copy", w2a2_b[:], stage[:, 0:512], R=[stage], W=[w2a2_b])
        S.dma(stage[:, 0:512].rearrange("p (h j) -> p h j", h=4), Wd["sgu_w"][l].rearrange("h i j -> i h j"), W=[stage], key=stage)
        for h in range(4):
            V("tensor_tensor", stage2[:, h * 128:(h + 1) * 128], stage[:, h * 128:(h + 1) * 128], cb[:, CB_M3 + 128:CB_M3 + 256],
              ALU.mult, R=[stage, cb], W=[stage2])
            V("tensor_sub", stage2[:, h * 128:(h + 1) * 128], stage[:, h * 128:(h + 1) * 128], stage2[:, h * 128:(h + 1) * 128],
              R=[stage, stage2], W=[stage2])
            M("transpose", PF[5][:, h * 128:(h + 1) * 128], stage2[:, h * 128:(h + 1) * 128], identf(128), R=[stage2, cf], W=[PF[5]])
        V("tensor_copy", wsT_b[:].rearrange("p h i -> p (h i)"), PF[5][:, 0:512], R=[PF[5]], W=[wsT_b])
        V("memset", stage[:, 0:256], 0.0, W=[stage])
        for g in range(4):
            c = g // 2
            r0 = 64 * (g % 2)
            S.dma(stage[r0:r0 + 64, c * 128 + r0: c * 128 + r0 + 64], Wd["pool_w"][l][g], W=[stage], key=stage)
        S.dma(stage2[:, 0:256], Wd["pool_scale"][l].partition_broadcast(128), W=[stage2], key=stage2)
        V("tensor_tensor", poolw_b[:].rearrange("p c d -> p (c d)"), stage[:, 0:256], stage2[:, 0:256], ALU.mult,
          R=[stage, stage2], W=[poolw_b])

    def load_win(l, win_b):
        for k in range(8):
            yield
            for c0 in range(0, DIN, 1024):
                n = min(1024, DIN - c0)
                st = stage if ((k * 3 + c0 // 1024) % 2 == 0) else stage2
                S.dma(st[:, 0:n], Wd["w_in"][l][k * 128:(k + 1) * 128, c0:c0 + n], W=[st], key=st)
                V("tensor_scalar", win_b[:, k, c0:c0 + n], st[:, 0:n], ng[:, k:k + 1], None, ALU.mult, R=[st, ng], W=[win_b])

    def rstd_from_ss(ss_ap, out_ap, P, n, eps, Rb, Wb, tmp_ap):
        V("tensor_scalar", tmp_ap, ss_ap, 1.0 / n, eps, ALU.mult, ALU.add, R=Rb, W=Wb)
        A("activation", out=tmp_ap, in_=tmp_ap, func=AF.Sqrt, R=Wb, W=Wb)
        V("reciprocal", out_ap, tmp_ap, R=Wb, W=Wb)

    def transposes_b(src, P, widths, pb, dst_ap_fn, Rb, Wdst, evac="act"):
        for i, (ap, w) in enumerate(zip(src, widths)):
            M("transpose", pb[0:w, i * P:(i + 1) * P], ap, identb(P), R=Rb + [cb], W=[pb])

    def rope_apply(dst_a, dst_b, x1, x2, cosv, sinv, t1, t2, Rb, Wb, Tb):
        V("tensor_tensor", t1, x1, cosv, ALU.mult, R=Rb, W=Tb)
        V("tensor_tensor", t2, x2, sinv, ALU.mult, R=Rb, W=Tb)
        V("tensor_tensor", dst_a, t1, t2, ALU.subtract, R=Tb, W=Wb)
        V("tensor_tensor", t1, x1, sinv, ALU.mult, R=Rb + Wb, W=Tb)
        V("tensor_tensor", t2, x2, cosv, ALU.mult, R=Rb, W=Tb)
        V("tensor_tensor", dst_b, t1, t2, ALU.add, R=Tb, W=Wb)

    KEY_KT_SW = Buf("kT_sw")
    KEY_V_SW = Buf("vaug_sw")

    def kv_from_ckv(P, ckvf_ap, krf_ap, Rb, sc, tok0, Q=None):
        V("tensor_copy", ckv_b[0:P, :], ckvf_ap, R=Rb, W=[ckv_b])
        M("transpose", PB[1][:, 0:P], ckv_b[0:P, :], identb(P), R=[ckv_b, cb], W=[PB[1]])
        A("copy", ckvT[:, 0:P], PB[1][:, 0:P], R=[PB[1]], W=[ckvT])
        M("matmul", PF[1][0:P, :], ckvT[:, 0:P], wukv_b[:], start=True, stop=True, R=[ckvT, wukv_b], W=[PF[1]])
        A("activation", out=w1o[0:P, :], in_=PF[1][0:P, 0:256], func=AF.Square, R=[PF[1]], W=[w1o])
        V("tensor_reduce", st2o[0:P, 0:4], w1o[0:P, :].rearrange("p (h d) -> p h d", h=4), AX.X, ALU.add, R=[w1o], W=[st2o])
        rstd_from_ss(st2o[0:P, 0:4], st2o[0:P, 4:8], P, 64.0, 1e-6, [st2o], [st2o], st2o[0:P, 8:12])
        for h in range(4):
            V("scalar_tensor_tensor", kfull[0:P, h, 0:64], PF[1][0:P, 64 * h:64 * h + 64], st2o[0:P, 4 + h:5 + h],
              BCv("mla_k_norm_g", P, 0, 64), ALU.mult, ALU.mult, R=[PF[1], st2o, bc], W=[kfull])
            V("tensor_copy", kfull[0:P, h, 64:96], krf_ap, R=Rb, W=[kfull])
        V("tensor_copy", vaug[0:P, :, 0:64], PF[1][0:P, 256:512].rearrange("p (h d) -> p h d", h=4), R=[PF[1]], W=[vaug])
        for h in range(4):
            M("transpose", PB[1][0:96, h * P:(h + 1) * P], kfull[0:P, h, :], identb(P), R=[kfull, cb], W=[PB[1]])
        A("copy", kT[:, :, 0:P], PB[1][0:96, 0:4 * P].rearrange("p (h t) -> p h t", h=4), R=[PB[1]], W=[kT])
        S.dma(sc["kt"][:, :, tok0:tok0 + P].rearrange("h d t -> d h t"), kT[:, :, 0:P], R=[kT], key=(kT if Q is None else KEY_KT_SW), Q=Q)
        S.dma(sc["v"][tok0:tok0 + P, :], vaug[0:P, :, :].rearrange("p h d -> p (h d)"), R=[vaug], key=(vaug if Q is None else KEY_V_SW), Q=Q)

    def phase1_tile(l, grp, ti, P, xsrc, win_b, sc, first, last, tok0, nlev, st):
        xb_ = xt[ti % 2]
        rp_ = ropet[ti % 2]
        cosv = rp_[0:P, 0:64].rearrange("p (h d) -> p h d", h=4)
        sinv = rp_[0:P, 64:128].rearrange("p (h d) -> p h d", h=4)
        A("activation", out=junk[0:P, :], in_=xb_[0:P, :], func=AF.Square, accum_out=st1[0:P, 0:1], R=[xb_], W=[junk, st1])
        rstd_from_ss(st1[0:P, 0:1], st1[0:P, 1:2], P, float(D), 1e-6, [st1], [st1], st1[0:P, 2:3])
        V("tensor_scalar", xn_b[0:P, :], xb_[0:P, :], st1[0:P, 1:2], None, ALU.mult, R=[xb_, st1], W=[xn_b])
        for k in range(8):
            M("transpose", PB[0][:, k * P:(k + 1) * P], xn_b[0:P, k * 128:(k + 1) * 128], identb(P), R=[xn_b, cb], W=[PB[0]])
        A("copy", xnT[:, :, 0:P], PB[0][:, 0:8 * P].rearrange("p (k t) -> p k t", k=8), R=[PB[0]], W=[xnT])

        def proj(bank, c0, n):
            for k in range(8):
                M("matmul", bank[0:P, 0:n], xnT[:, k, 0:P], win_b[:, k, c0:c0 + n], start=(k == 0), stop=(k == 7),
                  R=[xnT, win_b], W=[bank])

        proj(PF[0], OFF_G, 512)
        proj(PF[1], OFF_G + 512, 512)
        A("activation", out=sg[0:P, 0:512], in_=PF[0][0:P, :], func=AF.Silu, R=[PF[0]], W=[sg])
        A("activation", out=sg[0:P, 512:1024], in_=PF[1][0:P, :], func=AF.Silu, R=[PF[1]], W=[sg])
        S.dma(sc["gb"][tok0:tok0 + P, :], sg[0:P, 256:512], R=[sg], key=sg)

        proj(PF[0], OFF_A, 512)
        proj(PF[1], OFF_A + 512, 384)
        tc_, tp_ = tmpm[ti % 2], tmpm[(ti + 1) % 2]
        V("tensor_tensor", tc_[0:P, 0:512], PF[0][0:P, 0:512], BCv("rw_mu", P, 0, 512), ALU.mult, R=[PF[0], bc], W=[tc_])
        V("tensor_tensor", tc_[0:P, 512:896], PF[1][0:P, 0:384], BCv("rw_mu", P, 512, 896), ALU.mult, R=[PF[1], bc], W=[tc_])
        if last:
            V("tensor_copy", za_f[0:P, 0:512], PF[0][0:P, 0:512], R=[PF[0]], W=[za_f])
            V("tensor_copy", za_f[0:P, 512:896], PF[1][0:P, 0:384], R=[PF[1]], W=[za_f])
            S.dma(st["o_shift"][l:l + 1, :], za_f[P - 1:P, :], R=[za_f], key=za_f)
        V("tensor_tensor", zs[0:P, 0:512], PF[0][0:P, 0:512], BCv("omm", P, 0, 512), ALU.mult, R=[PF[0], bc], W=[zs])
        V("tensor_tensor", zs[0:P, 512:896], PF[1][0:P, 0:384], BCv("omm", P, 512, 896), ALU.mult, R=[PF[1], bc], W=[zs])
        def gen_R():
            for (bank, c0, n) in ((PF[2], 0, 512), (PF[3], 512, 384)):
                M("matmul", bank[0:P, 0:n], cb[0:P, CB_SH:CB_SH + P], tc_[0:P, c0:c0 + n], start=True, stop=False, R=[cb, tc_], W=[bank])
                M("matmul", bank[0:P, 0:n], cb[:, CB_EL:CB_EL + P], tp_[:, c0:c0 + n], start=False, stop=True, R=[cb, tp_], W=[bank])
            V("tensor_add", zs[0:P, 0:512], zs[0:P, 0:512], PF[2][0:P, 0:512], R=[zs, PF[2]], W=[zs])
            V("tensor_add", zs[0:P, 512:896], zs[0:P, 512:896], PF[3][0:P, 0:384], R=[zs, PF[3]], W=[zs])
            r_ = zs[0:P, 0:256]
            k_ = zs[0:P, 256:512]
            v_ = zs[0:P, 512:768]
            A("activation", out=lin[0:P, 0:64], in_=zs[0:P, 768:832], func=AF.Tanh, R=[zs], W=[lin])
            A("copy", lin[0:P, 64:128], zs[0:P, 832:896], R=[zs], W=[lin])
            M("transpose", PB[0][:, 0:P], lin[0:P, :], identb(P), R=[lin, cb], W=[PB[0]])
            A("copy", linT[:, 0:P], PB[0][:, 0:P], R=[PB[0]], W=[linT])
            M("matmul", PF[2][0:P, :], linT[:, 0:P], w2a2_b[:], start=True, stop=True, R=[linT, w2a2_b], W=[PF[2]])
            V("tensor_add", w1[0:P, :], PF[2][0:P, 0:256], BCv("rw_w0", P), R=[PF[2], bc], W=[w1])
            A("activation", out=sw[0:P, :], in_=w1[0:P, :], func=AF.Sigmoid, R=[w1], W=[sw])
            V("tensor_add", w2[0:P, :], PF[2][0:P, 256:512], BCv("rw_a0", P), R=[PF[2], bc], W=[w2])
            A("activation", out=sa[0:P, :], in_=w2[0:P, :], func=AF.Sigmoid, R=[w2], W=[sa])
            V("tensor_tensor", kk[0:P, :], k_, BCv("rw_kk", P), ALU.mult, R=[zs, bc], W=[kk])
            V("tensor_tensor", w3[0:P, :], kk[0:P, :], kk[0:P, :], ALU.mult, R=[kk], W=[w3])
            V("tensor_reduce", st2[0:P, 0:4], w3[0:P, :].rearrange("p (h d) -> p h d", h=4), AX.X, ALU.add, R=[w3], W=[st2])
            V("tensor_scalar", st2[0:P, 4:8], st2[0:P, 0:4], 1e-24, None, ALU.max, R=[st2], W=[st2])
            A("activation", out=st2[0:P, 4:8], in_=st2[0:P, 4:8], func=AF.Sqrt, R=[st2], W=[st2])
            V("reciprocal", st2[0:P, 8:12], st2[0:P, 4:8], R=[st2], W=[st2])
            for h in range(4):
                V("tensor_scalar", kk[0:P, 64 * h:64 * h + 64], kk[0:P, 64 * h:64 * h + 64], st2[0:P, 8 + h:9 + h], None, ALU.mult,
                  R=[kk, st2], W=[kk])
            V("scalar_tensor_tensor", w1[0:P, :], sa[0:P, :], -1.0, BCv("rw_ka", P), ALU.add, ALU.mult, R=[sa, bc], W=[w1])
            V("scalar_tensor_tensor", kp[0:P, :], w1[0:P, :], 1.0, k_, ALU.add, ALU.mult, R=[w1, zs], W=[kp])
            V("tensor_tensor", bb[0:P, :], kk[0:P, :], sa[0:P, :], ALU.mult, R=[kk, sa], W=[bb])
            G("tensor_tensor", w2[0:P, :], r_, kp[0:P, :], ALU.mult, R=[zs, kp], W=[w2])
            G("tensor_tensor", w2[0:P, :], w2[0:P, :], BCv("rw_rk", P), ALU.mult, R=[w2, bc], W=[w2])
            V("tensor_reduce", bcf[0:P, 0:4], w2[0:P, :].rearrange("p (h d) -> p h d", h=4), AX.X, ALU.add, R=[w2], W=[bcf])
            M("matmul", PF[2][0:P, 0:256], cf[0:P, CF_TRI:CF_TRI + P], sw[0:P, :], start=True, stop=True, R=[cf, sw], W=[PF[2]])
            M("matmul", PF[2][0:P, 256:512], cf[0:P, CF_ONES:CF_ONES + P], sw[0:P, :], start=True, stop=True, R=[cf, sw], W=[PF[2]])
            V("tensor_copy", cs_sb[0:P, :], PF[2][0:P, 0:256], R=[PF[2]], W=[cs_sb])
            A("activation", out=e1[0:P, :], in_=PF[2][0:P, 0:256], func=AF.Exp, scale=CDEC, R=[PF[2]], W=[e1])
            A("activation", out=e2[0:P, :], in_=PF[2][0:P, 0:256], func=AF.Exp, scale=-CDEC, R=[PF[2]], W=[e2])
            V("tensor_sub", w1[0:P, :], cs_sb[0:P, :], sw[0:P, :], R=[cs_sb, sw], W=[w1])
            A("activation", out=e3[0:P, :], in_=w1[0:P, :], func=AF.Exp, scale=CDEC, R=[w1], W=[e3])
            V("tensor_sub", w3[0:P, :], PF[2][0:P, 256:512], cs_sb[0:P, :], R=[PF[2], cs_sb], W=[w3])
            A("activation", out=e4[0:P, :], in_=w3[0:P, :], func=AF.Exp, scale=CDEC, R=[w3], W=[e4])
            V("scalar_tensor_tensor", hat[0:P, 0, :], kk[0:P, :], -1.0, e3[0:P, :], ALU.mult, ALU.mult, R=[kk, e3], W=[hat])
            V("tensor_tensor", hat[0:P, 1, :], bb[0:P, :], e2[0:P, :], ALU.mult, R=[bb, e2], W=[hat])
            V("tensor_tensor", hat[0:P, 2, :], kp[0:P, :], e2[0:P, :], ALU.mult, R=[kp, e2], W=[hat])
            V("tensor_tensor", hat[0:P, 3, :], r_, e1[0:P, :], ALU.mult, R=[zs, e1], W=[hat])
            G("tensor_tensor", bp_b[0:P, :], bb[0:P, :], e4[0:P, :], ALU.mult, R=[bb, e4], W=[bp_b])
            G("tensor_tensor", kp4[0:P, :], kp[0:P, :], e4[0:P, :], ALU.mult, R=[kp, e4], W=[kp4])
            ohc = CF_OH + (0 if P == 128 else 1)
            for p in range(2):
                M("matmul", PF[3][:, p:p + 1], e1[0:P, p * 128:(p + 1) * 128], cf[0:P, ohc:ohc + 1], start=True, stop=True,
                  R=[e1, cf], W=[PF[3]])
            V("tensor_copy", wc[:, 0:2], PF[3][:, 0:2], R=[PF[3]], W=[wc])
            for vi in range(4):
                for p in range(2):
                    M("transpose", PB[0][:, (vi * 2 + p) * P:(vi * 2 + p + 1) * P], hat[0:P, vi, p * 128:(p + 1) * 128], identb(P),
                      R=[hat, cb], W=[PB[0]])
            A("copy", hT[:, :, :, 0:P], PB[0][:, 0:8 * P].rearrange("p (v q t) -> p v q t", v=4, q=2), R=[PB[0]], W=[hT])
            def fm(h, vi):
                return hT[64 * (h % 2):64 * (h % 2) + 64, vi, h // 2, 0:P]

            def pv(bank, w=128, n=None):
                n = P if n is None else n
                return bank[0:P, 0:4 * w].rearrange("p (h t) -> p h t", h=4)[:, :, 0:n]

            def msk(kind):
                return cbr[0:P, kind, :, 0:P]

            for h in range(4):
                M("matmul", PF[2][0:P, h * 128:h * 128 + P], fm(h, 0), fm(h, 1), start=True, stop=True, R=[hT], W=[PF[2]])
                M("matmul", PF[3][0:P, h * 128:h * 128 + P], fm(h, 1), fm(h, 0), start=True, stop=True, R=[hT], W=[PF[3]])
                M("matmul", PF[4][0:P, h * 128:h * 128 + P], fm(h, 0), fm(h, 2), start=True, stop=True, R=[hT], W=[PF[4]])
                M("matmul", PF[5][0:P, h * 128:h * 128 + P], fm(h, 1), fm(h, 3), start=True, stop=True, R=[hT], W=[PF[5]])
            V("tensor_tensor", scL[0:P, :, 0:P], pv(PF[2]), msk(0), ALU.mult, R=[PF[2], cbr], W=[scL])
            V("tensor_tensor", scN[0:P, :, 0:P], pv(PF[3]), msk(1), ALU.mult, R=[PF[3], cbr], W=[scN])
            for h in range(4):
                bk = PF[2 + (h % 2)]
                M("matmul", bk[0:P, h * 128:h * 128 + P], fm(h, 2), fm(h, 3), start=True, stop=True, R=[hT], W=[bk])
            V("tensor_add", Ttb[0][0:P, :, 0:P], scL[0:P, :, 0:P], msk(3), R=[scL, cbr], W=[Ttb[0]])
            V("tensor_tensor", scK[0:P, :, 0:P], pv(PF[4]), msk(0), ALU.mult, R=[PF[4], cbr], W=[scK])
            V("tensor_tensor", mbr_b[0:P, :, 0:P], pv(PF[5]), msk(2), ALU.mult, R=[PF[5], cbr], W=[mbr_b])
            for h in range(4):
                bk = PF[2 + (h % 2)]
                V("tensor_tensor", mkr_f[0:P, h, 0:P], bk[0:P, h * 128:h * 128 + P], cb[0:P, CB_UI:CB_UI + P], ALU.mult, R=[bk, cb], W=[mkr_f])
            def acc_(tb, idx):
                return (lambda h: tb[0:P, h, 0:P]) if idx is None else (lambda h: tb[0:P, idx, h, 0:P])

            def squares(Af, ATf, Rb, dst, need_A):
                if need_A:
                    for h in range(4):
                        M("matmul", PF[2][0:P, h * 128:h * 128 + P], ATf(h), Af(h), start=True, stop=True, R=Rb, W=[PF[2]])
                for h in range(4):
                    M("matmul", PF[3][0:P, h * 128:h * 128 + P], Af(h), ATf(h), start=True, stop=True, R=Rb, W=[PF[3]])

            def squares_evac(dst, need_A):
                if need_A:
                    V("tensor_copy", dst[0:P, 0, :, 0:P], pv(PF[2]), R=[PF[2]], W=[dst])
                A("copy", dst[0:P, 1, :, 0:P], pv(PF[3]), R=[PF[3]], W=[dst])

            tcur = 0
            squares(acc_(scL, None), acc_(scN, None), [scL, scN], Ab[1], nlev > 2)
            squares_evac(Ab[1], nlev > 2)
            for k in range(1, nlev):
                cur = Ab[k % 2]
                Af, ATf = acc_(cur, 0), acc_(cur, 1)
                lastk = (k == nlev - 1)
                pbank = PF[4 + (k % 2)]
                for h in range(4):
                    M("matmul", pbank[0:P, h * 128:h * 128 + P], ATf(h), Ttb[tcur][0:P, h, 0:P], start=True, stop=True,
                      R=[cur, Ttb[tcur]], W=[pbank])
                if not lastk:
                    nxt = Ab[(k + 1) % 2]
                    squares(Af, ATf, [cur], nxt, k + 1 < nlev - 1)
                yield "mm"
                V("tensor_add", Ttb[1 - tcur][0:P, :, 0:P], pv(pbank), Ttb[tcur][0:P, :, 0:P], R=[pbank, Ttb[tcur]], W=[Ttb[1 - tcur]])
                if not lastk:
                    squares_evac(nxt, k + 1 < nlev - 1)
                tcur = 1 - tcur
                yield "ev"
            Tt = Ttb[tcur]
            for h in range(4):
                M("matmul", PF[2][0:P, h * 128:h * 128 + P], Tt[0:P, h, 0:P], mbr_b[0:P, h, 0:P], start=True, stop=True, R=[Tt, mbr_b], W=[PF[2]])
            for h in range(4):
                M("matmul", PF[3][0:P, h * 64:h * 64 + 64], Tt[0:P, h, 0:P], bp_b[0:P, 64 * h:64 * h + 64], start=True, stop=True, R=[Tt, bp_b], W=[PF[3]])
            A("copy", G_b[0:P, :, 0:P], pv(PF[2]), R=[PF[2]], W=[G_b])
            V("tensor_copy", H_b[0:P, :, :], PF[3][0:P, 0:256].rearrange("p (h d) -> p h d", h=4), R=[PF[3]], W=[H_b])
            for h in range(4):
                M("matmul", PF[2][0:P, h * 128:h * 128 + P], scK[0:P, h, 0:P], G_b[0:P, h, 0:P], start=True, stop=True, R=[scK, G_b], W=[PF[2]])
            for h in range(4):
                M("matmul", PF[3][:, h * 128:h * 128 + P], hat[0:P, 0, (h // 2) * 128:(h // 2) * 128 + 128], G_b[0:P, h, 0:P], start=True, stop=True,
                  R=[hat, G_b], W=[PF[3]])
            for h in range(4):
                M("matmul", PF[4][0:P, h * 64:h * 64 + 64], scK[0:P, h, 0:P], H_b[0:P, h, :], start=True, stop=True, R=[scK, H_b], W=[PF[4]])
            for h in range(4):
                M("matmul", PF[4][:, 256 + h * 64:256 + h * 64 + 64], hat[0:P, 0, (h // 2) * 128:(h // 2) * 128 + 128], H_b[0:P, h, :],
                  start=True, stop=True, R=[hat, H_b], W=[PF[4]])
            V("tensor_add", z_f[0:P, :, 0:P], pv(PF[2]), mkr_f[0:P, :, 0:P], R=[PF[2], mkr_f], W=[z_f])
            for o_ in (0, 64):
                h0 = o_ // 64
                qv = PF[3][o_:o_ + 64, 0:512].rearrange("p (q r t) -> p q r t", q=2, r=2)[:, :, h0, 0:P]
                V("tensor_add", q_f[o_:o_ + 64, :, 0:P], qv, hT[o_:o_ + 64, 3, :, 0:P], R=[PF[3], hT], W=[q_f])
            for h in range(4):
                p = h // 2
                o_ = 64 * (h % 2)
                V("tensor_add", x_f[0:P, h, o_:o_ + 64], PF[4][0:P, h * 64:h * 64 + 64], kp4[0:P, 64 * h:64 * h + 64], R=[PF[4], kp4], W=[x_f])
                V("scalar_tensor_tensor", p_f[o_:o_ + 64, p, o_:o_ + 64], cf[o_:o_ + 64, CF_ID + o_:CF_ID + o_ + 64], wc[o_:o_ + 64, p:p + 1],
                  PF[4][o_:o_ + 64, 256 + h * 64:256 + h * 64 + 64], ALU.mult, ALU.add, R=[cf, wc, PF[4]], W=[p_f])
            for h in range(4):
                p = h // 2
                o_ = 64 * (h % 2)
                vh = zs[0:P, 512 + 64 * h:512 + 64 * h + 64]
                M("matmul", PF[5][0:P, 64 * h:64 * h + 64], q_f[o_:o_ + 64, p, 0:P], ST[o_:o_ + 64, p, :], start=True, stop=False, R=[q_f, ST], W=[PF[5]])
                M("matmul", PF[5][0:P, 64 * h:64 * h + 64], z_f[0:P, h, 0:P], vh, start=False, stop=True, R=[z_f, zs], W=[PF[5]])
            for p in range(2):
                M("matmul", PF[5][:, 256 + 64 * p:256 + 64 * p + 64], p_f[:, p, :], ST[:, p, :], start=True, stop=False, R=[p_f, ST], W=[PF[5]])
                for h in (2 * p, 2 * p + 1):
                    M("matmul", PF[5][:, 256 + 64 * p:256 + 64 * p + 64], x_f[0:P, h, :], zs[0:P, 512 + 64 * h:512 + 64 * h + 64],
                      start=False, stop=(h == 2 * p + 1), R=[x_f, zs], W=[PF[5]])
            V("tensor_copy", ST[:, :, :], PF[5][:, 256:384].rearrange("k (p v) -> k p v", p=2), R=[PF[5]], W=[ST])
            Y = PF[5]
            V("tensor_reduce", st3[0:P, 0:4], Y[0:P, 0:256].rearrange("p (h d) -> p h d", h=4), AX.X, ALU.add, R=[Y], W=[st3])
            V("tensor_scalar", st3[0:P, 0:4], st3[0:P, 0:4], 1.0 / 64, None, ALU.mult, R=[st3], W=[st3])
            for h in range(4):
                V("tensor_scalar", yrw[0:P, 64 * h:64 * h + 64], Y[0:P, 64 * h:64 * h + 64], st3[0:P, h:h + 1], None, ALU.subtract,
                  R=[Y, st3], W=[yrw])
            V("tensor_tensor", w1[0:P, :], yrw[0:P, :], yrw[0:P, :], ALU.mult, R=[yrw], W=[w1])
            V("tensor_reduce", st3[0:P, 4:8], w1[0:P, :].rearrange("p (h d) -> p h d", h=4), AX.X, ALU.add, R=[w1], W=[st3])
            rstd_from_ss(st3[0:P, 4:8], st3[0:P, 8:12], P, 64.0, 64e-5, [st3], [st3], st3[0:P, 12:16])
            for h in range(4):
                V("tensor_scalar", yrw[0:P, 64 * h:64 * h + 64], yrw[0:P, 64 * h:64 * h + 64], st3[0:P, 8 + h:9 + h], None, ALU.mult,
                  R=[yrw, st3], W=[yrw])
            V("tensor_tensor", yrw[0:P, :], yrw[0:P, :], BCv("rw_gn_g", P), ALU.mult, R=[yrw, bc], W=[yrw])
            V("tensor_add", yrw[0:P, :], yrw[0:P, :], BCv("rw_gn_b", P), R=[yrw, bc], W=[yrw])
            for h in range(4):
                V("scalar_tensor_tensor", yrw[0:P, 64 * h:64 * h + 64], zs[0:P, 512 + 64 * h:512 + 64 * h + 64], bcf[0:P, h:h + 1],
                  yrw[0:P, 64 * h:64 * h + 64], ALU.mult, ALU.add, R=[zs, bcf, yrw], W=[yrw])
            V("tensor_tensor", ygp[0:P, 0:256], yrw[0:P, :], sg[0:P, 0:256], ALU.mult, R=[yrw, sg], W=[ygp])
            if last:
                for p in range(2):
                    M("transpose", PF[2][0:64, p * 128:(p + 1) * 128], ST[:, p, :], identf(128), R=[ST, cf], W=[PF[2]])
                V("tensor_copy", wkv_o[:].rearrange("i h j -> i (h j)"), PF[2][0:64, 0:256], R=[PF[2]], W=[wkv_o])
                S.dma(st["o_wkv"][l].rearrange("h i j -> i h j"), wkv_o[:], R=[wkv_o], key=wkv_o)
            yield

        def gen_O():
            proj(PF[0], OFF_B, 352)
            ZB = PF[0]
            A("activation", out=junk[0:P, 0:192], in_=ZB[0:P, 0:192], func=AF.Square, accum_out=st1[0:P, 4:5], R=[ZB], W=[junk, st1])
            yield
            A("activation", out=junk[0:P, 0:128], in_=ZB[0:P, 192:320], func=AF.Square, accum_out=st1[0:P, 5:6], R=[ZB], W=[junk, st1])
            A("activation", out=junk[0:P, 0:32], in_=ZB[0:P, 320:352], func=AF.Square, accum_out=st1[0:P, 6:7], R=[ZB], W=[junk, st1])
            V("tensor_tensor", st1[0:P, 12:15], st1[0:P, 4:7], invc[0:P, 8:11], ALU.mult, R=[st1, invc], W=[st1])
            V("tensor_scalar", st1[0:P, 12:15], st1[0:P, 12:15], 1e-6, None, ALU.add, R=[st1], W=[st1])
            A("activation", out=st1[0:P, 12:15], in_=st1[0:P, 12:15], func=AF.Sqrt, R=[st1], W=[st1])
            V("reciprocal", st1[0:P, 8:11], st1[0:P, 12:15], R=[st1], W=[st1])
            yield
            V("tensor_scalar", qa_b[0:P, :], ZB[0:P, 0:192], st1[0:P, 8:9], None, ALU.mult, R=[ZB, st1], W=[qa_b])
            yield
            M("transpose", PB[1][:, 0:P], qa_b[0:P, 0:128], identb(P), R=[qa_b, cb], W=[PB[1]])
            M("transpose", PB[1][0:64, P:2 * P], qa_b[0:P, 128:192], identb(P), R=[qa_b, cb], W=[PB[1]])
            A("copy", qaT[:, 0, 0:P], PB[1][:, 0:P], R=[PB[1]], W=[qaT])
            yield
            A("copy", qaT[0:64, 1, 0:P], PB[1][0:64, P:2 * P], R=[PB[1]], W=[qaT])
            M("matmul", PF[1][0:P, 0:384], qaT[:, 0, 0:P], wuq_b[:, 0, :], start=True, stop=False, R=[qaT, wuq_b], W=[PF[1]])
            M("matmul", PF[1][0:P, 0:384], qaT[:, 1, 0:P], wuq_b[:, 1, :], start=False, stop=True, R=[qaT, wuq_b], W=[PF[1]])
            yield
            A("activation", out=zb_f[0:P, :], in_=PF[1][0:P, 0:384], func=AF.Square, R=[PF[1]], W=[zb_f])
            sq3 = zb_f[0:P, :].rearrange("p (h d) -> p h d", h=4)
            V("tensor_reduce", st2o[0:P, 0:4], sq3[:, :, 0:64], AX.X, ALU.add, R=[zb_f], W=[st2o])
            yield
            V("tensor_reduce", st2o[0:P, 4:8], sq3[:, :, 64:96], AX.X, ALU.add, R=[zb_f], W=[st2o])
            V("tensor_tensor", st3o[0:P, 0:8], st2o[0:P, 0:8], invc[0:P, 11:19], ALU.mult, R=[st2o, invc], W=[st3o])
            V("tensor_scalar", st3o[0:P, 0:8], st3o[0:P, 0:8], 1e-6, None, ALU.add, R=[st3o], W=[st3o])
            A("activation", out=st3o[0:P, 0:8], in_=st3o[0:P, 0:8], func=AF.Sqrt, R=[st3o], W=[st3o])
            V("reciprocal", st2o[0:P, 8:16], st3o[0:P, 0:8], R=[st3o], W=[st2o])
            yield
            for h in range(4):
                V("scalar_tensor_tensor", qn[0:P, h, 0:64], PF[1][0:P, 96 * h:96 * h + 64], st2o[0:P, 8 + h:9 + h],
                  BCv("mla_q_norm_g", P, 0, 64), ALU.mult, ALU.mult, R=[PF[1], st2o, bc], W=[qn])
                V("scalar_tensor_tensor", qn[0:P, h, 64:96], PF[1][0:P, 96 * h + 64:96 * h + 96], st2o[0:P, 12 + h:13 + h],
                  BCv("mla_q_norm_g", P, 64, 96), ALU.mult, ALU.mult, R=[PF[1], st2o, bc], W=[qn])
            V("tensor_copy", qfb[0:P, :, 0:64], qn[0:P, :, 0:64], R=[qn], W=[qfb])
            t1 = w1o[0:P, 0:64].rearrange("p (h d) -> p h d", h=4)
            yield
            t2 = w2o[0:P, 0:64].rearrange("p (h d) -> p h d", h=4)
            rope_apply(qfb[0:P, :, 64:80], qfb[0:P, :, 80:96], qn[0:P, :, 64:80], qn[0:P, :, 80:96], cosv, sinv, t1, t2,
                       [qn, rp_], [qfb], [w1o, w2o])
            for h in range(4):
                M("transpose", PB[1][0:96, h * P:(h + 1) * P], qfb[0:P, h, :], identb(P), R=[qfb, cb], W=[PB[1]])
            yield
            A("copy", qT[:, :, 0:P], PB[1][0:96, 0:4 * P].rearrange("p (h t) -> p h t", h=4), R=[PB[1]], W=[qT])
            S.dma(sc["qt"][:, :, tok0:tok0 + P].rearrange("h d t -> d h t"), qT[:, :, 0:P], R=[qT], key=qT)
            V("scalar_tensor_tensor", ckv_f[0:P, :], ZB[0:P, 192:320], st1[0:P, 9:10], BCv("mla_kva_g", P), ALU.mult, ALU.mult,
              R=[ZB, st1, bc], W=[ckv_f])
            yield
            S.dma(st["o_ckv"][l, tok0:tok0 + P, :] if grp == "p" else st["o_ckv"][l, 0:P, :], ckv_f[0:P, :], R=[ckv_f], key=ckv_f)
            V("scalar_tensor_tensor", kr_f[0:P, :], ZB[0:P, 320:352], st1[0:P, 10:11], BCv("mla_k_norm_g", P, 64, 96), ALU.mult, ALU.mult,
              R=[ZB, st1, bc], W=[kr_f])
            rope_apply(kr_r[0:P, 0:16], kr_r[0:P, 16:32], kr_f[0:P, 0:16], kr_f[0:P, 16:32], rp_[0:P, 0:16], rp_[0:P, 64:80],
                       w1o[0:P, 0:16], w2o[0:P, 0:16], [kr_f, rp_], [kr_r], [w1o, w2o])
            yield
            S.dma(st["o_kr"][l, tok0:tok0 + P, :] if grp == "p" else st["o_kr"][l, 0:P, :], kr_r[0:P, :], R=[kr_r], key=kr_r)
            kv_from_ckv(P, ckv_f[0:P, :], kr_r[0:P, :], [ckv_f, kr_r], sc, st["ktok0"] + tok0)

            proj(PF[0], OFF_C, 512)
            yield
            ZC = PF[0]
            V("tensor_tensor", ug[0:P, :], ZC[0:P, 0:256], sg[0:P, 512:768], ALU.mult, R=[ZC, sg], W=[ug])
            V("bn_stats", bnst[0:P, 0:6], ZC[0:P, 256:512], R=[ZC], W=[bnst])
            yield
            V("bn_aggr", bnst[0:P, 6:8], bnst[0:P, 0:6], R=[bnst], W=[bnst])
            rstd_from_ss(bnst[0:P, 7:8], st1[0:P, 3:4], P, 1.0, 1e-5, [bnst], [st1], st1[0:P, 15:16])
            V("tensor_scalar", vn[0:P, :], ZC[0:P, 256:512], bnst[0:P, 6:7], st1[0:P, 3:4], ALU.subtract, ALU.mult, R=[ZC, bnst, st1], W=[vn])
            yield
            V("tensor_tensor", vn[0:P, :], vn[0:P, :], BCv("sgu_ln_g", P), ALU.mult, R=[vn, bc], W=[vn])
            V("tensor_add", vn[0:P, :], vn[0:P, :], BCv("sgu_ln_b", P), R=[vn, bc], W=[vn])
            if grp == "s":
                S.dma(o_sgv[l, 0:P, :], vn[0:P, :], R=[vn], key=vn)
            yield
            V("tensor_copy", vn_b[0:P, :], vn[0:P, :], R=[vn], W=[vn_b])
            for h in range(4):
                M("matmul", PF[1][0:P, 64 * h:64 * h + 64], wsT_b[0:P, h, 0:P], vn_b[0:P, 64 * h:64 * h + 64], start=True, stop=True,
                  R=[wsT_b, vn_b], W=[PF[1]])
            for h in range(4):
                V("scalar_tensor_tensor", ygp[0:P, 256 + 64 * h:256 + 64 * h + 64], PF[1][0:P, 64 * h:64 * h + 64], sgub[0:P, h:h + 1],
                  ug[0:P, 64 * h:64 * h + 64], ALU.add, ALU.mult, R=[PF[1], sgub, ug], W=[ygp])

            yield
            proj(PF[0], OFF_D, 256)
            zc_, zp_ = zd_f[ti % 2], zd_f[(ti + 1) % 2]
            V("tensor_copy", zc_[0:P, :], PF[0][0:P, 0:256], R=[PF[0]], W=[zc_])
            yield
            if last:
                if grp == "p":
                    S.dma(st["o_pool"][l], zc_[P - 15:P, :], R=[zc_], key=zc_)
                else:
                    S.dma(st["o_pool"][l], zc_[1:16, :], R=[zc_], key=zc_)
            for g in range(4):
                M("matmul", PF[1][0:P, 64 * g:64 * g + 64], cf[0:P, CF_BAND + 128 * g:CF_BAND + 128 * g + P], zc_[0:P, 64 * g:64 * g + 64],
                  start=True, stop=False, R=[cf, zc_], W=[PF[1]])
                M("matmul", PF[1][0:P, 64 * g:64 * g + 64], cf[:, CF_BANDP + 128 * g:CF_BANDP + 128 * g + P], zp_[:, 64 * g:64 * g + 64],
                  start=False, stop=True, R=[cf, zp_], W=[PF[1]])
            for g in range(4):
                ic = invc[0:P, g:g + 1] if (first and grp == "p") else invc[0:P, 4 + g:5 + g]
                V("scalar_tensor_tensor", d_b[0:P, 64 * g:64 * g + 64], PF[1][0:P, 64 * g:64 * g + 64], ic, zc_[0:P, 64 * g:64 * g + 64],
                  ALU.mult, ALU.subtract, R=[PF[1], invc, zc_], W=[d_b])
            yield
            for c in range(2):
                M("transpose", PB[1][:, c * P:(c + 1) * P], d_b[0:P, c * 128:(c + 1) * 128], identb(P), R=[d_b, cb], W=[PB[1]])
            A("copy", dT[:, :, 0:P], PB[1][:, 0:2 * P].rearrange("p (c t) -> p c t", c=2), R=[PB[1]], W=[dT])
            for c in range(2):
                M("matmul", PF[0][0:P, c * 128:(c + 1) * 128], dT[:, c, 0:P], poolw_b[:, c, :], start=True, stop=True, R=[dT, poolw_b], W=[PF[0]])
            yield
            V("tensor_tensor", ygp[0:P, 512:768], PF[0][0:P, 0:256], sg[0:P, 768:1024], ALU.mult, R=[PF[0], sg], W=[ygp])
            yield

        gr, go = gen_R(), gen_O()
        if P == 128 and ILV > 0:
            for tag in gr:
                if tag == "mm":
                    for _k in range(ILV):
                        next(go, None)
                elif tag == "ev":
                    for _k in range(ILV2):
                        next(go, None)
        for _ in gr:
            pass
        for _ in go:
            pass
        S.dma(sc["yg"][tok0:tok0 + P, :], ygp[0:P, :], R=[ygp], key=ygp)

    def phase2(l, grp, Tq, QB, nkt_total, klast, xsrc, ydst, sc, KT, VV, bufs):
        (QTb, PT, yfull4, yT, gbt4, xres, xout, rr) = bufs
        nqb = Tq // QB
        nsub = max(1, QB // 128)
        Pq = min(QB, 128)
        for qb in range(nqb):
            q0 = qb * QB
            S.dma(QTb[:, :, 0:QB], sc["qt"][:, :, q0:q0 + QB].rearrange("h d t -> d h t"), W=[QTb], key=QTb)
            for i in range(nsub):
                t0 = q0 + i * 128
                S.dma(yfull4[i][0:Pq, 0:256], sc["yg"][t0:t0 + Pq, 0:256], W=[yfull4[i]], key=yfull4[i])
                S.dma(yfull4[i][0:Pq, 512:1024], sc["yg"][t0:t0 + Pq, 256:768], W=[yfull4[i]], key=yfull4[i])
                S.dma(gbt4[i][0:Pq, :], sc["gb"][t0:t0 + Pq, :], W=[gbt4[i]], key=gbt4[i])
            if grp == "p":
                nkt = 4 * qb + 4
            else:
                nkt = nkt_total
            for h in range(4):
                def kinfo(kt):
                    kp_ = 128 if (grp == "p" or kt < nkt_total - 1) else klast
                    jd = kt - 4 * qb if grp == "p" else -1
                    c0 = 128 * jd if jd > 0 else 0
                    return kp_, jd, c0

                def emit_scores(kt):
                    kp_, jd, c0 = kinfo(kt)
                    n = QB - c0
                    sbk = (PF[4], PF[5], PB[1])[kt % 3]
                    sview = sbk[0:kp_, 0:n] if (kt % 3) < 2 else PB[1][0:kp_, 0:1024].bitcast(F32)[:, 0:n]
                    M("matmul", sview, KT[:, h, kt * 128:kt * 128 + kp_], QTb[:, h, c0:QB], start=True, stop=True,
                      R=[KT, QTb], W=[sbk])
                    pt_ = PT[kt % 3]
                    A("activation", out=pt_[0:kp_, c0:QB], in_=sview, func=AF.Exp, scale=float(1.0 / np.sqrt(96.0)),
                      R=[sbk], W=[pt_])
                    if jd >= 0:
                        G("memset", pt_[64:128, c0:c0 + 64], 0.0, W=[pt_])

                def emit_pv(kt):
                    kp_, jd, c0 = kinfo(kt)
                    pt_ = PT[kt % 3]
                    for i in range(c0 // 128, nsub):
                        first_k = (kt == 0)
                        last_k = (kt == (4 * qb + i if grp == "p" else nkt - 1))
                        M("matmul", PF[i][0:Pq, 0:65], pt_[0:kp_, i * 128:i * 128 + Pq], VV[0:kp_, kt, 65 * h:65 * h + 65],
                          start=first_k, stop=last_k, R=[pt_, VV], W=[PF[i]])

                emit_scores(0)
                if nkt > 1:
                    emit_scores(1)
                for kt in range(nkt):
                    if kt + 2 < nkt:
                        emit_scores(kt + 2)
                    emit_pv(kt)
                for i in range(nsub):
                    V("reciprocal", rr[0:Pq, h:h + 1], PF[i][0:Pq, 64:65], R=[PF[i]], W=[rr])
                    V("scalar_tensor_tensor", yfull4[i][0:Pq, 256 + 64 * h:256 + 64 * h + 64], PF[i][0:Pq, 0:64], rr[0:Pq, h:h + 1],
                      gbt4[i][0:Pq, 64 * h:64 * h + 64], ALU.mult, ALU.mult, R=[PF[i], rr, gbt4[i]], W=[yfull4[i]])
            for i in range(nsub):
                t0 = q0 + i * 128
                yfull = yfull4[i]
                S.dma(xres[0:Pq, :], xsrc[t0:t0 + Pq, :], W=[xres], key=xres)
                for k in range(8):
                    M("transpose", PB[0][:, k * Pq:(k + 1) * Pq], yfull[0:Pq, k * 128:(k + 1) * 128], identb(Pq), R=[yfull, cb], W=[PB[0]])
                A("copy", yT[:, :, 0:Pq], PB[0][:, 0:8 * Pq].rearrange("p (k t) -> p k t", k=8), R=[PB[0]], W=[yT])
                for cbk in range(2):
                    bank = PF[4 + cbk]
                    for k in range(8):
                        M("matmul", bank[0:Pq, :], yT[:, k, 0:Pq], wout_b[:, k, cbk * 512:(cbk + 1) * 512], start=(k == 0), stop=(k == 7),
                          R=[yT, wout_b], W=[bank])
                    V("tensor_add", xout[0:Pq, cbk * 512:(cbk + 1) * 512], bank[0:Pq, :], xres[0:Pq, cbk * 512:(cbk + 1) * 512],
                      R=[bank, xres], W=[xout])
                S.dma(ydst[t0:t0 + Pq, :], xout[0:Pq, :], R=[xout], key=xout, Q=S.pool)

    stp = dict(o_shift=o_shp, o_wkv=o_wkvp, o_pool=o_plp, o_ckv=o_ckvp, o_kr=o_krp, ktok0=0)
    sts = dict(o_shift=o_shs, o_wkv=o_wkvs, o_pool=o_pls, o_ckv=o_ckvs, o_kr=o_krs, ktok0=PAST)
    nlev_p = 7
    nlev_s = 4
    for l in range(NL):
        load_small(l)
        xsrc_p = x_p if l == 0 else o_yp
        xsrc_s = x_s if l == 0 else o_ys
        with ExitStack() as c1:
            alloc_work(c1)
            win_b = sb("win_b", [128, 8, DIN], BF16, c1)
            def cache_tiles():
                for kt in range(PAST // 128):
                    cs_ = cstage[kt % 2]
                    S.dma(cs_[:, 0:128], c_ckv[l, kt * 128:(kt + 1) * 128, :], W=[cs_], key=cs_)
                    S.dma(cs_[:, 128:160], c_kr[l, kt * 128:(kt + 1) * 128, :], W=[cs_], key=cs_)
                    kv_from_ckv(128, cs_[:, 0:128], cs_[:, 128:160], [cs_], scr_s, kt * 128, Q=S.pool)
                    yield
            gw = load_win(l, win_b)
            gc = cache_tiles() if do_sample else iter(())
            for _ in gw:
                next(gc, None)
                next(gc, None)
            for _ in gc:
                pass
            V("memset", tmpm[1][:], 0.0, W=[tmpm[1]])
            V("memset", zd_f[1][:], 0.0, W=[zd_f[1]])
            V("memset", ST[:], 0.0, W=[ST])
            S.dma(xt[0][:], xsrc_p[0:128, :], W=[xt[0]], key=xt[0])
            S.dma(ropet[0][:], rope_p[0:128, :], W=[ropet[0]], key=ropet[0])
            for ti in range(NT):
                if ti + 1 < NT:
                    S.dma(xt[(ti + 1) % 2][:], xsrc_p[(ti + 1) * 128:(ti + 2) * 128, :], W=[xt[(ti + 1) % 2]], key=xt[(ti + 1) % 2])
                    S.dma(ropet[(ti + 1) % 2][:], rope_p[(ti + 1) * 128:(ti + 2) * 128, :], W=[ropet[(ti + 1) % 2]], key=ropet[(ti + 1) % 2])
                phase1_tile(l, "p", ti, 128, xsrc_p, win_b, scr_p, ti == 0, ti == NT - 1, ti * 128, nlev_p, stp)
            if do_sample:
                P = TS
                V("memset", stage[:, 0:896], 0.0, W=[stage])
                S.dma(stage[127:128, 0:896], st_shift[l:l + 1, :], W=[stage], key=stage)
                V("tensor_tensor", tmpm[1][:, :], stage[:, 0:896], BCv("rw_mu", 128), ALU.mult, R=[stage, bc], W=[tmpm[1]])
                V("memset", zd_f[1][:], 0.0, W=[zd_f[1]])
                S.dma(zd_f[1][113:128, :], st_pool[l], W=[zd_f[1]], key=zd_f[1])
                S.dma(stage2[0:64, 0:256].rearrange("i (h j) -> i h j", h=4), st_wkv[l].rearrange("h i j -> i h j"), W=[stage2], key=stage2)
                for p in range(2):
                    M("transpose", PF[2][:, p * 64:(p + 1) * 64], stage2[0:64, p * 128:(p + 1) * 128], identf(64), R=[stage2, cf], W=[PF[2]])
                V("tensor_copy", ST[:].rearrange("k p v -> k (p v)"), PF[2][:, 0:128], R=[PF[2]], W=[ST])
                S.dma(xt[0][0:P, :], xsrc_s[0:P, :], W=[xt[0]], key=xt[0])
                S.dma(ropet[0][0:P, :], rope_s[0:P, :], W=[ropet[0]], key=ropet[0])
                phase1_tile(l, "s", 0, P, xsrc_s, win_b, scr_s, True, True, 0, nlev_s, sts)
            S.barrier()
        with ExitStack() as c2:
            KT = sb("KT", [96, 4, max(T, PAST + 128)], BF16, c2)
            VV = sb("VV", [128, max(NT, PAST // 128 + 1), 260], BF16, c2)
            QTb = sb("QTb", [96, 4, 512], BF16, c2)
            PT = [sb("PT%d" % i, [128, 512], BF16, c2) for i in range(3)]
            yfull4 = [sb("yfull%d" % i, [128, D], BF16, c2) for i in range(4)]
            yT = sb("yT", [128, 8, 128], BF16, c2)
            gbt4 = [sb("gbt%d" % i, [128, 256], BF16, c2) for i in range(4)]
            xres = sb("xres", [128, D], F32, c2)
            xout = sb("xout", [128, D], F32, c2)
            rr = sb("rr", [128, 4], F32, c2)
            bufs = (QTb, PT, yfull4, yT, gbt4, xres, xout, rr)
            for h in range(4):
                S.dma(KT[:, h, 0:T], scr_p["kt"][h], W=[KT], key=KT)
            vview = scr_p["v"].rearrange("(n p) c -> p n c", p=128)
            for n0 in range(0, NT, 8):
                n1 = min(NT, n0 + 8)
                S.dma(VV[:, n0:n1, :], vview[:, n0:n1, :], W=[VV], key=VV)
            phase2(l, "p", T, 512, NT, 128, xsrc_p, o_yp, scr_p, KT, VV, bufs)
            if do_sample:
                NKS = PAST // 128 + 1
                for h in range(4):
                    S.dma(KT[:, h, 0:PAST + TS], scr_s["kt"][h][:, 0:PAST + TS], W=[KT], key=KT)
                vview = scr_s["v"].rearrange("(n p) c -> p n c", p=128)
                for n0 in range(0, NKS - 1, 8):
                    n1 = min(NKS - 1, n0 + 8)
                    S.dma(VV[:, n0:n1, :], vview[:, n0:n1, :], W=[VV], key=VV)
                S.dma(VV[0:TS, NKS - 1, :], scr_s["v"][PAST:PAST + TS, :], W=[VV], key=VV)
                phase2(l, "s", TS, TS, NKS, TS, xsrc_s, o_ys, scr_s, KT, VV, bufs)
            S.barrier()
    S.final_wait()
    ctx.close()
    return nc, S.ninst


_CACHE = {}


def kernel(**inputs):
    x_prompt = np.asarray(inputs["x_prompt"], np.float32)
    x_sample = np.asarray(inputs["x_sample"], np.float32)
    B, T, _ = x_prompt.shape
    NL = inputs["norm_g"].shape[0]
    key = (T, NL)
    if key not in _CACHE:
        _CACHE[key] = build(T, NL)[0]
    nc = _CACHE[key]
    cf, cb, invc = make_consts()
    rope_p = rope_table(np.arange(T))
    rope_s = rope_table(PAST + np.arange(TS))
    in_maps = []
    for c in range(8):
        m = {
            "x_p": np.ascontiguousarray(x_prompt[c // 2]),
            "x_s": np.ascontiguousarray(x_sample[c]),
            "c_ckv": np.ascontiguousarray(np.asarray(inputs["cache_ckv"], np.float32)[:, c]),
            "c_kr": np.ascontiguousarray(np.asarray(inputs["cache_krope"], np.float32)[:, c]),
            "st_wkv": np.ascontiguousarray(np.asarray(inputs["state_wkv"], np.float32)[:, c]),
            "st_shift": np.ascontiguousarray(np.asarray(inputs["state_shift"], np.float32)[:, c]),
            "st_pool": np.ascontiguousarray(np.asarray(inputs["state_pool"], np.float32)[:, c]),
            "cf": cf, "cb": cb, "invc": invc, "rope_p": rope_p, "rope_s": rope_s,
        }
        for n in WNAMES:
            m[n] = np.ascontiguousarray(np.asarray(inputs[n], np.float32).reshape([NL] + WSHAPES[n]))
        in_maps.append(m)
    res = run_bass_kernel_spmd(nc, in_maps, core_ids=list(range(8)))
    R = res.results
    pc = [R[2 * b] for b in range(B)]
    sc = [R[c] for c in range(8)]
    stk = lambda L, n, ax: np.stack([np.asarray(r[n], np.float32) for r in L], axis=ax)
    out = (
        stk(pc, "o_yp", 0), stk(sc, "o_ys", 0),
        stk(pc, "o_ckvp", 1), stk(pc, "o_krp", 1), stk(pc, "o_wkvp", 1), stk(pc, "o_shp", 1), stk(pc, "o_plp", 1),
        stk(sc, "o_ckvs", 1), stk(sc, "o_krs", 1), stk(sc, "o_wkvs", 1), stk(sc, "o_shs", 1), stk(sc, "o_pls", 1),
        stk(sc, "o_sgv", 1),
    )
    return out
```

```python
import numpy as np
from contextlib import ExitStack
import concourse.bass as bass
import concourse.mybir as mybir
from concourse.bass_utils import run_bass_kernel_spmd

F32 = mybir.dt.float32
BF16 = mybir.dt.bfloat16
AF = mybir.ActivationFunctionType
ALU = mybir.AluOpType
AX = mybir.AxisListType

D = 1024
DIN = 3040
OFF_A, OFF_B, OFF_C, OFF_D, OFF_G = 0, 896, 1248, 1760, 2016
PAST = 2048
TS = 16
WINS = (2, 4, 8, 16)
CDEC = -0.6065306597126334
ILV = 1
ILV2 = 2


class Buf:
    def __init__(self, name):
        self.name = name
        self.w = None
        self.r = []


class TB:
    def __init__(self, t, name):
        self.t = t
        self.b = Buf(name)

    def __getitem__(self, k):
        return self.t[k]


class Eng:
    def __init__(self, S, name, eng):
        self.name = name
        self.eng = eng
        self.sem = S.newsem("e_" + name)
        self.cnt = 0
        self.waited = {}

    def wait_tok(self, tok):
        if tok is None:
            return
        sem, val = tok
        if sem is self.sem and self.name == "pe":
            return
        key = id(sem)
        if self.waited.get(key, 0) >= val:
            return
        self.eng.wait_ge(sem, val)
        self.waited[key] = val


class Sched:
    def __init__(self, nc, ctx):
        self.nc = nc
        self.ctx = ctx
        self.pe = Eng(self, "pe", nc.tensor)
        self.act = Eng(self, "act", nc.scalar)
        self.dve = Eng(self, "dve", nc.vector)
        self.pool = Eng(self, "pool", nc.gpsimd)
        self.sp = Eng(self, "sp", nc.sync)
        self.engs = [self.pe, self.act, self.dve, self.pool, self.sp]
        self.dma_sems = {}
        self.ninst = 0

    def newsem(self, name):
        return self.ctx.enter_context(self.nc.semaphore(name))

    def _bufs(self, L):
        return [x.b if isinstance(x, TB) else x for x in L]

    def deps(self, E, R, W):
        for b in R:
            E.wait_tok(b.w)
        for b in W:
            E.wait_tok(b.w)
            for t in (b.r.values() if isinstance(b.r, dict) else b.r):
                E.wait_tok(t)

    def _mark(self, tok, R, W):
        for b in W:
            b.w = tok
            b.r = {}
        for b in R:
            if isinstance(b.r, list):
                b.r = {}
            k = id(tok[0])
            if k not in b.r or b.r[k][1] < tok[1]:
                b.r[k] = tok

    enabled = True

    def ck(self, name):
        import os
        if os.environ.get("KSTOP") == name:
            self.enabled = False

    def op(self, E, fn, *args, R=(), W=(), **kw):
        if not self.enabled:
            return None
        R = self._bufs(R)
        W = self._bufs(W)
        self.deps(E, R, W)
        ins = getattr(E.eng, fn)(*args, **kw)
        E.cnt += 1
        ins.then_inc(E.sem, 1)
        self.ninst += 1
        self._mark((E.sem, E.cnt), R, W)
        return ins

    def dma(self, out, in_, R=(), W=(), key=None, Q=None, **kw):
        if not self.enabled:
            return None
        Q = Q or self.sp
        R = self._bufs(R)
        W = self._bufs(W)
        kb = key.b if isinstance(key, TB) else key
        self.deps(Q, R, W)
        if kb.name not in self.dma_sems:
            self.dma_sems[kb.name] = [self.newsem("d_" + kb.name), 0]
        ent = self.dma_sems[kb.name]
        ins = Q.eng.dma_start(out=out, in_=in_, **kw)
        ent[1] += 16
        ins.then_inc(ent[0], 16)
        self.ninst += 1
        self._mark((ent[0], ent[1]), R, W)
        return ins

    def all_tokens(self):
        toks = [(e.sem, e.cnt) for e in self.engs if e.cnt > 0]
        toks += [(s, v) for (s, v) in self.dma_sems.values()]
        return toks

    def barrier(self):
        toks = self.all_tokens()
        for e in self.engs:
            for t in toks:
                e.wait_tok(t)

    def final_wait(self):
        for (s, v) in self.dma_sems.values():
            self.sp.wait_tok((s, v))


def make_consts():
    c = {}
    i = np.arange(128)
    s = i[:, None]
    t = i[None, :]
    ident = (s == t).astype(np.float32)
    tri_ui = (s <= t).astype(np.float32)
    tri_ut = (s < t).astype(np.float32)
    tri_lt = (s > t).astype(np.float32)
    bands = []
    bandp = []
    for w in WINS:
        bands.append(((s <= t) & (s > t - w)).astype(np.float32))
        bandp.append(((s - 128) > (t - w)).astype(np.float32))
    ones = np.ones((128, 128), np.float32)
    onehot_last = np.zeros((128, 2), np.float32)
    onehot_last[127, 0] = 1.0
    onehot_last[15, 1] = 1.0
    cf = np.concatenate([ident, tri_ui, ones] + bands + bandp + [onehot_last], axis=1)
    sh = (t == s + 1).astype(np.float32)
    elast = np.zeros((128, 128), np.float32)
    elast[127, 0] = 1.0
    cb = np.concatenate([ident, tri_lt, tri_ut, tri_lt, tri_ui, sh, elast], axis=1)
    invc = np.zeros((128, 24), np.float32)
    invc[:, 8:11] = np.array([1 / 192.0, 1 / 128.0, 1 / 32.0], np.float32)
    invc[:, 11:15] = 1 / 64.0
    invc[:, 15:19] = 1 / 32.0
    for g, w in enumerate(WINS):
        invc[:, g] = 1.0 / np.minimum(i + 1, w)
        invc[:, 4 + g] = 1.0 / w
    return cf.astype(np.float32), cb.astype(np.float32), invc


def rope_table(pos):
    half = 16
    inv = (10000.0 ** (-np.arange(half, dtype=np.float32) / half)).astype(np.float32)
    ang = pos.astype(np.float32)[:, None] * inv[None, :]
    cos = np.cos(ang).astype(np.float32)
    sin = np.sin(ang).astype(np.float32)
    return np.concatenate([np.tile(cos, (1, 4)), np.tile(sin, (1, 4))], axis=1).astype(np.float32)


CF_ID, CF_TRI, CF_ONES, CF_BAND, CF_BANDP, CF_OH = 0, 128, 256, 384, 896, 1408
CB_ID, CB_M3, CB_UI, CB_SH, CB_EL = 0, 128, 512, 640, 768

WNAMES = ["norm_g", "w_in", "w_out", "rw_mu", "rw_w0", "rw_w2", "rw_a0", "rw_a2", "rw_kk", "rw_ka", "rw_rk",
          "rw_gn_g", "rw_gn_b", "mla_qa_g", "mla_w_uq", "mla_kva_g", "mla_w_uk", "mla_w_uv",
          "mla_q_norm_g", "mla_k_norm_g", "sgu_w", "sgu_b", "sgu_ln_g", "sgu_ln_b", "pool_w", "pool_scale"]
WSHAPES = {
    "norm_g": [D], "w_in": [D, DIN], "w_out": [D, D], "rw_mu": [896], "rw_w0": [256], "rw_w2": [64, 256],
    "rw_a0": [256], "rw_a2": [64, 256], "rw_kk": [256], "rw_ka": [256], "rw_rk": [256], "rw_gn_g": [256],
    "rw_gn_b": [256], "mla_qa_g": [192], "mla_w_uq": [192, 384], "mla_kva_g": [128], "mla_w_uk": [128, 256],
    "mla_w_uv": [128, 256], "mla_q_norm_g": [96], "mla_k_norm_g": [96], "sgu_w": [4, 128, 128], "sgu_b": [4, 128],
    "sgu_ln_g": [256], "sgu_ln_b": [256], "pool_w": [4, 64, 64], "pool_scale": [256],
}


def build(T, NL, do_sample=True):
    nc = bass.Bass("TRN2", target_bir_lowering=False)
    ctx = ExitStack()
    NT = T // 128

    def din(name, shape, dt=F32):
        return nc.dram_tensor(name, list(shape), dt, kind="ExternalInput").ap()

    def dout(name, shape, dt=F32):
        return nc.dram_tensor(name, list(shape), dt, kind="ExternalOutput").ap()

    def dscr(name, shape, dt):
        return nc.dram_tensor(name, list(shape), dt, kind="Internal").ap()

    x_p = din("x_p", [T, D])
    x_s = din("x_s", [TS, D])
    c_ckv = din("c_ckv", [NL, PAST, 128])
    c_kr = din("c_kr", [NL, PAST, 32])
    st_wkv = din("st_wkv", [NL, 4, 64, 64])
    st_shift = din("st_shift", [NL, 896])
    st_pool = din("st_pool", [NL, 15, 256])
    Wd = {n: din(n, [NL] + WSHAPES[n]) for n in WNAMES}
    cf_d = din("cf", [128, 1410])
    cb_d = din("cb", [128, 896])
    invc_d = din("invc", [128, 24])
    rope_p = din("rope_p", [T, 128])
    rope_s = din("rope_s", [TS, 128])

    o_yp = dout("o_yp", [T, D])
    o_ys = dout("o_ys", [TS, D])
    o_ckvp = dout("o_ckvp", [NL, T, 128])
    o_krp = dout("o_krp", [NL, T, 32])
    o_wkvp = dout("o_wkvp", [NL, 4, 64, 64])
    o_shp = dout("o_shp", [NL, 896])
    o_plp = dout("o_plp", [NL, 15, 256])
    o_ckvs = dout("o_ckvs", [NL, TS, 128])
    o_krs = dout("o_krs", [NL, TS, 32])
    o_wkvs = dout("o_wkvs", [NL, 4, 64, 64])
    o_shs = dout("o_shs", [NL, 896])
    o_pls = dout("o_pls", [NL, 15, 256])
    o_sgv = dout("o_sgv", [NL, TS, 256])

    def scr(tag, TT, TK):
        return dict(
            yg=dscr("yg_" + tag, [TT, 768], BF16), gb=dscr("gb_" + tag, [TT, 256], BF16),
            qt=dscr("qt_" + tag, [4, 96, TT], BF16), kt=dscr("kt_" + tag, [4, 96, TK], BF16),
            v=dscr("v_" + tag, [TK, 260], BF16))
    scr_p = scr("p", T, T)
    scr_s = scr("s", TS, PAST + 128)

    S = Sched(nc, ctx)
    V = lambda fn, *a, **k: S.op(S.dve, fn, *a, **k)
    A = lambda fn, *a, **k: S.op(S.act, fn, *a, **k)
    G = lambda fn, *a, **k: S.op(S.pool, fn, *a, **k)
    M = lambda fn, *a, **k: S.op(S.pe, fn, *a, **k)

    uid = [0]

    def uname(name):
        uid[0] += 1
        return "s%d_%s" % (uid[0], name)

    def sb(name, shape, dt, c=None):
        t = (c or ctx).enter_context(nc.sbuf_tensor(uname(name), list(shape), dt))
        return TB(t, name)

    def ps(name, shape, dt):
        t = ctx.enter_context(nc.psum_tensor(name, list(shape), dt))
        return TB(t, name)

    PF = [ps("pf%d" % i, [128, 512], F32) for i in range(6)]
    PB = [ps("pb%d" % i, [128, 1024], BF16) for i in range(2)]

    cf = sb("cf", [128, 1410], F32)
    cb = sb("cb", [128, 896], BF16)
    cb_stage = sb("cb_stage", [128, 896], F32)
    invc = sb("invc", [128, 24], F32)
    S.dma(cf[:], cf_d[:, :], W=[cf], key=cf)
    S.dma(cb_stage[:], cb_d[:, :], W=[cb_stage], key=cb_stage)
    S.dma(invc[:], invc_d[:, :], W=[invc], key=invc)
    V("tensor_copy", cb[:], cb_stage[:], R=[cb_stage], W=[cb])
    cbr = sb("cbr", [128, 4, 4, 128], BF16)
    for kind, off in enumerate((CB_M3, CB_M3 + 128, CB_UI, CB_ID)):
        for h in range(4):
            V("tensor_copy", cbr[:, kind, h, :], cb[:, off:off + 128], R=[cb], W=[cbr])
    identf = lambda n: cf[0:n, CF_ID:CF_ID + n]
    identb = lambda n: cb[0:n, CB_ID:CB_ID + n]

    wout_b = sb("wout_b", [128, 8, D], BF16)
    wuq_b = sb("wuq_b", [128, 2, 384], BF16)
    wukv_b = sb("wukv_b", [128, 512], BF16)
    w2a2_b = sb("w2a2_b", [128, 512], BF16)
    wsT_b = sb("wsT_b", [128, 4, 128], BF16)
    poolw_b = sb("poolw_b", [128, 2, 128], BF16)
    sgub = sb("sgub", [128, 4], F32)
    ng = sb("ng", [128, 8], F32)
    qag = sb("qag", [128, 2], F32)
    BC_SPEC = [("rw_mu", 896), ("omm", 896), ("rw_w0", 256), ("rw_a0", 256), ("rw_kk", 256), ("rw_ka", 256),
               ("rw_rk", 256), ("rw_gn_g", 256), ("rw_gn_b", 256), ("mla_kva_g", 128), ("mla_q_norm_g", 96),
               ("mla_k_norm_g", 96), ("sgu_ln_g", 256), ("sgu_ln_b", 256)]
    bc_off = {}
    o = 0
    for n, w in BC_SPEC:
        bc_off[n] = (o, w)
        o += w
    bc = sb("bc", [128, o], F32)
    BCv = lambda n, P, a=0, b=None: bc[0:P, bc_off[n][0] + a: bc_off[n][0] + (bc_off[n][1] if b is None else b)]
    stage = sb("stage", [128, 1024], F32)
    stage2 = sb("stage2", [128, 1024], F32)

    WORK = []

    def wsb(name, shape, dt):
        tb = TB(None, name)
        WORK.append((tb, name, list(shape), dt))
        return tb

    def alloc_work(c):
        for tb, name, shape, dt in WORK:
            tb.t = c.enter_context(nc.sbuf_tensor(uname(name), shape, dt))
            tb.b = Buf(name)
        for i in range(4):
            STb[i].w = None
            STb[i].r = {}
        V("memset", vaug[:], 1.0, W=[vaug])
        V("memset", x_f[:], 0.0, W=[x_f])
        V("memset", p_f[:], 0.0, W=[p_f])
        V("memset", qaT[:], 0.0, W=[qaT])

    xt = [wsb("xt%d" % i, [128, D], F32) for i in range(2)]
    ropet = [wsb("ropet%d" % i, [128, 128], F32) for i in range(2)]
    junk = wsb("junk", [128, D], BF16)
    xn_b = wsb("xn_b", [128, D], BF16)
    xnT = wsb("xnT", [128, 8, 128], BF16)
    st1 = wsb("st1", [128, 16], F32)
    st2 = wsb("st2", [128, 16], F32)
    st3 = wsb("st3", [128, 16], F32)
    cstage = [wsb("cstage%d" % i, [128, 160], F32) for i in range(2)]
    st2o = wsb("st2o", [128, 16], F32)
    st3o = wsb("st3o", [128, 16], F32)
    w1o = wsb("w1o", [128, 256], F32)
    w2o = wsb("w2o", [128, 64], F32)
    sg = wsb("sg", [128, D], BF16)
    tmpm = [wsb("tmpm%d" % i, [128, 896], BF16) for i in range(2)]
    zs = wsb("zs", [128, 896], F32)
    lin = wsb("lin", [128, 128], BF16)
    linT = wsb("linT", [128, 128], BF16)
    w1 = wsb("w1", [128, 256], F32)
    w2 = wsb("w2", [128, 256], F32)
    w3 = wsb("w3", [128, 256], F32)
    sw = wsb("sw", [128, 256], F32)
    sa = wsb("sa", [128, 256], F32)
    kk = wsb("kk", [128, 256], F32)
    kp = wsb("kp", [128, 256], F32)
    bb = wsb("bb", [128, 256], F32)
    bcf = wsb("bcf", [128, 4], F32)
    cs_sb = wsb("cs_sb", [128, 256], F32)
    e1 = wsb("e1", [128, 256], F32)
    e2 = wsb("e2", [128, 256], F32)
    e3 = wsb("e3", [128, 256], F32)
    e4 = wsb("e4", [128, 256], F32)
    hat = wsb("hat", [128, 4, 256], BF16)
    bp_b = wsb("bp_b", [128, 256], BF16)
    kp4 = wsb("kp4", [128, 256], F32)
    hT = wsb("hT", [128, 4, 2, 128], BF16)
    wc = wsb("wc", [128, 2], F32)
    scL = wsb("scL", [128, 4, 128], BF16)
    scN = wsb("scN", [128, 4, 128], BF16)
    scK = wsb("scK", [128, 4, 128], BF16)
    mbr_b = wsb("mbr_b", [128, 4, 128], BF16)
    mkr_f = wsb("mkr_f", [128, 4, 128], F32)
    Ab = [wsb("Ab%d" % i, [128, 2, 4, 128], BF16) for i in range(2)]
    Ttb = [wsb("Ttb%d" % i, [128, 4, 128], BF16) for i in range(2)]
    G_b = wsb("G_b", [128, 4, 128], BF16)
    H_b = wsb("H_b", [128, 4, 64], BF16)
    x_f = wsb("x_f", [128, 4, 128], F32)
    z_f = wsb("z_f", [128, 4, 128], F32)
    q_f = wsb("q_f", [128, 2, 128], F32)
    p_f = wsb("p_f", [128, 2, 128], F32)
    ST = wsb("ST", [128, 2, 64], F32)
    STb = [Buf("ST%d" % h) for h in range(4)]
    yrw = wsb("yrw", [128, 256], F32)
    ygp = wsb("ygp", [128, 768], BF16)
    zb_f = wsb("zb_f", [128, 384], F32)
    qa_b = wsb("qa_b", [128, 192], BF16)
    qaT = wsb("qaT", [128, 2, 128], BF16)
    qn = wsb("qn", [128, 4, 96], F32)
    qfb = wsb("qfb", [128, 4, 96], BF16)
    qT = wsb("qT", [96, 4, 128], BF16)
    ckv_f = wsb("ckv_f", [128, 128], F32)
    ckv_b = wsb("ckv_b", [128, 128], BF16)
    ckvT = wsb("ckvT", [128, 128], BF16)
    kr_f = wsb("kr_f", [128, 32], F32)
    kr_r = wsb("kr_r", [128, 32], F32)
    kfull = wsb("kfull", [128, 4, 96], BF16)
    kT = wsb("kT", [96, 4, 128], BF16)
    vaug = wsb("vaug", [128, 4, 65], BF16)
    ug = wsb("ug", [128, 256], F32)
    vn = wsb("vn", [128, 256], F32)
    vn_b = wsb("vn_b", [128, 256], BF16)
    bnst = wsb("bnst", [128, 8], F32)
    zd_f = [wsb("zd_f%d" % i, [128, 256], F32) for i in range(2)]
    d_b = wsb("d_b", [128, 256], BF16)
    dT = wsb("dT", [128, 2, 128], BF16)
    za_f = wsb("za_f", [128, 896], F32)
    wkv_o = wsb("wkv_o", [64, 4, 64], F32)


    def load_small(l):
        P = 128
        for n, w in BC_SPEC:
            if n == "omm":
                continue
            S.dma(BCv(n, P), Wd[n][l].partition_broadcast(128), W=[bc], key=bc)
        V("tensor_scalar", BCv("omm", P), BCv("rw_mu", P), -1.0, 1.0, ALU.mult, ALU.add, R=[bc], W=[bc])
        S.dma(ng[:], Wd["norm_g"][l].rearrange("(k p) -> p k", p=128), W=[ng], key=ng, allow_slow_non_contiguous=True)
        S.dma(qag[:, 0:1], Wd["mla_qa_g"][l][0:128].rearrange("(p o) -> p o", o=1), W=[qag], key=qag)
        S.dma(qag[0:64, 1:2], Wd["mla_qa_g"][l][128:192].rearrange("(p o) -> p o", o=1), W=[qag], key=qag)
        S.dma(sgub[:], Wd["sgu_b"][l].rearrange("h i -> i h"), W=[sgub], key=sgub, allow_slow_non_contiguous=True)
        for k in range(8):
            S.dma(stage[:, 0:D], Wd["w_out"][l][k * 128:(k + 1) * 128, :], W=[stage], key=stage)
            V("tensor_copy", wout_b[:, k, :], stage[:, 0:D], R=[stage], W=[wout_b])
        S.dma(stage[:, 0:384], Wd["mla_w_uq"][l][0:128, :], W=[stage], key=stage)
        V("tensor_scalar", wuq_b[:, 0, :], stage[:, 0:384], qag[:, 0:1], None, ALU.mult, R=[stage, qag], W=[wuq_b])
        S.dma(stage[0:64, 0:384], Wd["mla_w_uq"][l][128:192, :], W=[stage], key=stage)
        V("memset", wuq_b[:, 1, :], 0.0, W=[wuq_b])
        V("tensor_scalar", wuq_b[0:64, 1, :], stage[0:64, 0:384], qag[0:64, 1:2], None, ALU.mult, R=[stage, qag], W=[wuq_b])
        S.dma(stage[:, 0:256], Wd["mla_w_uk"][l], W=[stage], key=stage)
        S.dma(stage[:, 256:512], Wd["mla_w_uv"][l], W=[stage], key=stage)
        V("tensor_copy", wukv_b[:], stage[:, 0:512], R=[stage], W=[wukv_b])
        V("memset", stage[:, 0:512], 0.0, W=[stage])
        S.dma(stage[0:64, 0:256], Wd["rw_w2"][l], W=[stage], key=stage)
        S.dma(stage[64:128, 256:512], Wd["rw_a2"][l], W=[stage], key=stage)
        V("tensor_copy", w2a2_b[:], stage[:, 0:512], R=[stage], W=[w2a2_b])
        S.dma(stage[:, 0:512].rearrange("p (h j) -> p h j", h=4), Wd["sgu_w"][l].rearrange("h i j -> i h j"), W=[stage], key=stage)
        for h in range(4):
            V("tensor_tensor", stage2[:, h * 128:(h + 1) * 128], stage[:, h * 128:(h + 1) * 128], cb[:, CB_M3 + 128:CB_M3 + 256],
              ALU.mult, R=[stage, cb], W=[stage2])
            V("tensor_sub", stage2[:, h * 128:(h + 1) * 128], stage[:, h * 128:(h + 1) * 128], stage2[:, h * 128:(h + 1) * 128],
              R=[stage, stage2], W=[stage2])
            M("transpose", PF[5][:, h * 128:(h + 1) * 128], stage2[:, h * 128:(h + 1) * 128], identf(128), R=[stage2, cf], W=[PF[5]])
        V("tensor_copy", wsT_b[:].rearrange("p h i -> p (h i)"), PF[5][:, 0:512], R=[PF[5]], W=[wsT_b])
        V("memset", stage[:, 0:256], 0.0, W=[stage])
        for g in range(4):
            c = g // 2
            r0 = 64 * (g % 2)
            S.dma(stage[r0:r0 + 64, c * 128 + r0: c * 128 + r0 + 64], Wd["pool_w"][l][g], W=[stage], key=stage)
        S.dma(stage2[:, 0:256], Wd["pool_scale"][l].partition_broadcast(128), W=[stage2], key=stage2)
        V("tensor_tensor", poolw_b[:].rearrange("p c d -> p (c d)"), stage[:, 0:256], stage2[:, 0:256], ALU.mult,
          R=[stage, stage2], W=[poolw_b])

    def load_win(l, win_b):
        for k in range(8):
            yield
            for c0 in range(0, DIN, 1024):
                n = min(1024, DIN - c0)
                st = stage if ((k * 3 + c0 // 1024) % 2 == 0) else stage2
                S.dma(st[:, 0:n], Wd["w_in"][l][k * 128:(k + 1) * 128, c0:c0 + n], W=[st], key=st)
                V("tensor_scalar", win_b[:, k, c0:c0 + n], st[:, 0:n], ng[:, k:k + 1], None, ALU.mult, R=[st, ng], W=[win_b])

    def rstd_from_ss(ss_ap, out_ap, P, n, eps, Rb, Wb, tmp_ap):
        V("tensor_scalar", tmp_ap, ss_ap, 1.0 / n, eps, ALU.mult, ALU.add, R=Rb, W=Wb)
        A("activation", out=tmp_ap, in_=tmp_ap, func=AF.Sqrt, R=Wb, W=Wb)
        V("reciprocal", out_ap, tmp_ap, R=Wb, W=Wb)

    def transposes_b(src, P, widths, pb, dst_ap_fn, Rb, Wdst, evac="act"):
        for i, (ap, w) in enumerate(zip(src, widths)):
            M("transpose", pb[0:w, i * P:(i + 1) * P], ap, identb(P), R=Rb + [cb], W=[pb])

    def rope_apply(dst_a, dst_b, x1, x2, cosv, sinv, t1, t2, Rb, Wb, Tb):
        V("tensor_tensor", t1, x1, cosv, ALU.mult, R=Rb, W=Tb)
        V("tensor_tensor", t2, x2, sinv, ALU.mult, R=Rb, W=Tb)
        V("tensor_tensor", dst_a, t1, t2, ALU.subtract, R=Tb, W=Wb)
        V("tensor_tensor", t1, x1, sinv, ALU.mult, R=Rb + Wb, W=Tb)
        V("tensor_tensor", t2, x2, cosv, ALU.mult, R=Rb, W=Tb)
        V("tensor_tensor", dst_b, t1, t2, ALU.add, R=Tb, W=Wb)

    KEY_KT_SW = Buf("kT_sw")
    KEY_V_SW = Buf("vaug_sw")

    def kv_from_ckv(P, ckvf_ap, krf_ap, Rb, sc, tok0, Q=None):
        V("tensor_copy", ckv_b[0:P, :], ckvf_ap, R=Rb, W=[ckv_b])
        M("transpose", PB[1][:, 0:P], ckv_b[0:P, :], identb(P), R=[ckv_b, cb], W=[PB[1]])
        A("copy", ckvT[:, 0:P], PB[1][:, 0:P], R=[PB[1]], W=[ckvT])
        M("matmul", PF[1][0:P, :], ckvT[:, 0:P], wukv_b[:], start=True, stop=True, R=[ckvT, wukv_b], W=[PF[1]])
        A("activation", out=w1o[0:P, :], in_=PF[1][0:P, 0:256], func=AF.Square, R=[PF[1]], W=[w1o])
        V("tensor_reduce", st2o[0:P, 0:4], w1o[0:P, :].rearrange("p (h d) -> p h d", h=4), AX.X, ALU.add, R=[w1o], W=[st2o])
        rstd_from_ss(st2o[0:P, 0:4], st2o[0:P, 4:8], P, 64.0, 1e-6, [st2o], [st2o], st2o[0:P, 8:12])
        for h in range(4):
            V("scalar_tensor_tensor", kfull[0:P, h, 0:64], PF[1][0:P, 64 * h:64 * h + 64], st2o[0:P, 4 + h:5 + h],
              BCv("mla_k_norm_g", P, 0, 64), ALU.mult, ALU.mult, R=[PF[1], st2o, bc], W=[kfull])
            V("tensor_copy", kfull[0:P, h, 64:96], krf_ap, R=Rb, W=[kfull])
        V("tensor_copy", vaug[0:P, :, 0:64], PF[1][0:P, 256:512].rearrange("p (h d) -> p h d", h=4), R=[PF[1]], W=[vaug])
        for h in range(4):
            M("transpose", PB[1][0:96, h * P:(h + 1) * P], kfull[0:P, h, :], identb(P), R=[kfull, cb], W=[PB[1]])
        A("copy", kT[:, :, 0:P], PB[1][0:96, 0:4 * P].rearrange("p (h t) -> p h t", h=4), R=[PB[1]], W=[kT])
        S.dma(sc["kt"][:, :, tok0:tok0 + P].rearrange("h d t -> d h t"), kT[:, :, 0:P], R=[kT], key=(kT if Q is None else KEY_KT_SW), Q=Q)
        S.dma(sc["v"][tok0:tok0 + P, :], vaug[0:P, :, :].rearrange("p h d -> p (h d)"), R=[vaug], key=(vaug if Q is None else KEY_V_SW), Q=Q)

    def phase1_tile(l, grp, ti, P, xsrc, win_b, sc, first, last, tok0, nlev, st):
        xb_ = xt[ti % 2]
        rp_ = ropet[ti % 2]
        cosv = rp_[0:P, 0:64].rearrange("p (h d) -> p h d", h=4)
        sinv = rp_[0:P, 64:128].rearrange("p (h d) -> p h d", h=4)
        A("activation", out=junk[0:P, :], in_=xb_[0:P, :], func=AF.Square, accum_out=st1[0:P, 0:1], R=[xb_], W=[junk, st1])
        rstd_from_ss(st1[0:P, 0:1], st1[0:P, 1:2], P, float(D), 1e-6, [st1], [st1], st1[0:P, 2:3])
        V("tensor_scalar", xn_b[0:P, :], xb_[0:P, :], st1[0:P, 1:2], None, ALU.mult, R=[xb_, st1], W=[xn_b])
        for k in range(8):
            M("transpose", PB[0][:, k * P:(k + 1) * P], xn_b[0:P, k * 128:(k + 1) * 128], identb(P), R=[xn_b, cb], W=[PB[0]])
        A("copy", xnT[:, :, 0:P], PB[0][:, 0:8 * P].rearrange("p (k t) -> p k t", k=8), R=[PB[0]], W=[xnT])

        def proj(bank, c0, n):
            for k in range(8):
                M("matmul", bank[0:P, 0:n], xnT[:, k, 0:P], win_b[:, k, c0:c0 + n], start=(k == 0), stop=(k == 7),
                  R=[xnT, win_b], W=[bank])

        proj(PF[0], OFF_G, 512)
        proj(PF[1], OFF_G + 512, 512)
        A("activation", out=sg[0:P, 0:512], in_=PF[0][0:P, :], func=AF.Silu, R=[PF[0]], W=[sg])
        A("activation", out=sg[0:P, 512:1024], in_=PF[1][0:P, :], func=AF.Silu, R=[PF[1]], W=[sg])
        S.dma(sc["gb"][tok0:tok0 + P, :], sg[0:P, 256:512], R=[sg], key=sg)

        proj(PF[0], OFF_A, 512)
        proj(PF[1], OFF_A + 512, 384)
        tc_, tp_ = tmpm[ti % 2], tmpm[(ti + 1) % 2]
        V("tensor_tensor", tc_[0:P, 0:512], PF[0][0:P, 0:512], BCv("rw_mu", P, 0, 512), ALU.mult, R=[PF[0], bc], W=[tc_])
        V("tensor_tensor", tc_[0:P, 512:896], PF[1][0:P, 0:384], BCv("rw_mu", P, 512, 896), ALU.mult, R=[PF[1], bc], W=[tc_])
        if last:
            V("tensor_copy", za_f[0:P, 0:512], PF[0][0:P, 0:512], R=[PF[0]], W=[za_f])
            V("tensor_copy", za_f[0:P, 512:896], PF[1][0:P, 0:384], R=[PF[1]], W=[za_f])
            S.dma(st["o_shift"][l:l + 1, :], za_f[P - 1:P, :], R=[za_f], key=za_f)
        V("tensor_tensor", zs[0:P, 0:512], PF[0][0:P, 0:512], BCv("omm", P, 0, 512), ALU.mult, R=[PF[0], bc], W=[zs])
        V("tensor_tensor", zs[0:P, 512:896], PF[1][0:P, 0:384], BCv("omm", P, 512, 896), ALU.mult, R=[PF[1], bc], W=[zs])
        def gen_R():
            for (bank, c0, n) in ((PF[2], 0, 512), (PF[3], 512, 384)):
                M("matmul", bank[0:P, 0:n], cb[0:P, CB_SH:CB_SH + P], tc_[0:P, c0:c0 + n], start=True, stop=False, R=[cb, tc_], W=[bank])
                M("matmul", bank[0:P, 0:n], cb[:, CB_EL:CB_EL + P], tp_[:, c0:c0 + n], start=False, stop=True, R=[cb, tp_], W=[bank])
            V("tensor_add", zs[0:P, 0:512], zs[0:P, 0:512], PF[2][0:P, 0:512], R=[zs, PF[2]], W=[zs])
            V("tensor_add", zs[0:P, 512:896], zs[0:P, 512:896], PF[3][0:P, 0:384], R=[zs, PF[3]], W=[zs])
            r_ = zs[0:P, 0:256]
            k_ = zs[0:P, 256:512]
            v_ = zs[0:P, 512:768]
            A("activation", out=lin[0:P, 0:64], in_=zs[0:P, 768:832], func=AF.Tanh, R=[zs], W=[lin])
            A("copy", lin[0:P, 64:128], zs[0:P, 832:896], R=[zs], W=[lin])
            M("transpose", PB[0][:, 0:P], lin[0:P, :], identb(P), R=[lin, cb], W=[PB[0]])
            A("copy", linT[:, 0:P], PB[0][:, 0:P], R=[PB[0]], W=[linT])
            M("matmul", PF[2][0:P, :], linT[:, 0:P], w2a2_b[:], start=True, stop=True, R=[linT, w2a2_b], W=[PF[2]])
            V("tensor_add", w1[0:P, :], PF[2][0:P, 0:256], BCv("rw_w0", P), R=[PF[2], bc], W=[w1])
            A("activation", out=sw[0:P, :], in_=w1[0:P, :], func=AF.Sigmoid, R=[w1], W=[sw])
            V("tensor_add", w2[0:P, :], PF[2][0:P, 256:512], BCv("rw_a0", P), R=[PF[2], bc], W=[w2])
            A("activation", out=sa[0:P, :], in_=w2[0:P, :], func=AF.Sigmoid, R=[w2], W=[sa])
            V("tensor_tensor", kk[0:P, :], k_, BCv("rw_kk", P), ALU.mult, R=[zs, bc], W=[kk])
            V("tensor_tensor", w3[0:P, :], kk[0:P, :], kk[0:P, :], ALU.mult, R=[kk], W=[w3])
            V("tensor_reduce", st2[0:P, 0:4], w3[0:P, :].rearrange("p (h d) -> p h d", h=4), AX.X, ALU.add, R=[w3], W=[st2])
            V("tensor_scalar", st2[0:P, 4:8], st2[0:P, 0:4], 1e-24, None, ALU.max, R=[st2], W=[st2])
            A("activation", out=st2[0:P, 4:8], in_=st2[0:P, 4:8], func=AF.Sqrt, R=[st2], W=[st2])
            V("reciprocal", st2[0:P, 8:12], st2[0:P, 4:8], R=[st2], W=[st2])
            for h in range(4):
                V("tensor_scalar", kk[0:P, 64 * h:64 * h + 64], kk[0:P, 64 * h:64 * h + 64], st2[0:P, 8 + h:9 + h], None, ALU.mult,
                  R=[kk, st2], W=[kk])
            V("scalar_tensor_tensor", w1[0:P, :], sa[0:P, :], -1.0, BCv("rw_ka", P), ALU.add, ALU.mult, R=[sa, bc], W=[w1])
            V("scalar_tensor_tensor", kp[0:P, :], w1[0:P, :], 1.0, k_, ALU.add, ALU.mult, R=[w1, zs], W=[kp])
            V("tensor_tensor", bb[0:P, :], kk[0:P, :], sa[0:P, :], ALU.mult, R=[kk, sa], W=[bb])
            G("tensor_tensor", w2[0:P, :], r_, kp[0:P, :], ALU.mult, R=[zs, kp], W=[w2])
            G("tensor_tensor", w2[0:P, :], w2[0:P, :], BCv("rw_rk", P), ALU.mult, R=[w2, bc], W=[w2])
            V("tensor_reduce", bcf[0:P, 0:4], w2[0:P, :].rearrange("p (h d) -> p h d", h=4), AX.X, ALU.add, R=[w2], W=[bcf])
            M("matmul", PF[2][0:P, 0:256], cf[0:P, CF_TRI:CF_TRI + P], sw[0:P, :], start=True, stop=True, R=[cf, sw], W=[PF[2]])
            M("matmul", PF[2][0:P, 256:512], cf[0:P, CF_ONES:CF_ONES + P], sw[0:P, :], start=True, stop=True, R=[cf, sw], W=[PF[2]])
            V("tensor_copy", cs_sb[0:P, :], PF[2][0:P, 0:256], R=[PF[2]], W=[cs_sb])
            A("activation", out=e1[0:P, :], in_=PF[2][0:P, 0:256], func=AF.Exp, scale=CDEC, R=[PF[2]], W=[e1])
            A("activation", out=e2[0:P, :], in_=PF[2][0:P, 0:256], func=AF.Exp, scale=-CDEC, R=[PF[2]], W=[e2])
            V("tensor_sub", w1[0:P, :], cs_sb[0:P, :], sw[0:P, :], R=[cs_sb, sw], W=[w1])
            A("activation", out=e3[0:P, :], in_=w1[0:P, :], func=AF.Exp, scale=CDEC, R=[w1], W=[e3])
            V("tensor_sub", w3[0:P, :], PF[2][0:P, 256:512], cs_sb[0:P, :], R=[PF[2], cs_sb], W=[w3])
            A("activation", out=e4[0:P, :], in_=w3[0:P, :], func=AF.Exp, scale=CDEC, R=[w3], W=[e4])
            V("scalar_tensor_tensor", hat[0:P, 0, :], kk[0:P, :], -1.0, e3[0:P, :], ALU.mult, ALU.mult, R=[kk, e3], W=[hat])
            V("tensor_tensor", hat[0:P, 1, :], bb[0:P, :], e2[0:P, :], ALU.mult, R=[bb, e2], W=[hat])
            V("tensor_tensor", hat[0:P, 2, :], kp[0:P, :], e2[0:P, :], ALU.mult, R=[kp, e2], W=[hat])
            V("tensor_tensor", hat[0:P, 3, :], r_, e1[0:P, :], ALU.mult, R=[zs, e1], W=[hat])
            G("tensor_tensor", bp_b[0:P, :], bb[0:P, :], e4[0:P, :], ALU.mult, R=[bb, e4], W=[bp_b])
            G("tensor_tensor", kp4[0:P, :], kp[0:P, :], e4[0:P, :], ALU.mult, R=[kp, e4], W=[kp4])
            ohc = CF_OH + (0 if P == 128 else 1)
            for p in range(2):
                M("matmul", PF[3][:, p:p + 1], e1[0:P, p * 128:(p + 1) * 128], cf[0:P, ohc:ohc + 1], start=True, stop=True,
                  R=[e1, cf], W=[PF[3]])
            V("tensor_copy", wc[:, 0:2], PF[3][:, 0:2], R=[PF[3]], W=[wc])
            for vi in range(4):
                for p in range(2):
                    M("transpose", PB[0][:, (vi * 2 + p) * P:(vi * 2 + p + 1) * P], hat[0:P, vi, p * 128:(p + 1) * 128], identb(P),
                      R=[hat, cb], W=[PB[0]])
            A("copy", hT[:, :, :, 0:P], PB[0][:, 0:8 * P].rearrange("p (v q t) -> p v q t", v=4, q=2), R=[PB[0]], W=[hT])
            def fm(h, vi):
                return hT[64 * (h % 2):64 * (h % 2) + 64, vi, h // 2, 0:P]

            def pv(bank, w=128, n=None):
                n = P if n is None else n
                return bank[0:P, 0:4 * w].rearrange("p (h t) -> p h t", h=4)[:, :, 0:n]

            def msk(kind):
                return cbr[0:P, kind, :, 0:P]

            for h in range(4):
                M("matmul", PF[2][0:P, h * 128:h * 128 + P], fm(h, 0), fm(h, 1), start=True, stop=True, R=[hT], W=[PF[2]])
                M("matmul", PF[3][0:P, h * 128:h * 128 + P], fm(h, 1), fm(h, 0), start=True, stop=True, R=[hT], W=[PF[3]])
                M("matmul", PF[4][0:P, h * 128:h * 128 + P], fm(h, 0), fm(h, 2), start=True, stop=True, R=[hT], W=[PF[4]])
                M("matmul", PF[5][0:P, h * 128:h * 128 + P], fm(h, 1), fm(h, 3), start=True, stop=True, R=[hT], W=[PF[5]])
            V("tensor_tensor", scL[0:P, :, 0:P], pv(PF[2]), msk(0), ALU.mult, R=[PF[2], cbr], W=[scL])
            V("tensor_tensor", scN[0:P, :, 0:P], pv(PF[3]), msk(1), ALU.mult, R=[PF[3], cbr], W=[scN])
            for h in range(4):
                bk = PF[2 + (h % 2)]
                M("matmul", bk[0:P, h * 128:h * 128 + P], fm(h, 2), fm(h, 3), start=True, stop=True, R=[hT], W=[bk])
            V("tensor_add", Ttb[0][0:P, :, 0:P], scL[0:P, :, 0:P], msk(3), R=[scL, cbr], W=[Ttb[0]])
            V("tensor_tensor", scK[0:P, :, 0:P], pv(PF[4]), msk(0), ALU.mult, R=[PF[4], cbr], W=[scK])
            V("tensor_tensor", mbr_b[0:P, :, 0:P], pv(PF[5]), msk(2), ALU.mult, R=[PF[5], cbr], W=[mbr_b])
            for h in range(4):
                bk = PF[2 + (h % 2)]
                V("tensor_tensor", mkr_f[0:P, h, 0:P], bk[0:P, h * 128:h * 128 + P], cb[0:P, CB_UI:CB_UI + P], ALU.mult, R=[bk, cb], W=[mkr_f])
            def acc_(tb, idx):
                return (lambda h: tb[0:P, h, 0:P]) if idx is None else (lambda h: tb[0:P, idx, h, 0:P])

            def squares(Af, ATf, Rb, dst, need_A):
                if need_A:
                    for h in range(4):
                        M("matmul", PF[2][0:P, h * 128:h * 128 + P], ATf(h), Af(h), start=True, stop=True, R=Rb, W=[PF[2]])
                for h in range(4):
                    M("matmul", PF[3][0:P, h * 128:h * 128 + P], Af(h), ATf(h), start=True, stop=True, R=Rb, W=[PF[3]])

            def squares_evac(dst, need_A):
                if need_A:
                    V("tensor_copy", dst[0:P, 0, :, 0:P], pv(PF[2]), R=[PF[2]], W=[dst])
                A("copy", dst[0:P, 1, :, 0:P], pv(PF[3]), R=[PF[3]], W=[dst])

            tcur = 0
            squares(acc_(scL, None), acc_(scN, None), [scL, scN], Ab[1], nlev > 2)
            squares_evac(Ab[1], nlev > 2)
            for k in range(1, nlev):
                cur = Ab[k % 2]
                Af, ATf = acc_(cur, 0), acc_(cur, 1)
                lastk = (k == nlev - 1)
                pbank = PF[4 + (k % 2)]
                for h in range(4):
                    M("matmul", pbank[0:P, h * 128:h * 128 + P], ATf(h), Ttb[tcur][0:P, h, 0:P], start=True, stop=True,
                      R=[cur, Ttb[tcur]], W=[pbank])
                if not lastk:
                    nxt = Ab[(k + 1) % 2]
                    squares(Af, ATf, [cur], nxt, k + 1 < nlev - 1)
                yield "mm"
                V("tensor_add", Ttb[1 - tcur][0:P, :, 0:P], pv(pbank), Ttb[tcur][0:P, :, 0:P], R=[pbank, Ttb[tcur]], W=[Ttb[1 - tcur]])
                if not lastk:
                    squares_evac(nxt, k + 1 < nlev - 1)
                tcur = 1 - tcur
                yield "ev"
            Tt = Ttb[tcur]
            for h in range(4):
                M("matmul", PF[2][0:P, h * 128:h * 128 + P], Tt[0:P, h, 0:P], mbr_b[0:P, h, 0:P], start=True, stop=True, R=[Tt, mbr_b], W=[PF[2]])
            for h in range(4):
                M("matmul", PF[3][0:P, h * 64:h * 64 + 64], Tt[0:P, h, 0:P], bp_b[0:P, 64 * h:64 * h + 64], start=True, stop=True, R=[Tt, bp_b], W=[PF[3]])
            A("copy", G_b[0:P, :, 0:P], pv(PF[2]), R=[PF[2]], W=[G_b])
            V("tensor_copy", H_b[0:P, :, :], PF[3][0:P, 0:256].rearrange("p (h d) -> p h d", h=4), R=[PF[3]], W=[H_b])
            for h in range(4):
                M("matmul", PF[2][0:P, h * 128:h * 128 + P], scK[0:P, h, 0:P], G_b[0:P, h, 0:P], start=True, stop=True, R=[scK, G_b], W=[PF[2]])
            for h in range(4):
                M("matmul", PF[3][:, h * 128:h * 128 + P], hat[0:P, 0, (h // 2) * 128:(h // 2) * 128 + 128], G_b[0:P, h, 0:P], start=True, stop=True,
                  R=[hat, G_b], W=[PF[3]])
            for h in range(4):
                M("matmul", PF[4][0:P, h * 64:h * 64 + 64], scK[0:P, h, 0:P], H_b[0:P, h, :], start=True, stop=True, R=[scK, H_b], W=[PF[4]])
            for h in range(4):
                M("matmul", PF[4][:, 256 + h * 64:256 + h * 64 + 64], hat[0:P, 0, (h // 2) * 128:(h // 2) * 128 + 128], H_b[0:P, h, :],
                  start=True, stop=True, R=[hat, H_b], W=[PF[4]])
            V("tensor_add", z_f[0:P, :, 0:P], pv(PF[2]), mkr_f[0:P, :, 0:P], R=[PF[2], mkr_f], W=[z_f])
            for o_ in (0, 64):
                h0 = o_ // 64
                qv = PF[3][o_:o_ + 64, 0:512].rearrange("p (q r t) -> p q r t", q=2, r=2)[:, :, h0, 0:P]
                V("tensor_add", q_f[o_:o_ + 64, :, 0:P], qv, hT[o_:o_ + 64, 3, :, 0:P], R=[PF[3], hT], W=[q_f])
            for h in range(4):
                p = h // 2
                o_ = 64 * (h % 2)
                V("tensor_add", x_f[0:P, h, o_:o_ + 64], PF[4][0:P, h * 64:h * 64 + 64], kp4[0:P, 64 * h:64 * h + 64], R=[PF[4], kp4], W=[x_f])
                V("scalar_tensor_tensor", p_f[o_:o_ + 64, p, o_:o_ + 64], cf[o_:o_ + 64, CF_ID + o_:CF_ID + o_ + 64], wc[o_:o_ + 64, p:p + 1],
                  PF[4][o_:o_ + 64, 256 + h * 64:256 + h * 64 + 64], ALU.mult, ALU.add, R=[cf, wc, PF[4]], W=[p_f])
            for h in range(4):
                p = h // 2
                o_ = 64 * (h % 2)
                vh = zs[0:P, 512 + 64 * h:512 + 64 * h + 64]
                M("matmul", PF[5][0:P, 64 * h:64 * h + 64], q_f[o_:o_ + 64, p, 0:P], ST[o_:o_ + 64, p, :], start=True, stop=False, R=[q_f, ST], W=[PF[5]])
                M("matmul", PF[5][0:P, 64 * h:64 * h + 64], z_f[0:P, h, 0:P], vh, start=False, stop=True, R=[z_f, zs], W=[PF[5]])
            for p in range(2):
                M("matmul", PF[5][:, 256 + 64 * p:256 + 64 * p + 64], p_f[:, p, :], ST[:, p, :], start=True, stop=False, R=[p_f, ST], W=[PF[5]])
                for h in (2 * p, 2 * p + 1):
                    M("matmul", PF[5][:, 256 + 64 * p:256 + 64 * p + 64], x_f[0:P, h, :], zs[0:P, 512 + 64 * h:512 + 64 * h + 64],
                      start=False, stop=(h == 2 * p + 1), R=[x_f, zs], W=[PF[5]])
            V("tensor_copy", ST[:, :, :], PF[5][:, 256:384].rearrange("k (p v) -> k p v", p=2), R=[PF[5]], W=[ST])
            Y = PF[5]
            V("tensor_reduce", st3[0:P, 0:4], Y[0:P, 0:256].rearrange("p (h d) -> p h d", h=4), AX.X, ALU.add, R=[Y], W=[st3])
            V("tensor_scalar", st3[0:P, 0:4], st3[0:P, 0:4], 1.0 / 64, None, ALU.mult, R=[st3], W=[st3])
            for h in range(4):
                V("tensor_scalar", yrw[0:P, 64 * h:64 * h + 64], Y[0:P, 64 * h:64 * h + 64], st3[0:P, h:h + 1], None, ALU.subtract,
                  R=[Y, st3], W=[yrw])
            V("tensor_tensor", w1[0:P, :], yrw[0:P, :], yrw[0:P, :], ALU.mult, R=[yrw], W=[w1])
            V("tensor_reduce", st3[0:P, 4:8], w1[0:P, :].rearrange("p (h d) -> p h d", h=4), AX.X, ALU.add, R=[w1], W=[st3])
            rstd_from_ss(st3[0:P, 4:8], st3[0:P, 8:12], P, 64.0, 64e-5, [st3], [st3], st3[0:P, 12:16])
            for h in range(4):
                V("tensor_scalar", yrw[0:P, 64 * h:64 * h + 64], yrw[0:P, 64 * h:64 * h + 64], st3[0:P, 8 + h:9 + h], None, ALU.mult,
                  R=[yrw, st3], W=[yrw])
            V("tensor_tensor", yrw[0:P, :], yrw[0:P, :], BCv("rw_gn_g", P), ALU.mult, R=[yrw, bc], W=[yrw])
            V("tensor_add", yrw[0:P, :], yrw[0:P, :], BCv("rw_gn_b", P), R=[yrw, bc], W=[yrw])
            for h in range(4):
                V("scalar_tensor_tensor", yrw[0:P, 64 * h:64 * h + 64], zs[0:P, 512 + 64 * h:512 + 64 * h + 64], bcf[0:P, h:h + 1],
                  yrw[0:P, 64 * h:64 * h + 64], ALU.mult, ALU.add, R=[zs, bcf, yrw], W=[yrw])
            V("tensor_tensor", ygp[0:P, 0:256], yrw[0:P, :], sg[0:P, 0:256], ALU.mult, R=[yrw, sg], W=[ygp])
            if last:
                for p in range(2):
                    M("transpose", PF[2][0:64, p * 128:(p + 1) * 128], ST[:, p, :], identf(128), R=[ST, cf], W=[PF[2]])
                V("tensor_copy", wkv_o[:].rearrange("i h j -> i (h j)"), PF[2][0:64, 0:256], R=[PF[2]], W=[wkv_o])
                S.dma(st["o_wkv"][l].rearrange("h i j -> i h j"), wkv_o[:], R=[wkv_o], key=wkv_o)
            yield

        def gen_O():
            proj(PF[0], OFF_B, 352)
            ZB = PF[0]
            A("activation", out=junk[0:P, 0:192], in_=ZB[0:P, 0:192], func=AF.Square, accum_out=st1[0:P, 4:5], R=[ZB], W=[junk, st1])
            yield
            A("activation", out=junk[0:P, 0:128], in_=ZB[0:P, 192:320], func=AF.Square, accum_out=st1[0:P, 5:6], R=[ZB], W=[junk, st1])
            A("activation", out=junk[0:P, 0:32], in_=ZB[0:P, 320:352], func=AF.Square, accum_out=st1[0:P, 6:7], R=[ZB], W=[junk, st1])
            V("tensor_tensor", st1[0:P, 12:15], st1[0:P, 4:7], invc[0:P, 8:11], ALU.mult, R=[st1, invc], W=[st1])
            V("tensor_scalar", st1[0:P, 12:15], st1[0:P, 12:15], 1e-6, None, ALU.add, R=[st1], W=[st1])
            A("activation", out=st1[0:P, 12:15], in_=st1[0:P, 12:15], func=AF.Sqrt, R=[st1], W=[st1])
            V("reciprocal", st1[0:P, 8:11], st1[0:P, 12:15], R=[st1], W=[st1])
            yield
            V("tensor_scalar", qa_b[0:P, :], ZB[0:P, 0:192], st1[0:P, 8:9], None, ALU.mult, R=[ZB, st1], W=[qa_b])
            yield
            M("transpose", PB[1][:, 0:P], qa_b[0:P, 0:128], identb(P), R=[qa_b, cb], W=[PB[1]])
            M("transpose", PB[1][0:64, P:2 * P], qa_b[0:P, 128:192], identb(P), R=[qa_b, cb], W=[PB[1]])
            A("copy", qaT[:, 0, 0:P], PB[1][:, 0:P], R=[PB[1]], W=[qaT])
            yield
            A("copy", qaT[0:64, 1, 0:P], PB[1][0:64, P:2 * P], R=[PB[1]], W=[qaT])
            M("matmul", PF[1][0:P, 0:384], qaT[:, 0, 0:P], wuq_b[:, 0, :], start=True, stop=False, R=[qaT, wuq_b], W=[PF[1]])
            M("matmul", PF[1][0:P, 0:384], qaT[:, 1, 0:P], wuq_b[:, 1, :], start=False, stop=True, R=[qaT, wuq_b], W=[PF[1]])
            yield
            A("activation", out=zb_f[0:P, :], in_=PF[1][0:P, 0:384], func=AF.Square, R=[PF[1]], W=[zb_f])
            sq3 = zb_f[0:P, :].rearrange("p (h d) -> p h d", h=4)
            V("tensor_reduce", st2o[0:P, 0:4], sq3[:, :, 0:64], AX.X, ALU.add, R=[zb_f], W=[st2o])
            yield
            V("tensor_reduce", st2o[0:P, 4:8], sq3[:, :, 64:96], AX.X, ALU.add, R=[zb_f], W=[st2o])
            V("tensor_tensor", st3o[0:P, 0:8], st2o[0:P, 0:8], invc[0:P, 11:19], ALU.mult, R=[st2o, invc], W=[st3o])
            V("tensor_scalar", st3o[0:P, 0:8], st3o[0:P, 0:8], 1e-6, None, ALU.add, R=[st3o], W=[st3o])
            A("activation", out=st3o[0:P, 0:8], in_=st3o[0:P, 0:8], func=AF.Sqrt, R=[st3o], W=[st3o])
            V("reciprocal", st2o[0:P, 8:16], st3o[0:P, 0:8], R=[st3o], W=[st2o])
            yield
            for h in range(4):
                V("scalar_tensor_tensor", qn[0:P, h, 0:64], PF[1][0:P, 96 * h:96 * h + 64], st2o[0:P, 8 + h:9 + h],
                  BCv("mla_q_norm_g", P, 0, 64), ALU.mult, ALU.mult, R=[PF[1], st2o, bc], W=[qn])
                V("scalar_tensor_tensor", qn[0:P, h, 64:96], PF[1][0:P, 96 * h + 64:96 * h + 96], st2o[0:P, 12 + h:13 + h],
                  BCv("mla_q_norm_g", P, 64, 96), ALU.mult, ALU.mult, R=[PF[1], st2o, bc], W=[qn])
            V("tensor_copy", qfb[0:P, :, 0:64], qn[0:P, :, 0:64], R=[qn], W=[qfb])
            t1 = w1o[0:P, 0:64].rearrange("p (h d) -> p h d", h=4)
            yield
            t2 = w2o[0:P, 0:64].rearrange("p (h d) -> p h d", h=4)
            rope_apply(qfb[0:P, :, 64:80], qfb[0:P, :, 80:96], qn[0:P, :, 64:80], qn[0:P, :, 80:96], cosv, sinv, t1, t2,
                       [qn, rp_], [qfb], [w1o, w2o])
            for h in range(4):
                M("transpose", PB[1][0:96, h * P:(h + 1) * P], qfb[0:P, h, :], identb(P), R=[qfb, cb], W=[PB[1]])
            yield
            A("copy", qT[:, :, 0:P], PB[1][0:96, 0:4 * P].rearrange("p (h t) -> p h t", h=4), R=[PB[1]], W=[qT])
            S.dma(sc["qt"][:, :, tok0:tok0 + P].rearrange("h d t -> d h t"), qT[:, :, 0:P], R=[qT], key=qT)
            V("scalar_tensor_tensor", ckv_f[0:P, :], ZB[0:P, 192:320], st1[0:P, 9:10], BCv("mla_kva_g", P), ALU.mult, ALU.mult,
              R=[ZB, st1, bc], W=[ckv_f])
            yield
            S.dma(st["o_ckv"][l, tok0:tok0 + P, :] if grp == "p" else st["o_ckv"][l, 0:P, :], ckv_f[0:P, :], R=[ckv_f], key=ckv_f)
            V("scalar_tensor_tensor", kr_f[0:P, :], ZB[0:P, 320:352], st1[0:P, 10:11], BCv("mla_k_norm_g", P, 64, 96), ALU.mult, ALU.mult,
              R=[ZB, st1, bc], W=[kr_f])
            rope_apply(kr_r[0:P, 0:16], kr_r[0:P, 16:32], kr_f[0:P, 0:16], kr_f[0:P, 16:32], rp_[0:P, 0:16], rp_[0:P, 64:80],
                       w1o[0:P, 0:16], w2o[0:P, 0:16], [kr_f, rp_], [kr_r], [w1o, w2o])
            yield
            S.dma(st["o_kr"][l, tok0:tok0 + P, :] if grp == "p" else st["o_kr"][l, 0:P, :], kr_r[0:P, :], R=[kr_r], key=kr_r)
            kv_from_ckv(P, ckv_f[0:P, :], kr_r[0:P, :], [ckv_f, kr_r], sc, st["ktok0"] + tok0)

            proj(PF[0], OFF_C, 512)
            yield
            ZC = PF[0]
            V("tensor_tensor", ug[0:P, :], ZC[0:P, 0:256], sg[0:P, 512:768], ALU.mult, R=[ZC, sg], W=[ug])
            V("bn_stats", bnst[0:P, 0:6], ZC[0:P, 256:512], R=[ZC], W=[bnst])
            yield
            V("bn_aggr", bnst[0:P, 6:8], bnst[0:P, 0:6], R=[bnst], W=[bnst])
            rstd_from_ss(bnst[0:P, 7:8], st1[0:P, 3:4], P, 1.0, 1e-5, [bnst], [st1], st1[0:P, 15:16])
            V("tensor_scalar", vn[0:P, :], ZC[0:P, 256:512], bnst[0:P, 6:7], st1[0:P, 3:4], ALU.subtract, ALU.mult, R=[ZC, bnst, st1], W=[vn])
            yield
            V("tensor_tensor", vn[0:P, :], vn[0:P, :], BCv("sgu_ln_g", P), ALU.mult, R=[vn, bc], W=[vn])
            V("tensor_add", vn[0:P, :], vn[0:P, :], BCv("sgu_ln_b", P), R=[vn, bc], W=[vn])
            if grp == "s":
                S.dma(o_sgv[l, 0:P, :], vn[0:P, :], R=[vn], key=vn)
            yield
            V("tensor_copy", vn_b[0:P, :], vn[0:P, :], R=[vn], W=[vn_b])
            for h in range(4):
                M("matmul", PF[1][0:P, 64 * h:64 * h + 64], wsT_b[0:P, h, 0:P], vn_b[0:P, 64 * h:64 * h + 64], start=True, stop=True,
                  R=[wsT_b, vn_b], W=[PF[1]])
            for h in range(4):
                V("scalar_tensor_tensor", ygp[0:P, 256 + 64 * h:256 + 64 * h + 64], PF[1][0:P, 64 * h:64 * h + 64], sgub[0:P, h:h + 1],
                  ug[0:P, 64 * h:64 * h + 64], ALU.add, ALU.mult, R=[PF[1], sgub, ug], W=[ygp])

            yield
            proj(PF[0], OFF_D, 256)
            zc_, zp_ = zd_f[ti % 2], zd_f[(ti + 1) % 2]
            V("tensor_copy", zc_[0:P, :], PF[0][0:P, 0:256], R=[PF[0]], W=[zc_])
            yield
            if last:
                if grp == "p":
                    S.dma(st["o_pool"][l], zc_[P - 15:P, :], R=[zc_], key=zc_)
                else:
                    S.dma(st["o_pool"][l], zc_[1:16, :], R=[zc_], key=zc_)
            for g in range(4):
                M("matmul", PF[1][0:P, 64 * g:64 * g + 64], cf[0:P, CF_BAND + 128 * g:CF_BAND + 128 * g + P], zc_[0:P, 64 * g:64 * g + 64],
                  start=True, stop=False, R=[cf, zc_], W=[PF[1]])
                M("matmul", PF[1][0:P, 64 * g:64 * g + 64], cf[:, CF_BANDP + 128 * g:CF_BANDP + 128 * g + P], zp_[:, 64 * g:64 * g + 64],
                  start=False, stop=True, R=[cf, zp_], W=[PF[1]])
            for g in range(4):
                ic = invc[0:P, g:g + 1] if (first and grp == "p") else invc[0:P, 4 + g:5 + g]
                V("scalar_tensor_tensor", d_b[0:P, 64 * g:64 * g + 64], PF[1][0:P, 64 * g:64 * g + 64], ic, zc_[0:P, 64 * g:64 * g + 64],
                  ALU.mult, ALU.subtract, R=[PF[1], invc, zc_], W=[d_b])
            yield
            for c in range(2):
                M("transpose", PB[1][:, c * P:(c + 1) * P], d_b[0:P, c * 128:(c + 1) * 128], identb(P), R=[d_b, cb], W=[PB[1]])
            A("copy", dT[:, :, 0:P], PB[1][:, 0:2 * P].rearrange("p (c t) -> p c t", c=2), R=[PB[1]], W=[dT])
            for c in range(2):
                M("matmul", PF[0][0:P, c * 128:(c + 1) * 128], dT[:, c, 0:P], poolw_b[:, c, :], start=True, stop=True, R=[dT, poolw_b], W=[PF[0]])
            yield
            V("tensor_tensor", ygp[0:P, 512:768], PF[0][0:P, 0:256], sg[0:P, 768:1024], ALU.mult, R=[PF[0], sg], W=[ygp])
            yield

        gr, go = gen_R(), gen_O()
        if P == 128 and ILV > 0:
            for tag in gr:
                if tag == "mm":
                    for _k in range(ILV):
                        next(go, None)
                elif tag == "ev":
                    for _k in range(ILV2):
                        next(go, None)
        for _ in gr:
            pass
        for _ in go:
            pass
        S.dma(sc["yg"][tok0:tok0 + P, :], ygp[0:P, :], R=[ygp], key=ygp)

    def phase2(l, grp, Tq, QB, nkt_total, klast, xsrc, ydst, sc, KT, VV, bufs):
        (QTb, PT, yfull4, yT, gbt4, xres, xout, rr) = bufs
        nqb = Tq // QB
        nsub = max(1, QB // 128)
        Pq = min(QB, 128)
        for qb in range(nqb):
            q0 = qb * QB
            S.dma(QTb[:, :, 0:QB], sc["qt"][:, :, q0:q0 + QB].rearrange("h d t -> d h t"), W=[QTb], key=QTb)
            for i in range(nsub):
                t0 = q0 + i * 128
                S.dma(yfull4[i][0:Pq, 0:256], sc["yg"][t0:t0 + Pq, 0:256], W=[yfull4[i]], key=yfull4[i])
                S.dma(yfull4[i][0:Pq, 512:1024], sc["yg"][t0:t0 + Pq, 256:768], W=[yfull4[i]], key=yfull4[i])
                S.dma(gbt4[i][0:Pq, :], sc["gb"][t0:t0 + Pq, :], W=[gbt4[i]], key=gbt4[i])
            if grp == "p":
                nkt = 4 * qb + 4
            else:
                nkt = nkt_total
            def kinfo(kt):
                kp_ = 128 if (grp == "p" or kt < nkt_total - 1) else klast
                jd = kt - 4 * qb if grp == "p" else -1
                c0 = 128 * jd if jd > 0 else 0
                return kp_, jd, c0

            def emit_scores(h, kt, j):
                kp_, jd, c0 = kinfo(kt)
                n = QB - c0
                sbk = (PF[4], PF[5], PB[1])[j % 3]
                sview = sbk[0:kp_, 0:n] if (j % 3) < 2 else PB[1][0:kp_, 0:1024].bitcast(F32)[:, 0:n]
                M("matmul", sview, KT[:, h, kt * 128:kt * 128 + kp_], QTb[:, h, c0:QB], start=True, stop=True,
                  R=[KT, QTb], W=[sbk])
                pt_ = PT[j % 3]
                A("activation", out=pt_[0:kp_, c0:QB], in_=sview, func=AF.Exp, scale=float(1.0 / np.sqrt(96.0)),
                  R=[sbk], W=[pt_])
                if jd >= 0:
                    G("memset", pt_[64:128, c0:c0 + 64], 0.0, W=[pt_])

            def emit_pv(h, kt, j):
                kp_, jd, c0 = kinfo(kt)
                pt_ = PT[j % 3]
                for i in range(c0 // 128, nsub):
                    first_k = (kt == 0)
                    last_k = (kt == (4 * qb + i if grp == "p" else nkt - 1))
                    M("matmul", PF[i][0:Pq, 0:65], pt_[0:kp_, i * 128:i * 128 + Pq], VV[0:kp_, kt, 65 * h:65 * h + 65],
                      start=first_k, stop=last_k, R=[pt_, VV], W=[PF[i]])

            steps = [(h, kt) for h in range(4) for kt in range(nkt)]
            for j in range(min(2, len(steps))):
                emit_scores(steps[j][0], steps[j][1], j)
            for j, (h, kt) in enumerate(steps):
                if j + 2 < len(steps):
                    emit_scores(steps[j + 2][0], steps[j + 2][1], j + 2)
                emit_pv(h, kt, j)
                if kt == nkt - 1:
                    for i in range(nsub):
                        V("reciprocal", rr[0:Pq, h:h + 1], PF[i][0:Pq, 64:65], R=[PF[i]], W=[rr])
                        V("scalar_tensor_tensor", yfull4[i][0:Pq, 256 + 64 * h:256 + 64 * h + 64], PF[i][0:Pq, 0:64], rr[0:Pq, h:h + 1],
                          gbt4[i][0:Pq, 64 * h:64 * h + 64], ALU.mult, ALU.mult, R=[PF[i], rr, gbt4[i]], W=[yfull4[i]])
            for i in range(nsub):
                t0 = q0 + i * 128
                yfull = yfull4[i]
                S.dma(xres[0:Pq, :], xsrc[t0:t0 + Pq, :], W=[xres], key=xres)
                for k in range(8):
                    M("transpose", PB[0][:, k * Pq:(k + 1) * Pq], yfull[0:Pq, k * 128:(k + 1) * 128], identb(Pq), R=[yfull, cb], W=[PB[0]])
                A("copy", yT[:, :, 0:Pq], PB[0][:, 0:8 * Pq].rearrange("p (k t) -> p k t", k=8), R=[PB[0]], W=[yT])
                for cbk in range(2):
                    bank = PF[4 + cbk]
                    for k in range(8):
                        M("matmul", bank[0:Pq, :], yT[:, k, 0:Pq], wout_b[:, k, cbk * 512:(cbk + 1) * 512], start=(k == 0), stop=(k == 7),
                          R=[yT, wout_b], W=[bank])
                    V("tensor_add", xout[0:Pq, cbk * 512:(cbk + 1) * 512], bank[0:Pq, :], xres[0:Pq, cbk * 512:(cbk + 1) * 512],
                      R=[bank, xres], W=[xout])
                S.dma(ydst[t0:t0 + Pq, :], xout[0:Pq, :], R=[xout], key=xout, Q=S.pool)

    stp = dict(o_shift=o_shp, o_wkv=o_wkvp, o_pool=o_plp, o_ckv=o_ckvp, o_kr=o_krp, ktok0=0)
    sts = dict(o_shift=o_shs, o_wkv=o_wkvs, o_pool=o_pls, o_ckv=o_ckvs, o_kr=o_krs, ktok0=PAST)
    nlev_p = 7
    nlev_s = 4
    for l in range(NL):
        load_small(l)
        xsrc_p = x_p if l == 0 else o_yp
        xsrc_s = x_s if l == 0 else o_ys
        with ExitStack() as c1:
            alloc_work(c1)
            win_b = sb("win_b", [128, 8, DIN], BF16, c1)
            def cache_tiles():
                for kt in range(PAST // 128):
                    cs_ = cstage[kt % 2]
                    S.dma(cs_[:, 0:128], c_ckv[l, kt * 128:(kt + 1) * 128, :], W=[cs_], key=cs_)
                    S.dma(cs_[:, 128:160], c_kr[l, kt * 128:(kt + 1) * 128, :], W=[cs_], key=cs_)
                    kv_from_ckv(128, cs_[:, 0:128], cs_[:, 128:160], [cs_], scr_s, kt * 128, Q=S.pool)
                    yield
            gw = load_win(l, win_b)
            gc = cache_tiles() if do_sample else iter(())
            for _ in gw:
                next(gc, None)
                next(gc, None)
            for _ in gc:
                pass
            V("memset", tmpm[1][:], 0.0, W=[tmpm[1]])
            V("memset", zd_f[1][:], 0.0, W=[zd_f[1]])
            V("memset", ST[:], 0.0, W=[ST])
            S.dma(xt[0][:], xsrc_p[0:128, :], W=[xt[0]], key=xt[0])
            S.dma(ropet[0][:], rope_p[0:128, :], W=[ropet[0]], key=ropet[0])
            for ti in range(NT):
                if ti + 1 < NT:
                    S.dma(xt[(ti + 1) % 2][:], xsrc_p[(ti + 1) * 128:(ti + 2) * 128, :], W=[xt[(ti + 1) % 2]], key=xt[(ti + 1) % 2])
                    S.dma(ropet[(ti + 1) % 2][:], rope_p[(ti + 1) * 128:(ti + 2) * 128, :], W=[ropet[(ti + 1) % 2]], key=ropet[(ti + 1) % 2])
                phase1_tile(l, "p", ti, 128, xsrc_p, win_b, scr_p, ti == 0, ti == NT - 1, ti * 128, nlev_p, stp)
            if do_sample:
                P = TS
                V("memset", stage[:, 0:896], 0.0, W=[stage])
                S.dma(stage[127:128, 0:896], st_shift[l:l + 1, :], W=[stage], key=stage)
                V("tensor_tensor", tmpm[1][:, :], stage[:, 0:896], BCv("rw_mu", 128), ALU.mult, R=[stage, bc], W=[tmpm[1]])
                V("memset", zd_f[1][:], 0.0, W=[zd_f[1]])
                S.dma(zd_f[1][113:128, :], st_pool[l], W=[zd_f[1]], key=zd_f[1])
                S.dma(stage2[0:64, 0:256].rearrange("i (h j) -> i h j", h=4), st_wkv[l].rearrange("h i j -> i h j"), W=[stage2], key=stage2)
                for p in range(2):
                    M("transpose", PF[2][:, p * 64:(p + 1) * 64], stage2[0:64, p * 128:(p + 1) * 128], identf(64), R=[stage2, cf], W=[PF[2]])
                V("tensor_copy", ST[:].rearrange("k p v -> k (p v)"), PF[2][:, 0:128], R=[PF[2]], W=[ST])
                S.dma(xt[0][0:P, :], xsrc_s[0:P, :], W=[xt[0]], key=xt[0])
                S.dma(ropet[0][0:P, :], rope_s[0:P, :], W=[ropet[0]], key=ropet[0])
                phase1_tile(l, "s", 0, P, xsrc_s, win_b, scr_s, True, True, 0, nlev_s, sts)
            S.barrier()
        with ExitStack() as c2:
            KT = sb("KT", [96, 4, max(T, PAST + 128)], BF16, c2)
            VV = sb("VV", [128, max(NT, PAST // 128 + 1), 260], BF16, c2)
            QTb = sb("QTb", [96, 4, 512], BF16, c2)
            PT = [sb("PT%d" % i, [128, 512], BF16, c2) for i in range(3)]
            yfull4 = [sb("yfull%d" % i, [128, D], BF16, c2) for i in range(4)]
            yT = sb("yT", [128, 8, 128], BF16, c2)
            gbt4 = [sb("gbt%d" % i, [128, 256], BF16, c2) for i in range(4)]
            xres = sb("xres", [128, D], F32, c2)
            xout = sb("xout", [128, D], F32, c2)
            rr = sb("rr", [128, 4], F32, c2)
            bufs = (QTb, PT, yfull4, yT, gbt4, xres, xout, rr)
            for h in range(4):
                S.dma(KT[:, h, 0:T], scr_p["kt"][h], W=[KT], key=KT)
            vview = scr_p["v"].rearrange("(n p) c -> p n c", p=128)
            for n0 in range(0, NT, 8):
                n1 = min(NT, n0 + 8)
                S.dma(VV[:, n0:n1, :], vview[:, n0:n1, :], W=[VV], key=VV)
            phase2(l, "p", T, 512, NT, 128, xsrc_p, o_yp, scr_p, KT, VV, bufs)
            if do_sample:
                NKS = PAST // 128 + 1
                for h in range(4):
                    S.dma(KT[:, h, 0:PAST + TS], scr_s["kt"][h][:, 0:PAST + TS], W=[KT], key=KT)
                vview = scr_s["v"].rearrange("(n p) c -> p n c", p=128)
                for n0 in range(0, NKS - 1, 8):
                    n1 = min(NKS - 1, n0 + 8)
                    S.dma(VV[:, n0:n1, :], vview[:, n0:n1, :], W=[VV], key=VV)
                S.dma(VV[0:TS, NKS - 1, :], scr_s["v"][PAST:PAST + TS, :], W=[VV], key=VV)
                phase2(l, "s", TS, TS, NKS, TS, xsrc_s, o_ys, scr_s, KT, VV, bufs)
            S.barrier()
    S.final_wait()
    ctx.close()
    return nc, S.ninst


_CACHE = {}


def kernel(**inputs):
    x_prompt = np.asarray(inputs["x_prompt"], np.float32)
    x_sample = np.asarray(inputs["x_sample"], np.float32)
    B, T, _ = x_prompt.shape
    NL = inputs["norm_g"].shape[0]
    key = (T, NL)
    if key not in _CACHE:
        _CACHE[key] = build(T, NL)[0]
    nc = _CACHE[key]
    cf, cb, invc = make_consts()
    rope_p = rope_table(np.arange(T))
    rope_s = rope_table(PAST + np.arange(TS))
    in_maps = []
    for c in range(8):
        m = {
            "x_p": np.ascontiguousarray(x_prompt[c // 2]),
            "x_s": np.ascontiguousarray(x_sample[c]),
            "c_ckv": np.ascontiguousarray(np.asarray(inputs["cache_ckv"], np.float32)[:, c]),
            "c_kr": np.ascontiguousarray(np.asarray(inputs["cache_krope"], np.float32)[:, c]),
            "st_wkv": np.ascontiguousarray(np.asarray(inputs["state_wkv"], np.float32)[:, c]),
            "st_shift": np.ascontiguousarray(np.asarray(inputs["state_shift"], np.float32)[:, c]),
            "st_pool": np.ascontiguousarray(np.asarray(inputs["state_pool"], np.float32)[:, c]),
            "cf": cf, "cb": cb, "invc": invc, "rope_p": rope_p, "rope_s": rope_s,
        }
        for n in WNAMES:
            m[n] = np.ascontiguousarray(np.asarray(inputs[n], np.float32).reshape([NL] + WSHAPES[n]))
        in_maps.append(m)
    res = run_bass_kernel_spmd(nc, in_maps, core_ids=list(range(8)))
    R = res.results
    pc = [R[2 * b] for b in range(B)]
    sc = [R[c] for c in range(8)]
    stk = lambda L, n, ax: np.stack([np.asarray(r[n], np.float32) for r in L], axis=ax)
    out = (
        stk(pc, "o_yp", 0), stk(sc, "o_ys", 0),
        stk(pc, "o_ckvp", 1), stk(pc, "o_krp", 1), stk(pc, "o_wkvp", 1), stk(pc, "o_shp", 1), stk(pc, "o_plp", 1),
        stk(sc, "o_ckvs", 1), stk(sc, "o_krs", 1), stk(sc, "o_wkvs", 1), stk(sc, "o_shs", 1), stk(sc, "o_pls", 1),
        stk(sc, "o_sgv", 1),
    )
    return out
```

```python
import numpy as np
from contextlib import ExitStack
import concourse.bass as bass
import concourse.mybir as mybir
from concourse.bass_utils import run_bass_kernel_spmd

F32 = mybir.dt.float32
BF16 = mybir.dt.bfloat16
AF = mybir.ActivationFunctionType
ALU = mybir.AluOpType
AX = mybir.AxisListType

D = 1024
DIN = 3040
OFF_A, OFF_B, OFF_C, OFF_D, OFF_G = 0, 896, 1248, 1760, 2016
PAST = 2048
TS = 16
WINS = (2, 4, 8, 16)
CDEC = -0.6065306597126334
ILV = 1
ILV2 = 2


class Buf:
    def __init__(self, name):
        self.name = name
        self.w = None
        self.r = []


class TB:
    def __init__(self, t, name):
        self.t = t
        self.b = Buf(name)

    def __getitem__(self, k):
        return self.t[k]


class Eng:
    def __init__(self, S, name, eng):
        self.name = name
        self.eng = eng
        self.sem = S.newsem("e_" + name)
        self.cnt = 0
        self.waited = {}

    def wait_tok(self, tok):
        if tok is None:
            return
        sem, val = tok
        if sem is self.sem and self.name == "pe":
            return
        key = id(sem)
        if self.waited.get(key, 0) >= val:
            return
        self.eng.wait_ge(sem, val)
        self.waited[key] = val


class Sched:
    def __init__(self, nc, ctx):
        self.nc = nc
        self.ctx = ctx
        self.pe = Eng(self, "pe", nc.tensor)
        self.act = Eng(self, "act", nc.scalar)
        self.dve = Eng(self, "dve", nc.vector)
        self.pool = Eng(self, "pool", nc.gpsimd)
        self.sp = Eng(self, "sp", nc.sync)
        self.engs = [self.pe, self.act, self.dve, self.pool, self.sp]
        self.dma_sems = {}
        self.ninst = 0

    def newsem(self, name):
        return self.ctx.enter_context(self.nc.semaphore(name))

    def _bufs(self, L):
        return [x.b if isinstance(x, TB) else x for x in L]

    def deps(self, E, R, W):
        for b in R:
            E.wait_tok(b.w)
        for b in W:
            E.wait_tok(b.w)
            for t in (b.r.values() if isinstance(b.r, dict) else b.r):
                E.wait_tok(t)

    def _mark(self, tok, R, W):
        for b in W:
            b.w = tok
            b.r = {}
        for b in R:
            if isinstance(b.r, list):
                b.r = {}
            k = id(tok[0])
            if k not in b.r or b.r[k][1] < tok[1]:
                b.r[k] = tok

    enabled = True

    def ck(self, name):
        import os
        if os.environ.get("KSTOP") == name:
            self.enabled = False

    def op(self, E, fn, *args, R=(), W=(), **kw):
        if not self.enabled:
            return None
        R = self._bufs(R)
        W = self._bufs(W)
        inc = kw.pop("inc", True)
        self.deps(E, R, W)
        ins = getattr(E.eng, fn)(*args, **kw)
        self.ninst += 1
        if inc or E is not self.pe:
            E.cnt += 1
            ins.then_inc(E.sem, 1)
            self._mark((E.sem, E.cnt), R, W)
        else:
            self._mark((E.sem, E.cnt + 1), R, W)
        return ins

    def dma(self, out, in_, R=(), W=(), key=None, Q=None, **kw):
        if not self.enabled:
            return None
        Q = Q or self.sp
        R = self._bufs(R)
        W = self._bufs(W)
        kb = key.b if isinstance(key, TB) else key
        self.deps(Q, R, W)
        if kb.name not in self.dma_sems:
            self.dma_sems[kb.name] = [self.newsem("d_" + kb.name), 0]
        ent = self.dma_sems[kb.name]
        ins = Q.eng.dma_start(out=out, in_=in_, **kw)
        ent[1] += 16
        ins.then_inc(ent[0], 16)
        self.ninst += 1
        self._mark((ent[0], ent[1]), R, W)
        return ins

    def all_tokens(self):
        toks = [(e.sem, e.cnt) for e in self.engs if e.cnt > 0]
        toks += [(s, v) for (s, v) in self.dma_sems.values()]
        return toks

    def barrier(self):
        toks = self.all_tokens()
        for e in self.engs:
            for t in toks:
                e.wait_tok(t)

    def final_wait(self):
        for (s, v) in self.dma_sems.values():
            self.sp.wait_tok((s, v))


def make_consts():
    c = {}
    i = np.arange(128)
    s = i[:, None]
    t = i[None, :]
    ident = (s == t).astype(np.float32)
    tri_ui = (s <= t).astype(np.float32)
    tri_ut = (s < t).astype(np.float32)
    tri_lt = (s > t).astype(np.float32)
    bands = []
    bandp = []
    for w in WINS:
        bands.append(((s <= t) & (s > t - w)).astype(np.float32))
        bandp.append(((s - 128) > (t - w)).astype(np.float32))
    ones = np.ones((128, 128), np.float32)
    onehot_last = np.zeros((128, 2), np.float32)
    onehot_last[127, 0] = 1.0
    onehot_last[15, 1] = 1.0
    cf = np.concatenate([ident, tri_ui, ones] + bands + bandp + [onehot_last], axis=1)
    sh = (t == s + 1).astype(np.float32)
    elast = np.zeros((128, 128), np.float32)
    elast[127, 0] = 1.0
    cb = np.concatenate([ident, tri_lt, tri_ut, tri_lt, tri_ui, sh, elast], axis=1)
    invc = np.zeros((128, 24), np.float32)
    invc[:, 8:11] = np.array([1 / 192.0, 1 / 128.0, 1 / 32.0], np.float32)
    invc[:, 11:15] = 1 / 64.0
    invc[:, 15:19] = 1 / 32.0
    for g, w in enumerate(WINS):
        invc[:, g] = 1.0 / np.minimum(i + 1, w)
        invc[:, 4 + g] = 1.0 / w
    return cf.astype(np.float32), cb.astype(np.float32), invc


def rope_table(pos):
    half = 16
    inv = (10000.0 ** (-np.arange(half, dtype=np.float32) / half)).astype(np.float32)
    ang = pos.astype(np.float32)[:, None] * inv[None, :]
    cos = np.cos(ang).astype(np.float32)
    sin = np.sin(ang).astype(np.float32)
    return np.concatenate([np.tile(cos, (1, 4)), np.tile(sin, (1, 4))], axis=1).astype(np.float32)


CF_ID, CF_TRI, CF_ONES, CF_BAND, CF_BANDP, CF_OH = 0, 128, 256, 384, 896, 1408
CB_ID, CB_M3, CB_UI, CB_SH, CB_EL = 0, 128, 512, 640, 768

WNAMES = ["norm_g", "w_in", "w_out", "rw_mu", "rw_w0", "rw_w2", "rw_a0", "rw_a2", "rw_kk", "rw_ka", "rw_rk",
          "rw_gn_g", "rw_gn_b", "mla_qa_g", "mla_w_uq", "mla_kva_g", "mla_w_uk", "mla_w_uv",
          "mla_q_norm_g", "mla_k_norm_g", "sgu_w", "sgu_b", "sgu_ln_g", "sgu_ln_b", "pool_w", "pool_scale"]
WSHAPES = {
    "norm_g": [D], "w_in": [D, DIN], "w_out": [D, D], "rw_mu": [896], "rw_w0": [256], "rw_w2": [64, 256],
    "rw_a0": [256], "rw_a2": [64, 256], "rw_kk": [256], "rw_ka": [256], "rw_rk": [256], "rw_gn_g": [256],
    "rw_gn_b": [256], "mla_qa_g": [192], "mla_w_uq": [192, 384], "mla_kva_g": [128], "mla_w_uk": [128, 256],
    "mla_w_uv": [128, 256], "mla_q_norm_g": [96], "mla_k_norm_g": [96], "sgu_w": [4, 128, 128], "sgu_b": [4, 128],
    "sgu_ln_g": [256], "sgu_ln_b": [256], "pool_w": [4, 64, 64], "pool_scale": [256],
}


def build(T, NL, do_sample=True):
    nc = bass.Bass("TRN2", target_bir_lowering=False)
    ctx = ExitStack()
    NT = T // 128

    def din(name, shape, dt=F32):
        return nc.dram_tensor(name, list(shape), dt, kind="ExternalInput").ap()

    def dout(name, shape, dt=F32):
        return nc.dram_tensor(name, list(shape), dt, kind="ExternalOutput").ap()

    def dscr(name, shape, dt):
        return nc.dram_tensor(name, list(shape), dt, kind="Internal").ap()

    x_p = din("x_p", [T, D])
    x_s = din("x_s", [TS, D])
    c_ckv = din("c_ckv", [NL, PAST, 128])
    c_kr = din("c_kr", [NL, PAST, 32])
    st_wkv = din("st_wkv", [NL, 4, 64, 64])
    st_shift = din("st_shift", [NL, 896])
    st_pool = din("st_pool", [NL, 15, 256])
    Wd = {n: din(n, [NL] + WSHAPES[n]) for n in WNAMES}
    cf_d = din("cf", [128, 1410])
    cb_d = din("cb", [128, 896])
    invc_d = din("invc", [128, 24])
    rope_p = din("rope_p", [T, 128])
    rope_s = din("rope_s", [TS, 128])

    o_yp = dout("o_yp", [T, D])
    o_ys = dout("o_ys", [TS, D])
    o_ckvp = dout("o_ckvp", [NL, T, 128])
    o_krp = dout("o_krp", [NL, T, 32])
    o_wkvp = dout("o_wkvp", [NL, 4, 64, 64])
    o_shp = dout("o_shp", [NL, 896])
    o_plp = dout("o_plp", [NL, 15, 256])
    o_ckvs = dout("o_ckvs", [NL, TS, 128])
    o_krs = dout("o_krs", [NL, TS, 32])
    o_wkvs = dout("o_wkvs", [NL, 4, 64, 64])
    o_shs = dout("o_shs", [NL, 896])
    o_pls = dout("o_pls", [NL, 15, 256])
    o_sgv = dout("o_sgv", [NL, TS, 256])

    def scr(tag, TT, TK):
        return dict(
            yg=dscr("yg_" + tag, [TT, 768], BF16), gb=dscr("gb_" + tag, [TT, 256], BF16),
            qt=dscr("qt_" + tag, [4, 96, TT], BF16), kt=dscr("kt_" + tag, [4, 96, TK], BF16),
            v=dscr("v_" + tag, [TK, 260], BF16))
    scr_p = scr("p", T, T)
    scr_s = scr("s", TS, PAST + 128)

    S = Sched(nc, ctx)
    V = lambda fn, *a, **k: S.op(S.dve, fn, *a, **k)
    A = lambda fn, *a, **k: S.op(S.act, fn, *a, **k)
    G = lambda fn, *a, **k: S.op(S.pool, fn, *a, **k)
    M = lambda fn, *a, **k: S.op(S.pe, fn, *a, **k)

    uid = [0]

    def uname(name):
        uid[0] += 1
        return "s%d_%s" % (uid[0], name)

    def sb(name, shape, dt, c=None):
        t = (c or ctx).enter_context(nc.sbuf_tensor(uname(name), list(shape), dt))
        return TB(t, name)

    def ps(name, shape, dt):
        t = ctx.enter_context(nc.psum_tensor(name, list(shape), dt))
        return TB(t, name)

    PF = [ps("pf%d" % i, [128, 512], F32) for i in range(6)]
    PB = [ps("pb%d" % i, [128, 1024], BF16) for i in range(2)]

    cf = sb("cf", [128, 1410], F32)
    cb = sb("cb", [128, 896], BF16)
    cb_stage = sb("cb_stage", [128, 896], F32)
    invc = sb("invc", [128, 24], F32)
    S.dma(cf[:], cf_d[:, :], W=[cf], key=cf)
    S.dma(cb_stage[:], cb_d[:, :], W=[cb_stage], key=cb_stage)
    S.dma(invc[:], invc_d[:, :], W=[invc], key=invc)
    V("tensor_copy", cb[:], cb_stage[:], R=[cb_stage], W=[cb])
    cbr = sb("cbr", [128, 4, 4, 128], BF16)
    for kind, off in enumerate((CB_M3, CB_M3 + 128, CB_UI, CB_ID)):
        for h in range(4):
            V("tensor_copy", cbr[:, kind, h, :], cb[:, off:off + 128], R=[cb], W=[cbr])
    identf = lambda n: cf[0:n, CF_ID:CF_ID + n]
    identb = lambda n: cb[0:n, CB_ID:CB_ID + n]

    wout_b = sb("wout_b", [128, 8, D], BF16)
    wuq_b = sb("wuq_b", [128, 2, 384], BF16)
    wukv_b = sb("wukv_b", [128, 512], BF16)
    w2a2_b = sb("w2a2_b", [128, 512], BF16)
    wsT_b = sb("wsT_b", [128, 4, 128], BF16)
    poolw_b = sb("poolw_b", [128, 2, 128], BF16)
    sgub = sb("sgub", [128, 4], F32)
    ng = sb("ng", [128, 8], F32)
    qag = sb("qag", [128, 2], F32)
    BC_SPEC = [("rw_mu", 896), ("omm", 896), ("rw_w0", 256), ("rw_a0", 256), ("rw_kk", 256), ("rw_ka", 256),
               ("rw_rk", 256), ("rw_gn_g", 256), ("rw_gn_b", 256), ("mla_kva_g", 128), ("mla_q_norm_g", 96),
               ("mla_k_norm_g", 96), ("sgu_ln_g", 256), ("sgu_ln_b", 256)]
    bc_off = {}
    o = 0
    for n, w in BC_SPEC:
        bc_off[n] = (o, w)
        o += w
    bc = sb("bc", [128, o], F32)
    BCv = lambda n, P, a=0, b=None: bc[0:P, bc_off[n][0] + a: bc_off[n][0] + (bc_off[n][1] if b is None else b)]
    stage = sb("stage", [128, 1024], F32)
    stage2 = sb("stage2", [128, 1024], F32)

    WORK = []

    def wsb(name, shape, dt):
        tb = TB(None, name)
        WORK.append((tb, name, list(shape), dt))
        return tb

    def alloc_work(c):
        for tb, name, shape, dt in WORK:
            tb.t = c.enter_context(nc.sbuf_tensor(uname(name), shape, dt))
            tb.b = Buf(name)
        for i in range(4):
            STb[i].w = None
            STb[i].r = {}
        V("memset", vaug[:], 1.0, W=[vaug])
        V("memset", x_f[:], 0.0, W=[x_f])
        V("memset", p_f[:], 0.0, W=[p_f])
        V("memset", qaT[:], 0.0, W=[qaT])

    xt = [wsb("xt%d" % i, [128, D], F32) for i in range(2)]
    ropet = [wsb("ropet%d" % i, [128, 128], F32) for i in range(2)]
    junk = wsb("junk", [128, D], BF16)
    xn_b = wsb("xn_b", [128, D], BF16)
    xnT = wsb("xnT", [128, 8, 128], BF16)
    st1 = wsb("st1", [128, 16], F32)
    st2 = wsb("st2", [128, 16], F32)
    st3 = wsb("st3", [128, 16], F32)
    cstage = [wsb("cstage%d" % i, [128, 160], F32) for i in range(2)]
    st2o = wsb("st2o", [128, 16], F32)
    st3o = wsb("st3o", [128, 16], F32)
    w1o = wsb("w1o", [128, 256], F32)
    w2o = wsb("w2o", [128, 64], F32)
    sg = wsb("sg", [128, D], BF16)
    tmpm = [wsb("tmpm%d" % i, [128, 896], BF16) for i in range(2)]
    zs = wsb("zs", [128, 896], F32)
    lin = wsb("lin", [128, 128], BF16)
    linT = wsb("linT", [128, 128], BF16)
    w1 = wsb("w1", [128, 256], F32)
    w2 = wsb("w2", [128, 256], F32)
    w3 = wsb("w3", [128, 256], F32)
    sw = wsb("sw", [128, 256], F32)
    sa = wsb("sa", [128, 256], F32)
    kk = wsb("kk", [128, 256], F32)
    kp = wsb("kp", [128, 256], F32)
    bb = wsb("bb", [128, 256], F32)
    bcf = wsb("bcf", [128, 4], F32)
    cs_sb = wsb("cs_sb", [128, 256], F32)
    e1 = wsb("e1", [128, 256], F32)
    e2 = wsb("e2", [128, 256], F32)
    e3 = wsb("e3", [128, 256], F32)
    e4 = wsb("e4", [128, 256], F32)
    hat = wsb("hat", [128, 4, 256], BF16)
    bp_b = wsb("bp_b", [128, 256], BF16)
    kp4 = wsb("kp4", [128, 256], F32)
    hT = wsb("hT", [128, 4, 2, 128], BF16)
    wc = wsb("wc", [128, 2], F32)
    scL = wsb("scL", [128, 4, 128], BF16)
    scN = wsb("scN", [128, 4, 128], BF16)
    scK = wsb("scK", [128, 4, 128], BF16)
    mbr_b = wsb("mbr_b", [128, 4, 128], BF16)
    mkr_f = wsb("mkr_f", [128, 4, 128], F32)
    Ab = [wsb("Ab%d" % i, [128, 2, 4, 128], BF16) for i in range(2)]
    Ttb = [wsb("Ttb%d" % i, [128, 4, 128], BF16) for i in range(2)]
    G_b = wsb("G_b", [128, 4, 128], BF16)
    H_b = wsb("H_b", [128, 4, 64], BF16)
    x_f = wsb("x_f", [128, 4, 128], F32)
    z_f = wsb("z_f", [128, 4, 128], F32)
    q_f = wsb("q_f", [128, 2, 128], F32)
    p_f = wsb("p_f", [128, 2, 128], F32)
    ST = wsb("ST", [128, 2, 64], F32)
    STb = [Buf("ST%d" % h) for h in range(4)]
    yrw = wsb("yrw", [128, 256], F32)
    ygp = wsb("ygp", [128, 768], BF16)
    zb_f = wsb("zb_f", [128, 384], F32)
    qa_b = wsb("qa_b", [128, 192], BF16)
    qaT = wsb("qaT", [128, 2, 128], BF16)
    qn = wsb("qn", [128, 4, 96], F32)
    qfb = wsb("qfb", [128, 4, 96], BF16)
    qT = wsb("qT", [96, 4, 128], BF16)
    ckv_f = wsb("ckv_f", [128, 128], F32)
    ckv_b = wsb("ckv_b", [128, 128], BF16)
    ckvT = wsb("ckvT", [128, 128], BF16)
    kr_f = wsb("kr_f", [128, 32], F32)
    kr_r = wsb("kr_r", [128, 32], F32)
    kfull = wsb("kfull", [128, 4, 96], BF16)
    kT = wsb("kT", [96, 4, 128], BF16)
    vaug = wsb("vaug", [128, 4, 65], BF16)
    ug = wsb("ug", [128, 256], F32)
    vn = wsb("vn", [128, 256], F32)
    vn_b = wsb("vn_b", [128, 256], BF16)
    bnst = wsb("bnst", [128, 8], F32)
    zd_f = [wsb("zd_f%d" % i, [128, 256], F32) for i in range(2)]
    d_b = wsb("d_b", [128, 256], BF16)
    dT = wsb("dT", [128, 2, 128], BF16)
    za_f = wsb("za_f", [128, 896], F32)
    wkv_o = wsb("wkv_o", [64, 4, 64], F32)


    def load_small(l):
        P = 128
        for n, w in BC_SPEC:
            if n == "omm":
                continue
            S.dma(BCv(n, P), Wd[n][l].partition_broadcast(128), W=[bc], key=bc)
        V("tensor_scalar", BCv("omm", P), BCv("rw_mu", P), -1.0, 1.0, ALU.mult, ALU.add, R=[bc], W=[bc])
        S.dma(ng[:], Wd["norm_g"][l].rearrange("(k p) -> p k", p=128), W=[ng], key=ng, allow_slow_non_contiguous=True)
        S.dma(qag[:, 0:1], Wd["mla_qa_g"][l][0:128].rearrange("(p o) -> p o", o=1), W=[qag], key=qag)
        S.dma(qag[0:64, 1:2], Wd["mla_qa_g"][l][128:192].rearrange("(p o) -> p o", o=1), W=[qag], key=qag)
        S.dma(sgub[:], Wd["sgu_b"][l].rearrange("h i -> i h"), W=[sgub], key=sgub, allow_slow_non_contiguous=True)
        for k in range(8):
            S.dma(stage[:, 0:D], Wd["w_out"][l][k * 128:(k + 1) * 128, :], W=[stage], key=stage)
            V("tensor_copy", wout_b[:, k, :], stage[:, 0:D], R=[stage], W=[wout_b])
        S.dma(stage[:, 0:384], Wd["mla_w_uq"][l][0:128, :], W=[stage], key=stage)
        V("tensor_scalar", wuq_b[:, 0, :], stage[:, 0:384], qag[:, 0:1], None, ALU.mult, R=[stage, qag], W=[wuq_b])
        S.dma(stage[0:64, 0:384], Wd["mla_w_uq"][l][128:192, :], W=[stage], key=stage)
        V("memset", wuq_b[:, 1, :], 0.0, W=[wuq_b])
        V("tensor_scalar", wuq_b[0:64, 1, :], stage[0:64, 0:384], qag[0:64, 1:2], None, ALU.mult, R=[stage, qag], W=[wuq_b])
        S.dma(stage[:, 0:256], Wd["mla_w_uk"][l], W=[stage], key=stage)
        S.dma(stage[:, 256:512], Wd["mla_w_uv"][l], W=[stage], key=stage)
        V("tensor_copy", wukv_b[:], stage[:, 0:512], R=[stage], W=[wukv_b])
        V("memset", stage[:, 0:512], 0.0, W=[stage])
        S.dma(stage[0:64, 0:256], Wd["rw_w2"][l], W=[stage], key=stage)
        S.dma(stage[64:128, 256:512], Wd["rw_a2"][l], W=[stage], key=stage)
        V("tensor_copy", w2a2_b[:], stage[:, 0:512], R=[stage], W=[w2a2_b])
        S.dma(stage[:, 0:512].rearrange("p (h j) -> p h j", h=4), Wd["sgu_w"][l].rearrange("h i j -> i h j"), W=[stage], key=stage)
        for h in range(4):
            V("tensor_tensor", stage2[:, h * 128:(h + 1) * 128], stage[:, h * 128:(h + 1) * 128], cb[:, CB_M3 + 128:CB_M3 + 256],
              ALU.mult, R=[stage, cb], W=[stage2])
            V("tensor_sub", stage2[:, h * 128:(h + 1) * 128], stage[:, h * 128:(h + 1) * 128], stage2[:, h * 128:(h + 1) * 128],
              R=[stage, stage2], W=[stage2])
            M("transpose", PF[5][:, h * 128:(h + 1) * 128], stage2[:, h * 128:(h + 1) * 128], identf(128), R=[stage2, cf], W=[PF[5]])
        V("tensor_copy", wsT_b[:].rearrange("p h i -> p (h i)"), PF[5][:, 0:512], R=[PF[5]], W=[wsT_b])
        V("memset", stage[:, 0:256], 0.0, W=[stage])
        for g in range(4):
            c = g // 2
            r0 = 64 * (g % 2)
            S.dma(stage[r0:r0 + 64, c * 128 + r0: c * 128 + r0 + 64], Wd["pool_w"][l][g], W=[stage], key=stage)
        S.dma(stage2[:, 0:256], Wd["pool_scale"][l].partition_broadcast(128), W=[stage2], key=stage2)
        V("tensor_tensor", poolw_b[:].rearrange("p c d -> p (c d)"), stage[:, 0:256], stage2[:, 0:256], ALU.mult,
          R=[stage, stage2], W=[poolw_b])

    def load_win(l, win_b):
        for k in range(8):
            yield
            for c0 in range(0, DIN, 1024):
                n = min(1024, DIN - c0)
                st = stage if ((k * 3 + c0 // 1024) % 2 == 0) else stage2
                S.dma(st[:, 0:n], Wd["w_in"][l][k * 128:(k + 1) * 128, c0:c0 + n], W=[st], key=st)
                V("tensor_scalar", win_b[:, k, c0:c0 + n], st[:, 0:n], ng[:, k:k + 1], None, ALU.mult, R=[st, ng], W=[win_b])

    def rstd_from_ss(ss_ap, out_ap, P, n, eps, Rb, Wb, tmp_ap):
        V("tensor_scalar", tmp_ap, ss_ap, 1.0 / n, eps, ALU.mult, ALU.add, R=Rb, W=Wb)
        A("activation", out=tmp_ap, in_=tmp_ap, func=AF.Sqrt, R=Wb, W=Wb)
        V("reciprocal", out_ap, tmp_ap, R=Wb, W=Wb)

    def transposes_b(src, P, widths, pb, dst_ap_fn, Rb, Wdst, evac="act"):
        for i, (ap, w) in enumerate(zip(src, widths)):
            M("transpose", pb[0:w, i * P:(i + 1) * P], ap, identb(P), R=Rb + [cb], W=[pb])

    def rope_apply(dst_a, dst_b, x1, x2, cosv, sinv, t1, t2, Rb, Wb, Tb):
        V("tensor_tensor", t1, x1, cosv, ALU.mult, R=Rb, W=Tb)
        V("tensor_tensor", t2, x2, sinv, ALU.mult, R=Rb, W=Tb)
        V("tensor_tensor", dst_a, t1, t2, ALU.subtract, R=Tb, W=Wb)
        V("tensor_tensor", t1, x1, sinv, ALU.mult, R=Rb + Wb, W=Tb)
        V("tensor_tensor", t2, x2, cosv, ALU.mult, R=Rb, W=Tb)
        V("tensor_tensor", dst_b, t1, t2, ALU.add, R=Tb, W=Wb)

    KEY_KT_SW = Buf("kT_sw")
    KEY_V_SW = Buf("vaug_sw")

    def kv_from_ckv(P, ckvf_ap, krf_ap, Rb, sc, tok0, Q=None):
        V("tensor_copy", ckv_b[0:P, :], ckvf_ap, R=Rb, W=[ckv_b])
        M("transpose", PB[1][:, 0:P], ckv_b[0:P, :], identb(P), R=[ckv_b, cb], W=[PB[1]])
        A("copy", ckvT[:, 0:P], PB[1][:, 0:P], R=[PB[1]], W=[ckvT])
        M("matmul", PF[1][0:P, :], ckvT[:, 0:P], wukv_b[:], start=True, stop=True, R=[ckvT, wukv_b], W=[PF[1]])
        A("activation", out=w1o[0:P, :], in_=PF[1][0:P, 0:256], func=AF.Square, R=[PF[1]], W=[w1o])
        V("tensor_reduce", st2o[0:P, 0:4], w1o[0:P, :].rearrange("p (h d) -> p h d", h=4), AX.X, ALU.add, R=[w1o], W=[st2o])
        rstd_from_ss(st2o[0:P, 0:4], st2o[0:P, 4:8], P, 64.0, 1e-6, [st2o], [st2o], st2o[0:P, 8:12])
        for h in range(4):
            V("scalar_tensor_tensor", kfull[0:P, h, 0:64], PF[1][0:P, 64 * h:64 * h + 64], st2o[0:P, 4 + h:5 + h],
              BCv("mla_k_norm_g", P, 0, 64), ALU.mult, ALU.mult, R=[PF[1], st2o, bc], W=[kfull])
            V("tensor_copy", kfull[0:P, h, 64:96], krf_ap, R=Rb, W=[kfull])
        V("tensor_copy", vaug[0:P, :, 0:64], PF[1][0:P, 256:512].rearrange("p (h d) -> p h d", h=4), R=[PF[1]], W=[vaug])
        for h in range(4):
            M("transpose", PB[1][0:96, h * P:(h + 1) * P], kfull[0:P, h, :], identb(P), R=[kfull, cb], W=[PB[1]])
        A("copy", kT[:, :, 0:P], PB[1][0:96, 0:4 * P].rearrange("p (h t) -> p h t", h=4), R=[PB[1]], W=[kT])
        S.dma(sc["kt"][:, :, tok0:tok0 + P].rearrange("h d t -> d h t"), kT[:, :, 0:P], R=[kT], key=(kT if Q is None else KEY_KT_SW), Q=Q)
        S.dma(sc["v"][tok0:tok0 + P, :], vaug[0:P, :, :].rearrange("p h d -> p (h d)"), R=[vaug], key=(vaug if Q is None else KEY_V_SW), Q=Q)

    def phase1_tile(l, grp, ti, P, xsrc, win_b, sc, first, last, tok0, nlev, st):
        xb_ = xt[ti % 2]
        rp_ = ropet[ti % 2]
        cosv = rp_[0:P, 0:64].rearrange("p (h d) -> p h d", h=4)
        sinv = rp_[0:P, 64:128].rearrange("p (h d) -> p h d", h=4)
        A("activation", out=junk[0:P, :], in_=xb_[0:P, :], func=AF.Square, accum_out=st1[0:P, 0:1], R=[xb_], W=[junk, st1])
        rstd_from_ss(st1[0:P, 0:1], st1[0:P, 1:2], P, float(D), 1e-6, [st1], [st1], st1[0:P, 2:3])
        V("tensor_scalar", xn_b[0:P, :], xb_[0:P, :], st1[0:P, 1:2], None, ALU.mult, R=[xb_, st1], W=[xn_b])
        for k in range(8):
            M("transpose", PB[0][:, k * P:(k + 1) * P], xn_b[0:P, k * 128:(k + 1) * 128], identb(P), R=[xn_b, cb], W=[PB[0]], inc=(k == 7))
        A("copy", xnT[:, :, 0:P], PB[0][:, 0:8 * P].rearrange("p (k t) -> p k t", k=8), R=[PB[0]], W=[xnT])

        def proj(bank, c0, n):
            for k in range(8):
                M("matmul", bank[0:P, 0:n], xnT[:, k, 0:P], win_b[:, k, c0:c0 + n], start=(k == 0), stop=(k == 7),
                  R=[xnT, win_b], W=[bank], inc=(k == 7))

        proj(PF[0], OFF_G, 512)
        proj(PF[1], OFF_G + 512, 512)
        A("activation", out=sg[0:P, 0:512], in_=PF[0][0:P, :], func=AF.Silu, R=[PF[0]], W=[sg])
        A("activation", out=sg[0:P, 512:1024], in_=PF[1][0:P, :], func=AF.Silu, R=[PF[1]], W=[sg])
        S.dma(sc["gb"][tok0:tok0 + P, :], sg[0:P, 256:512], R=[sg], key=sg)

        proj(PF[0], OFF_A, 512)
        proj(PF[1], OFF_A + 512, 384)
        tc_, tp_ = tmpm[ti % 2], tmpm[(ti + 1) % 2]
        V("tensor_tensor", tc_[0:P, 0:512], PF[0][0:P, 0:512], BCv("rw_mu", P, 0, 512), ALU.mult, R=[PF[0], bc], W=[tc_])
        V("tensor_tensor", tc_[0:P, 512:896], PF[1][0:P, 0:384], BCv("rw_mu", P, 512, 896), ALU.mult, R=[PF[1], bc], W=[tc_])
        if last:
            V("tensor_copy", za_f[0:P, 0:512], PF[0][0:P, 0:512], R=[PF[0]], W=[za_f])
            V("tensor_copy", za_f[0:P, 512:896], PF[1][0:P, 0:384], R=[PF[1]], W=[za_f])
            S.dma(st["o_shift"][l:l + 1, :], za_f[P - 1:P, :], R=[za_f], key=za_f)
        V("tensor_tensor", zs[0:P, 0:512], PF[0][0:P, 0:512], BCv("omm", P, 0, 512), ALU.mult, R=[PF[0], bc], W=[zs])
        V("tensor_tensor", zs[0:P, 512:896], PF[1][0:P, 0:384], BCv("omm", P, 512, 896), ALU.mult, R=[PF[1], bc], W=[zs])
        def gen_R():
            for (bank, c0, n) in ((PF[2], 0, 512), (PF[3], 512, 384)):
                M("matmul", bank[0:P, 0:n], cb[0:P, CB_SH:CB_SH + P], tc_[0:P, c0:c0 + n], start=True, stop=False, R=[cb, tc_], W=[bank])
                M("matmul", bank[0:P, 0:n], cb[:, CB_EL:CB_EL + P], tp_[:, c0:c0 + n], start=False, stop=True, R=[cb, tp_], W=[bank])
            V("tensor_add", zs[0:P, 0:512], zs[0:P, 0:512], PF[2][0:P, 0:512], R=[zs, PF[2]], W=[zs])
            V("tensor_add", zs[0:P, 512:896], zs[0:P, 512:896], PF[3][0:P, 0:384], R=[zs, PF[3]], W=[zs])
            r_ = zs[0:P, 0:256]
            k_ = zs[0:P, 256:512]
            v_ = zs[0:P, 512:768]
            A("activation", out=lin[0:P, 0:64], in_=zs[0:P, 768:832], func=AF.Tanh, R=[zs], W=[lin])
            A("copy", lin[0:P, 64:128], zs[0:P, 832:896], R=[zs], W=[lin])
            M("transpose", PB[0][:, 0:P], lin[0:P, :], identb(P), R=[lin, cb], W=[PB[0]])
            A("copy", linT[:, 0:P], PB[0][:, 0:P], R=[PB[0]], W=[linT])
            M("matmul", PF[2][0:P, :], linT[:, 0:P], w2a2_b[:], start=True, stop=True, R=[linT, w2a2_b], W=[PF[2]])
            V("tensor_add", w1[0:P, :], PF[2][0:P, 0:256], BCv("rw_w0", P), R=[PF[2], bc], W=[w1])
            A("activation", out=sw[0:P, :], in_=w1[0:P, :], func=AF.Sigmoid, R=[w1], W=[sw])
            V("tensor_add", w2[0:P, :], PF[2][0:P, 256:512], BCv("rw_a0", P), R=[PF[2], bc], W=[w2])
            A("activation", out=sa[0:P, :], in_=w2[0:P, :], func=AF.Sigmoid, R=[w2], W=[sa])
            V("tensor_tensor", kk[0:P, :], k_, BCv("rw_kk", P), ALU.mult, R=[zs, bc], W=[kk])
            V("tensor_tensor", w3[0:P, :], kk[0:P, :], kk[0:P, :], ALU.mult, R=[kk], W=[w3])
            V("tensor_reduce", st2[0:P, 0:4], w3[0:P, :].rearrange("p (h d) -> p h d", h=4), AX.X, ALU.add, R=[w3], W=[st2])
            V("tensor_scalar", st2[0:P, 4:8], st2[0:P, 0:4], 1e-24, None, ALU.max, R=[st2], W=[st2])
            A("activation", out=st2[0:P, 4:8], in_=st2[0:P, 4:8], func=AF.Sqrt, R=[st2], W=[st2])
            V("reciprocal", st2[0:P, 8:12], st2[0:P, 4:8], R=[st2], W=[st2])
            for h in range(4):
                V("tensor_scalar", kk[0:P, 64 * h:64 * h + 64], kk[0:P, 64 * h:64 * h + 64], st2[0:P, 8 + h:9 + h], None, ALU.mult,
                  R=[kk, st2], W=[kk])
            V("scalar_tensor_tensor", w1[0:P, :], sa[0:P, :], -1.0, BCv("rw_ka", P), ALU.add, ALU.mult, R=[sa, bc], W=[w1])
            V("scalar_tensor_tensor", kp[0:P, :], w1[0:P, :], 1.0, k_, ALU.add, ALU.mult, R=[w1, zs], W=[kp])
            V("tensor_tensor", bb[0:P, :], kk[0:P, :], sa[0:P, :], ALU.mult, R=[kk, sa], W=[bb])
            G("tensor_tensor", w2[0:P, :], r_, kp[0:P, :], ALU.mult, R=[zs, kp], W=[w2])
            G("tensor_tensor", w2[0:P, :], w2[0:P, :], BCv("rw_rk", P), ALU.mult, R=[w2, bc], W=[w2])
            V("tensor_reduce", bcf[0:P, 0:4], w2[0:P, :].rearrange("p (h d) -> p h d", h=4), AX.X, ALU.add, R=[w2], W=[bcf])
            M("matmul", PF[2][0:P, 0:256], cf[0:P, CF_TRI:CF_TRI + P], sw[0:P, :], start=True, stop=True, R=[cf, sw], W=[PF[2]])
            M("matmul", PF[2][0:P, 256:512], cf[0:P, CF_ONES:CF_ONES + P], sw[0:P, :], start=True, stop=True, R=[cf, sw], W=[PF[2]])
            V("tensor_copy", cs_sb[0:P, :], PF[2][0:P, 0:256], R=[PF[2]], W=[cs_sb])
            A("activation", out=e1[0:P, :], in_=PF[2][0:P, 0:256], func=AF.Exp, scale=CDEC, R=[PF[2]], W=[e1])
            A("activation", out=e2[0:P, :], in_=PF[2][0:P, 0:256], func=AF.Exp, scale=-CDEC, R=[PF[2]], W=[e2])
            V("tensor_sub", w1[0:P, :], cs_sb[0:P, :], sw[0:P, :], R=[cs_sb, sw], W=[w1])
            A("activation", out=e3[0:P, :], in_=w1[0:P, :], func=AF.Exp, scale=CDEC, R=[w1], W=[e3])
            V("tensor_sub", w3[0:P, :], PF[2][0:P, 256:512], cs_sb[0:P, :], R=[PF[2], cs_sb], W=[w3])
            A("activation", out=e4[0:P, :], in_=w3[0:P, :], func=AF.Exp, scale=CDEC, R=[w3], W=[e4])
            V("scalar_tensor_tensor", hat[0:P, 0, :], kk[0:P, :], -1.0, e3[0:P, :], ALU.mult, ALU.mult, R=[kk, e3], W=[hat])
            V("tensor_tensor", hat[0:P, 1, :], bb[0:P, :], e2[0:P, :], ALU.mult, R=[bb, e2], W=[hat])
            V("tensor_tensor", hat[0:P, 2, :], kp[0:P, :], e2[0:P, :], ALU.mult, R=[kp, e2], W=[hat])
            V("tensor_tensor", hat[0:P, 3, :], r_, e1[0:P, :], ALU.mult, R=[zs, e1], W=[hat])
            G("tensor_tensor", bp_b[0:P, :], bb[0:P, :], e4[0:P, :], ALU.mult, R=[bb, e4], W=[bp_b])
            G("tensor_tensor", kp4[0:P, :], kp[0:P, :], e4[0:P, :], ALU.mult, R=[kp, e4], W=[kp4])
            ohc = CF_OH + (0 if P == 128 else 1)
            for p in range(2):
                M("matmul", PF[3][:, p:p + 1], e1[0:P, p * 128:(p + 1) * 128], cf[0:P, ohc:ohc + 1], start=True, stop=True,
                  R=[e1, cf], W=[PF[3]])
            V("tensor_copy", wc[:, 0:2], PF[3][:, 0:2], R=[PF[3]], W=[wc])
            for vi in range(4):
                for p in range(2):
                    M("transpose", PB[0][:, (vi * 2 + p) * P:(vi * 2 + p + 1) * P], hat[0:P, vi, p * 128:(p + 1) * 128], identb(P),
                      R=[hat, cb], W=[PB[0]])
            A("copy", hT[:, :, :, 0:P], PB[0][:, 0:8 * P].rearrange("p (v q t) -> p v q t", v=4, q=2), R=[PB[0]], W=[hT])
            def fm(h, vi):
                return hT[64 * (h % 2):64 * (h % 2) + 64, vi, h // 2, 0:P]

            def pv(bank, w=128, n=None):
                n = P if n is None else n
                return bank[0:P, 0:4 * w].rearrange("p (h t) -> p h t", h=4)[:, :, 0:n]

            def msk(kind):
                return cbr[0:P, kind, :, 0:P]

            for h in range(4):
                M("matmul", PF[2][0:P, h * 128:h * 128 + P], fm(h, 0), fm(h, 1), start=True, stop=True, R=[hT], W=[PF[2]])
                M("matmul", PF[3][0:P, h * 128:h * 128 + P], fm(h, 1), fm(h, 0), start=True, stop=True, R=[hT], W=[PF[3]])
                M("matmul", PF[4][0:P, h * 128:h * 128 + P], fm(h, 0), fm(h, 2), start=True, stop=True, R=[hT], W=[PF[4]])
                M("matmul", PF[5][0:P, h * 128:h * 128 + P], fm(h, 1), fm(h, 3), start=True, stop=True, R=[hT], W=[PF[5]])
            V("tensor_tensor", scL[0:P, :, 0:P], pv(PF[2]), msk(0), ALU.mult, R=[PF[2], cbr], W=[scL])
            V("tensor_tensor", scN[0:P, :, 0:P], pv(PF[3]), msk(1), ALU.mult, R=[PF[3], cbr], W=[scN])
            for h in range(4):
                bk = PF[2 + (h % 2)]
                M("matmul", bk[0:P, h * 128:h * 128 + P], fm(h, 2), fm(h, 3), start=True, stop=True, R=[hT], W=[bk])
            V("tensor_add", Ttb[0][0:P, :, 0:P], scL[0:P, :, 0:P], msk(3), R=[scL, cbr], W=[Ttb[0]])
            V("tensor_tensor", scK[0:P, :, 0:P], pv(PF[4]), msk(0), ALU.mult, R=[PF[4], cbr], W=[scK])
            V("tensor_tensor", mbr_b[0:P, :, 0:P], pv(PF[5]), msk(2), ALU.mult, R=[PF[5], cbr], W=[mbr_b])
            for h in range(4):
                bk = PF[2 + (h % 2)]
                V("tensor_tensor", mkr_f[0:P, h, 0:P], bk[0:P, h * 128:h * 128 + P], cb[0:P, CB_UI:CB_UI + P], ALU.mult, R=[bk, cb], W=[mkr_f])
            def acc_(tb, idx):
                return (lambda h: tb[0:P, h, 0:P]) if idx is None else (lambda h: tb[0:P, idx, h, 0:P])

            def squares(Af, ATf, Rb, dst, need_A):
                if need_A:
                    for h in range(4):
                        M("matmul", PF[2][0:P, h * 128:h * 128 + P], ATf(h), Af(h), start=True, stop=True, R=Rb, W=[PF[2]], inc=(h == 3))
                for h in range(4):
                    M("matmul", PF[3][0:P, h * 128:h * 128 + P], Af(h), ATf(h), start=True, stop=True, R=Rb, W=[PF[3]], inc=(h == 3))

            def squares_evac(dst, need_A):
                if need_A:
                    V("tensor_copy", dst[0:P, 0, :, 0:P], pv(PF[2]), R=[PF[2]], W=[dst])
                A("copy", dst[0:P, 1, :, 0:P], pv(PF[3]), R=[PF[3]], W=[dst])

            tcur = 0
            squares(acc_(scL, None), acc_(scN, None), [scL, scN], Ab[1], nlev > 2)
            squares_evac(Ab[1], nlev > 2)
            for k in range(1, nlev):
                cur = Ab[k % 2]
                Af, ATf = acc_(cur, 0), acc_(cur, 1)
                lastk = (k == nlev - 1)
                pbank = PF[4 + (k % 2)]
                for h in range(4):
                    M("matmul", pbank[0:P, h * 128:h * 128 + P], ATf(h), Ttb[tcur][0:P, h, 0:P], start=True, stop=True,
                      R=[cur, Ttb[tcur]], W=[pbank], inc=(h == 3))
                if not lastk:
                    nxt = Ab[(k + 1) % 2]
                    squares(Af, ATf, [cur], nxt, k + 1 < nlev - 1)
                yield "mm"
                V("tensor_add", Ttb[1 - tcur][0:P, :, 0:P], pv(pbank), Ttb[tcur][0:P, :, 0:P], R=[pbank, Ttb[tcur]], W=[Ttb[1 - tcur]])
                if not lastk:
                    squares_evac(nxt, k + 1 < nlev - 1)
                tcur = 1 - tcur
                yield "ev"
            Tt = Ttb[tcur]
            for h in range(4):
                M("matmul", PF[2][0:P, h * 128:h * 128 + P], Tt[0:P, h, 0:P], mbr_b[0:P, h, 0:P], start=True, stop=True, R=[Tt, mbr_b], W=[PF[2]])
            for h in range(4):
                M("matmul", PF[3][0:P, h * 64:h * 64 + 64], Tt[0:P, h, 0:P], bp_b[0:P, 64 * h:64 * h + 64], start=True, stop=True, R=[Tt, bp_b], W=[PF[3]])
            A("copy", G_b[0:P, :, 0:P], pv(PF[2]), R=[PF[2]], W=[G_b])
            V("tensor_copy", H_b[0:P, :, :], PF[3][0:P, 0:256].rearrange("p (h d) -> p h d", h=4), R=[PF[3]], W=[H_b])
            for h in range(4):
                M("matmul", PF[2][0:P, h * 128:h * 128 + P], scK[0:P, h, 0:P], G_b[0:P, h, 0:P], start=True, stop=True, R=[scK, G_b], W=[PF[2]])
            for h in range(4):
                M("matmul", PF[3][:, h * 128:h * 128 + P], hat[0:P, 0, (h // 2) * 128:(h // 2) * 128 + 128], G_b[0:P, h, 0:P], start=True, stop=True,
                  R=[hat, G_b], W=[PF[3]])
            for h in range(4):
                M("matmul", PF[4][0:P, h * 64:h * 64 + 64], scK[0:P, h, 0:P], H_b[0:P, h, :], start=True, stop=True, R=[scK, H_b], W=[PF[4]])
            for h in range(4):
                M("matmul", PF[4][:, 256 + h * 64:256 + h * 64 + 64], hat[0:P, 0, (h // 2) * 128:(h // 2) * 128 + 128], H_b[0:P, h, :],
                  start=True, stop=True, R=[hat, H_b], W=[PF[4]])
            V("tensor_add", z_f[0:P, :, 0:P], pv(PF[2]), mkr_f[0:P, :, 0:P], R=[PF[2], mkr_f], W=[z_f])
            for o_ in (0, 64):
                h0 = o_ // 64
                qv = PF[3][o_:o_ + 64, 0:512].rearrange("p (q r t) -> p q r t", q=2, r=2)[:, :, h0, 0:P]
                V("tensor_add", q_f[o_:o_ + 64, :, 0:P], qv, hT[o_:o_ + 64, 3, :, 0:P], R=[PF[3], hT], W=[q_f])
            for h in range(4):
                p = h // 2
                o_ = 64 * (h % 2)
                V("tensor_add", x_f[0:P, h, o_:o_ + 64], PF[4][0:P, h * 64:h * 64 + 64], kp4[0:P, 64 * h:64 * h + 64], R=[PF[4], kp4], W=[x_f])
                V("scalar_tensor_tensor", p_f[o_:o_ + 64, p, o_:o_ + 64], cf[o_:o_ + 64, CF_ID + o_:CF_ID + o_ + 64], wc[o_:o_ + 64, p:p + 1],
                  PF[4][o_:o_ + 64, 256 + h * 64:256 + h * 64 + 64], ALU.mult, ALU.add, R=[cf, wc, PF[4]], W=[p_f])
            for h in range(4):
                p = h // 2
                o_ = 64 * (h % 2)
                vh = zs[0:P, 512 + 64 * h:512 + 64 * h + 64]
                M("matmul", PF[5][0:P, 64 * h:64 * h + 64], q_f[o_:o_ + 64, p, 0:P], ST[o_:o_ + 64, p, :], start=True, stop=False, R=[q_f, ST], W=[PF[5]])
                M("matmul", PF[5][0:P, 64 * h:64 * h + 64], z_f[0:P, h, 0:P], vh, start=False, stop=True, R=[z_f, zs], W=[PF[5]])
            for p in range(2):
                M("matmul", PF[5][:, 256 + 64 * p:256 + 64 * p + 64], p_f[:, p, :], ST[:, p, :], start=True, stop=False, R=[p_f, ST], W=[PF[5]])
                for h in (2 * p, 2 * p + 1):
                    M("matmul", PF[5][:, 256 + 64 * p:256 + 64 * p + 64], x_f[0:P, h, :], zs[0:P, 512 + 64 * h:512 + 64 * h + 64],
                      start=False, stop=(h == 2 * p + 1), R=[x_f, zs], W=[PF[5]])
            V("tensor_copy", ST[:, :, :], PF[5][:, 256:384].rearrange("k (p v) -> k p v", p=2), R=[PF[5]], W=[ST])
            Y = PF[5]
            V("tensor_reduce", st3[0:P, 0:4], Y[0:P, 0:256].rearrange("p (h d) -> p h d", h=4), AX.X, ALU.add, R=[Y], W=[st3])
            V("tensor_scalar", st3[0:P, 0:4], st3[0:P, 0:4], 1.0 / 64, None, ALU.mult, R=[st3], W=[st3])
            for h in range(4):
                V("tensor_scalar", yrw[0:P, 64 * h:64 * h + 64], Y[0:P, 64 * h:64 * h + 64], st3[0:P, h:h + 1], None, ALU.subtract,
                  R=[Y, st3], W=[yrw])
            V("tensor_tensor", w1[0:P, :], yrw[0:P, :], yrw[0:P, :], ALU.mult, R=[yrw], W=[w1])
            V("tensor_reduce", st3[0:P, 4:8], w1[0:P, :].rearrange("p (h d) -> p h d", h=4), AX.X, ALU.add, R=[w1], W=[st3])
            rstd_from_ss(st3[0:P, 4:8], st3[0:P, 8:12], P, 64.0, 64e-5, [st3], [st3], st3[0:P, 12:16])
            for h in range(4):
                V("tensor_scalar", yrw[0:P, 64 * h:64 * h + 64], yrw[0:P, 64 * h:64 * h + 64], st3[0:P, 8 + h:9 + h], None, ALU.mult,
                  R=[yrw, st3], W=[yrw])
            V("tensor_tensor", yrw[0:P, :], yrw[0:P, :], BCv("rw_gn_g", P), ALU.mult, R=[yrw, bc], W=[yrw])
            V("tensor_add", yrw[0:P, :], yrw[0:P, :], BCv("rw_gn_b", P), R=[yrw, bc], W=[yrw])
            for h in range(4):
                V("scalar_tensor_tensor", yrw[0:P, 64 * h:64 * h + 64], zs[0:P, 512 + 64 * h:512 + 64 * h + 64], bcf[0:P, h:h + 1],
                  yrw[0:P, 64 * h:64 * h + 64], ALU.mult, ALU.add, R=[zs, bcf, yrw], W=[yrw])
            V("tensor_tensor", ygp[0:P, 0:256], yrw[0:P, :], sg[0:P, 0:256], ALU.mult, R=[yrw, sg], W=[ygp])
            if last:
                for p in range(2):
                    M("transpose", PF[2][0:64, p * 128:(p + 1) * 128], ST[:, p, :], identf(128), R=[ST, cf], W=[PF[2]])
                V("tensor_copy", wkv_o[:].rearrange("i h j -> i (h j)"), PF[2][0:64, 0:256], R=[PF[2]], W=[wkv_o])
                S.dma(st["o_wkv"][l].rearrange("h i j -> i h j"), wkv_o[:], R=[wkv_o], key=wkv_o)
            yield

        def gen_O():
            proj(PF[0], OFF_B, 352)
            ZB = PF[0]
            A("activation", out=junk[0:P, 0:192], in_=ZB[0:P, 0:192], func=AF.Square, accum_out=st1[0:P, 4:5], R=[ZB], W=[junk, st1])
            yield
            A("activation", out=junk[0:P, 0:128], in_=ZB[0:P, 192:320], func=AF.Square, accum_out=st1[0:P, 5:6], R=[ZB], W=[junk, st1])
            A("activation", out=junk[0:P, 0:32], in_=ZB[0:P, 320:352], func=AF.Square, accum_out=st1[0:P, 6:7], R=[ZB], W=[junk, st1])
            V("tensor_tensor", st1[0:P, 12:15], st1[0:P, 4:7], invc[0:P, 8:11], ALU.mult, R=[st1, invc], W=[st1])
            V("tensor_scalar", st1[0:P, 12:15], st1[0:P, 12:15], 1e-6, None, ALU.add, R=[st1], W=[st1])
            A("activation", out=st1[0:P, 12:15], in_=st1[0:P, 12:15], func=AF.Sqrt, R=[st1], W=[st1])
            V("reciprocal", st1[0:P, 8:11], st1[0:P, 12:15], R=[st1], W=[st1])
            yield
            V("tensor_scalar", qa_b[0:P, :], ZB[0:P, 0:192], st1[0:P, 8:9], None, ALU.mult, R=[ZB, st1], W=[qa_b])
            yield
            M("transpose", PB[1][:, 0:P], qa_b[0:P, 0:128], identb(P), R=[qa_b, cb], W=[PB[1]])
            M("transpose", PB[1][0:64, P:2 * P], qa_b[0:P, 128:192], identb(P), R=[qa_b, cb], W=[PB[1]])
            A("copy", qaT[:, 0, 0:P], PB[1][:, 0:P], R=[PB[1]], W=[qaT])
            yield
            A("copy", qaT[0:64, 1, 0:P], PB[1][0:64, P:2 * P], R=[PB[1]], W=[qaT])
            M("matmul", PF[1][0:P, 0:384], qaT[:, 0, 0:P], wuq_b[:, 0, :], start=True, stop=False, R=[qaT, wuq_b], W=[PF[1]])
            M("matmul", PF[1][0:P, 0:384], qaT[:, 1, 0:P], wuq_b[:, 1, :], start=False, stop=True, R=[qaT, wuq_b], W=[PF[1]])
            yield
            A("activation", out=zb_f[0:P, :], in_=PF[1][0:P, 0:384], func=AF.Square, R=[PF[1]], W=[zb_f])
            sq3 = zb_f[0:P, :].rearrange("p (h d) -> p h d", h=4)
            V("tensor_reduce", st2o[0:P, 0:4], sq3[:, :, 0:64], AX.X, ALU.add, R=[zb_f], W=[st2o])
            yield
            V("tensor_reduce", st2o[0:P, 4:8], sq3[:, :, 64:96], AX.X, ALU.add, R=[zb_f], W=[st2o])
            V("tensor_tensor", st3o[0:P, 0:8], st2o[0:P, 0:8], invc[0:P, 11:19], ALU.mult, R=[st2o, invc], W=[st3o])
            V("tensor_scalar", st3o[0:P, 0:8], st3o[0:P, 0:8], 1e-6, None, ALU.add, R=[st3o], W=[st3o])
            A("activation", out=st3o[0:P, 0:8], in_=st3o[0:P, 0:8], func=AF.Sqrt, R=[st3o], W=[st3o])
            V("reciprocal", st2o[0:P, 8:16], st3o[0:P, 0:8], R=[st3o], W=[st2o])
            yield
            for h in range(4):
                V("scalar_tensor_tensor", qn[0:P, h, 0:64], PF[1][0:P, 96 * h:96 * h + 64], st2o[0:P, 8 + h:9 + h],
                  BCv("mla_q_norm_g", P, 0, 64), ALU.mult, ALU.mult, R=[PF[1], st2o, bc], W=[qn])
                V("scalar_tensor_tensor", qn[0:P, h, 64:96], PF[1][0:P, 96 * h + 64:96 * h + 96], st2o[0:P, 12 + h:13 + h],
                  BCv("mla_q_norm_g", P, 64, 96), ALU.mult, ALU.mult, R=[PF[1], st2o, bc], W=[qn])
            V("tensor_copy", qfb[0:P, :, 0:64], qn[0:P, :, 0:64], R=[qn], W=[qfb])
            t1 = w1o[0:P, 0:64].rearrange("p (h d) -> p h d", h=4)
            yield
            t2 = w2o[0:P, 0:64].rearrange("p (h d) -> p h d", h=4)
            rope_apply(qfb[0:P, :, 64:80], qfb[0:P, :, 80:96], qn[0:P, :, 64:80], qn[0:P, :, 80:96], cosv, sinv, t1, t2,
                       [qn, rp_], [qfb], [w1o, w2o])
            for h in range(4):
                M("transpose", PB[1][0:96, h * P:(h + 1) * P], qfb[0:P, h, :], identb(P), R=[qfb, cb], W=[PB[1]])
            yield
            A("copy", qT[:, :, 0:P], PB[1][0:96, 0:4 * P].rearrange("p (h t) -> p h t", h=4), R=[PB[1]], W=[qT])
            S.dma(sc["qt"][:, :, tok0:tok0 + P].rearrange("h d t -> d h t"), qT[:, :, 0:P], R=[qT], key=qT)
            V("scalar_tensor_tensor", ckv_f[0:P, :], ZB[0:P, 192:320], st1[0:P, 9:10], BCv("mla_kva_g", P), ALU.mult, ALU.mult,
              R=[ZB, st1, bc], W=[ckv_f])
            yield
            S.dma(st["o_ckv"][l, tok0:tok0 + P, :] if grp == "p" else st["o_ckv"][l, 0:P, :], ckv_f[0:P, :], R=[ckv_f], key=ckv_f)
            V("scalar_tensor_tensor", kr_f[0:P, :], ZB[0:P, 320:352], st1[0:P, 10:11], BCv("mla_k_norm_g", P, 64, 96), ALU.mult, ALU.mult,
              R=[ZB, st1, bc], W=[kr_f])
            rope_apply(kr_r[0:P, 0:16], kr_r[0:P, 16:32], kr_f[0:P, 0:16], kr_f[0:P, 16:32], rp_[0:P, 0:16], rp_[0:P, 64:80],
                       w1o[0:P, 0:16], w2o[0:P, 0:16], [kr_f, rp_], [kr_r], [w1o, w2o])
            yield
            S.dma(st["o_kr"][l, tok0:tok0 + P, :] if grp == "p" else st["o_kr"][l, 0:P, :], kr_r[0:P, :], R=[kr_r], key=kr_r)
            kv_from_ckv(P, ckv_f[0:P, :], kr_r[0:P, :], [ckv_f, kr_r], sc, st["ktok0"] + tok0)

            proj(PF[0], OFF_C, 512)
            yield
            ZC = PF[0]
            V("tensor_tensor", ug[0:P, :], ZC[0:P, 0:256], sg[0:P, 512:768], ALU.mult, R=[ZC, sg], W=[ug])
            V("bn_stats", bnst[0:P, 0:6], ZC[0:P, 256:512], R=[ZC], W=[bnst])
            yield
            V("bn_aggr", bnst[0:P, 6:8], bnst[0:P, 0:6], R=[bnst], W=[bnst])
            rstd_from_ss(bnst[0:P, 7:8], st1[0:P, 3:4], P, 1.0, 1e-5, [bnst], [st1], st1[0:P, 15:16])
            V("tensor_scalar", vn[0:P, :], ZC[0:P, 256:512], bnst[0:P, 6:7], st1[0:P, 3:4], ALU.subtract, ALU.mult, R=[ZC, bnst, st1], W=[vn])
            yield
            V("tensor_tensor", vn[0:P, :], vn[0:P, :], BCv("sgu_ln_g", P), ALU.mult, R=[vn, bc], W=[vn])
            V("tensor_add", vn[0:P, :], vn[0:P, :], BCv("sgu_ln_b", P), R=[vn, bc], W=[vn])
            if grp == "s":
                S.dma(o_sgv[l, 0:P, :], vn[0:P, :], R=[vn], key=vn)
            yield
            V("tensor_copy", vn_b[0:P, :], vn[0:P, :], R=[vn], W=[vn_b])
            for h in range(4):
                M("matmul", PF[1][0:P, 64 * h:64 * h + 64], wsT_b[0:P, h, 0:P], vn_b[0:P, 64 * h:64 * h + 64], start=True, stop=True,
                  R=[wsT_b, vn_b], W=[PF[1]])
            for h in range(4):
                V("scalar_tensor_tensor", ygp[0:P, 256 + 64 * h:256 + 64 * h + 64], PF[1][0:P, 64 * h:64 * h + 64], sgub[0:P, h:h + 1],
                  ug[0:P, 64 * h:64 * h + 64], ALU.add, ALU.mult, R=[PF[1], sgub, ug], W=[ygp])

            yield
            proj(PF[0], OFF_D, 256)
            zc_, zp_ = zd_f[ti % 2], zd_f[(ti + 1) % 2]
            V("tensor_copy", zc_[0:P, :], PF[0][0:P, 0:256], R=[PF[0]], W=[zc_])
            yield
            if last:
                if grp == "p":
                    S.dma(st["o_pool"][l], zc_[P - 15:P, :], R=[zc_], key=zc_)
                else:
                    S.dma(st["o_pool"][l], zc_[1:16, :], R=[zc_], key=zc_)
            for g in range(4):
                M("matmul", PF[1][0:P, 64 * g:64 * g + 64], cf[0:P, CF_BAND + 128 * g:CF_BAND + 128 * g + P], zc_[0:P, 64 * g:64 * g + 64],
                  start=True, stop=False, R=[cf, zc_], W=[PF[1]])
                M("matmul", PF[1][0:P, 64 * g:64 * g + 64], cf[:, CF_BANDP + 128 * g:CF_BANDP + 128 * g + P], zp_[:, 64 * g:64 * g + 64],
                  start=False, stop=True, R=[cf, zp_], W=[PF[1]])
            for g in range(4):
                ic = invc[0:P, g:g + 1] if (first and grp == "p") else invc[0:P, 4 + g:5 + g]
                V("scalar_tensor_tensor", d_b[0:P, 64 * g:64 * g + 64], PF[1][0:P, 64 * g:64 * g + 64], ic, zc_[0:P, 64 * g:64 * g + 64],
                  ALU.mult, ALU.subtract, R=[PF[1], invc, zc_], W=[d_b])
            yield
            for c in range(2):
                M("transpose", PB[1][:, c * P:(c + 1) * P], d_b[0:P, c * 128:(c + 1) * 128], identb(P), R=[d_b, cb], W=[PB[1]])
            A("copy", dT[:, :, 0:P], PB[1][:, 0:2 * P].rearrange("p (c t) -> p c t", c=2), R=[PB[1]], W=[dT])
            for c in range(2):
                M("matmul", PF[0][0:P, c * 128:(c + 1) * 128], dT[:, c, 0:P], poolw_b[:, c, :], start=True, stop=True, R=[dT, poolw_b], W=[PF[0]])
            yield
            V("tensor_tensor", ygp[0:P, 512:768], PF[0][0:P, 0:256], sg[0:P, 768:1024], ALU.mult, R=[PF[0], sg], W=[ygp])
            yield

        gr, go = gen_R(), gen_O()
        if P == 128 and ILV > 0:
            for tag in gr:
                if tag == "mm":
                    for _k in range(ILV):
                        next(go, None)
                elif tag == "ev":
                    for _k in range(ILV2):
                        next(go, None)
        for _ in gr:
            pass
        for _ in go:
            pass
        S.dma(sc["yg"][tok0:tok0 + P, :], ygp[0:P, :], R=[ygp], key=ygp)

    def phase2(l, grp, Tq, QB, nkt_total, klast, xsrc, ydst, sc, KT, VV, bufs):
        (QTb, PT, yfull4, yT, gbt4, xres, xout, rr) = bufs
        nqb = Tq // QB
        nsub = max(1, QB // 128)
        Pq = min(QB, 128)
        for qb in range(nqb):
            q0 = qb * QB
            S.dma(QTb[:, :, 0:QB], sc["qt"][:, :, q0:q0 + QB].rearrange("h d t -> d h t"), W=[QTb], key=QTb)
            for i in range(nsub):
                t0 = q0 + i * 128
                S.dma(yfull4[i][0:Pq, 0:256], sc["yg"][t0:t0 + Pq, 0:256], W=[yfull4[i]], key=yfull4[i])
                S.dma(yfull4[i][0:Pq, 512:1024], sc["yg"][t0:t0 + Pq, 256:768], W=[yfull4[i]], key=yfull4[i])
                S.dma(gbt4[i][0:Pq, :], sc["gb"][t0:t0 + Pq, :], W=[gbt4[i]], key=gbt4[i])
            if grp == "p":
                nkt = 4 * qb + 4
            else:
                nkt = nkt_total
            def kinfo(kt):
                kp_ = 128 if (grp == "p" or kt < nkt_total - 1) else klast
                jd = kt - 4 * qb if grp == "p" else -1
                c0 = 128 * jd if jd > 0 else 0
                return kp_, jd, c0

            def emit_scores(h, kt, j):
                kp_, jd, c0 = kinfo(kt)
                n = QB - c0
                sbk = (PF[4], PF[5], PB[1])[j % 3]
                sview = sbk[0:kp_, 0:n] if (j % 3) < 2 else PB[1][0:kp_, 0:1024].bitcast(F32)[:, 0:n]
                M("matmul", sview, KT[:, h, kt * 128:kt * 128 + kp_], QTb[:, h, c0:QB], start=True, stop=True,
                  R=[KT, QTb], W=[sbk])
                pt_ = PT[j % 3]
                A("activation", out=pt_[0:kp_, c0:QB], in_=sview, func=AF.Exp, scale=float(1.0 / np.sqrt(96.0)),
                  R=[sbk], W=[pt_])
                if jd >= 0:
                    G("memset", pt_[64:128, c0:c0 + 64], 0.0, W=[pt_])

            def emit_pv(h, kt, j):
                kp_, jd, c0 = kinfo(kt)
                pt_ = PT[j % 3]
                for i in range(c0 // 128, nsub):
                    first_k = (kt == 0)
                    last_k = (kt == (4 * qb + i if grp == "p" else nkt - 1))
                    M("matmul", PF[i][0:Pq, 0:65], pt_[0:kp_, i * 128:i * 128 + Pq], VV[0:kp_, kt, 65 * h:65 * h + 65],
                      start=first_k, stop=last_k, R=[pt_, VV], W=[PF[i]], inc=(i == nsub - 1))

            steps = [(h, kt) for h in range(4) for kt in range(nkt)]
            for j in range(min(2, len(steps))):
                emit_scores(steps[j][0], steps[j][1], j)
            for j, (h, kt) in enumerate(steps):
                if j + 2 < len(steps):
                    emit_scores(steps[j + 2][0], steps[j + 2][1], j + 2)
                emit_pv(h, kt, j)
                if kt == nkt - 1:
                    for i in range(nsub):
                        V("reciprocal", rr[0:Pq, h:h + 1], PF[i][0:Pq, 64:65], R=[PF[i]], W=[rr])
                        V("scalar_tensor_tensor", yfull4[i][0:Pq, 256 + 64 * h:256 + 64 * h + 64], PF[i][0:Pq, 0:64], rr[0:Pq, h:h + 1],
                          gbt4[i][0:Pq, 64 * h:64 * h + 64], ALU.mult, ALU.mult, R=[PF[i], rr, gbt4[i]], W=[yfull4[i]])
            for i in range(nsub):
                t0 = q0 + i * 128
                yfull = yfull4[i]
                S.dma(xres[0:Pq, :], xsrc[t0:t0 + Pq, :], W=[xres], key=xres)
                for k in range(8):
                    M("transpose", PB[0][:, k * Pq:(k + 1) * Pq], yfull[0:Pq, k * 128:(k + 1) * 128], identb(Pq), R=[yfull, cb], W=[PB[0]], inc=(k == 7))
                A("copy", yT[:, :, 0:Pq], PB[0][:, 0:8 * Pq].rearrange("p (k t) -> p k t", k=8), R=[PB[0]], W=[yT])
                for cbk in range(2):
                    bank = PF[4 + cbk]
                    for k in range(8):
                        M("matmul", bank[0:Pq, :], yT[:, k, 0:Pq], wout_b[:, k, cbk * 512:(cbk + 1) * 512], start=(k == 0), stop=(k == 7),
                          R=[yT, wout_b], W=[bank], inc=(k == 7))
                    V("tensor_add", xout[0:Pq, cbk * 512:(cbk + 1) * 512], bank[0:Pq, :], xres[0:Pq, cbk * 512:(cbk + 1) * 512],
                      R=[bank, xres], W=[xout])
                S.dma(ydst[t0:t0 + Pq, :], xout[0:Pq, :], R=[xout], key=xout, Q=S.pool)

    stp = dict(o_shift=o_shp, o_wkv=o_wkvp, o_pool=o_plp, o_ckv=o_ckvp, o_kr=o_krp, ktok0=0)
    sts = dict(o_shift=o_shs, o_wkv=o_wkvs, o_pool=o_pls, o_ckv=o_ckvs, o_kr=o_krs, ktok0=PAST)
    nlev_p = 7
    nlev_s = 4
    for l in range(NL):
        load_small(l)
        xsrc_p = x_p if l == 0 else o_yp
        xsrc_s = x_s if l == 0 else o_ys
        with ExitStack() as c1:
            alloc_work(c1)
            win_b = sb("win_b", [128, 8, DIN], BF16, c1)
            def cache_tiles():
                for kt in range(PAST // 128):
                    cs_ = cstage[kt % 2]
                    S.dma(cs_[:, 0:128], c_ckv[l, kt * 128:(kt + 1) * 128, :], W=[cs_], key=cs_)
                    S.dma(cs_[:, 128:160], c_kr[l, kt * 128:(kt + 1) * 128, :], W=[cs_], key=cs_)
                    kv_from_ckv(128, cs_[:, 0:128], cs_[:, 128:160], [cs_], scr_s, kt * 128, Q=S.pool)
                    yield
            gw = load_win(l, win_b)
            gc = cache_tiles() if do_sample else iter(())
            for _ in gw:
                next(gc, None)
                next(gc, None)
            for _ in gc:
                pass
            V("memset", tmpm[1][:], 0.0, W=[tmpm[1]])
            V("memset", zd_f[1][:], 0.0, W=[zd_f[1]])
            V("memset", ST[:], 0.0, W=[ST])
            S.dma(xt[0][:], xsrc_p[0:128, :], W=[xt[0]], key=xt[0])
            S.dma(ropet[0][:], rope_p[0:128, :], W=[ropet[0]], key=ropet[0])
            for ti in range(NT):
                if ti + 1 < NT:
                    S.dma(xt[(ti + 1) % 2][:], xsrc_p[(ti + 1) * 128:(ti + 2) * 128, :], W=[xt[(ti + 1) % 2]], key=xt[(ti + 1) % 2])
                    S.dma(ropet[(ti + 1) % 2][:], rope_p[(ti + 1) * 128:(ti + 2) * 128, :], W=[ropet[(ti + 1) % 2]], key=ropet[(ti + 1) % 2])
                phase1_tile(l, "p", ti, 128, xsrc_p, win_b, scr_p, ti == 0, ti == NT - 1, ti * 128, nlev_p, stp)
            if do_sample:
                P = TS
                V("memset", stage[:, 0:896], 0.0, W=[stage])
                S.dma(stage[127:128, 0:896], st_shift[l:l + 1, :], W=[stage], key=stage)
                V("tensor_tensor", tmpm[1][:, :], stage[:, 0:896], BCv("rw_mu", 128), ALU.mult, R=[stage, bc], W=[tmpm[1]])
                V("memset", zd_f[1][:], 0.0, W=[zd_f[1]])
                S.dma(zd_f[1][113:128, :], st_pool[l], W=[zd_f[1]], key=zd_f[1])
                S.dma(stage2[0:64, 0:256].rearrange("i (h j) -> i h j", h=4), st_wkv[l].rearrange("h i j -> i h j"), W=[stage2], key=stage2)
                for p in range(2):
                    M("transpose", PF[2][:, p * 64:(p + 1) * 64], stage2[0:64, p * 128:(p + 1) * 128], identf(64), R=[stage2, cf], W=[PF[2]])
                V("tensor_copy", ST[:].rearrange("k p v -> k (p v)"), PF[2][:, 0:128], R=[PF[2]], W=[ST])
                S.dma(xt[0][0:P, :], xsrc_s[0:P, :], W=[xt[0]], key=xt[0])
                S.dma(ropet[0][0:P, :], rope_s[0:P, :], W=[ropet[0]], key=ropet[0])
                phase1_tile(l, "s", 0, P, xsrc_s, win_b, scr_s, True, True, 0, nlev_s, sts)
            S.barrier()
        with ExitStack() as c2:
            KT = sb("KT", [96, 4, max(T, PAST + 128)], BF16, c2)
            VV = sb("VV", [128, max(NT, PAST // 128 + 1), 260], BF16, c2)
            QTb = sb("QTb", [96, 4, 512], BF16, c2)
            PT = [sb("PT%d" % i, [128, 512], BF16, c2) for i in range(3)]
            yfull4 = [sb("yfull%d" % i, [128, D], BF16, c2) for i in range(4)]
            yT = sb("yT", [128, 8, 128], BF16, c2)
            gbt4 = [sb("gbt%d" % i, [128, 256], BF16, c2) for i in range(4)]
            xres = sb("xres", [128, D], F32, c2)
            xout = sb("xout", [128, D], F32, c2)
            rr = sb("rr", [128, 4], F32, c2)
            bufs = (QTb, PT, yfull4, yT, gbt4, xres, xout, rr)
            for h in range(4):
                S.dma(KT[:, h, 0:T], scr_p["kt"][h], W=[KT], key=KT)
            vview = scr_p["v"].rearrange("(n p) c -> p n c", p=128)
            for n0 in range(0, NT, 8):
                n1 = min(NT, n0 + 8)
                S.dma(VV[:, n0:n1, :], vview[:, n0:n1, :], W=[VV], key=VV)
            phase2(l, "p", T, 512, NT, 128, xsrc_p, o_yp, scr_p, KT, VV, bufs)
            if do_sample:
                NKS = PAST // 128 + 1
                for h in range(4):
                    S.dma(KT[:, h, 0:PAST + TS], scr_s["kt"][h][:, 0:PAST + TS], W=[KT], key=KT)
                vview = scr_s["v"].rearrange("(n p) c -> p n c", p=128)
                for n0 in range(0, NKS - 1, 8):
                    n1 = min(NKS - 1, n0 + 8)
                    S.dma(VV[:, n0:n1, :], vview[:, n0:n1, :], W=[VV], key=VV)
                S.dma(VV[0:TS, NKS - 1, :], scr_s["v"][PAST:PAST + TS, :], W=[VV], key=VV)
                phase2(l, "s", TS, TS, NKS, TS, xsrc_s, o_ys, scr_s, KT, VV, bufs)
            S.barrier()
    S.final_wait()
    ctx.close()
    return nc, S.ninst


_CACHE = {}


def kernel(**inputs):
    x_prompt = np.asarray(inputs["x_prompt"], np.float32)
    x_sample = np.asarray(inputs["x_sample"], np.float32)
    B, T, _ = x_prompt.shape
    NL = inputs["norm_g"].shape[0]
    key = (T, NL)
    if key not in _CACHE:
        _CACHE[key] = build(T, NL)[0]
    nc = _CACHE[key]
    cf, cb, invc = make_consts()
    rope_p = rope_table(np.arange(T))
    rope_s = rope_table(PAST + np.arange(TS))
    in_maps = []
    for c in range(8):
        m = {
            "x_p": np.ascontiguousarray(x_prompt[c // 2]),
            "x_s": np.ascontiguousarray(x_sample[c]),
            "c_ckv": np.ascontiguousarray(np.asarray(inputs["cache_ckv"], np.float32)[:, c]),
            "c_kr": np.ascontiguousarray(np.asarray(inputs["cache_krope"], np.float32)[:, c]),
            "st_wkv": np.ascontiguousarray(np.asarray(inputs["state_wkv"], np.float32)[:, c]),
            "st_shift": np.ascontiguousarray(np.asarray(inputs["state_shift"], np.float32)[:, c]),
            "st_pool": np.ascontiguousarray(np.asarray(inputs["state_pool"], np.float32)[:, c]),
            "cf": cf, "cb": cb, "invc": invc, "rope_p": rope_p, "rope_s": rope_s,
        }
        for n in WNAMES:
            m[n] = np.ascontiguousarray(np.asarray(inputs[n], np.float32).reshape([NL] + WSHAPES[n]))
        in_maps.append(m)
    res = run_bass_kernel_spmd(nc, in_maps, core_ids=list(range(8)))
    R = res.results
    pc = [R[2 * b] for b in range(B)]
    sc = [R[c] for c in range(8)]
    stk = lambda L, n, ax: np.stack([np.asarray(r[n], np.float32) for r in L], axis=ax)
    out = (
        stk(pc, "o_yp", 0), stk(sc, "o_ys", 0),
        stk(pc, "o_ckvp", 1), stk(pc, "o_krp", 1), stk(pc, "o_wkvp", 1), stk(pc, "o_shp", 1), stk(pc, "o_plp", 1),
        stk(sc, "o_ckvs", 1), stk(sc, "o_krs", 1), stk(sc, "o_wkvs", 1), stk(sc, "o_shs", 1), stk(sc, "o_pls", 1),
        stk(sc, "o_sgv", 1),
    )
    return out
```

```python
import numpy as np
from contextlib import ExitStack
import concourse.bass as bass
import concourse.mybir as mybir
from concourse.bass_utils import run_bass_kernel_spmd

F32 = mybir.dt.float32
BF16 = mybir.dt.bfloat16
AF = mybir.ActivationFunctionType
ALU = mybir.AluOpType
AX = mybir.AxisListType

D = 1024
DIN = 3040
OFF_A, OFF_B, OFF_C, OFF_D, OFF_G = 0, 896, 1248, 1760, 2016
PAST = 2048
TS = 16
WINS = (2, 4, 8, 16)
CDEC = -0.6065306597126334
ILV = 1
ILV2 = 2


class Buf:
    def __init__(self, name):
        self.name = name
        self.w = None
        self.r = []


class TB:
    def __init__(self, t, name):
        self.t = t
        self.b = Buf(name)

    def __getitem__(self, k):
        return self.t[k]


class Eng:
    def __init__(self, S, name, eng):
        self.name = name
        self.eng = eng
        self.sem = S.newsem("e_" + name)
        self.cnt = 0
        self.waited = {}

    def wait_tok(self, tok):
        if tok is None:
            return
        sem, val = tok
        if sem is self.sem and self.name == "pe":
            return
        key = id(sem)
        if self.waited.get(key, 0) >= val:
            return
        self.eng.wait_ge(sem, val)
        self.waited[key] = val


class Sched:
    def __init__(self, nc, ctx):
        self.nc = nc
        self.ctx = ctx
        self.pe = Eng(self, "pe", nc.tensor)
        self.act = Eng(self, "act", nc.scalar)
        self.dve = Eng(self, "dve", nc.vector)
        self.pool = Eng(self, "pool", nc.gpsimd)
        self.sp = Eng(self, "sp", nc.sync)
        self.engs = [self.pe, self.act, self.dve, self.pool, self.sp]
        self.dma_sems = {}
        self.ninst = 0

    def newsem(self, name):
        return self.ctx.enter_context(self.nc.semaphore(name))

    def _bufs(self, L):
        return [x.b if isinstance(x, TB) else x for x in L]

    def deps(self, E, R, W):
        for b in R:
            E.wait_tok(b.w)
        for b in W:
            E.wait_tok(b.w)
            for t in (b.r.values() if isinstance(b.r, dict) else b.r):
                E.wait_tok(t)

    def _mark(self, tok, R, W):
        for b in W:
            b.w = tok
            b.r = {}
        for b in R:
            if isinstance(b.r, list):
                b.r = {}
            k = id(tok[0])
            if k not in b.r or b.r[k][1] < tok[1]:
                b.r[k] = tok

    enabled = True

    def ck(self, name):
        import os
        if os.environ.get("KSTOP") == name:
            self.enabled = False

    def op(self, E, fn, *args, R=(), W=(), **kw):
        if not self.enabled:
            return None
        R = self._bufs(R)
        W = self._bufs(W)
        inc = kw.pop("inc", True)
        self.deps(E, R, W)
        ins = getattr(E.eng, fn)(*args, **kw)
        self.ninst += 1
        if inc or E is not self.pe:
            E.cnt += 1
            ins.then_inc(E.sem, 1)
            self._mark((E.sem, E.cnt), R, W)
        else:
            self._mark((E.sem, E.cnt + 1), R, W)
        return ins

    def dma(self, out, in_, R=(), W=(), key=None, Q=None, **kw):
        if not self.enabled:
            return None
        Q = Q or self.sp
        R = self._bufs(R)
        W = self._bufs(W)
        kb = key.b if isinstance(key, TB) else key
        self.deps(Q, R, W)
        if kb.name not in self.dma_sems:
            self.dma_sems[kb.name] = [self.newsem("d_" + kb.name), 0]
        ent = self.dma_sems[kb.name]
        ins = Q.eng.dma_start(out=out, in_=in_, **kw)
        ent[1] += 16
        ins.then_inc(ent[0], 16)
        self.ninst += 1
        self._mark((ent[0], ent[1]), R, W)
        return ins

    def all_tokens(self):
        toks = [(e.sem, e.cnt) for e in self.engs if e.cnt > 0]
        toks += [(s, v) for (s, v) in self.dma_sems.values()]
        return toks

    def barrier(self):
        toks = self.all_tokens()
        for e in self.engs:
            for t in toks:
                e.wait_tok(t)

    def final_wait(self):
        for (s, v) in self.dma_sems.values():
            self.sp.wait_tok((s, v))


def make_consts():
    c = {}
    i = np.arange(128)
    s = i[:, None]
    t = i[None, :]
    ident = (s == t).astype(np.float32)
    tri_ui = (s <= t).astype(np.float32)
    tri_ut = (s < t).astype(np.float32)
    tri_lt = (s > t).astype(np.float32)
    bands = []
    bandp = []
    for w in WINS:
        bands.append(((s <= t) & (s > t - w)).astype(np.float32))
        bandp.append(((s - 128) > (t - w)).astype(np.float32))
    ones = np.ones((128, 128), np.float32)
    onehot_last = np.zeros((128, 2), np.float32)
    onehot_last[127, 0] = 1.0
    onehot_last[15, 1] = 1.0
    cf = np.concatenate([ident, tri_ui, ones] + bands + bandp + [onehot_last], axis=1)
    sh = (t == s + 1).astype(np.float32)
    elast = np.zeros((128, 128), np.float32)
    elast[127, 0] = 1.0
    cb = np.concatenate([ident, tri_lt, tri_ut, tri_lt, tri_ui, sh, elast], axis=1)
    invc = np.zeros((128, 24), np.float32)
    invc[:, 8:11] = np.array([1 / 192.0, 1 / 128.0, 1 / 32.0], np.float32)
    invc[:, 11:15] = 1 / 64.0
    invc[:, 15:19] = 1 / 32.0
    for g, w in enumerate(WINS):
        invc[:, g] = 1.0 / np.minimum(i + 1, w)
        invc[:, 4 + g] = 1.0 / w
    return cf.astype(np.float32), cb.astype(np.float32), invc


def rope_table(pos):
    half = 16
    inv = (10000.0 ** (-np.arange(half, dtype=np.float32) / half)).astype(np.float32)
    ang = pos.astype(np.float32)[:, None] * inv[None, :]
    cos = np.cos(ang).astype(np.float32)
    sin = np.sin(ang).astype(np.float32)
    return np.concatenate([np.tile(cos, (1, 4)), np.tile(sin, (1, 4))], axis=1).astype(np.float32)


CF_ID, CF_TRI, CF_ONES, CF_BAND, CF_BANDP, CF_OH = 0, 128, 256, 384, 896, 1408
CB_ID, CB_M3, CB_UI, CB_SH, CB_EL = 0, 128, 512, 640, 768

WNAMES = ["norm_g", "w_in", "w_out", "rw_mu", "rw_w0", "rw_w2", "rw_a0", "rw_a2", "rw_kk", "rw_ka", "rw_rk",
          "rw_gn_g", "rw_gn_b", "mla_qa_g", "mla_w_uq", "mla_kva_g", "mla_w_uk", "mla_w_uv",
          "mla_q_norm_g", "mla_k_norm_g", "sgu_w", "sgu_b", "sgu_ln_g", "sgu_ln_b", "pool_w", "pool_scale"]
WSHAPES = {
    "norm_g": [D], "w_in": [D, DIN], "w_out": [D, D], "rw_mu": [896], "rw_w0": [256], "rw_w2": [64, 256],
    "rw_a0": [256], "rw_a2": [64, 256], "rw_kk": [256], "rw_ka": [256], "rw_rk": [256], "rw_gn_g": [256],
    "rw_gn_b": [256], "mla_qa_g": [192], "mla_w_uq": [192, 384], "mla_kva_g": [128], "mla_w_uk": [128, 256],
    "mla_w_uv": [128, 256], "mla_q_norm_g": [96], "mla_k_norm_g": [96], "sgu_w": [4, 128, 128], "sgu_b": [4, 128],
    "sgu_ln_g": [256], "sgu_ln_b": [256], "pool_w": [4, 64, 64], "pool_scale": [256],
}


def build(T, NL, do_sample=True):
    nc = bass.Bass("TRN2", target_bir_lowering=False)
    ctx = ExitStack()
    NT = T // 128

    def din(name, shape, dt=F32):
        return nc.dram_tensor(name, list(shape), dt, kind="ExternalInput").ap()

    def dout(name, shape, dt=F32):
        return nc.dram_tensor(name, list(shape), dt, kind="ExternalOutput").ap()

    def dscr(name, shape, dt):
        return nc.dram_tensor(name, list(shape), dt, kind="Internal").ap()

    x_p = din("x_p", [T, D])
    x_s = din("x_s", [TS, D])
    c_ckv = din("c_ckv", [NL, PAST, 128])
    c_kr = din("c_kr", [NL, PAST, 32])
    st_wkv = din("st_wkv", [NL, 4, 64, 64])
    st_shift = din("st_shift", [NL, 896])
    st_pool = din("st_pool", [NL, 15, 256])
    Wd = {n: din(n, [NL] + WSHAPES[n]) for n in WNAMES}
    cf_d = din("cf", [128, 1410])
    cb_d = din("cb", [128, 896])
    invc_d = din("invc", [128, 24])
    rope_p = din("rope_p", [T, 128])
    rope_s = din("rope_s", [TS, 128])

    o_yp = dout("o_yp", [T, D])
    o_ys = dout("o_ys", [TS, D])
    o_ckvp = dout("o_ckvp", [NL, T, 128])
    o_krp = dout("o_krp", [NL, T, 32])
    o_wkvp = dout("o_wkvp", [NL, 4, 64, 64])
    o_shp = dout("o_shp", [NL, 896])
    o_plp = dout("o_plp", [NL, 15, 256])
    o_ckvs = dout("o_ckvs", [NL, TS, 128])
    o_krs = dout("o_krs", [NL, TS, 32])
    o_wkvs = dout("o_wkvs", [NL, 4, 64, 64])
    o_shs = dout("o_shs", [NL, 896])
    o_pls = dout("o_pls", [NL, 15, 256])
    o_sgv = dout("o_sgv", [NL, TS, 256])

    def scr(tag, TT, TK):
        return dict(
            yg=dscr("yg_" + tag, [TT, 768], BF16), gb=dscr("gb_" + tag, [TT, 256], BF16),
            qt=dscr("qt_" + tag, [4, 96, TT], BF16), kt=dscr("kt_" + tag, [4, 96, TK], BF16),
            v=dscr("v_" + tag, [TK, 260], BF16))
    scr_p = scr("p", T, T)
    scr_s = scr("s", TS, PAST + 128)

    S = Sched(nc, ctx)
    V = lambda fn, *a, **k: S.op(S.dve, fn, *a, **k)
    A = lambda fn, *a, **k: S.op(S.act, fn, *a, **k)
    G = lambda fn, *a, **k: S.op(S.pool, fn, *a, **k)
    M = lambda fn, *a, **k: S.op(S.pe, fn, *a, **k)

    uid = [0]

    def uname(name):
        uid[0] += 1
        return "s%d_%s" % (uid[0], name)

    def sb(name, shape, dt, c=None):
        t = (c or ctx).enter_context(nc.sbuf_tensor(uname(name), list(shape), dt))
        return TB(t, name)

    def ps(name, shape, dt):
        t = ctx.enter_context(nc.psum_tensor(name, list(shape), dt))
        return TB(t, name)

    PF = [ps("pf%d" % i, [128, 512], F32) for i in range(6)]
    PB = [ps("pb%d" % i, [128, 1024], BF16) for i in range(2)]

    cf = sb("cf", [128, 1410], F32)
    cb = sb("cb", [128, 896], BF16)
    cb_stage = sb("cb_stage", [128, 896], F32)
    invc = sb("invc", [128, 24], F32)
    S.dma(cf[:], cf_d[:, :], W=[cf], key=cf)
    S.dma(cb_stage[:], cb_d[:, :], W=[cb_stage], key=cb_stage)
    S.dma(invc[:], invc_d[:, :], W=[invc], key=invc)
    V("tensor_copy", cb[:], cb_stage[:], R=[cb_stage], W=[cb])
    cbr = sb("cbr", [128, 4, 4, 128], BF16)
    for kind, off in enumerate((CB_M3, CB_M3 + 128, CB_UI, CB_ID)):
        for h in range(4):
            V("tensor_copy", cbr[:, kind, h, :], cb[:, off:off + 128], R=[cb], W=[cbr])
    identf = lambda n: cf[0:n, CF_ID:CF_ID + n]
    identb = lambda n: cb[0:n, CB_ID:CB_ID + n]

    wout_b = sb("wout_b", [128, 8, D], BF16)
    wuq_b = sb("wuq_b", [128, 2, 384], BF16)
    wukv_b = sb("wukv_b", [128, 512], BF16)
    w2a2_b = sb("w2a2_b", [128, 512], BF16)
    wsT_b = sb("wsT_b", [128, 4, 128], BF16)
    poolw_b = sb("poolw_b", [128, 2, 128], BF16)
    sgub = sb("sgub", [128, 4], F32)
    ng = sb("ng", [128, 8], F32)
    qag = sb("qag", [128, 2], F32)
    BC_SPEC = [("rw_mu", 896), ("omm", 896), ("rw_w0", 256), ("rw_a0", 256), ("rw_kk", 256), ("rw_ka", 256),
               ("rw_rk", 256), ("rw_gn_g", 256), ("rw_gn_b", 256), ("mla_kva_g", 128), ("mla_q_norm_g", 96),
               ("mla_k_norm_g", 96), ("sgu_ln_g", 256), ("sgu_ln_b", 256)]
    bc_off = {}
    o = 0
    for n, w in BC_SPEC:
        bc_off[n] = (o, w)
        o += w
    bc = sb("bc", [128, o], F32)
    BCv = lambda n, P, a=0, b=None: bc[0:P, bc_off[n][0] + a: bc_off[n][0] + (bc_off[n][1] if b is None else b)]
    stage = sb("stage", [128, 1024], F32)
    stage2 = sb("stage2", [128, 1024], F32)

    WORK = []

    def wsb(name, shape, dt):
        tb = TB(None, name)
        WORK.append((tb, name, list(shape), dt))
        return tb

    def alloc_work(c):
        for tb, name, shape, dt in WORK:
            tb.t = c.enter_context(nc.sbuf_tensor(uname(name), shape, dt))
            tb.b = Buf(name)
        for i in range(4):
            STb[i].w = None
            STb[i].r = {}
        V("memset", vaug[:], 1.0, W=[vaug])
        V("memset", x_f[:], 0.0, W=[x_f])
        V("memset", p_f[:], 0.0, W=[p_f])
        V("memset", qaT[:], 0.0, W=[qaT])

    xt = [wsb("xt%d" % i, [128, D], F32) for i in range(2)]
    ropet = [wsb("ropet%d" % i, [128, 128], F32) for i in range(2)]
    junk = wsb("junk", [128, D], BF16)
    xn_b = wsb("xn_b", [128, D], BF16)
    xnT = wsb("xnT", [128, 8, 128], BF16)
    st1 = wsb("st1", [128, 16], F32)
    st2 = wsb("st2", [128, 16], F32)
    st3 = wsb("st3", [128, 16], F32)
    cstage = [wsb("cstage%d" % i, [128, 160], F32) for i in range(2)]
    st2o = wsb("st2o", [128, 16], F32)
    st3o = wsb("st3o", [128, 16], F32)
    w1o = wsb("w1o", [128, 256], F32)
    w2o = wsb("w2o", [128, 64], F32)
    sg = wsb("sg", [128, D], BF16)
    tmpm = [wsb("tmpm%d" % i, [128, 896], BF16) for i in range(2)]
    zs = wsb("zs", [128, 896], F32)
    lin = wsb("lin", [128, 128], BF16)
    linT = wsb("linT", [128, 128], BF16)
    w1 = wsb("w1", [128, 256], F32)
    w2 = wsb("w2", [128, 256], F32)
    w3 = wsb("w3", [128, 256], F32)
    sw = wsb("sw", [128, 256], F32)
    sa = wsb("sa", [128, 256], F32)
    kk = wsb("kk", [128, 256], F32)
    kp = wsb("kp", [128, 256], F32)
    bb = wsb("bb", [128, 256], F32)
    bcf = wsb("bcf", [128, 4], F32)
    cs_sb = wsb("cs_sb", [128, 256], F32)
    e1 = wsb("e1", [128, 256], F32)
    e2 = wsb("e2", [128, 256], F32)
    e3 = wsb("e3", [128, 256], F32)
    e4 = wsb("e4", [128, 256], F32)
    hat = wsb("hat", [128, 4, 256], BF16)
    bp_b = wsb("bp_b", [128, 256], BF16)
    kp4 = wsb("kp4", [128, 256], F32)
    hT = wsb("hT", [128, 4, 2, 128], BF16)
    wc = wsb("wc", [128, 2], F32)
    scL = wsb("scL", [128, 4, 128], BF16)
    scN = wsb("scN", [128, 4, 128], BF16)
    scK = wsb("scK", [128, 4, 128], BF16)
    mbr_b = wsb("mbr_b", [128, 4, 128], BF16)
    mkr_f = wsb("mkr_f", [128, 4, 128], F32)
    Ab = [wsb("Ab%d" % i, [128, 2, 4, 128], BF16) for i in range(2)]
    Ttb = [wsb("Ttb%d" % i, [128, 4, 128], BF16) for i in range(2)]
    G_b = wsb("G_b", [128, 4, 128], BF16)
    H_b = wsb("H_b", [128, 4, 64], BF16)
    x_f = wsb("x_f", [128, 4, 128], F32)
    z_f = wsb("z_f", [128, 4, 128], F32)
    q_f = wsb("q_f", [128, 2, 128], F32)
    p_f = wsb("p_f", [128, 2, 128], F32)
    ST = wsb("ST", [128, 2, 64], F32)
    STb = [Buf("ST%d" % h) for h in range(4)]
    yrw = wsb("yrw", [128, 256], F32)
    ygp = wsb("ygp", [128, 768], BF16)
    zb_f = wsb("zb_f", [128, 384], F32)
    qa_b = wsb("qa_b", [128, 192], BF16)
    qaT = wsb("qaT", [128, 2, 128], BF16)
    qn = wsb("qn", [128, 4, 96], F32)
    qfb = wsb("qfb", [128, 4, 96], BF16)
    qT = wsb("qT", [96, 4, 128], BF16)
    ckv_f = wsb("ckv_f", [128, 128], F32)
    ckv_b = wsb("ckv_b", [128, 128], BF16)
    ckvT = wsb("ckvT", [128, 128], BF16)
    kr_f = wsb("kr_f", [128, 32], F32)
    kr_r = wsb("kr_r", [128, 32], F32)
    kfull = wsb("kfull", [128, 4, 96], BF16)
    kT = wsb("kT", [96, 4, 128], BF16)
    vaug = wsb("vaug", [128, 4, 65], BF16)
    ug = wsb("ug", [128, 256], F32)
    vn = wsb("vn", [128, 256], F32)
    vn_b = wsb("vn_b", [128, 256], BF16)
    bnst = wsb("bnst", [128, 8], F32)
    zd_f = [wsb("zd_f%d" % i, [128, 256], F32) for i in range(2)]
    d_b = wsb("d_b", [128, 256], BF16)
    dT = wsb("dT", [128, 2, 128], BF16)
    za_f = wsb("za_f", [128, 896], F32)
    wkv_o = wsb("wkv_o", [64, 4, 64], F32)


    def load_small(l):
        P = 128
        for n, w in BC_SPEC:
            if n == "omm":
                continue
            S.dma(BCv(n, P), Wd[n][l].partition_broadcast(128), W=[bc], key=bc)
        V("tensor_scalar", BCv("omm", P), BCv("rw_mu", P), -1.0, 1.0, ALU.mult, ALU.add, R=[bc], W=[bc])
        S.dma(ng[:], Wd["norm_g"][l].rearrange("(k p) -> p k", p=128), W=[ng], key=ng, allow_slow_non_contiguous=True)
        S.dma(qag[:, 0:1], Wd["mla_qa_g"][l][0:128].rearrange("(p o) -> p o", o=1), W=[qag], key=qag)
        S.dma(qag[0:64, 1:2], Wd["mla_qa_g"][l][128:192].rearrange("(p o) -> p o", o=1), W=[qag], key=qag)
        S.dma(sgub[:], Wd["sgu_b"][l].rearrange("h i -> i h"), W=[sgub], key=sgub, allow_slow_non_contiguous=True)
        for k in range(8):
            st_ = stage if k % 2 == 0 else stage2
            S.dma(st_[:, 0:D], Wd["w_out"][l][k * 128:(k + 1) * 128, :], W=[st_], key=st_)
            V("tensor_copy", wout_b[:, k, :], st_[:, 0:D], R=[st_], W=[wout_b])
        S.dma(stage[:, 0:384], Wd["mla_w_uq"][l][0:128, :], W=[stage], key=stage)
        V("tensor_scalar", wuq_b[:, 0, :], stage[:, 0:384], qag[:, 0:1], None, ALU.mult, R=[stage, qag], W=[wuq_b])
        S.dma(stage[0:64, 0:384], Wd["mla_w_uq"][l][128:192, :], W=[stage], key=stage)
        V("memset", wuq_b[:, 1, :], 0.0, W=[wuq_b])
        V("tensor_scalar", wuq_b[0:64, 1, :], stage[0:64, 0:384], qag[0:64, 1:2], None, ALU.mult, R=[stage, qag], W=[wuq_b])
        S.dma(stage[:, 0:256], Wd["mla_w_uk"][l], W=[stage], key=stage)
        S.dma(stage[:, 256:512], Wd["mla_w_uv"][l], W=[stage], key=stage)
        V("tensor_copy", wukv_b[:], stage[:, 0:512], R=[stage], W=[wukv_b])
        V("memset", stage[:, 0:512], 0.0, W=[stage])
        S.dma(stage[0:64, 0:256], Wd["rw_w2"][l], W=[stage], key=stage)
        S.dma(stage[64:128, 256:512], Wd["rw_a2"][l], W=[stage], key=stage)
        V("tensor_copy", w2a2_b[:], stage[:, 0:512], R=[stage], W=[w2a2_b])
        S.dma(stage[:, 0:512].rearrange("p (h j) -> p h j", h=4), Wd["sgu_w"][l].rearrange("h i j -> i h j"), W=[stage], key=stage)
        for h in range(4):
            V("tensor_tensor", stage2[:, h * 128:(h + 1) * 128], stage[:, h * 128:(h + 1) * 128], cb[:, CB_M3 + 128:CB_M3 + 256],
              ALU.mult, R=[stage, cb], W=[stage2])
            V("tensor_sub", stage2[:, h * 128:(h + 1) * 128], stage[:, h * 128:(h + 1) * 128], stage2[:, h * 128:(h + 1) * 128],
              R=[stage, stage2], W=[stage2])
            M("transpose", PF[5][:, h * 128:(h + 1) * 128], stage2[:, h * 128:(h + 1) * 128], identf(128), R=[stage2, cf], W=[PF[5]])
        V("tensor_copy", wsT_b[:].rearrange("p h i -> p (h i)"), PF[5][:, 0:512], R=[PF[5]], W=[wsT_b])
        V("memset", stage[:, 0:256], 0.0, W=[stage])
        for g in range(4):
            c = g // 2
            r0 = 64 * (g % 2)
            S.dma(stage[r0:r0 + 64, c * 128 + r0: c * 128 + r0 + 64], Wd["pool_w"][l][g], W=[stage], key=stage)
        S.dma(stage2[:, 0:256], Wd["pool_scale"][l].partition_broadcast(128), W=[stage2], key=stage2)
        V("tensor_tensor", poolw_b[:].rearrange("p c d -> p (c d)"), stage[:, 0:256], stage2[:, 0:256], ALU.mult,
          R=[stage, stage2], W=[poolw_b])

    def load_win(l, win_b):
        for k in range(8):
            yield
            for c0 in range(0, DIN, 1024):
                n = min(1024, DIN - c0)
                st = stage if ((k * 3 + c0 // 1024) % 2 == 0) else stage2
                S.dma(st[:, 0:n], Wd["w_in"][l][k * 128:(k + 1) * 128, c0:c0 + n], W=[st], key=st)
                V("tensor_scalar", win_b[:, k, c0:c0 + n], st[:, 0:n], ng[:, k:k + 1], None, ALU.mult, R=[st, ng], W=[win_b])

    def rstd_from_ss(ss_ap, out_ap, P, n, eps, Rb, Wb, tmp_ap):
        V("tensor_scalar", tmp_ap, ss_ap, 1.0 / n, eps, ALU.mult, ALU.add, R=Rb, W=Wb)
        A("activation", out=tmp_ap, in_=tmp_ap, func=AF.Sqrt, R=Wb, W=Wb)
        V("reciprocal", out_ap, tmp_ap, R=Wb, W=Wb)

    def transposes_b(src, P, widths, pb, dst_ap_fn, Rb, Wdst, evac="act"):
        for i, (ap, w) in enumerate(zip(src, widths)):
            M("transpose", pb[0:w, i * P:(i + 1) * P], ap, identb(P), R=Rb + [cb], W=[pb])

    def rope_apply(dst_a, dst_b, x1, x2, cosv, sinv, t1, t2, Rb, Wb, Tb):
        V("tensor_tensor", t1, x1, cosv, ALU.mult, R=Rb, W=Tb)
        V("tensor_tensor", t2, x2, sinv, ALU.mult, R=Rb, W=Tb)
        V("tensor_tensor", dst_a, t1, t2, ALU.subtract, R=Tb, W=Wb)
        V("tensor_tensor", t1, x1, sinv, ALU.mult, R=Rb + Wb, W=Tb)
        V("tensor_tensor", t2, x2, cosv, ALU.mult, R=Rb, W=Tb)
        V("tensor_tensor", dst_b, t1, t2, ALU.add, R=Tb, W=Wb)

    KEY_KT_SW = Buf("kT_sw")
    KEY_V_SW = Buf("vaug_sw")

    def kv_from_ckv(P, ckvf_ap, krf_ap, Rb, sc, tok0, Q=None):
        V("tensor_copy", ckv_b[0:P, :], ckvf_ap, R=Rb, W=[ckv_b])
        M("transpose", PB[1][:, 0:P], ckv_b[0:P, :], identb(P), R=[ckv_b, cb], W=[PB[1]])
        A("copy", ckvT[:, 0:P], PB[1][:, 0:P], R=[PB[1]], W=[ckvT])
        M("matmul", PF[1][0:P, :], ckvT[:, 0:P], wukv_b[:], start=True, stop=True, R=[ckvT, wukv_b], W=[PF[1]])
        A("activation", out=w1o[0:P, :], in_=PF[1][0:P, 0:256], func=AF.Square, R=[PF[1]], W=[w1o])
        V("tensor_reduce", st2o[0:P, 0:4], w1o[0:P, :].rearrange("p (h d) -> p h d", h=4), AX.X, ALU.add, R=[w1o], W=[st2o])
        rstd_from_ss(st2o[0:P, 0:4], st2o[0:P, 4:8], P, 64.0, 1e-6, [st2o], [st2o], st2o[0:P, 8:12])
        for h in range(4):
            V("scalar_tensor_tensor", kfull[0:P, h, 0:64], PF[1][0:P, 64 * h:64 * h + 64], st2o[0:P, 4 + h:5 + h],
              BCv("mla_k_norm_g", P, 0, 64), ALU.mult, ALU.mult, R=[PF[1], st2o, bc], W=[kfull])
            V("tensor_copy", kfull[0:P, h, 64:96], krf_ap, R=Rb, W=[kfull])
        V("tensor_copy", vaug[0:P, :, 0:64], PF[1][0:P, 256:512].rearrange("p (h d) -> p h d", h=4), R=[PF[1]], W=[vaug])
        for h in range(4):
            M("transpose", PB[1][0:96, h * P:(h + 1) * P], kfull[0:P, h, :], identb(P), R=[kfull, cb], W=[PB[1]])
        A("copy", kT[:, :, 0:P], PB[1][0:96, 0:4 * P].rearrange("p (h t) -> p h t", h=4), R=[PB[1]], W=[kT])
        S.dma(sc["kt"][:, :, tok0:tok0 + P].rearrange("h d t -> d h t"), kT[:, :, 0:P], R=[kT], key=(kT if Q is None else KEY_KT_SW), Q=Q)
        S.dma(sc["v"][tok0:tok0 + P, :], vaug[0:P, :, :].rearrange("p h d -> p (h d)"), R=[vaug], key=(vaug if Q is None else KEY_V_SW), Q=Q)

    def phase1_tile(l, grp, ti, P, xsrc, win_b, sc, first, last, tok0, nlev, st):
        xb_ = xt[ti % 2]
        rp_ = ropet[ti % 2]
        cosv = rp_[0:P, 0:64].rearrange("p (h d) -> p h d", h=4)
        sinv = rp_[0:P, 64:128].rearrange("p (h d) -> p h d", h=4)
        A("activation", out=junk[0:P, :], in_=xb_[0:P, :], func=AF.Square, accum_out=st1[0:P, 0:1], R=[xb_], W=[junk, st1])
        rstd_from_ss(st1[0:P, 0:1], st1[0:P, 1:2], P, float(D), 1e-6, [st1], [st1], st1[0:P, 2:3])
        V("tensor_scalar", xn_b[0:P, :], xb_[0:P, :], st1[0:P, 1:2], None, ALU.mult, R=[xb_, st1], W=[xn_b])
        for k in range(8):
            M("transpose", PB[0][:, k * P:(k + 1) * P], xn_b[0:P, k * 128:(k + 1) * 128], identb(P), R=[xn_b, cb], W=[PB[0]], inc=(k == 7))
        A("copy", xnT[:, :, 0:P], PB[0][:, 0:8 * P].rearrange("p (k t) -> p k t", k=8), R=[PB[0]], W=[xnT])

        def proj(bank, c0, n):
            for k in range(8):
                M("matmul", bank[0:P, 0:n], xnT[:, k, 0:P], win_b[:, k, c0:c0 + n], start=(k == 0), stop=(k == 7),
                  R=[xnT, win_b], W=[bank], inc=(k == 7))

        proj(PF[0], OFF_G, 512)
        proj(PF[1], OFF_G + 512, 512)
        A("activation", out=sg[0:P, 0:512], in_=PF[0][0:P, :], func=AF.Silu, R=[PF[0]], W=[sg])
        A("activation", out=sg[0:P, 512:1024], in_=PF[1][0:P, :], func=AF.Silu, R=[PF[1]], W=[sg])
        S.dma(sc["gb"][tok0:tok0 + P, :], sg[0:P, 256:512], R=[sg], key=sg)

        proj(PF[0], OFF_A, 512)
        proj(PF[1], OFF_A + 512, 384)
        tc_, tp_ = tmpm[ti % 2], tmpm[(ti + 1) % 2]
        V("tensor_tensor", tc_[0:P, 0:512], PF[0][0:P, 0:512], BCv("rw_mu", P, 0, 512), ALU.mult, R=[PF[0], bc], W=[tc_])
        V("tensor_tensor", tc_[0:P, 512:896], PF[1][0:P, 0:384], BCv("rw_mu", P, 512, 896), ALU.mult, R=[PF[1], bc], W=[tc_])
        if last:
            V("tensor_copy", za_f[0:P, 0:512], PF[0][0:P, 0:512], R=[PF[0]], W=[za_f])
            V("tensor_copy", za_f[0:P, 512:896], PF[1][0:P, 0:384], R=[PF[1]], W=[za_f])
            S.dma(st["o_shift"][l:l + 1, :], za_f[P - 1:P, :], R=[za_f], key=za_f)
        V("tensor_tensor", zs[0:P, 0:512], PF[0][0:P, 0:512], BCv("omm", P, 0, 512), ALU.mult, R=[PF[0], bc], W=[zs])
        V("tensor_tensor", zs[0:P, 512:896], PF[1][0:P, 0:384], BCv("omm", P, 512, 896), ALU.mult, R=[PF[1], bc], W=[zs])
        def gen_R():
            for (bank, c0, n) in ((PF[2], 0, 512), (PF[3], 512, 384)):
                M("matmul", bank[0:P, 0:n], cb[0:P, CB_SH:CB_SH + P], tc_[0:P, c0:c0 + n], start=True, stop=False, R=[cb, tc_], W=[bank])
                M("matmul", bank[0:P, 0:n], cb[:, CB_EL:CB_EL + P], tp_[:, c0:c0 + n], start=False, stop=True, R=[cb, tp_], W=[bank])
            V("tensor_add", zs[0:P, 0:512], zs[0:P, 0:512], PF[2][0:P, 0:512], R=[zs, PF[2]], W=[zs])
            V("tensor_add", zs[0:P, 512:896], zs[0:P, 512:896], PF[3][0:P, 0:384], R=[zs, PF[3]], W=[zs])
            r_ = zs[0:P, 0:256]
            k_ = zs[0:P, 256:512]
            v_ = zs[0:P, 512:768]
            A("activation", out=lin[0:P, 0:64], in_=zs[0:P, 768:832], func=AF.Tanh, R=[zs], W=[lin])
            A("copy", lin[0:P, 64:128], zs[0:P, 832:896], R=[zs], W=[lin])
            M("transpose", PB[0][:, 0:P], lin[0:P, :], identb(P), R=[lin, cb], W=[PB[0]])
            A("copy", linT[:, 0:P], PB[0][:, 0:P], R=[PB[0]], W=[linT])
            M("matmul", PF[2][0:P, :], linT[:, 0:P], w2a2_b[:], start=True, stop=True, R=[linT, w2a2_b], W=[PF[2]])
            V("tensor_add", w1[0:P, :], PF[2][0:P, 0:256], BCv("rw_w0", P), R=[PF[2], bc], W=[w1])
            A("activation", out=sw[0:P, :], in_=w1[0:P, :], func=AF.Sigmoid, R=[w1], W=[sw])
            V("tensor_add", w2[0:P, :], PF[2][0:P, 256:512], BCv("rw_a0", P), R=[PF[2], bc], W=[w2])
            A("activation", out=sa[0:P, :], in_=w2[0:P, :], func=AF.Sigmoid, R=[w2], W=[sa])
            V("tensor_tensor", kk[0:P, :], k_, BCv("rw_kk", P), ALU.mult, R=[zs, bc], W=[kk])
            V("tensor_tensor", w3[0:P, :], kk[0:P, :], kk[0:P, :], ALU.mult, R=[kk], W=[w3])
            V("tensor_reduce", st2[0:P, 0:4], w3[0:P, :].rearrange("p (h d) -> p h d", h=4), AX.X, ALU.add, R=[w3], W=[st2])
            V("tensor_scalar", st2[0:P, 4:8], st2[0:P, 0:4], 1e-24, None, ALU.max, R=[st2], W=[st2])
            A("activation", out=st2[0:P, 4:8], in_=st2[0:P, 4:8], func=AF.Sqrt, R=[st2], W=[st2])
            V("reciprocal", st2[0:P, 8:12], st2[0:P, 4:8], R=[st2], W=[st2])
            for h in range(4):
                V("tensor_scalar", kk[0:P, 64 * h:64 * h + 64], kk[0:P, 64 * h:64 * h + 64], st2[0:P, 8 + h:9 + h], None, ALU.mult,
                  R=[kk, st2], W=[kk])
            V("scalar_tensor_tensor", w1[0:P, :], sa[0:P, :], -1.0, BCv("rw_ka", P), ALU.add, ALU.mult, R=[sa, bc], W=[w1])
            V("scalar_tensor_tensor", kp[0:P, :], w1[0:P, :], 1.0, k_, ALU.add, ALU.mult, R=[w1, zs], W=[kp])
            V("tensor_tensor", bb[0:P, :], kk[0:P, :], sa[0:P, :], ALU.mult, R=[kk, sa], W=[bb])
            G("tensor_tensor", w2[0:P, :], r_, kp[0:P, :], ALU.mult, R=[zs, kp], W=[w2])
            G("tensor_tensor", w2[0:P, :], w2[0:P, :], BCv("rw_rk", P), ALU.mult, R=[w2, bc], W=[w2])
            V("tensor_reduce", bcf[0:P, 0:4], w2[0:P, :].rearrange("p (h d) -> p h d", h=4), AX.X, ALU.add, R=[w2], W=[bcf])
            M("matmul", PF[2][0:P, 0:256], cf[0:P, CF_TRI:CF_TRI + P], sw[0:P, :], start=True, stop=True, R=[cf, sw], W=[PF[2]])
            M("matmul", PF[2][0:P, 256:512], cf[0:P, CF_ONES:CF_ONES + P], sw[0:P, :], start=True, stop=True, R=[cf, sw], W=[PF[2]])
            V("tensor_copy", cs_sb[0:P, :], PF[2][0:P, 0:256], R=[PF[2]], W=[cs_sb])
            A("activation", out=e1[0:P, :], in_=PF[2][0:P, 0:256], func=AF.Exp, scale=CDEC, R=[PF[2]], W=[e1])
            A("activation", out=e2[0:P, :], in_=PF[2][0:P, 0:256], func=AF.Exp, scale=-CDEC, R=[PF[2]], W=[e2])
            V("tensor_sub", w1[0:P, :], cs_sb[0:P, :], sw[0:P, :], R=[cs_sb, sw], W=[w1])
            A("activation", out=e3[0:P, :], in_=w1[0:P, :], func=AF.Exp, scale=CDEC, R=[w1], W=[e3])
            V("tensor_sub", w3[0:P, :], PF[2][0:P, 256:512], cs_sb[0:P, :], R=[PF[2], cs_sb], W=[w3])
            A("activation", out=e4[0:P, :], in_=w3[0:P, :], func=AF.Exp, scale=CDEC, R=[w3], W=[e4])
            V("scalar_tensor_tensor", hat[0:P, 0, :], kk[0:P, :], -1.0, e3[0:P, :], ALU.mult, ALU.mult, R=[kk, e3], W=[hat])
            V("tensor_tensor", hat[0:P, 1, :], bb[0:P, :], e2[0:P, :], ALU.mult, R=[bb, e2], W=[hat])
            V("tensor_tensor", hat[0:P, 2, :], kp[0:P, :], e2[0:P, :], ALU.mult, R=[kp, e2], W=[hat])
            V("tensor_tensor", hat[0:P, 3, :], r_, e1[0:P, :], ALU.mult, R=[zs, e1], W=[hat])
            G("tensor_tensor", bp_b[0:P, :], bb[0:P, :], e4[0:P, :], ALU.mult, R=[bb, e4], W=[bp_b])
            G("tensor_tensor", kp4[0:P, :], kp[0:P, :], e4[0:P, :], ALU.mult, R=[kp, e4], W=[kp4])
            ohc = CF_OH + (0 if P == 128 else 1)
            for p in range(2):
                M("matmul", PF[3][:, p:p + 1], e1[0:P, p * 128:(p + 1) * 128], cf[0:P, ohc:ohc + 1], start=True, stop=True,
                  R=[e1, cf], W=[PF[3]])
            V("tensor_copy", wc[:, 0:2], PF[3][:, 0:2], R=[PF[3]], W=[wc])
            for vi in range(4):
                for p in range(2):
                    M("transpose", PB[0][:, (vi * 2 + p) * P:(vi * 2 + p + 1) * P], hat[0:P, vi, p * 128:(p + 1) * 128], identb(P),
                      R=[hat, cb], W=[PB[0]])
            A("copy", hT[:, :, :, 0:P], PB[0][:, 0:8 * P].rearrange("p (v q t) -> p v q t", v=4, q=2), R=[PB[0]], W=[hT])
            def fm(h, vi):
                return hT[64 * (h % 2):64 * (h % 2) + 64, vi, h // 2, 0:P]

            def pv(bank, w=128, n=None):
                n = P if n is None else n
                return bank[0:P, 0:4 * w].rearrange("p (h t) -> p h t", h=4)[:, :, 0:n]

            def msk(kind):
                return cbr[0:P, kind, :, 0:P]

            for h in range(4):
                M("matmul", PF[2][0:P, h * 128:h * 128 + P], fm(h, 0), fm(h, 1), start=True, stop=True, R=[hT], W=[PF[2]])
                M("matmul", PF[3][0:P, h * 128:h * 128 + P], fm(h, 1), fm(h, 0), start=True, stop=True, R=[hT], W=[PF[3]])
                M("matmul", PF[4][0:P, h * 128:h * 128 + P], fm(h, 0), fm(h, 2), start=True, stop=True, R=[hT], W=[PF[4]])
                M("matmul", PF[5][0:P, h * 128:h * 128 + P], fm(h, 1), fm(h, 3), start=True, stop=True, R=[hT], W=[PF[5]])
            V("tensor_tensor", scL[0:P, :, 0:P], pv(PF[2]), msk(0), ALU.mult, R=[PF[2], cbr], W=[scL])
            V("tensor_tensor", scN[0:P, :, 0:P], pv(PF[3]), msk(1), ALU.mult, R=[PF[3], cbr], W=[scN])
            for h in range(4):
                bk = PF[2 + (h % 2)]
                M("matmul", bk[0:P, h * 128:h * 128 + P], fm(h, 2), fm(h, 3), start=True, stop=True, R=[hT], W=[bk])
            V("tensor_add", Ttb[0][0:P, :, 0:P], scL[0:P, :, 0:P], msk(3), R=[scL, cbr], W=[Ttb[0]])
            V("tensor_tensor", scK[0:P, :, 0:P], pv(PF[4]), msk(0), ALU.mult, R=[PF[4], cbr], W=[scK])
            V("tensor_tensor", mbr_b[0:P, :, 0:P], pv(PF[5]), msk(2), ALU.mult, R=[PF[5], cbr], W=[mbr_b])
            for h in range(4):
                bk = PF[2 + (h % 2)]
                V("tensor_tensor", mkr_f[0:P, h, 0:P], bk[0:P, h * 128:h * 128 + P], cb[0:P, CB_UI:CB_UI + P], ALU.mult, R=[bk, cb], W=[mkr_f])
            def acc_(tb, idx):
                return (lambda h: tb[0:P, h, 0:P]) if idx is None else (lambda h: tb[0:P, idx, h, 0:P])

            def squares(Af, ATf, Rb, dst, need_A):
                if need_A:
                    for h in range(4):
                        M("matmul", PF[2][0:P, h * 128:h * 128 + P], ATf(h), Af(h), start=True, stop=True, R=Rb, W=[PF[2]], inc=(h == 3))
                for h in range(4):
                    M("matmul", PF[3][0:P, h * 128:h * 128 + P], Af(h), ATf(h), start=True, stop=True, R=Rb, W=[PF[3]], inc=(h == 3))

            def squares_evac(dst, need_A):
                if need_A:
                    V("tensor_copy", dst[0:P, 0, :, 0:P], pv(PF[2]), R=[PF[2]], W=[dst])
                A("copy", dst[0:P, 1, :, 0:P], pv(PF[3]), R=[PF[3]], W=[dst])

            tcur = 0
            squares(acc_(scL, None), acc_(scN, None), [scL, scN], Ab[1], nlev > 2)
            squares_evac(Ab[1], nlev > 2)
            for k in range(1, nlev):
                cur = Ab[k % 2]
                Af, ATf = acc_(cur, 0), acc_(cur, 1)
                lastk = (k == nlev - 1)
                pbank = PF[4 + (k % 2)]
                for h in range(4):
                    M("matmul", pbank[0:P, h * 128:h * 128 + P], ATf(h), Ttb[tcur][0:P, h, 0:P], start=True, stop=True,
                      R=[cur, Ttb[tcur]], W=[pbank], inc=(h == 3))
                if not lastk:
                    nxt = Ab[(k + 1) % 2]
                    squares(Af, ATf, [cur], nxt, k + 1 < nlev - 1)
                yield "mm"
                V("tensor_add", Ttb[1 - tcur][0:P, :, 0:P], pv(pbank), Ttb[tcur][0:P, :, 0:P], R=[pbank, Ttb[tcur]], W=[Ttb[1 - tcur]])
                if not lastk:
                    squares_evac(nxt, k + 1 < nlev - 1)
                tcur = 1 - tcur
                yield "ev"
            Tt = Ttb[tcur]
            for h in range(4):
                M("matmul", PF[2][0:P, h * 128:h * 128 + P], Tt[0:P, h, 0:P], mbr_b[0:P, h, 0:P], start=True, stop=True, R=[Tt, mbr_b], W=[PF[2]])
            for h in range(4):
                M("matmul", PF[3][0:P, h * 64:h * 64 + 64], Tt[0:P, h, 0:P], bp_b[0:P, 64 * h:64 * h + 64], start=True, stop=True, R=[Tt, bp_b], W=[PF[3]])
            A("copy", G_b[0:P, :, 0:P], pv(PF[2]), R=[PF[2]], W=[G_b])
            V("tensor_copy", H_b[0:P, :, :], PF[3][0:P, 0:256].rearrange("p (h d) -> p h d", h=4), R=[PF[3]], W=[H_b])
            for h in range(4):
                M("matmul", PF[2][0:P, h * 128:h * 128 + P], scK[0:P, h, 0:P], G_b[0:P, h, 0:P], start=True, stop=True, R=[scK, G_b], W=[PF[2]])
            for h in range(4):
                M("matmul", PF[3][:, h * 128:h * 128 + P], hat[0:P, 0, (h // 2) * 128:(h // 2) * 128 + 128], G_b[0:P, h, 0:P], start=True, stop=True,
                  R=[hat, G_b], W=[PF[3]])
            for h in range(4):
                M("matmul", PF[4][0:P, h * 64:h * 64 + 64], scK[0:P, h, 0:P], H_b[0:P, h, :], start=True, stop=True, R=[scK, H_b], W=[PF[4]])
            for h in range(4):
                M("matmul", PF[4][:, 256 + h * 64:256 + h * 64 + 64], hat[0:P, 0, (h // 2) * 128:(h // 2) * 128 + 128], H_b[0:P, h, :],
                  start=True, stop=True, R=[hat, H_b], W=[PF[4]])
            V("tensor_add", z_f[0:P, :, 0:P], pv(PF[2]), mkr_f[0:P, :, 0:P], R=[PF[2], mkr_f], W=[z_f])
            for o_ in (0, 64):
                h0 = o_ // 64
                qv = PF[3][o_:o_ + 64, 0:512].rearrange("p (q r t) -> p q r t", q=2, r=2)[:, :, h0, 0:P]
                V("tensor_add", q_f[o_:o_ + 64, :, 0:P], qv, hT[o_:o_ + 64, 3, :, 0:P], R=[PF[3], hT], W=[q_f])
            for h in range(4):
                p = h // 2
                o_ = 64 * (h % 2)
                V("tensor_add", x_f[0:P, h, o_:o_ + 64], PF[4][0:P, h * 64:h * 64 + 64], kp4[0:P, 64 * h:64 * h + 64], R=[PF[4], kp4], W=[x_f])
                V("scalar_tensor_tensor", p_f[o_:o_ + 64, p, o_:o_ + 64], cf[o_:o_ + 64, CF_ID + o_:CF_ID + o_ + 64], wc[o_:o_ + 64, p:p + 1],
                  PF[4][o_:o_ + 64, 256 + h * 64:256 + h * 64 + 64], ALU.mult, ALU.add, R=[cf, wc, PF[4]], W=[p_f])
            for h in range(4):
                p = h // 2
                o_ = 64 * (h % 2)
                vh = zs[0:P, 512 + 64 * h:512 + 64 * h + 64]
                M("matmul", PF[5][0:P, 64 * h:64 * h + 64], q_f[o_:o_ + 64, p, 0:P], ST[o_:o_ + 64, p, :], start=True, stop=False, R=[q_f, ST], W=[PF[5]])
                M("matmul", PF[5][0:P, 64 * h:64 * h + 64], z_f[0:P, h, 0:P], vh, start=False, stop=True, R=[z_f, zs], W=[PF[5]])
            for p in range(2):
                M("matmul", PF[5][:, 256 + 64 * p:256 + 64 * p + 64], p_f[:, p, :], ST[:, p, :], start=True, stop=False, R=[p_f, ST], W=[PF[5]])
                for h in (2 * p, 2 * p + 1):
                    M("matmul", PF[5][:, 256 + 64 * p:256 + 64 * p + 64], x_f[0:P, h, :], zs[0:P, 512 + 64 * h:512 + 64 * h + 64],
                      start=False, stop=(h == 2 * p + 1), R=[x_f, zs], W=[PF[5]])
            V("tensor_copy", ST[:, :, :], PF[5][:, 256:384].rearrange("k (p v) -> k p v", p=2), R=[PF[5]], W=[ST])
            Y = PF[5]
            V("tensor_reduce", st3[0:P, 0:4], Y[0:P, 0:256].rearrange("p (h d) -> p h d", h=4), AX.X, ALU.add, R=[Y], W=[st3])
            V("tensor_scalar", st3[0:P, 0:4], st3[0:P, 0:4], 1.0 / 64, None, ALU.mult, R=[st3], W=[st3])
            for h in range(4):
                V("tensor_scalar", yrw[0:P, 64 * h:64 * h + 64], Y[0:P, 64 * h:64 * h + 64], st3[0:P, h:h + 1], None, ALU.subtract,
                  R=[Y, st3], W=[yrw])
            V("tensor_tensor", w1[0:P, :], yrw[0:P, :], yrw[0:P, :], ALU.mult, R=[yrw], W=[w1])
            V("tensor_reduce", st3[0:P, 4:8], w1[0:P, :].rearrange("p (h d) -> p h d", h=4), AX.X, ALU.add, R=[w1], W=[st3])
            rstd_from_ss(st3[0:P, 4:8], st3[0:P, 8:12], P, 64.0, 64e-5, [st3], [st3], st3[0:P, 12:16])
            for h in range(4):
                V("tensor_scalar", yrw[0:P, 64 * h:64 * h + 64], yrw[0:P, 64 * h:64 * h + 64], st3[0:P, 8 + h:9 + h], None, ALU.mult,
                  R=[yrw, st3], W=[yrw])
            V("tensor_tensor", yrw[0:P, :], yrw[0:P, :], BCv("rw_gn_g", P), ALU.mult, R=[yrw, bc], W=[yrw])
            V("tensor_add", yrw[0:P, :], yrw[0:P, :], BCv("rw_gn_b", P), R=[yrw, bc], W=[yrw])
            for h in range(4):
                V("scalar_tensor_tensor", yrw[0:P, 64 * h:64 * h + 64], zs[0:P, 512 + 64 * h:512 + 64 * h + 64], bcf[0:P, h:h + 1],
                  yrw[0:P, 64 * h:64 * h + 64], ALU.mult, ALU.add, R=[zs, bcf, yrw], W=[yrw])
            V("tensor_tensor", ygp[0:P, 0:256], yrw[0:P, :], sg[0:P, 0:256], ALU.mult, R=[yrw, sg], W=[ygp])
            if last:
                for p in range(2):
                    M("transpose", PF[2][0:64, p * 128:(p + 1) * 128], ST[:, p, :], identf(128), R=[ST, cf], W=[PF[2]])
                V("tensor_copy", wkv_o[:].rearrange("i h j -> i (h j)"), PF[2][0:64, 0:256], R=[PF[2]], W=[wkv_o])
                S.dma(st["o_wkv"][l].rearrange("h i j -> i h j"), wkv_o[:], R=[wkv_o], key=wkv_o)
            yield

        def gen_O():
            proj(PF[0], OFF_B, 352)
            ZB = PF[0]
            A("activation", out=junk[0:P, 0:192], in_=ZB[0:P, 0:192], func=AF.Square, accum_out=st1[0:P, 4:5], R=[ZB], W=[junk, st1])
            yield
            A("activation", out=junk[0:P, 0:128], in_=ZB[0:P, 192:320], func=AF.Square, accum_out=st1[0:P, 5:6], R=[ZB], W=[junk, st1])
            A("activation", out=junk[0:P, 0:32], in_=ZB[0:P, 320:352], func=AF.Square, accum_out=st1[0:P, 6:7], R=[ZB], W=[junk, st1])
            V("tensor_tensor", st1[0:P, 12:15], st1[0:P, 4:7], invc[0:P, 8:11], ALU.mult, R=[st1, invc], W=[st1])
            V("tensor_scalar", st1[0:P, 12:15], st1[0:P, 12:15], 1e-6, None, ALU.add, R=[st1], W=[st1])
            A("activation", out=st1[0:P, 12:15], in_=st1[0:P, 12:15], func=AF.Sqrt, R=[st1], W=[st1])
            V("reciprocal", st1[0:P, 8:11], st1[0:P, 12:15], R=[st1], W=[st1])
            yield
            V("tensor_scalar", qa_b[0:P, :], ZB[0:P, 0:192], st1[0:P, 8:9], None, ALU.mult, R=[ZB, st1], W=[qa_b])
            yield
            M("transpose", PB[1][:, 0:P], qa_b[0:P, 0:128], identb(P), R=[qa_b, cb], W=[PB[1]])
            M("transpose", PB[1][0:64, P:2 * P], qa_b[0:P, 128:192], identb(P), R=[qa_b, cb], W=[PB[1]])
            A("copy", qaT[:, 0, 0:P], PB[1][:, 0:P], R=[PB[1]], W=[qaT])
            yield
            A("copy", qaT[0:64, 1, 0:P], PB[1][0:64, P:2 * P], R=[PB[1]], W=[qaT])
            M("matmul", PF[1][0:P, 0:384], qaT[:, 0, 0:P], wuq_b[:, 0, :], start=True, stop=False, R=[qaT, wuq_b], W=[PF[1]])
            M("matmul", PF[1][0:P, 0:384], qaT[:, 1, 0:P], wuq_b[:, 1, :], start=False, stop=True, R=[qaT, wuq_b], W=[PF[1]])
            yield
            A("activation", out=zb_f[0:P, :], in_=PF[1][0:P, 0:384], func=AF.Square, R=[PF[1]], W=[zb_f])
            sq3 = zb_f[0:P, :].rearrange("p (h d) -> p h d", h=4)
            V("tensor_reduce", st2o[0:P, 0:4], sq3[:, :, 0:64], AX.X, ALU.add, R=[zb_f], W=[st2o])
            yield
            V("tensor_reduce", st2o[0:P, 4:8], sq3[:, :, 64:96], AX.X, ALU.add, R=[zb_f], W=[st2o])
            V("tensor_tensor", st3o[0:P, 0:8], st2o[0:P, 0:8], invc[0:P, 11:19], ALU.mult, R=[st2o, invc], W=[st3o])
            V("tensor_scalar", st3o[0:P, 0:8], st3o[0:P, 0:8], 1e-6, None, ALU.add, R=[st3o], W=[st3o])
            A("activation", out=st3o[0:P, 0:8], in_=st3o[0:P, 0:8], func=AF.Sqrt, R=[st3o], W=[st3o])
            V("reciprocal", st2o[0:P, 8:16], st3o[0:P, 0:8], R=[st3o], W=[st2o])
            yield
            for h in range(4):
                V("scalar_tensor_tensor", qn[0:P, h, 0:64], PF[1][0:P, 96 * h:96 * h + 64], st2o[0:P, 8 + h:9 + h],
                  BCv("mla_q_norm_g", P, 0, 64), ALU.mult, ALU.mult, R=[PF[1], st2o, bc], W=[qn])
                V("scalar_tensor_tensor", qn[0:P, h, 64:96], PF[1][0:P, 96 * h + 64:96 * h + 96], st2o[0:P, 12 + h:13 + h],
                  BCv("mla_q_norm_g", P, 64, 96), ALU.mult, ALU.mult, R=[PF[1], st2o, bc], W=[qn])
            V("tensor_copy", qfb[0:P, :, 0:64], qn[0:P, :, 0:64], R=[qn], W=[qfb])
            t1 = w1o[0:P, 0:64].rearrange("p (h d) -> p h d", h=4)
            yield
            t2 = w2o[0:P, 0:64].rearrange("p (h d) -> p h d", h=4)
            rope_apply(qfb[0:P, :, 64:80], qfb[0:P, :, 80:96], qn[0:P, :, 64:80], qn[0:P, :, 80:96], cosv, sinv, t1, t2,
                       [qn, rp_], [qfb], [w1o, w2o])
            for h in range(4):
                M("transpose", PB[1][0:96, h * P:(h + 1) * P], qfb[0:P, h, :], identb(P), R=[qfb, cb], W=[PB[1]])
            yield
            A("copy", qT[:, :, 0:P], PB[1][0:96, 0:4 * P].rearrange("p (h t) -> p h t", h=4), R=[PB[1]], W=[qT])
            S.dma(sc["qt"][:, :, tok0:tok0 + P].rearrange("h d t -> d h t"), qT[:, :, 0:P], R=[qT], key=qT)
            V("scalar_tensor_tensor", ckv_f[0:P, :], ZB[0:P, 192:320], st1[0:P, 9:10], BCv("mla_kva_g", P), ALU.mult, ALU.mult,
              R=[ZB, st1, bc], W=[ckv_f])
            yield
            S.dma(st["o_ckv"][l, tok0:tok0 + P, :] if grp == "p" else st["o_ckv"][l, 0:P, :], ckv_f[0:P, :], R=[ckv_f], key=ckv_f)
            V("scalar_tensor_tensor", kr_f[0:P, :], ZB[0:P, 320:352], st1[0:P, 10:11], BCv("mla_k_norm_g", P, 64, 96), ALU.mult, ALU.mult,
              R=[ZB, st1, bc], W=[kr_f])
            rope_apply(kr_r[0:P, 0:16], kr_r[0:P, 16:32], kr_f[0:P, 0:16], kr_f[0:P, 16:32], rp_[0:P, 0:16], rp_[0:P, 64:80],
                       w1o[0:P, 0:16], w2o[0:P, 0:16], [kr_f, rp_], [kr_r], [w1o, w2o])
            yield
            S.dma(st["o_kr"][l, tok0:tok0 + P, :] if grp == "p" else st["o_kr"][l, 0:P, :], kr_r[0:P, :], R=[kr_r], key=kr_r)
            kv_from_ckv(P, ckv_f[0:P, :], kr_r[0:P, :], [ckv_f, kr_r], sc, st["ktok0"] + tok0)

            proj(PF[0], OFF_C, 512)
            yield
            ZC = PF[0]
            V("tensor_tensor", ug[0:P, :], ZC[0:P, 0:256], sg[0:P, 512:768], ALU.mult, R=[ZC, sg], W=[ug])
            V("bn_stats", bnst[0:P, 0:6], ZC[0:P, 256:512], R=[ZC], W=[bnst])
            yield
            V("bn_aggr", bnst[0:P, 6:8], bnst[0:P, 0:6], R=[bnst], W=[bnst])
            rstd_from_ss(bnst[0:P, 7:8], st1[0:P, 3:4], P, 1.0, 1e-5, [bnst], [st1], st1[0:P, 15:16])
            V("tensor_scalar", vn[0:P, :], ZC[0:P, 256:512], bnst[0:P, 6:7], st1[0:P, 3:4], ALU.subtract, ALU.mult, R=[ZC, bnst, st1], W=[vn])
            yield
            V("tensor_tensor", vn[0:P, :], vn[0:P, :], BCv("sgu_ln_g", P), ALU.mult, R=[vn, bc], W=[vn])
            V("tensor_add", vn[0:P, :], vn[0:P, :], BCv("sgu_ln_b", P), R=[vn, bc], W=[vn])
            if grp == "s":
                S.dma(o_sgv[l, 0:P, :], vn[0:P, :], R=[vn], key=vn)
            yield
            V("tensor_copy", vn_b[0:P, :], vn[0:P, :], R=[vn], W=[vn_b])
            for h in range(4):
                M("matmul", PF[1][0:P, 64 * h:64 * h + 64], wsT_b[0:P, h, 0:P], vn_b[0:P, 64 * h:64 * h + 64], start=True, stop=True,
                  R=[wsT_b, vn_b], W=[PF[1]])
            for h in range(4):
                V("scalar_tensor_tensor", ygp[0:P, 256 + 64 * h:256 + 64 * h + 64], PF[1][0:P, 64 * h:64 * h + 64], sgub[0:P, h:h + 1],
                  ug[0:P, 64 * h:64 * h + 64], ALU.add, ALU.mult, R=[PF[1], sgub, ug], W=[ygp])

            yield
            proj(PF[0], OFF_D, 256)
            zc_, zp_ = zd_f[ti % 2], zd_f[(ti + 1) % 2]
            V("tensor_copy", zc_[0:P, :], PF[0][0:P, 0:256], R=[PF[0]], W=[zc_])
            yield
            if last:
                if grp == "p":
                    S.dma(st["o_pool"][l], zc_[P - 15:P, :], R=[zc_], key=zc_)
                else:
                    S.dma(st["o_pool"][l], zc_[1:16, :], R=[zc_], key=zc_)
            for g in range(4):
                M("matmul", PF[1][0:P, 64 * g:64 * g + 64], cf[0:P, CF_BAND + 128 * g:CF_BAND + 128 * g + P], zc_[0:P, 64 * g:64 * g + 64],
                  start=True, stop=False, R=[cf, zc_], W=[PF[1]])
                M("matmul", PF[1][0:P, 64 * g:64 * g + 64], cf[:, CF_BANDP + 128 * g:CF_BANDP + 128 * g + P], zp_[:, 64 * g:64 * g + 64],
                  start=False, stop=True, R=[cf, zp_], W=[PF[1]])
            for g in range(4):
                ic = invc[0:P, g:g + 1] if (first and grp == "p") else invc[0:P, 4 + g:5 + g]
                V("scalar_tensor_tensor", d_b[0:P, 64 * g:64 * g + 64], PF[1][0:P, 64 * g:64 * g + 64], ic, zc_[0:P, 64 * g:64 * g + 64],
                  ALU.mult, ALU.subtract, R=[PF[1], invc, zc_], W=[d_b])
            yield
            for c in range(2):
                M("transpose", PB[1][:, c * P:(c + 1) * P], d_b[0:P, c * 128:(c + 1) * 128], identb(P), R=[d_b, cb], W=[PB[1]])
            A("copy", dT[:, :, 0:P], PB[1][:, 0:2 * P].rearrange("p (c t) -> p c t", c=2), R=[PB[1]], W=[dT])
            for c in range(2):
                M("matmul", PF[0][0:P, c * 128:(c + 1) * 128], dT[:, c, 0:P], poolw_b[:, c, :], start=True, stop=True, R=[dT, poolw_b], W=[PF[0]])
            yield
            V("tensor_tensor", ygp[0:P, 512:768], PF[0][0:P, 0:256], sg[0:P, 768:1024], ALU.mult, R=[PF[0], sg], W=[ygp])
            yield

        gr, go = gen_R(), gen_O()
        if P == 128 and ILV > 0:
            for tag in gr:
                if tag == "mm":
                    for _k in range(ILV):
                        next(go, None)
                elif tag == "ev":
                    for _k in range(ILV2):
                        next(go, None)
        for _ in gr:
            pass
        for _ in go:
            pass
        S.dma(sc["yg"][tok0:tok0 + P, :], ygp[0:P, :], R=[ygp], key=ygp)

    def phase2(l, grp, Tq, QB, nkt_total, klast, xsrc, ydst, sc, KT, VV, bufs):
        (QTb, PT, yfull4, yT, gbt4, xres, xout, rr) = bufs
        nqb = Tq // QB
        nsub = max(1, QB // 128)
        Pq = min(QB, 128)
        for qb in range(nqb):
            q0 = qb * QB
            S.dma(QTb[:, :, 0:QB], sc["qt"][:, :, q0:q0 + QB].rearrange("h d t -> d h t"), W=[QTb], key=QTb)
            for i in range(nsub):
                t0 = q0 + i * 128
                S.dma(yfull4[i][0:Pq, 0:256], sc["yg"][t0:t0 + Pq, 0:256], W=[yfull4[i]], key=yfull4[i])
                S.dma(yfull4[i][0:Pq, 512:1024], sc["yg"][t0:t0 + Pq, 256:768], W=[yfull4[i]], key=yfull4[i])
                S.dma(gbt4[i][0:Pq, :], sc["gb"][t0:t0 + Pq, :], W=[gbt4[i]], key=gbt4[i])
            if grp == "p":
                nkt = 4 * qb + 4
            else:
                nkt = nkt_total
            def kinfo(kt):
                kp_ = 128 if (grp == "p" or kt < nkt_total - 1) else klast
                jd = kt - 4 * qb if grp == "p" else -1
                c0 = 128 * jd if jd > 0 else 0
                return kp_, jd, c0

            def emit_scores(h, kt, j):
                kp_, jd, c0 = kinfo(kt)
                n = QB - c0
                sbk = (PF[4], PF[5], PB[1])[j % 3]
                sview = sbk[0:kp_, 0:n] if (j % 3) < 2 else PB[1][0:kp_, 0:1024].bitcast(F32)[:, 0:n]
                M("matmul", sview, KT[:, h, kt * 128:kt * 128 + kp_], QTb[:, h, c0:QB], start=True, stop=True,
                  R=[KT, QTb], W=[sbk])
                pt_ = PT[j % 3]
                A("activation", out=pt_[0:kp_, c0:QB], in_=sview, func=AF.Exp, scale=float(1.0 / np.sqrt(96.0)),
                  R=[sbk], W=[pt_])
                if jd >= 0:
                    G("memset", pt_[64:128, c0:c0 + 64], 0.0, W=[pt_])

            def emit_pv(h, kt, j):
                kp_, jd, c0 = kinfo(kt)
                pt_ = PT[j % 3]
                for i in range(c0 // 128, nsub):
                    first_k = (kt == 0)
                    last_k = (kt == (4 * qb + i if grp == "p" else nkt - 1))
                    M("matmul", PF[i][0:Pq, 0:65], pt_[0:kp_, i * 128:i * 128 + Pq], VV[0:kp_, kt, 65 * h:65 * h + 65],
                      start=first_k, stop=last_k, R=[pt_, VV], W=[PF[i]], inc=(i == nsub - 1))

            steps = [(h, kt) for h in range(4) for kt in range(nkt)]
            for j in range(min(2, len(steps))):
                emit_scores(steps[j][0], steps[j][1], j)
            for j, (h, kt) in enumerate(steps):
                if j + 2 < len(steps):
                    emit_scores(steps[j + 2][0], steps[j + 2][1], j + 2)
                emit_pv(h, kt, j)
                if kt == nkt - 1:
                    for i in range(nsub):
                        V("reciprocal", rr[0:Pq, h:h + 1], PF[i][0:Pq, 64:65], R=[PF[i]], W=[rr])
                        V("scalar_tensor_tensor", yfull4[i][0:Pq, 256 + 64 * h:256 + 64 * h + 64], PF[i][0:Pq, 0:64], rr[0:Pq, h:h + 1],
                          gbt4[i][0:Pq, 64 * h:64 * h + 64], ALU.mult, ALU.mult, R=[PF[i], rr, gbt4[i]], W=[yfull4[i]])
            for i in range(nsub):
                t0 = q0 + i * 128
                yfull = yfull4[i]
                S.dma(xres[0:Pq, :], xsrc[t0:t0 + Pq, :], W=[xres], key=xres)
                for k in range(8):
                    M("transpose", PB[0][:, k * Pq:(k + 1) * Pq], yfull[0:Pq, k * 128:(k + 1) * 128], identb(Pq), R=[yfull, cb], W=[PB[0]], inc=(k == 7))
                A("copy", yT[:, :, 0:Pq], PB[0][:, 0:8 * Pq].rearrange("p (k t) -> p k t", k=8), R=[PB[0]], W=[yT])
                for cbk in range(2):
                    bank = PF[4 + cbk]
                    for k in range(8):
                        M("matmul", bank[0:Pq, :], yT[:, k, 0:Pq], wout_b[:, k, cbk * 512:(cbk + 1) * 512], start=(k == 0), stop=(k == 7),
                          R=[yT, wout_b], W=[bank], inc=(k == 7))
                    V("tensor_add", xout[0:Pq, cbk * 512:(cbk + 1) * 512], bank[0:Pq, :], xres[0:Pq, cbk * 512:(cbk + 1) * 512],
                      R=[bank, xres], W=[xout])
                S.dma(ydst[t0:t0 + Pq, :], xout[0:Pq, :], R=[xout], key=xout, Q=S.pool)

    stp = dict(o_shift=o_shp, o_wkv=o_wkvp, o_pool=o_plp, o_ckv=o_ckvp, o_kr=o_krp, ktok0=0)
    sts = dict(o_shift=o_shs, o_wkv=o_wkvs, o_pool=o_pls, o_ckv=o_ckvs, o_kr=o_krs, ktok0=PAST)
    nlev_p = 7
    nlev_s = 4
    for l in range(NL):
        load_small(l)
        xsrc_p = x_p if l == 0 else o_yp
        xsrc_s = x_s if l == 0 else o_ys
        with ExitStack() as c1:
            alloc_work(c1)
            win_b = sb("win_b", [128, 8, DIN], BF16, c1)
            def cache_tiles():
                for kt in range(PAST // 128):
                    cs_ = cstage[kt % 2]
                    S.dma(cs_[:, 0:128], c_ckv[l, kt * 128:(kt + 1) * 128, :], W=[cs_], key=cs_)
                    S.dma(cs_[:, 128:160], c_kr[l, kt * 128:(kt + 1) * 128, :], W=[cs_], key=cs_)
                    kv_from_ckv(128, cs_[:, 0:128], cs_[:, 128:160], [cs_], scr_s, kt * 128, Q=S.pool)
                    yield
            gw = load_win(l, win_b)
            gc = cache_tiles() if do_sample else iter(())
            for _ in gw:
                next(gc, None)
                next(gc, None)
            for _ in gc:
                pass
            V("memset", tmpm[1][:], 0.0, W=[tmpm[1]])
            V("memset", zd_f[1][:], 0.0, W=[zd_f[1]])
            V("memset", ST[:], 0.0, W=[ST])
            S.dma(xt[0][:], xsrc_p[0:128, :], W=[xt[0]], key=xt[0])
            S.dma(ropet[0][:], rope_p[0:128, :], W=[ropet[0]], key=ropet[0])
            for ti in range(NT):
                if ti + 1 < NT:
                    S.dma(xt[(ti + 1) % 2][:], xsrc_p[(ti + 1) * 128:(ti + 2) * 128, :], W=[xt[(ti + 1) % 2]], key=xt[(ti + 1) % 2])
                    S.dma(ropet[(ti + 1) % 2][:], rope_p[(ti + 1) * 128:(ti + 2) * 128, :], W=[ropet[(ti + 1) % 2]], key=ropet[(ti + 1) % 2])
                phase1_tile(l, "p", ti, 128, xsrc_p, win_b, scr_p, ti == 0, ti == NT - 1, ti * 128, nlev_p, stp)
            if do_sample:
                P = TS
                V("memset", stage[:, 0:896], 0.0, W=[stage])
                S.dma(stage[127:128, 0:896], st_shift[l:l + 1, :], W=[stage], key=stage)
                V("tensor_tensor", tmpm[1][:, :], stage[:, 0:896], BCv("rw_mu", 128), ALU.mult, R=[stage, bc], W=[tmpm[1]])
                V("memset", zd_f[1][:], 0.0, W=[zd_f[1]])
                S.dma(zd_f[1][113:128, :], st_pool[l], W=[zd_f[1]], key=zd_f[1])
                S.dma(stage2[0:64, 0:256].rearrange("i (h j) -> i h j", h=4), st_wkv[l].rearrange("h i j -> i h j"), W=[stage2], key=stage2)
                for p in range(2):
                    M("transpose", PF[2][:, p * 64:(p + 1) * 64], stage2[0:64, p * 128:(p + 1) * 128], identf(64), R=[stage2, cf], W=[PF[2]])
                V("tensor_copy", ST[:].rearrange("k p v -> k (p v)"), PF[2][:, 0:128], R=[PF[2]], W=[ST])
                S.dma(xt[0][0:P, :], xsrc_s[0:P, :], W=[xt[0]], key=xt[0])
                S.dma(ropet[0][0:P, :], rope_s[0:P, :], W=[ropet[0]], key=ropet[0])
                phase1_tile(l, "s", 0, P, xsrc_s, win_b, scr_s, True, True, 0, nlev_s, sts)
            S.barrier()
        with ExitStack() as c2:
            KT = sb("KT", [96, 4, max(T, PAST + 128)], BF16, c2)
            VV = sb("VV", [128, max(NT, PAST // 128 + 1), 260], BF16, c2)
            QTb = sb("QTb", [96, 4, 512], BF16, c2)
            PT = [sb("PT%d" % i, [128, 512], BF16, c2) for i in range(3)]
            yfull4 = [sb("yfull%d" % i, [128, D], BF16, c2) for i in range(4)]
            yT = sb("yT", [128, 8, 128], BF16, c2)
            gbt4 = [sb("gbt%d" % i, [128, 256], BF16, c2) for i in range(4)]
            xres = sb("xres", [128, D], F32, c2)
            xout = sb("xout", [128, D], F32, c2)
            rr = sb("rr", [128, 4], F32, c2)
            bufs = (QTb, PT, yfull4, yT, gbt4, xres, xout, rr)
            for h in range(4):
                S.dma(KT[:, h, 0:T], scr_p["kt"][h], W=[KT], key=KT)
            vview = scr_p["v"].rearrange("(n p) c -> p n c", p=128)
            for n0 in range(0, NT, 8):
                n1 = min(NT, n0 + 8)
                S.dma(VV[:, n0:n1, :], vview[:, n0:n1, :], W=[VV], key=VV)
            phase2(l, "p", T, 512, NT, 128, xsrc_p, o_yp, scr_p, KT, VV, bufs)
            if do_sample:
                NKS = PAST // 128 + 1
                for h in range(4):
                    S.dma(KT[:, h, 0:PAST + TS], scr_s["kt"][h][:, 0:PAST + TS], W=[KT], key=KT)
                vview = scr_s["v"].rearrange("(n p) c -> p n c", p=128)
                for n0 in range(0, NKS - 1, 8):
                    n1 = min(NKS - 1, n0 + 8)
                    S.dma(VV[:, n0:n1, :], vview[:, n0:n1, :], W=[VV], key=VV)
                S.dma(VV[0:TS, NKS - 1, :], scr_s["v"][PAST:PAST + TS, :], W=[VV], key=VV)
                phase2(l, "s", TS, TS, NKS, TS, xsrc_s, o_ys, scr_s, KT, VV, bufs)
            S.barrier()
    S.final_wait()
    ctx.close()
    return nc, S.ninst


_CACHE = {}


def kernel(**inputs):
    x_prompt = np.asarray(inputs["x_prompt"], np.float32)
    x_sample = np.asarray(inputs["x_sample"], np.float32)
    B, T, _ = x_prompt.shape
    NL = inputs["norm_g"].shape[0]
    key = (T, NL)
    if key not in _CACHE:
        _CACHE[key] = build(T, NL)[0]
    nc = _CACHE[key]
    cf, cb, invc = make_consts()
    rope_p = rope_table(np.arange(T))
    rope_s = rope_table(PAST + np.arange(TS))
    in_maps = []
    for c in range(8):
        m = {
            "x_p": np.ascontiguousarray(x_prompt[c // 2]),
            "x_s": np.ascontiguousarray(x_sample[c]),
            "c_ckv": np.ascontiguousarray(np.asarray(inputs["cache_ckv"], np.float32)[:, c]),
            "c_kr": np.ascontiguousarray(np.asarray(inputs["cache_krope"], np.float32)[:, c]),
            "st_wkv": np.ascontiguousarray(np.asarray(inputs["state_wkv"], np.float32)[:, c]),
            "st_shift": np.ascontiguousarray(np.asarray(inputs["state_shift"], np.float32)[:, c]),
            "st_pool": np.ascontiguousarray(np.asarray(inputs["state_pool"], np.float32)[:, c]),
            "cf": cf, "cb": cb, "invc": invc, "rope_p": rope_p, "rope_s": rope_s,
        }
        for n in WNAMES:
            m[n] = np.ascontiguousarray(np.asarray(inputs[n], np.float32).reshape([NL] + WSHAPES[n]))
        in_maps.append(m)
    res = run_bass_kernel_spmd(nc, in_maps, core_ids=list(range(8)))
    R = res.results
    pc = [R[2 * b] for b in range(B)]
    sc = [R[c] for c in range(8)]
    stk = lambda L, n, ax: np.stack([np.asarray(r[n], np.float32) for r in L], axis=ax)
    out = (
        stk(pc, "o_yp", 0), stk(sc, "o_ys", 0),
        stk(pc, "o_ckvp", 1), stk(pc, "o_krp", 1), stk(pc, "o_wkvp", 1), stk(pc, "o_shp", 1), stk(pc, "o_plp", 1),
        stk(sc, "o_ckvs", 1), stk(sc, "o_krs", 1), stk(sc, "o_wkvs", 1), stk(sc, "o_shs", 1), stk(sc, "o_pls", 1),
        stk(sc, "o_sgv", 1),
    )
    return out
```
